# Optimizing a Trainium2 kernel written in Bass

```python
import jax
import jax.numpy as jnp
from jax import lax
import numpy as np

D_MODEL = 1024
BATCH = 4
SEQ = 8192
DEPTH = 2

GRID_W = 64
CTX_LEN = 256
N_MIXERS = 2
EPS = 1e-6
NA_HEADS = 16
NA_HEAD_DIM = D_MODEL // NA_HEADS
WIN_H = 8
WIN_W = 16
HG_HEADS = 8
HG_EXPAND = 128
HG_FDIM = HG_HEADS * HG_EXPAND
HG_HEAD_V = D_MODEL // HG_HEADS
CHUNK = 64
D_FF = -(-(8 * D_MODEL) // (3 * 256)) * 256
N_NA_LAYERS = (DEPTH + 1) // 2
N_HG_LAYERS = DEPTH // 2

kernel_name = 'hybrid_na_hgrn2_dit_block'


def rms_norm(x, g):
    xf = x.astype(jnp.float32)
    y = xf * lax.rsqrt(jnp.mean(xf * xf, axis=-1, keepdims=True) + EPS)
    return (y * g.astype(jnp.float32)).astype(x.dtype)


def ada_mod(cvec, w, b):
    return jnp.split(jax.nn.silu(cvec) @ w + b, 6, axis=-1)


def swiglu(h, w_in, w_out):
    a, u = jnp.split(h @ w_in, 2, axis=-1)
    return (jax.nn.silu(a) * u) @ w_out


def na_mixer(h_lat, h_ctx, w_qkv, w_o, q_gain, k_gain, rpb, need_ctx_out):
    B, T, _ = h_lat.shape
    rows = T // GRID_W
    kh = min(WIN_H, rows)
    scale = NA_HEAD_DIM ** -0.5

    def qkv(h):
        n = h.shape[1]
        q, k, v = jnp.split(h @ w_qkv, 3, axis=-1)
        q = rms_norm(q.reshape(B, n, NA_HEADS, NA_HEAD_DIM), q_gain) * scale
        k = rms_norm(k.reshape(B, n, NA_HEADS, NA_HEAD_DIM), k_gain)
        return q, k, v.reshape(B, n, NA_HEADS, NA_HEAD_DIM)

    q_c, k_c, v_c = qkv(h_ctx)
    q_l, k_l, v_l = qkv(h_lat)

    out_ctx = None
    if need_ctx_out:
        s = jnp.einsum('bqhd,bkhd->bhqk', q_c, k_c)
        p = jax.nn.softmax(s.astype(jnp.float32), axis=-1).astype(v_c.dtype)
        o_c = jnp.einsum('bhqk,bkhd->bqhd', p, v_c)
        out_ctx = o_c.reshape(B, h_ctx.shape[1], D_MODEL) @ w_o

    grid = lambda a: a.reshape(B, rows, GRID_W, NA_HEADS, NA_HEAD_DIM)
    q_g, k_g, v_g = grid(q_l), grid(k_l), grid(v_l)
    row_start = jnp.clip(jnp.arange(rows) - kh // 2, 0, rows - kh)
    cols = jnp.arange(GRID_W)
    col_idx = jnp.clip(cols - WIN_W // 2, 0, GRID_W - WIN_W)[:, None] + jnp.arange(WIN_W)
    rpb_x = rpb[:, :, col_idx - cols[:, None] + (WIN_W - 1)]
    n_loc = kh * WIN_W

    def row_block(r):
        rs = row_start[r]
        q_row = lax.dynamic_index_in_dim(q_g, r, axis=1, keepdims=False)
        k_win = lax.dynamic_slice_in_dim(k_g, rs, kh, axis=1)[:, :, col_idx]
        v_win = lax.dynamic_slice_in_dim(v_g, rs, kh, axis=1)[:, :, col_idx]
        bias = rpb_x[:, rs + jnp.arange(kh) - r + (WIN_H - 1)]
        s_loc = jnp.einsum('bqhd,bjqwhd->bhqjw', q_row, k_win) + bias.transpose(0, 2, 1, 3)[None]
        s_ctx = jnp.einsum('bqhd,bkhd->bhqk', q_row, k_c)
        s = jnp.concatenate([s_loc.reshape(B, NA_HEADS, GRID_W, n_loc), s_ctx], axis=-1)
        p = jax.nn.softmax(s.astype(jnp.float32), axis=-1).astype(v_win.dtype)
        p_loc = p[..., :n_loc].reshape(B, NA_HEADS, GRID_W, kh, WIN_W)
        return (jnp.einsum('bhqjw,bjqwhd->bqhd', p_loc, v_win)
                + jnp.einsum('bhqk,bkhd->bqhd', p[..., n_loc:], v_c))

    o = lax.map(row_block, jnp.arange(rows))
    out_lat = o.transpose(1, 0, 2, 3, 4).reshape(B, T, D_MODEL) @ w_o
    return out_lat, out_ctx


def forget_gate(f_pre, lb):
    f32 = f_pre.astype(jnp.float32)
    log_f = jnp.logaddexp(jnp.log(lb), jnp.log1p(-lb) + jax.nn.log_sigmoid(f32))
    k = (1.0 - lb) * jax.nn.sigmoid(-f32)
    return k.astype(f_pre.dtype), log_f


def chunked_gated_scan(q, k, v, log_f, s0):
    B, N, H, _ = q.shape
    dv = v.shape[-1]
    nc = N // CHUNK
    chunks = lambda a: a.reshape(B, nc, CHUNK, H, a.shape[-1]).transpose(1, 0, 3, 2, 4)
    lower = jnp.tril(jnp.ones((CHUNK, CHUNK), dtype=bool))[:, :, None]

    def step(S, xs):
        qc, kc, vc, gc = xs
        b = jnp.cumsum(gc, axis=2)
        decay = jnp.exp(jnp.where(lower, b[:, :, :, None, :] - b[:, :, None, :, :], -jnp.inf))
        a = jnp.einsum('bhtk,bhtsk,bhsk->bhts', qc, decay, kc)
        o = jnp.einsum('bhts,bhsv->bhtv', a, vc) + jnp.einsum('bhtk,bhkv->bhtv', qc * jnp.exp(b), S)
        b_end = b[:, :, -1]
        S_new = (jnp.exp(b_end)[..., None] * S
                 + jnp.einsum('bhsk,bhsv->bhkv', kc * jnp.exp(b_end[:, :, None] - b), vc))
        return S_new, o

    s_fin, o = lax.scan(step, s0, (chunks(q), chunks(k), chunks(v), chunks(log_f)))
    return o.transpose(1, 0, 3, 2, 4).reshape(B, N, H, dv).astype(v.dtype), s_fin


def hgrn2_mixer(h_lat, h_ctx, w_in, lb, norm_g, w_o, need_ctx_out):
    B = h_lat.shape[0]
    splits = [HG_FDIM, HG_FDIM + D_MODEL, HG_FDIM + 2 * D_MODEL, 2 * HG_FDIM + 2 * D_MODEL]

    def project(h):
        n = h.shape[1]
        q, v, g, f_fwd, f_bwd = jnp.split(h @ w_in, splits, axis=-1)
        heads = lambda a: a.reshape(B, n, HG_HEADS, -1)
        return heads(jax.nn.silu(q)), heads(v), g, heads(f_fwd), heads(f_bwd)

    def bidirectional(q, v, f_fwd, f_bwd, s_fwd, s_bwd):
        k_f, lf_f = forget_gate(f_fwd, lb[0])
        k_b, lf_b = forget_gate(f_bwd, lb[1])
        o_f, s_fwd = chunked_gated_scan(q, k_f, v, lf_f, s_fwd)
        o_b, s_bwd = chunked_gated_scan(q[:, ::-1], k_b[:, ::-1], v[:, ::-1], lf_b[:, ::-1], s_bwd)
        return o_f + o_b[:, ::-1], s_fwd, s_bwd

    def readout(o, g):
        n = o.shape[1]
        return (rms_norm(o, norm_g).reshape(B, n, D_MODEL) * jax.nn.silu(g)) @ w_o

    zero = jnp.zeros((B, HG_HEADS, HG_EXPAND, HG_HEAD_V), jnp.float32)
    q_c, v_c, g_c, ff_c, fb_c = project(h_ctx)
    o_c, s_fwd, s_bwd = bidirectional(q_c, v_c, ff_c, fb_c, zero, zero)
    q_l, v_l, g_l, ff_l, fb_l = project(h_lat)
    o_l, _, _ = bidirectional(q_l, v_l, ff_l, fb_l, s_fwd, s_bwd)
    out_lat = readout(o_l, g_l)
    out_ctx = readout(o_c, g_c) if need_ctx_out else None
    return out_lat, out_ctx


def setup_inputs(seed: int = 0) -> dict:
    key = jax.random.key(seed)
    ks = jax.random.split(key, 20)
    f32 = jnp.float32

    def w(k, shape, fan_in):
        return jax.random.normal(k, shape, f32) * fan_in ** -0.5

    def gain(k, shape):
        return 1.0 + 0.02 * jax.random.normal(k, shape, f32)

    return {
        'x': jax.random.normal(ks[0], (BATCH, SEQ, D_MODEL), f32),
        'c': jax.random.normal(ks[1], (BATCH, D_MODEL), f32),
        'ctx': jax.random.normal(ks[2], (BATCH, CTX_LEN, D_MODEL), f32),
        'c_ctx': jax.random.normal(ks[3], (D_MODEL,), f32),
        'ada_w': w(ks[4], (DEPTH, D_MODEL, 6 * D_MODEL), D_MODEL),
        'ada_b': 0.02 * jax.random.normal(ks[5], (DEPTH, 6 * D_MODEL), f32),
        'norm1_g': gain(ks[6], (DEPTH, D_MODEL)),
        'norm2_g': gain(ks[7], (DEPTH, D_MODEL)),
        'na_w_qkv': w(ks[8], (N_NA_LAYERS, D_MODEL, 3 * D_MODEL), D_MODEL),
        'na_w_o': w(ks[9], (N_NA_LAYERS, D_MODEL, D_MODEL), D_MODEL),
        'na_q_gain': gain(ks[10], (N_NA_LAYERS, NA_HEAD_DIM)),
        'na_k_gain': gain(ks[11], (N_NA_LAYERS, NA_HEAD_DIM)),
        'na_rpb': 0.2 * jax.random.normal(ks[12], (N_NA_LAYERS, NA_HEADS, 2 * WIN_H - 1, 2 * WIN_W - 1), f32),
        'hg_w_in': w(ks[13], (N_HG_LAYERS, D_MODEL, 3 * HG_FDIM + 2 * D_MODEL), D_MODEL),
        'hg_lower': 0.5 * jax.random.normal(ks[14], (DEPTH, 2, HG_FDIM), f32),
        'hg_norm_g': gain(ks[15], (N_HG_LAYERS, HG_HEAD_V)),
        'hg_w_o': w(ks[16], (N_HG_LAYERS, D_MODEL, D_MODEL), D_MODEL),
        'ffn_w_in': w(ks[17], (DEPTH, D_MODEL, 2 * D_FF), D_MODEL),
        'ffn_w_out': w(ks[18], (DEPTH, D_FF, D_MODEL), D_FF),
    }


def reference(x, c, ctx, c_ctx, ada_w, ada_b, norm1_g, norm2_g, na_w_qkv, na_w_o, na_q_gain,
              na_k_gain, na_rpb, hg_w_in, hg_lower, hg_norm_g, hg_w_o, ffn_w_in, ffn_w_out):
    lbs = jnp.cumsum(jax.nn.softmax(hg_lower.astype(jnp.float32), axis=0), axis=0)
    lbs = (lbs - lbs[:1]).reshape(DEPTH, 2, HG_HEADS, HG_EXPAND)
    x_lat, x_ctx = x, ctx
    for i in range(DEPTH):
        last = i == DEPTH - 1
        j = i // N_MIXERS
        sh1, sc1, g1, sh2, sc2, g2 = [m[:, None, :] for m in ada_mod(c, ada_w[i], ada_b[i])]
        csh1, csc1, cg1, csh2, csc2, cg2 = ada_mod(c_ctx, ada_w[i], ada_b[i])
        h_lat = rms_norm(x_lat, norm1_g[i]) * (1 + sc1) + sh1
        h_ctx = rms_norm(x_ctx, norm1_g[i]) * (1 + csc1) + csh1
        if i % N_MIXERS == 0:
            out_lat, out_ctx = na_mixer(h_lat, h_ctx, na_w_qkv[j], na_w_o[j], na_q_gain[j],
                                        na_k_gain[j], na_rpb[j], not last)
        else:
            out_lat, out_ctx = hgrn2_mixer(h_lat, h_ctx, hg_w_in[j], lbs[i], hg_norm_g[j],
                                           hg_w_o[j], not last)
        x_lat = x_lat + g1 * out_lat
        x_lat = x_lat + g2 * swiglu(rms_norm(x_lat, norm2_g[i]) * (1 + sc2) + sh2, ffn_w_in[i], ffn_w_out[i])
        if not last:
            x_ctx = x_ctx + cg1 * out_ctx
            x_ctx = x_ctx + cg2 * swiglu(rms_norm(x_ctx, norm2_g[i]) * (1 + csc2) + csh2,
                                         ffn_w_in[i], ffn_w_out[i])
    return x_lat
```

```python
import re
import numpy as np
from contextlib import ExitStack
import concourse.bass as bass
import concourse.mybir as mybir
from concourse.bass_utils import run_bass_kernel_spmd

F32 = mybir.dt.float32
BF16 = mybir.dt.bfloat16
AF = mybir.ActivationFunctionType
ALU = mybir.AluOpType
AX = mybir.AxisListType

D = 1024
EPS = 1e-6
NTQ = 32
NTK = 34
NTOK = NTQ * 128
DFF = 2816
PADV = -200.0


class Buf:
    __slots__ = ("w", "r")

    def __init__(self):
        self.w = None
        self.r = {}


NDSEM = 20


class Prog:
    ENG = {'pe': 'tensor', 'act': 'scalar', 'dve': 'vector', 'pool': 'gpsimd', 'sp': 'sync'}

    def __init__(self, nc, stack):
        self.nc = nc
        self.q = {e: [] for e in self.ENG}
        self.cnt = {e: 0 for e in self.ENG}
        self.sem = {e: stack.enter_context(nc.semaphore("s_" + e)) for e in self.ENG}
        self.cc_sem = stack.enter_context(nc.semaphore("s_cc"))
        self.cc_cnt = 0
        self.dsem = {}
        self.dcnt = {}
        self.drr = {}
        for qn in ('sp', 'act', 'pool'):
            self.dsem[qn] = [stack.enter_context(nc.semaphore(f"d_{qn}{i}")) for i in range(NDSEM)]
            self.dcnt[qn] = [0] * NDSEM
            self.drr[qn] = 0

    def _deps(self, eng, reads, writes):
        best = {}
        for b in reads:
            if b.w is not None:
                k, v = b.w
                if best.get(k, 0) < v:
                    best[k] = v
        for b in writes:
            if b.w is not None:
                k, v = b.w
                if k != eng and best.get(k, 0) < v:
                    best[k] = v
            for k, v in b.r.items():
                if k == eng:
                    continue
                if best.get(k, 0) < v:
                    best[k] = v
        if eng == 'pe':
            best.pop('pe', None)
        return best

    def _mark(self, tok, reads, writes):
        k, v = tok
        for b in reads:
            if b.r.get(k, 0) < v:
                b.r[k] = v
        for b in writes:
            b.w = tok
            b.r = {}

    def op(self, eng, fn, reads=(), writes=(), inc=True):
        for k, v in self._deps(eng, reads, writes).items():
            self.q[eng].append(('wait', k, v))
        tok = (eng, self.cnt[eng] + 1)
        if inc:
            self.cnt[eng] += 1
        self.q[eng].append(('op', fn, inc))
        self._mark(tok, reads, writes)
        return tok

    def dma(self, qn, fn, reads=(), writes=()):
        for k, v in self._deps(qn + '_q', reads, writes).items():
            self.q[qn].append(('wait', k, v))
        i = self.drr[qn]
        self.drr[qn] = (i + 1) % NDSEM
        prev = self.dcnt[qn][i]
        key = f"d_{qn}{i}"
        if prev > 0:
            self.q[qn].append(('wait', key, prev))
        self.dcnt[qn][i] = prev + 16
        tok = (key, prev + 16)
        self.q[qn].append(('dma', fn, i))
        self._mark(tok, reads, writes)
        return tok

    def collective(self, fn, reads=(), writes=()):
        for k, v in self._deps('pool_q', reads, writes).items():
            self.q['pool'].append(('wait', k, v))
        self.cc_cnt += 1
        tok = ('cc', self.cc_cnt)
        self.q['pool'].append(('cc', fn))
        self._mark(tok, reads, writes)
        return tok

    def barrier(self):
        toks = [(e, c) for e, c in self.cnt.items() if c > 0]
        if self.cc_cnt > 0:
            toks.append(('cc', self.cc_cnt))
        for qn in self.dcnt:
            for i, c in enumerate(self.dcnt[qn]):
                if c > 0:
                    toks.append((f"d_{qn}{i}", c))
        for e in self.ENG:
            for k, v in toks:
                if k != e:
                    self.q[e].append(('wait', k, v))

    def semof(self, key):
        if key == 'cc':
            return self.cc_sem
        if key in self.sem:
            return self.sem[key]
        m = re.match(r"d_([a-z]+)(\d+)", key)
        return self.dsem[m.group(1)][int(m.group(2))]

    def finish(self):
        self.barrier()
        nc = self.nc
        with nc.Block() as block:
            for eng, attr in self.ENG.items():
                items = self.q[eng]

                def body(e, items=items, eng=eng):
                    known = {}
                    for it in items:
                        if it[0] == 'wait':
                            _, k, v = it
                            if known.get(k, 0) >= v:
                                continue
                            known[k] = v
                            e.wait_ge(self.semof(k), v)
                        elif it[0] == 'cc':
                            it[1](e).then_inc(self.cc_sem)
                        elif it[0] == 'op':
                            ins = it[1](e)
                            if it[2]:
                                ins.then_inc(self.sem[eng], 1)
                        else:
                            ins = it[1](e)
                            ins.then_inc(self.dsem[eng][it[2]], 16)
                getattr(block, attr)(body)


_UNIQ = [0]


def _uniq(n):
    _UNIQ[0] += 1
    return f"{n}_{_UNIQ[0]}"


class SB:
    def __init__(self, t):
        self.t = t
        self.b = Buf()


class Env:
    def __init__(self, nc, ext_in, ext_out, shapes):
        self.nc = nc
        self.ext_in = ext_in
        self.ext_out = ext_out
        self.shapes = shapes
        self.t = {}
        self.bufs = {}

    def d(self, name):
        if name not in self.t:
            shape, dt = self.shapes[name]
            kind = "ExternalInput" if name in self.ext_in else ("ExternalOutput" if name in self.ext_out else "Internal")
            self.t[name] = self.nc.dram_tensor(name, list(shape), dt, kind=kind).ap()
        return self.t[name]

    def b(self, name, idx=0):
        key = (name, idx)
        if key not in self.bufs:
            self.bufs[key] = Buf()
        return self.bufs[key]


SHAPES = {
    'xin': ((NTK * 128, D), F32),
    'ctxin': ((256, D), F32),
    'csil': ((128, 16), F32),
    'ada_w': ((2, D, 6 * D), F32),
    'ada_b': ((1, 2 * 6 * D), F32),
    'n1g': ((128, 16), F32),
    'n2g': ((128, 16), F32),
    'identf': ((128, 128), F32),
    'w_qkv': ((D, 3 * D), F32),
    'w_o0': ((D, D), F32),
    'qkgain': ((128, 2), F32),
    'ebg': ((128, 16 * 5 * 128), F32),
    'ebs': ((2, 128, 16 * 4 * 128), F32),
    'ffn_w_in': ((2, D, 2 * DFF), F32),
    'ffn_w_out': ((2, DFF, D), F32),
    'hg_w_in': ((D, 5 * D), F32),
    'hg_w_o': ((D, D), F32),
    'hg_low': ((128, 32), F32),
    'hg_ng': ((1, D), F32),
    'trimask': ((128, 256), F32),
    'rstmask': ((128, 512), F32),
    'modrow': ((1, 4 * 6 * D), F32),
    'xa': ((NTOK, D), F32), 'xa_ctx': ((256, D), F32),
    'xb': ((NTOK, D), F32), 'xb_ctx': ((256, D), F32),
    'xc': ((NTOK, D), F32),
    'out': ((NTOK, D), F32),
    'o1': ((NTOK, D), F32),
    'vtok': ((NTOK, D), BF16), 'sgtok': ((NTOK, D), BF16), 'd2kh': ((NTOK, D), BF16),
    'd2qe': ((8, 128, NTOK), BF16), 'd2kn': ((8, 128, NTOK), BF16), 'd2qt': ((8, 128, NTOK), BF16),
    'd2eb': ((128, 8, NTOK // 64), F32),
    's_send': ((128, D), F32), 's_recv': ((128, D), F32), 's_gath': ((256, D), F32), 'sel': ((128, 2), F32),
}


def cast_load_w(p, nc, dst, dram2d, K, N, bufs, q='pool', split=1):
    kc = K // 128
    step = N // split
    for s in range(split):
        src = dram2d[:, s * step:(s + 1) * step].rearrange("(c p) n -> p c n", p=128)
        p.dma(q, lambda e, s=s, src=src: e.dma_start(out=dst[:, :, s * step:(s + 1) * step], in_=src),
              writes=[bufs[s]])


def ph_ada(p, nc, env, ps, pb):
    with ExitStack() as st:
        sb = lambda n, s, d: st.enter_context(nc.sbuf_tensor(_uniq(n), s, d))
        cs_f = SB(sb('cs_f', [128, 16], F32))
        cs_b = SB(sb('cs_b', [128, 16], BF16))
        brow = SB(sb('brow', [1, 2 * 6 * D], F32))
        mrow = SB(sb('mrow', [1, 4 * 6 * D], F32))
        wr = [SB(sb(f'adaw{i}', [128, 8, 512], BF16)) for i in range(3)]
        p.dma('sp', lambda e: e.dma_start(out=cs_f.t[:], in_=env.d('csil')[:, :]), writes=[cs_f.b])
        p.dma('sp', lambda e: e.dma_start(out=brow.t[:], in_=env.d('ada_b')[:, :]), writes=[brow.b])
        p.op('act', lambda e: e.activation(out=cs_b.t[:], in_=cs_f.t[:], func=AF.Silu), reads=[cs_f.b], writes=[cs_b.b])
        aw = env.d('ada_w')
        n = 0
        for i in range(2):
            for nb in range(12):
                w = wr[n % 3]
                cast_load_w(p, nc, w.t, aw[i][:, nb * 512:(nb + 1) * 512], D, 512, [w.b])
                for m in range(2):
                    bank = (n * 2 + m) % 8
                    for kc in range(8):
                        p.op('pe', lambda e, bank=bank, kc=kc, m=m, w=w: e.matmul(
                            ps[0:1, bank, :], lhsT=cs_b.t[:, 2 * kc + m:2 * kc + m + 1], rhs=w.t[:, kc, :],
                            start=(kc == 0), stop=(kc == 7)),
                            reads=[cs_b.b, w.b], writes=[pb[bank]], inc=(kc == 7))
                    o0 = (i * 2 + m) * 6 * D + nb * 512
                    b0 = i * 6 * D + nb * 512
                    p.op('dve', lambda e, bank=bank, o0=o0, b0=b0: e.tensor_tensor(
                        out=mrow.t[0:1, o0:o0 + 512], in0=ps[0:1, bank, :], in1=brow.t[0:1, b0:b0 + 512], op=ALU.add),
                        reads=[pb[bank], brow.b], writes=[mrow.b])
                n += 1
        p.dma('sp', lambda e: e.dma_start(out=env.d('modrow')[:, :], in_=mrow.t[0:1, :]),
              reads=[mrow.b], writes=[env.b('modrow')])
    p.barrier()


def load_modP(p, nc, env, dst, col0, layer, m, ncols=1):
    base = (layer * 2 + m) * 6 * D + col0
    src = env.d('modrow')[0, base:base + D].rearrange("(j p) -> p j", p=128)
    p.dma('sp', lambda e: e.dma_start(out=dst.t[:, 0:8], in_=src, allow_slow_non_contiguous=True),
          reads=[env.b('modrow')], writes=[dst.b])


def load_modB(p, nc, env, dst, col0, layer, m):
    base = (layer * 2 + m) * 6 * D + col0
    src = env.d('modrow')[0:1, base:base + D].partition_broadcast(128)
    p.dma('sp', lambda e: e.dma_start(out=dst.t[:], in_=src), reads=[env.b('modrow')], writes=[dst.b])


def rstd_from_ss(p, s, c_in, c_tmp, c_out, scale, n=1):
    p.op('dve', lambda e: e.tensor_scalar(out=s.t[:, c_tmp:c_tmp + n], in0=s.t[:, c_in:c_in + n], scalar1=scale,
                                          scalar2=EPS, op0=ALU.mult, op1=ALU.add), reads=[s.b], writes=[s.b])
    p.op('act', lambda e: e.activation(out=s.t[:, c_tmp:c_tmp + n], in_=s.t[:, c_tmp:c_tmp + n], func=AF.Ln),
         reads=[s.b], writes=[s.b])
    p.op('act', lambda e: e.activation(out=s.t[:, c_out:c_out + n], in_=s.t[:, c_tmp:c_tmp + n], func=AF.Exp, scale=-0.5),
         reads=[s.b], writes=[s.b])


class NormMod:
    def __init__(self, p, nc, st, env, ps, pb, layer, which, ms, banks, tag, nxn=2, junk=None):
        self.p, self.nc, self.ps, self.pb, self.banks = p, nc, ps, pb, banks
        sb = lambda n, s, d: st.enter_context(nc.sbuf_tensor(_uniq(n + tag), s, d))
        self.identf = SB(sb('identf', [128, 128], F32))
        p.dma('sp', lambda e: e.dma_start(out=self.identf.t[:], in_=env.d('identf')[:, :]), writes=[self.identf.b])
        if junk is None:
            self.junk_t = sb('junk', [128, D], BF16)
            self.junk_b = Buf()
        else:
            self.junk_t, self.junk_b = junk
        self.nxn = nxn
        self.xn = [SB(sb(f'xn{i}', [128, D], F32)) for i in range(nxn)]
        self.st = [SB(sb(f'st{i}', [128, 4], F32)) for i in range(2)]
        self.k = 0
        gP = SB(sb('gP', [128, 16], F32))
        p.dma('sp', lambda e: e.dma_start(out=gP.t[:], in_=env.d('n1g' if which == 0 else 'n2g')[:, :]), writes=[gP.b])
        self.geff = {}
        self.sh = {}
        for m in ms:
            shP = SB(sb(f'shP{m}', [128, 8], F32))
            scP = SB(sb(f'scP{m}', [128, 8], F32))
            ge = SB(sb(f'geff{m}', [128, 8], F32))
            load_modP(p, nc, env, shP, (0 if which == 0 else 3 * D), layer, m)
            load_modP(p, nc, env, scP, (D if which == 0 else 4 * D), layer, m)
            p.op('dve', lambda e, scP=scP, ge=ge: e.scalar_tensor_tensor(
                out=ge.t[:], in0=scP.t[:], scalar=1.0, in1=gP.t[:, layer * 8:layer * 8 + 8], op0=ALU.add, op1=ALU.mult),
                reads=[scP.b, gP.b], writes=[ge.b])
            self.geff[m] = ge
            self.sh[m] = shP

    def run(self, x, m, hT, col0):
        p, ps, pb = self.p, self.ps, self.pb
        k = self.k
        self.k += 1
        xn = self.xn[k % self.nxn]
        s = self.st[k % 2]
        p.op('act', lambda e: e.activation(out=self.junk_t[:], in_=x.t[:], func=AF.Square, accum_out=s.t[:, 0:1]),
             reads=[x.b], writes=[self.junk_b, s.b])
        rstd_from_ss(p, s, 0, 1, 2, 1.0 / D)
        p.op('pool', lambda e: e.tensor_scalar(out=xn.t[:], in0=x.t[:], scalar1=s.t[:, 2:3], scalar2=None, op0=ALU.mult),
             reads=[x.b, s.b], writes=[xn.b])
        b0, b1 = self.banks
        for j in range(8):
            bank = b0 if j < 4 else b1
            p.op('pe', lambda e, j=j, bank=bank: e.transpose(ps[:, bank, (j % 4) * 128:(j % 4 + 1) * 128],
                                                             xn.t[:, j * 128:(j + 1) * 128], self.identf.t[:]),
                 reads=[xn.b, self.identf.b], writes=[pb[bank]], inc=(j % 4 == 3))
        ge, shP = self.geff[m], self.sh[m]
        for j in range(8):
            bank = b0 if j < 4 else b1
            p.op('act', lambda e, j=j, bank=bank: e.activation(
                out=hT.t[:, j, col0:col0 + 128], in_=ps[:, bank, (j % 4) * 128:(j % 4 + 1) * 128],
                func=AF.Identity, scale=ge.t[:, j:j + 1], bias=shP.t[:, j:j + 1]),
                reads=[pb[bank], ge.b, shP.b], writes=[hT.b])


def interleave(gens):
    gens = list(gens)
    while gens:
        for g in list(gens):
            try:
                next(g)
            except StopIteration:
                gens.remove(g)


def ph_att0(p, nc, env, ps, pb):
    NK = 7
    NQ = 5
    with ExitStack() as st:
        sb = lambda n, s, d: st.enter_context(nc.sbuf_tensor(_uniq(n), s, d))
        wqkv = sb('wqkv', [128, 8, 3 * D], BF16)
        wq_b = [Buf() for _ in range(6)]
        cast_load_w(p, nc, wqkv, env.d('w_qkv'), D, 3 * D, wq_b, split=6)
        wo = sb('wo', [128, 8, D], BF16)
        wo_b = [Buf() for _ in range(2)]
        cast_load_w(p, nc, wo, env.d('w_o0'), D, D, wo_b, split=2)
        identb = SB(sb('identb', [128, 128], BF16))
        p.dma('pool', lambda e: e.dma_start(out=identb.t[:], in_=env.d('identf')[:, :]), writes=[identb.b])
        qkg = SB(sb('qkg', [128, 2], F32))
        p.dma('sp', lambda e: e.dma_start(out=qkg.t[:], in_=env.d('qkgain')[:, :]), writes=[qkg.b])
        G = SB(sb('G', [128, D], F32))
        sq = SB(sb('sq', [128, 512], F32))
        nm = NormMod(p, nc, st, env, ps, pb, 0, 0, [0, 1], (0, 1), 'a0', nxn=1, junk=(sq.t[:].bitcast(BF16), sq.b))
        EB = SB(sb('EB', [128, 16, 5, 128], BF16))
        EBS = SB(sb('EBS', [128, 16, 4, 128], BF16))
        stage = [SB(sb(f'ebst{i}', [128, 640], F32)) for i in range(2)]
        xr = [SB(sb(f'xr{i}', [128, D], F32)) for i in range(2)]
        hT = [SB(sb(f'hT{i}', [128, 8, 128], BF16)) for i in range(2)]
        KT = [SB(sb(f'KT{i}', [128, 8, 128], BF16)) for i in range(NK + 2)]
        QT = [SB(sb(f'QT{i}', [128, 8, 128], BF16)) for i in range(NQ)]
        V = [SB(sb(f'V{i}', [128, 16, 65], BF16)) for i in range(NK + 2)]
        qf = SB(sb('qf', [128, 2 * D], F32))
        hs = [SB(sb(f'hs{i}', [128, 32], F32)) for i in range(2)]
        qkt = SB(sb('qkt', [128, 2 * D], BF16))
        PT = [SB(sb(f'PT{i}', [128, 896], BF16)) for i in range(2)]
        otok = SB(sb('otok', [128, D], BF16))
        oT = SB(sb('oT', [128, 8, 128], BF16))
        rc = SB(sb('rc', [128, 4], F32))
        tmp = SB(sb('tmp', [128, D], F32))
        xres = SB(sb('xres', [128, D], F32))
        psb2 = ps[:, 2, :].bitcast(BF16)

        for v in V:
            p.op('pool', lambda e, v=v: e.memset(v.t[:, :, 64:65], 1.0), writes=[v.b])

        def load_table(dst, src2d, ndt):
            for h in range(16):
                sg = stage[h % 2]
                w = ndt * 128
                p.dma('sp', lambda e, h=h, sg=sg, w=w: e.dma_start(out=sg.t[:, 0:w], in_=src2d[:, h * w:(h + 1) * w]),
                      writes=[sg.b])
                p.op('act', lambda e, h=h, sg=sg, w=w: e.activation(
                    out=dst.t[:, h, :, :], in_=sg.t[:, 0:w].rearrange("p (a b) -> p a b", b=128), func=AF.Exp),
                    reads=[sg.b], writes=[dst.b])

        load_table(EB, env.d('ebg'), 5)

        gbank = [0]

        def next_bank():
            gbank[0] ^= 1
            return gbank[0]

        qtasks = [('ctx', 0), ('ctx', 1)] + [('lat', i) for i in range(NTK)]

        def xsrc(task):
            kind, i = task
            return (env.d('ctxin') if kind == 'ctx' else env.d('xin'))[i * 128:(i + 1) * 128, :]

        def issue_xload(pos):
            x = xr[pos % 2]
            src = xsrc(qtasks[pos])
            p.dma('sp', lambda e: e.dma_start(out=x.t[:], in_=src), writes=[x.b])

        def kslot(kind, i):
            return NK + i if kind == 'ctx' else i % NK

        def gen_qkv(pos):
            kind, i = qtasks[pos]
            m = 1 if kind == 'ctx' else 0
            if pos + 1 < len(qtasks):
                issue_xload(pos + 1)
            x = xr[pos % 2]
            h = hT[pos % 2]
            nm.run(x, m, h, 0)
            yield
            need_q = not (kind == 'lat' and i >= NTQ)
            ks = kslot(kind, i)
            blocks = ([0, 1] if need_q else []) + [2, 3, 4, 5]
            for nb in blocks:
                bank = next_bank()
                for kc in range(8):
                    p.op('pe', lambda e, bank=bank, kc=kc, nb=nb: e.matmul(
                        ps[:, bank, :], lhsT=h.t[:, kc, :], rhs=wqkv[:, kc, nb * 512:(nb + 1) * 512],
                        start=(kc == 0), stop=(kc == 7)),
                        reads=[h.b, wq_b[nb]], writes=[pb[bank]], inc=(kc == 7))
                if nb < 4:
                    p.op('act', lambda e, bank=bank, nb=nb: e.activation(
                        out=qf.t[:, nb * 512:(nb + 1) * 512], in_=ps[:, bank, :], func=AF.Identity),
                        reads=[pb[bank]], writes=[qf.b])
                else:
                    vv = V[ks]
                    p.op('dve', lambda e, bank=bank, nb=nb, vv=vv: e.tensor_copy(
                        out=vv.t[:, (nb - 4) * 8:(nb - 4) * 8 + 8, 0:64],
                        in_=ps[:, bank, :].rearrange("p (a b) -> p a b", b=64)),
                        reads=[pb[bank]], writes=[vv.b])
                yield
            s = hs[pos % 2]
            c0 = 0 if need_q else 2
            for nb in range(c0, 4):
                p.op('dve', lambda e, nb=nb: e.tensor_tensor(out=sq.t[:], in0=qf.t[:, nb * 512:(nb + 1) * 512],
                                                             in1=qf.t[:, nb * 512:(nb + 1) * 512], op=ALU.mult),
                     reads=[qf.b], writes=[sq.b])
                p.op('dve', lambda e, nb=nb: e.tensor_reduce(out=s.t[:, nb * 8:nb * 8 + 8],
                                                             in_=sq.t[:].rearrange("p (a b) -> p a b", b=64), axis=AX.X, op=ALU.add),
                     reads=[sq.b], writes=[s.b])
                cc = nb * 8
                p.op('dve', lambda e, cc=cc: e.tensor_scalar(out=s.t[:, cc:cc + 8], in0=s.t[:, cc:cc + 8], scalar1=1.0 / 64,
                                                             scalar2=EPS, op0=ALU.mult, op1=ALU.add), reads=[s.b], writes=[s.b])
                p.op('act', lambda e, cc=cc: e.activation(out=s.t[:, cc:cc + 8], in_=s.t[:, cc:cc + 8], func=AF.Ln),
                     reads=[s.b], writes=[s.b])
                p.op('act', lambda e, cc=cc: e.activation(out=s.t[:, cc:cc + 8], in_=s.t[:, cc:cc + 8], func=AF.Exp, scale=-0.5),
                     reads=[s.b], writes=[s.b])
                p.op('dve', lambda e, nb=nb, cc=cc: e.tensor_tensor(
                    out=qkt.t[:, nb * 512:(nb + 1) * 512].rearrange("p (a b) -> p a b", b=64),
                    in0=qf.t[:, nb * 512:(nb + 1) * 512].rearrange("p (a b) -> p a b", b=64),
                    in1=s.t[:, cc:cc + 8].unsqueeze(2).to_broadcast([128, 8, 64]), op=ALU.mult),
                    reads=[qf.b, s.b], writes=[qkt.b])
                yield
            for which in ([0, 1] if need_q else [1]):
                dst = (QTc[i] if kind == 'ctx' else QT[i % NQ]) if which == 0 else KT[ks]
                for j in range(8):
                    p.op('pe', lambda e, j=j, which=which: e.transpose(
                        psb2[:, j * 128:(j + 1) * 128], qkt.t[:, which * D + j * 128:which * D + (j + 1) * 128], identb.t[:]),
                        reads=[qkt.b, identb.b], writes=[pb[2]], inc=(j == 7))
                if which == 0:
                    p.op('dve', lambda e, dst=dst: e.tensor_scalar(
                        out=dst.t[:].rearrange("p a b -> p (a b)"), in0=psb2[:, 0:1024], scalar1=qkg.t[:, 0:1], scalar2=0.125,
                        op0=ALU.mult, op1=ALU.mult), reads=[pb[2], qkg.b], writes=[dst.b])
                else:
                    p.op('dve', lambda e, dst=dst: e.tensor_scalar(
                        out=dst.t[:].rearrange("p a b -> p (a b)"), in0=psb2[:, 0:1024], scalar1=qkg.t[:, 1:2], scalar2=None,
                        op0=ALU.mult), reads=[pb[2], qkg.b], writes=[dst.b])
                yield

        hcount = [0]

        def gen_att(task):
            kind, t = task
            m = 1 if kind == 'ctx' else 0
            src = xsrc(task)
            p.dma('sp', lambda e: e.dma_start(out=xres.t[:], in_=src), writes=[xres.b])
            if (kind, t) in (('ctx', 0), ('lat', 0)):
                load_modB(p, nc, env, G, 2 * D, 0, m)
            if kind == 'lat' and t in (0, 1):
                load_table(EBS, env.d('ebs')[t], 4)
            if kind == 'ctx':
                chunks = [('ctx', 0), ('ctx', 1)]
                nloc = 0
                tab = None
                q = KT
                qt = None
            else:
                if t in (0, 1):
                    kts = [0, 1, 2, 3]
                    tab = EBS
                else:
                    kts = [t - 2, t - 1, t, t + 1, t + 2]
                    tab = EB
                nloc = len(kts)
                chunks = [('lat', k) for k in kts] + [('ctx', 0), ('ctx', 1)]
            if kind == 'ctx':
                qsrc = QTc[t]
            else:
                qsrc = QT[t % NQ]
            nch = len(chunks)

            def ppos(ci):
                return (0, ci * 128) if ci < 4 else (1, (ci - 4) * 128)
            for h in range(16):
                hp, pbs = h // 2, 64 * (h % 2)
                hn = hcount[0]
                hcount[0] += 1
                sbank = (3 + 2 * (hn % 2), 4 + 2 * (hn % 2))
                pt = PT[hn % 2]
                for ci, (ck, kt) in enumerate(chunks):
                    bi, col = ppos(ci)
                    bank = sbank[bi]
                    kk = KT[kslot(ck, kt)]
                    last = (ci == nch - 1) or (ci == 3)
                    p.op('pe', lambda e, bank=bank, col=col, kk=kk, hp=hp, pbs=pbs: e.matmul(
                        ps[:, bank, col:col + 128], lhsT=kk.t[pbs:pbs + 64, hp, :], rhs=qsrc.t[pbs:pbs + 64, hp, :],
                        start=True, stop=True),
                        reads=[kk.b, qsrc.b], writes=[pb[bank]], inc=last)
                nA = min(4, nch) * 128
                nB = (nch - 4) * 128
                p.op('act', lambda e, sbank=sbank, pt=pt, nA=nA: e.activation(
                    out=pt.t[:, 0:nA], in_=ps[:, sbank[0], 0:nA], func=AF.Exp), reads=[pb[sbank[0]]], writes=[pt.b])
                if nB > 0:
                    p.op('act', lambda e, sbank=sbank, pt=pt, nB=nB: e.activation(
                        out=pt.t[:, 512:512 + nB], in_=ps[:, sbank[1], 0:nB], func=AF.Exp), reads=[pb[sbank[1]]], writes=[pt.b])
                if nloc > 0:
                    na = min(4, nloc)
                    p.op('dve', lambda e, pt=pt, na=na, h=h, tab=tab: e.tensor_tensor(
                        out=pt.t[:, 0:na * 128], in0=pt.t[:, 0:na * 128],
                        in1=tab.t[:, h, 0:na, :].rearrange("p a b -> p (a b)"), op=ALU.mult),
                        reads=[pt.b, tab.b], writes=[pt.b])
                    if nloc > 4:
                        p.op('dve', lambda e, pt=pt, h=h, tab=tab: e.tensor_tensor(
                            out=pt.t[:, 512:640], in0=pt.t[:, 512:640], in1=tab.t[:, h, 4, :], op=ALU.mult),
                            reads=[pt.b, tab.b], writes=[pt.b])
                g = h % 4
                for ci, (ck, kt) in enumerate(chunks):
                    bi, col = ppos(ci)
                    off = bi * 512 + col
                    vv = V[kslot(ck, kt)]
                    p.op('pe', lambda e, off=off, vv=vv, g=g, h=h, pt=pt, ci=ci: e.matmul(
                        ps[:, 7, g * 65:(g + 1) * 65], lhsT=pt.t[:, off:off + 128], rhs=vv.t[:, h, :],
                        start=(ci == 0), stop=(ci == nch - 1)),
                        reads=[pt.b, vv.b], writes=[pb[7]], inc=(ci == nch - 1))
                if g == 3:
                    o4 = ps[:, 7, 0:260].rearrange("p (g c) -> p g c", c=65)
                    p.op('dve', lambda e, o4=o4: e.reciprocal(out=rc.t[:, 0:4].unsqueeze(2), in_=o4[:, :, 64:65]),
                         reads=[pb[7]], writes=[rc.b])
                    p.op('dve', lambda e, o4=o4, h=h: e.tensor_tensor(
                        out=otok.t[:, (h - 3) * 64:(h + 1) * 64].rearrange("p (a b) -> p a b", b=64),
                        in0=o4[:, :, 0:64], in1=rc.t[:, 0:4].unsqueeze(2).to_broadcast([128, 4, 64]), op=ALU.mult),
                        reads=[pb[7], rc.b], writes=[otok.b])
                yield
            for j in range(8):
                p.op('pe', lambda e, j=j: e.transpose(psb2[:, j * 128:(j + 1) * 128], otok.t[:, j * 128:(j + 1) * 128],
                                                      identb.t[:]),
                     reads=[otok.b, identb.b], writes=[pb[2]], inc=(j == 7))
            p.op('act', lambda e: e.activation(out=oT.t[:].rearrange("p a b -> p (a b)"), in_=psb2[:, 0:1024],
                                               func=AF.Identity), reads=[pb[2]], writes=[oT.b])
            yield
            for nb in range(2):
                bank = next_bank()
                for kc in range(8):
                    p.op('pe', lambda e, bank=bank, kc=kc, nb=nb: e.matmul(
                        ps[:, bank, :], lhsT=oT.t[:, kc, :], rhs=wo[:, kc, nb * 512:(nb + 1) * 512],
                        start=(kc == 0), stop=(kc == 7)),
                        reads=[oT.b, wo_b[nb]], writes=[pb[bank]], inc=(kc == 7))
                p.op('dve', lambda e, bank=bank, nb=nb: e.tensor_tensor(
                    out=tmp.t[:, nb * 512:(nb + 1) * 512], in0=ps[:, bank, :], in1=G.t[:, nb * 512:(nb + 1) * 512], op=ALU.mult),
                    reads=[pb[bank], G.b], writes=[tmp.b])
                p.op('pool', lambda e, nb=nb: e.tensor_tensor(
                    out=tmp.t[:, nb * 512:(nb + 1) * 512], in0=tmp.t[:, nb * 512:(nb + 1) * 512],
                    in1=xres.t[:, nb * 512:(nb + 1) * 512], op=ALU.add),
                    reads=[tmp.b, xres.b], writes=[tmp.b])
                yield
            dname = 'xa_ctx' if kind == 'ctx' else 'xa'
            dst = env.d(dname)[t * 128:(t + 1) * 128, :]
            p.dma('sp', lambda e: e.dma_start(out=dst, in_=tmp.t[:]), reads=[tmp.b], writes=[env.b(dname, t)])
            yield

        QTc = [SB(sb(f'QTc{i}', [128, 8, 128], BF16)) for i in range(2)]
        atasks = [(('ctx', 0), 1), (('ctx', 1), 1)] + [(('lat', t), 2 + min(t + 3, NTK - 1)) for t in range(NTQ)]
        issue_xload(0)
        ai = 0
        for pos in range(len(qtasks)):
            gens = [gen_qkv(pos)]
            if ai < len(atasks) and atasks[ai][1] <= pos - 1:
                gens.append(gen_att(atasks[ai][0]))
                ai += 1
            interleave(gens)
        while ai < len(atasks):
            interleave([gen_att(atasks[ai][0])])
            ai += 1
    p.barrier()


def make_ffn(layer, src, dst, with_ctx):
    def ph(p, nc, env, ps, pb):
        with ExitStack() as st:
            sb = lambda n, s, d: st.enter_context(nc.sbuf_tensor(_uniq(n), s, d))
            win = sb('win', [128, 8, 2 * DFF], BF16)
            win_b = [Buf() for _ in range(11)]
            wsrc = env.d('ffn_w_in')[layer]
            for s_ in [0, 5, 6, 1, 7, 2, 8, 3, 9, 4, 10]:
                srcw = wsrc[:, s_ * 512:(s_ + 1) * 512].rearrange("(c p) n -> p c n", p=128)
                p.dma('pool', lambda e, s_=s_, srcw=srcw: e.dma_start(out=win[:, :, s_ * 512:(s_ + 1) * 512], in_=srcw),
                      writes=[win_b[s_]])
            wout = sb('wout', [128, 22, D], BF16)
            wout_b = [Buf() for _ in range(2)]
            cast_load_w(p, nc, wout, env.d('ffn_w_out')[layer], DFF, D, wout_b, split=2)
            sg = [SB(sb(f'sg{i}', [128, 512], F32)) for i in range(2)]
            nm = NormMod(p, nc, st, env, ps, pb, layer, 1, [0, 1] if with_ctx else [0], (0, 1), 'f%d' % layer, nxn=1,
                         junk=(sg[0].t[:].bitcast(BF16), sg[0].b))
            hid = SB(sb('hid', [128, 22, 512], BF16))
            h2T = [SB(sb(f'h2T{i}', [128, 8, 512], BF16)) for i in range(2)]
            xr = [SB(sb(f'xr{i}', [128, D], F32)) for i in range(2)]
            tmp = SB(sb('tmp', [128, D], F32))
            xe = SB(sb('xe', [128, D], F32))
            G = SB(sb('G', [128, D], F32))
            sts = []
            if with_ctx:
                sts.append([(src + '_ctx', dst + '_ctx', 0, 1), (src + '_ctx', dst + '_ctx', 1, 1)])
            for s_ in range(NTQ // 4):
                sts.append([(src, dst, s_ * 4 + j, 0) for j in range(4)])
            xcnt = [0]

            def gen_A(si):
                for j, (sn, dn, ti, m) in enumerate(sts[si]):
                    x = xr[xcnt[0] % 2]
                    xcnt[0] += 1
                    srcx = env.d(sn)[ti * 128:(ti + 1) * 128, :]
                    p.dma('sp', lambda e, x=x, srcx=srcx: e.dma_start(out=x.t[:], in_=srcx), reads=[env.b(sn, ti)], writes=[x.b])
                    nm.run(x, m, h2T[si % 2], j * 128)
                    yield

            pair = [0]

            def gen_BC(si):
                tiles = sts[si]
                ntok = len(tiles) * 128
                h = h2T[si % 2]
                m = tiles[0][3]
                if si == 0 or (with_ctx and si == 1):
                    load_modB(p, nc, env, G, 5 * D, layer, m)
                for oc in range(22):
                    bA = 2 + 2 * (pair[0] % 3)
                    bU = bA + 1
                    pair[0] += 1
                    for (bank, c0) in ((bA, oc * 128), (bU, DFF + oc * 128)):
                        for kc in range(8):
                            p.op('pe', lambda e, bank=bank, c0=c0, kc=kc: e.matmul(
                                ps[:, bank, 0:ntok], lhsT=win[:, kc, c0:c0 + 128], rhs=h.t[:, kc, 0:ntok],
                                start=(kc == 0), stop=(kc == 7)),
                                reads=[h.b, win_b[c0 // 512]], writes=[pb[bank]], inc=(kc == 7))
                    s2 = sg[oc % 2]
                    p.op('act', lambda e, bA=bA, s2=s2: e.activation(out=s2.t[:, 0:ntok], in_=ps[:, bA, 0:ntok], func=AF.Silu),
                         reads=[pb[bA]], writes=[s2.b])
                    p.op('dve', lambda e, bU=bU, s2=s2, oc=oc: e.tensor_tensor(
                        out=hid.t[:, oc, 0:ntok], in0=s2.t[:, 0:ntok], in1=ps[:, bU, 0:ntok], op=ALU.mult),
                        reads=[s2.b, pb[bU]], writes=[hid.b])
                    yield
                for j, (sn, dn, ti, m) in enumerate(tiles):
                    srcx = env.d(sn)[ti * 128:(ti + 1) * 128, :]
                    p.dma('sp', lambda e, srcx=srcx: e.dma_start(out=xe.t[:], in_=srcx), reads=[env.b(sn, ti)], writes=[xe.b])
                    for nb in range(2):
                        bank = nb
                        for oc in range(22):
                            p.op('pe', lambda e, bank=bank, oc=oc, nb=nb, j=j: e.matmul(
                                ps[:, bank, :], lhsT=hid.t[:, oc, j * 128:(j + 1) * 128], rhs=wout[:, oc, nb * 512:(nb + 1) * 512],
                                start=(oc == 0), stop=(oc == 21)),
                                reads=[hid.b, wout_b[nb]], writes=[pb[bank]], inc=(oc == 21))
                        p.op('dve', lambda e, bank=bank, nb=nb: e.tensor_tensor(
                            out=tmp.t[:, nb * 512:(nb + 1) * 512], in0=ps[:, bank, :], in1=G.t[:, nb * 512:(nb + 1) * 512],
                            op=ALU.mult), reads=[pb[bank], G.b], writes=[tmp.b])
                        p.op('pool', lambda e, nb=nb: e.tensor_tensor(
                            out=tmp.t[:, nb * 512:(nb + 1) * 512], in0=tmp.t[:, nb * 512:(nb + 1) * 512],
                            in1=xe.t[:, nb * 512:(nb + 1) * 512], op=ALU.add), reads=[tmp.b, xe.b], writes=[tmp.b])
                        yield
                    dstx = env.d(dn)[ti * 128:(ti + 1) * 128, :]
                    p.dma('sp', lambda e, dstx=dstx: e.dma_start(out=dstx, in_=tmp.t[:]), reads=[tmp.b], writes=[env.b(dn, ti)])

            interleave([gen_A(0)])
            for si in range(len(sts)):
                gens = [gen_BC(si)]
                if si + 1 < len(sts):
                    gens.append(gen_A(si + 1))
                interleave(gens)
        p.barrier()
    return ph


def scan_chunk(p, nc, ps, pb, banks, qt, qe, kn, kh, vt, ebend, S32, Sbf, msk, c, fwd, o_out):
    bA, bO0, bO1, bU0, bU1 = banks
    par = c % 2
    pbs = 64 * par
    mcol = pbs if fwd else 128 + pbs
    if o_out is not None:
        aT = o_out['aT']
        for hd in range(8):
            (q_ap, q_b), (k_ap, k_b) = qt(hd), kn(hd)
            p.op('pe', lambda e, hd=hd, q_ap=q_ap, k_ap=k_ap: e.matmul(
                ps[pbs:pbs + 64, bA, hd * 64:(hd + 1) * 64], lhsT=k_ap, rhs=q_ap, start=True, stop=True),
                reads=[q_b, k_b], writes=[pb[bA]], inc=(hd == 7))
        p.op('dve', lambda e: e.copy_predicated(
            out=aT.t[pbs:pbs + 64, :].rearrange("p (a b) -> p a b", b=64),
            mask=msk.t[pbs:pbs + 64, mcol:mcol + 64].bitcast(mybir.dt.uint32).unsqueeze(1).to_broadcast([64, 8, 64]),
            data=ps[pbs:pbs + 64, bA, :].rearrange("p (a b) -> p a b", b=64)),
            reads=[pb[bA], msk.b], writes=[aT.b])
        for hd in range(8):
            bank = bO0 if hd < 4 else bO1
            (q_ap, q_b) = qe(hd)
            v_ap, v_b = vt(hd)
            oap = ps[pbs:pbs + 64, bank, (hd % 4) * 128:(hd % 4 + 1) * 128]
            p.op('pe', lambda e, hd=hd, oap=oap, v_ap=v_ap: e.matmul(
                oap, lhsT=aT.t[pbs:pbs + 64, hd * 64:(hd + 1) * 64], rhs=v_ap, start=True, stop=False),
                reads=[aT.b, v_b], writes=[pb[bank]], inc=False)
            p.op('pe', lambda e, hd=hd, oap=oap, q_ap=q_ap: e.matmul(
                oap, lhsT=q_ap, rhs=Sbf.t[:, hd, :], start=False, stop=True),
                reads=[q_b, Sbf.b], writes=[pb[bank]], inc=(hd % 4 == 3))
        ot = o_out['o']
        for half, bank in ((0, bO0), (1, bO1)):
            p.op('act', lambda e, half=half, bank=bank: e.activation(
                out=ot.t[pbs:pbs + 64, half * 512:(half + 1) * 512], in_=ps[pbs:pbs + 64, bank, :], func=AF.Identity),
                reads=[pb[bank]], writes=[ot.b])
    for hd in range(8):
        bank = bU0 if hd < 4 else bU1
        kh_ap, kh_b = kh(hd)
        v_ap, v_b = vt(hd)
        p.op('pe', lambda e, hd=hd, bank=bank, kh_ap=kh_ap, v_ap=v_ap: e.matmul(
            ps[:, bank, (hd % 4) * 128:(hd % 4 + 1) * 128], lhsT=kh_ap, rhs=v_ap, start=True, stop=True),
            reads=[kh_b, v_b], writes=[pb[bank]], inc=(hd % 4 == 3))
    for hd in range(8):
        bank = bU0 if hd < 4 else bU1
        eb_ap, eb_b = ebend(hd)
        p.op('dve', lambda e, hd=hd, bank=bank, eb_ap=eb_ap: e.scalar_tensor_tensor(
            out=S32.t[:, hd, :], in0=S32.t[:, hd, :], scalar=eb_ap, in1=ps[:, bank, (hd % 4) * 128:(hd % 4 + 1) * 128],
            op0=ALU.mult, op1=ALU.add), reads=[S32.b, eb_b, pb[bank]], writes=[S32.b])
    p.op('pool', lambda e: e.tensor_copy(out=Sbf.t[:], in_=S32.t[:]), reads=[S32.b], writes=[Sbf.b])


def ph_hg1(p, nc, env, ps, pb):
    with ExitStack() as st:
        sb = lambda n, s, d: st.enter_context(nc.sbuf_tensor(_uniq(n), s, d))
        T = 512
        w = sb('hw', [128, 8, 5 * D], BF16)
        w_b = [Buf() for _ in range(10)]
        cast_load_w(p, nc, w, env.d('hg_w_in'), D, 5 * D, w_b, split=10)
        identb = SB(sb('identb', [128, 128], BF16))
        p.dma('pool', lambda e: e.dma_start(out=identb.t[:], in_=env.d('identf')[:, :]), writes=[identb.b])
        msk = SB(sb('msk', [128, 256], F32))
        p.dma('sp', lambda e: e.dma_start(out=msk.t[:], in_=env.d('trimask')[:, :]), writes=[msk.b])
        rst = SB(sb('rst', [128, 512], F32))
        p.dma('sp', lambda e: e.dma_start(out=rst.t[:], in_=env.d('rstmask')[:, :]), writes=[rst.b])
        low = SB(sb('low', [128, 32], F32))
        p.dma('sp', lambda e: e.dma_start(out=low.t[:], in_=env.d('hg_low')[:, :]), writes=[low.b])
        lbt = SB(sb('lbt', [128, 48], F32))
        p.op('dve', lambda e: e.tensor_tensor(out=lbt.t[:, 0:16], in0=low.t[:, 0:16], in1=low.t[:, 16:32], op=ALU.subtract),
             reads=[low.b], writes=[lbt.b])
        p.op('act', lambda e: e.activation(out=lbt.t[:, 16:32], in_=lbt.t[:, 0:16], func=AF.Sigmoid, scale=-1.0),
             reads=[lbt.b], writes=[lbt.b])
        p.op('act', lambda e: e.activation(out=lbt.t[:, 32:48], in_=lbt.t[:, 0:16], func=AF.Sigmoid),
             reads=[lbt.b], writes=[lbt.b])
        jk = SB(sb('jk', [128, 512], F32))
        nm = NormMod(p, nc, st, env, ps, pb, 1, 0, [0, 1], (0, 1), 'h1', nxn=1, junk=(jk.t[:].bitcast(BF16), jk.b))
        xr = [SB(sb(f'xr{i}', [128, D], F32)) for i in range(2)]
        hT = SB(sb('hT', [128, 8, T], BF16))
        tq = [SB(sb(f'tq{i}', [128, T], F32)) for i in range(2)]
        ta = [SB(sb(f'ta{i}', [128, T], F32)) for i in range(2)]
        tb = [SB(sb(f'tb{i}', [128, T], F32)) for i in range(2)]
        tc = [SB(sb(f'tc{i}', [128, T], F32)) for i in range(2)]
        td = [SB(sb(f'td{i}', [128, T], F32)) for i in range(2)]
        te = [SB(sb(f'te{i}', [128, T], F32)) for i in range(2)]
        tkh = [SB(sb(f'tkh{i}', [128, T], BF16)) for i in range(2)]
        QE1 = SB(sb('QE1', [128, 8, T], BF16))
        QT1 = SB(sb('QT1', [128, 8, T], BF16))
        KN1 = SB(sb('KN1', [128, 8, T], BF16))
        KH1 = SB(sb('KH1', [128, 4, D], BF16))
        EB1 = SB(sb('EB1', [128, 8, 8], F32))
        s2q = [SB(sb(f's2q{i}', [128, T], BF16)) for i in range(2)]
        s2k = [SB(sb(f's2k{i}', [128, T], BF16)) for i in range(2)]
        s2t = [SB(sb(f's2t{i}', [128, T], BF16)) for i in range(2)]
        KH2 = SB(sb('KH2', [128, 4, D], BF16))
        EB2 = SB(sb('EB2', [128, 8, 8], F32))
        Vt = SB(sb('Vt', [128, 4, D], BF16))
        SGt = SB(sb('SGt', [128, 4, D], BF16))
        S32 = SB(sb('S32', [128, 8, 128], F32))
        Sbf = SB(sb('Sbf', [128, 8, 128], BF16))
        aT = SB(sb('aT', [128, 512], BF16))
        o1t = [SB(sb(f'o1t{i}', [128, D], F32)) for i in range(1)]
        p.op('pool', lambda e: e.memset(aT.t[:], 0.0), writes=[aT.b])
        p.op('pool', lambda e: e.memset(S32.t[:], 0.0), writes=[S32.b])
        p.op('pool', lambda e: e.memset(Sbf.t[:], 0.0), writes=[Sbf.b])
        psb = lambda bank: ps[:, bank, :].bitcast(BF16)

        sts = [[('xb_ctx', 0, 1), ('xb_ctx', 1, 1)]] + [[('xb', s_ * 4 + j, 0) for j in range(4)] for s_ in range(NTQ // 4)]
        xcnt = [0]
        rot = [0]

        def nbank():
            rot[0] = (rot[0] + 1) % 3
            return 2 + rot[0]

        def do_supertile(si, tiles):
            nt = len(tiles)
            Tn = nt * 128
            nch = Tn // 64
            is_ctx = (si == 0)
            tok0 = None if is_ctx else tiles[0][1] * 128
            for j, (sn, ti, m) in enumerate(tiles):
                x = xr[xcnt[0] % 2]
                xcnt[0] += 1
                srcx = env.d(sn)[ti * 128:(ti + 1) * 128, :]
                p.dma('sp', lambda e, x=x, srcx=srcx: e.dma_start(out=x.t[:], in_=srcx), reads=[env.b(sn, ti)], writes=[x.b])
                nm.run(x, m, hT, j * 128)
            for j in range(nt):
                for nb in range(4):
                    bank = nbank()
                    c0 = D + nb * 512
                    for kc in range(8):
                        p.op('pe', lambda e, bank=bank, kc=kc, c0=c0, j=j: e.matmul(
                            ps[:, bank, :], lhsT=hT.t[:, kc, j * 128:(j + 1) * 128], rhs=w[:, kc, c0:c0 + 512],
                            start=(kc == 0), stop=(kc == 7)), reads=[hT.b, w_b[c0 // 512]], writes=[pb[bank]], inc=(kc == 7))
                    if nb < 2:
                        p.op('dve', lambda e, bank=bank, j=j, nb=nb: e.tensor_copy(
                            out=Vt.t[:, j, nb * 512:(nb + 1) * 512], in_=ps[:, bank, :]), reads=[pb[bank]], writes=[Vt.b])
                    elif not is_ctx:
                        p.op('act', lambda e, bank=bank, j=j, nb=nb: e.activation(
                            out=SGt.t[:, j, (nb - 2) * 512:(nb - 1) * 512], in_=ps[:, bank, :], func=AF.Silu),
                            reads=[pb[bank]], writes=[SGt.b])
            if not is_ctx:
                for (nm_, tl) in (('vtok', Vt), ('sgtok', SGt)):
                    dstd = env.d(nm_)[tok0:tok0 + Tn, :].rearrange("(j p) n -> p j n", p=128)
                    p.dma('sp', lambda e, dstd=dstd, tl=tl: e.dma_start(out=dstd, in_=tl.t[:, 0:nt, :]), reads=[tl.b],
                          writes=[env.b(nm_, si)])
            for hd in range(8):
                i2 = hd % 2
                bq, bf1, bf2 = 2, 3, 4
                for (bank, c0) in ((bq, hd * 128), (bf1, 3 * D + hd * 128), (bf2, 4 * D + hd * 128)):
                    for kc in range(8):
                        p.op('pe', lambda e, bank=bank, kc=kc, c0=c0: e.matmul(
                            ps[:, bank, 0:Tn], lhsT=w[:, kc, c0:c0 + 128], rhs=hT.t[:, kc, 0:Tn],
                            start=(kc == 0), stop=(kc == 7)), reads=[hT.b, w_b[c0 // 512]], writes=[pb[bank]], inc=(kc == 7))
                q_, a_, b_, c_, d_, e_, kh_ = tq[i2], ta[i2], tb[i2], tc[i2], td[i2], te[i2], tkh[i2]
                p.op('act', lambda e, q_=q_: e.activation(out=q_.t[:, 0:Tn], in_=ps[:, bq, 0:Tn], func=AF.Silu),
                     reads=[pb[bq]], writes=[q_.b])
                for d in (0, 1):
                    if is_ctx and d == 1:
                        continue
                    bank = bf1 if d == 0 else bf2
                    lcol = d * 8 + hd
                    lb_ap = lbt.t[:, 16 + lcol:17 + lcol]
                    oml_ap = lbt.t[:, 32 + lcol:33 + lcol]
                    p.op('act', lambda e, a_=a_, bank=bank: e.activation(out=a_.t[:, 0:Tn], in_=ps[:, bank, 0:Tn], func=AF.Sigmoid),
                         reads=[pb[bank]], writes=[a_.b])
                    p.op('act', lambda e, b_=b_, bank=bank: e.activation(out=b_.t[:, 0:Tn], in_=ps[:, bank, 0:Tn], func=AF.Sigmoid,
                                                                       scale=-1.0), reads=[pb[bank]], writes=[b_.b])
                    p.op('act', lambda e, a_=a_, lb_ap=lb_ap, oml_ap=oml_ap: e.activation(
                        out=a_.t[:, 0:Tn], in_=a_.t[:, 0:Tn], func=AF.Ln, scale=oml_ap, bias=lb_ap),
                        reads=[a_.b, lbt.b], writes=[a_.b])
                    p.op('dve', lambda e, a_=a_, c_=c_: e.tensor_tensor_scan(
                        out=c_.t[:, 0:Tn], data0=rst.t[:, 0:Tn], data1=a_.t[:, 0:Tn], initial=0.0, op0=ALU.mult, op1=ALU.add),
                        reads=[rst.b, a_.b], writes=[c_.b])
                    c3 = c_.t[:, 0:Tn].rearrange("p (c t) -> p c t", t=64)
                    bend = c3[:, :, 63:64]
                    p.op('dve', lambda e, d_=d_, c3=c3, bend=bend: e.tensor_tensor(
                        out=d_.t[:, 0:Tn].rearrange("p (c t) -> p c t", t=64), in0=bend.to_broadcast([128, nch, 64]), in1=c3,
                        op=ALU.subtract), reads=[c_.b], writes=[d_.b])
                    ebt = EB1 if d == 0 else EB2
                    p.op('act', lambda e, ebt=ebt, bend=bend, hd=hd: e.activation(
                        out=ebt.t[:, hd, 0:nch].unsqueeze(2), in_=bend, func=AF.Exp), reads=[c_.b], writes=[ebt.b])
                    if d == 0:
                        p.op('act', lambda e, e_=e_, c_=c_: e.activation(out=e_.t[:, 0:Tn], in_=c_.t[:, 0:Tn], func=AF.Exp),
                             reads=[c_.b], writes=[e_.b])
                        p.op('dve', lambda e, e_=e_, q_=q_, hd=hd: e.tensor_tensor(
                            out=QE1.t[:, hd, 0:Tn], in0=q_.t[:, 0:Tn], in1=e_.t[:, 0:Tn], op=ALU.mult),
                            reads=[q_.b, e_.b], writes=[QE1.b])
                        p.op('dve', lambda e, e_=e_, c3=c3: e.tensor_tensor(
                            out=e_.t[:, 0:Tn].rearrange("p (c t) -> p c t", t=64), in0=c3,
                            in1=c3[:, :, 31:32].to_broadcast([128, nch, 64]), op=ALU.subtract), reads=[c_.b], writes=[e_.b])
                        p.op('act', lambda e, e_=e_, a_=a_: e.activation(out=a_.t[:, 0:Tn], in_=e_.t[:, 0:Tn], func=AF.Exp),
                             reads=[e_.b], writes=[a_.b])
                        p.op('dve', lambda e, a_=a_, q_=q_, hd=hd: e.tensor_tensor(
                            out=QT1.t[:, hd, 0:Tn], in0=q_.t[:, 0:Tn], in1=a_.t[:, 0:Tn], op=ALU.mult),
                            reads=[q_.b, a_.b], writes=[QT1.b])
                        p.op('act', lambda e, e_=e_: e.activation(out=e_.t[:, 0:Tn], in_=e_.t[:, 0:Tn], func=AF.Exp, scale=-1.0),
                             reads=[e_.b], writes=[e_.b])
                        p.op('dve', lambda e, e_=e_, b_=b_, hd=hd, oml_ap=oml_ap: e.scalar_tensor_tensor(
                            out=KN1.t[:, hd, 0:Tn], in0=b_.t[:, 0:Tn], scalar=oml_ap, in1=e_.t[:, 0:Tn], op0=ALU.mult, op1=ALU.mult),
                            reads=[b_.b, e_.b, lbt.b], writes=[KN1.b])
                        p.op('act', lambda e, d_=d_: e.activation(out=d_.t[:, 0:Tn], in_=d_.t[:, 0:Tn], func=AF.Exp),
                             reads=[d_.b], writes=[d_.b])
                        p.op('dve', lambda e, d_=d_, b_=b_, kh_=kh_, oml_ap=oml_ap: e.scalar_tensor_tensor(
                            out=kh_.t[:, 0:Tn], in0=b_.t[:, 0:Tn], scalar=oml_ap, in1=d_.t[:, 0:Tn], op0=ALU.mult, op1=ALU.mult),
                            reads=[b_.b, d_.b, lbt.b], writes=[kh_.b])
                        KH = KH1
                    else:
                        p.op('dve', lambda e, d_=d_, a_=a_: e.tensor_tensor(out=d_.t[:, 0:Tn], in0=d_.t[:, 0:Tn], in1=a_.t[:, 0:Tn],
                                                                        op=ALU.add), reads=[d_.b, a_.b], writes=[d_.b])
                        sq_, sk_ = s2q[i2], s2k[i2]
                        p.op('act', lambda e, e_=e_, d_=d_: e.activation(out=e_.t[:, 0:Tn], in_=d_.t[:, 0:Tn], func=AF.Exp),
                             reads=[d_.b], writes=[e_.b])
                        p.op('dve', lambda e, e_=e_, q_=q_, sq_=sq_: e.tensor_tensor(
                            out=sq_.t[:, 0:Tn], in0=q_.t[:, 0:Tn], in1=e_.t[:, 0:Tn], op=ALU.mult),
                            reads=[q_.b, e_.b], writes=[sq_.b])
                        st_ = s2t[i2]
                        d3 = d_.t[:, 0:Tn].rearrange("p (c t) -> p c t", t=64)
                        p.op('dve', lambda e, e_=e_, d3=d3: e.tensor_tensor(
                            out=e_.t[:, 0:Tn].rearrange("p (c t) -> p c t", t=64), in0=d3,
                            in1=d3[:, :, 32:33].to_broadcast([128, nch, 64]), op=ALU.subtract), reads=[d_.b], writes=[e_.b])
                        p.op('act', lambda e, e_=e_, d_=d_: e.activation(out=d_.t[:, 0:Tn], in_=e_.t[:, 0:Tn], func=AF.Exp),
                             reads=[e_.b], writes=[d_.b])
                        p.op('dve', lambda e, d_=d_, q_=q_, st_=st_: e.tensor_tensor(
                            out=st_.t[:, 0:Tn], in0=q_.t[:, 0:Tn], in1=d_.t[:, 0:Tn], op=ALU.mult),
                            reads=[q_.b, d_.b], writes=[st_.b])
                        p.op('act', lambda e, e_=e_: e.activation(out=e_.t[:, 0:Tn], in_=e_.t[:, 0:Tn], func=AF.Exp, scale=-1.0),
                             reads=[e_.b], writes=[e_.b])
                        p.op('dve', lambda e, e_=e_, b_=b_, sk_=sk_, oml_ap=oml_ap: e.scalar_tensor_tensor(
                            out=sk_.t[:, 0:Tn], in0=b_.t[:, 0:Tn], scalar=oml_ap, in1=e_.t[:, 0:Tn], op0=ALU.mult, op1=ALU.mult),
                            reads=[b_.b, e_.b, lbt.b], writes=[sk_.b])
                        for (nm_, tl) in (('d2qe', sq_), ('d2kn', sk_), ('d2qt', st_)):
                            dstd = env.d(nm_)[hd, :, tok0:tok0 + Tn]
                            p.dma('sp', lambda e, dstd=dstd, tl=tl: e.dma_start(out=dstd, in_=tl.t[:, 0:Tn]), reads=[tl.b],
                                  writes=[env.b(nm_, si)])
                        p.op('dve', lambda e, d_=d_, c_=c_, a_=a_: e.tensor_tensor(
                            out=d_.t[:, 0:Tn], in0=c_.t[:, 0:Tn], in1=a_.t[:, 0:Tn], op=ALU.subtract),
                            reads=[c_.b, a_.b], writes=[d_.b])
                        p.op('act', lambda e, d_=d_: e.activation(out=d_.t[:, 0:Tn], in_=d_.t[:, 0:Tn], func=AF.Exp),
                             reads=[d_.b], writes=[d_.b])
                        p.op('dve', lambda e, d_=d_, b_=b_, kh_=kh_, oml_ap=oml_ap: e.scalar_tensor_tensor(
                            out=kh_.t[:, 0:Tn], in0=b_.t[:, 0:Tn], scalar=oml_ap, in1=d_.t[:, 0:Tn], op0=ALU.mult, op1=ALU.mult),
                            reads=[b_.b, d_.b, lbt.b], writes=[kh_.b])
                        KH = KH2
                    tbk = 5 + d
                    for j in range(nt):
                        p.op('pe', lambda e, j=j, tbk=tbk, kh_=kh_: e.transpose(
                            psb(tbk)[:, j * 128:(j + 1) * 128], kh_.t[:, j * 128:(j + 1) * 128], identb.t[:]),
                            reads=[kh_.b, identb.b], writes=[pb[tbk]], inc=(j == nt - 1))
                    p.op('dve', lambda e, tbk=tbk, KH=KH, hd=hd: e.tensor_copy(
                        out=KH.t[:, 0:nt, hd * 128:(hd + 1) * 128],
                        in_=psb(tbk)[:, 0:nt * 128].rearrange("p (j k) -> p j k", k=128)),
                        reads=[pb[tbk]], writes=[KH.b])
            if not is_ctx:
                dstd = env.d('d2kh')[tok0:tok0 + Tn, :].rearrange("(j p) n -> p j n", p=128)
                p.dma('sp', lambda e, dstd=dstd: e.dma_start(out=dstd, in_=KH2.t[:, 0:nt, :]), reads=[KH2.b],
                      writes=[env.b('d2kh', si)])
                ch0 = tok0 // 64
                dstd = env.d('d2eb')[:, :, ch0:ch0 + nch]
                p.dma('sp', lambda e, dstd=dstd: e.dma_start(out=dstd, in_=EB2.t[:, :, 0:nch]), reads=[EB2.b],
                      writes=[env.b('d2eb', si)])
            for c in range(nch):
                j = c // 2
                pbs = 64 * (c % 2)
                qt = lambda hd, c=c: (QT1.t[:, hd, c * 64:(c + 1) * 64], QT1.b)
                qe = lambda hd, c=c: (QE1.t[:, hd, c * 64:(c + 1) * 64], QE1.b)
                kn = lambda hd, c=c: (KN1.t[:, hd, c * 64:(c + 1) * 64], KN1.b)
                kh = lambda hd, j=j, pbs=pbs: (KH1.t[pbs:pbs + 64, j, hd * 128:(hd + 1) * 128], KH1.b)
                vt = lambda hd, j=j, pbs=pbs: (Vt.t[pbs:pbs + 64, j, hd * 128:(hd + 1) * 128], Vt.b)
                eb = lambda hd, c=c: (EB1.t[:, hd, c:c + 1], EB1.b)
                oo = None
                if not is_ctx:
                    ot = o1t[0]
                    oo = {'aT': aT, 'o': ot}
                scan_chunk(p, nc, ps, pb, (7, 5, 6, 2, 3), qt, qe, kn, kh, vt, eb, S32, Sbf, msk, c, True, oo)
                if oo is not None and c % 2 == 1:
                    ti = tiles[j][1]
                    dstd = env.d('o1')[ti * 128:(ti + 1) * 128, :]
                    p.dma('sp', lambda e, dstd=dstd, ot=ot: e.dma_start(out=dstd, in_=ot.t[:]), reads=[ot.b],
                          writes=[env.b('o1', ti)])
        for si, tiles in enumerate(sts):
            do_supertile(si, tiles)
        p.dma('sp', lambda e: e.dma_start(out=env.d('s_send')[:, :], in_=S32.t[:].rearrange("p a b -> p (a b)")),
              reads=[S32.b], writes=[env.b('s_send')])
    p.barrier()


def ph_hg2(p, nc, env, ps, pb):
    with ExitStack() as st:
        sb = lambda n, s, d: st.enter_context(nc.sbuf_tensor(_uniq(n), s, d))
        wo = sb('hwo', [128, 8, D], BF16)
        wo_b = [Buf() for _ in range(2)]
        cast_load_w(p, nc, wo, env.d('hg_w_o'), D, D, wo_b, split=2)
        identb = SB(sb('identb', [128, 128], BF16))
        p.dma('pool', lambda e: e.dma_start(out=identb.t[:], in_=env.d('identf')[:, :]), writes=[identb.b])
        msk = SB(sb('msk', [128, 256], F32))
        p.dma('sp', lambda e: e.dma_start(out=msk.t[:], in_=env.d('trimask')[:, :]), writes=[msk.b])
        ng = SB(sb('ng', [128, D], F32))
        p.dma('sp', lambda e: e.dma_start(out=ng.t[:], in_=env.d('hg_ng')[0:1, :].partition_broadcast(128)), writes=[ng.b])
        G = SB(sb('G', [128, D], F32))
        load_modB(p, nc, env, G, 2 * D, 1, 0)
        S32 = SB(sb('S32', [128, 8, 128], F32))
        Sbf = SB(sb('Sbf', [128, 8, 128], BF16))
        p.dma('sp', lambda e: e.dma_start(out=S32.t[:].rearrange("p a b -> p (a b)"), in_=env.d('s_recv')[:, :]),
              reads=[env.b('s_recv')], writes=[S32.b])
        p.op('pool', lambda e: e.tensor_copy(out=Sbf.t[:], in_=S32.t[:]), reads=[S32.b], writes=[Sbf.b])
        R = 2
        QE = [SB(sb(f'QE{i}', [128, 8, 128], BF16)) for i in range(R)]
        KN = [SB(sb(f'KN{i}', [128, 8, 128], BF16)) for i in range(R)]
        QTT = [SB(sb(f'QTT{i}', [128, 8, 128], BF16)) for i in range(R)]
        KH = [SB(sb(f'KH{i}', [128, D], BF16)) for i in range(R)]
        VT = [SB(sb(f'VT{i}', [128, D], BF16)) for i in range(R)]
        SG = [SB(sb(f'SG{i}', [128, D], BF16)) for i in range(R)]
        EBt = [SB(sb(f'EBt{i}', [128, 8, 2], F32)) for i in range(R)]
        O1 = [SB(sb(f'O1{i}', [128, D], F32)) for i in range(R)]
        XB = [SB(sb(f'XB{i}', [128, D], F32)) for i in range(R)]
        o2 = SB(sb('o2', [128, D], F32))
        sq = SB(sb('sq', [128, D], F32))
        hs = SB(sb('hs', [128, 16], F32))
        yb = SB(sb('yb', [128, D], BF16))
        oT = SB(sb('oT', [128, 8, 128], BF16))
        aT = SB(sb('aT', [128, 512], BF16))
        tmp = SB(sb('tmp', [128, D], F32))
        p.op('pool', lambda e: e.memset(aT.t[:], 0.0), writes=[aT.b])
        psb2 = ps[:, 4, :].bitcast(BF16)
        order = list(range(NTQ - 1, -1, -1))

        def loads(n):
            t = order[n]
            r = n % R
            sl = slice(t * 128, (t + 1) * 128)
            si = t // 4
            p.dma('sp', lambda e: e.dma_start(out=QE[r].t[:], in_=env.d('d2qe')[:, :, sl].rearrange("h k n -> k h n")),
                  reads=[env.b('d2qe', si)], writes=[QE[r].b])
            p.dma('sp', lambda e: e.dma_start(out=KN[r].t[:], in_=env.d('d2kn')[:, :, sl].rearrange("h k n -> k h n")),
                  reads=[env.b('d2kn', si)], writes=[KN[r].b])
            p.dma('sp', lambda e: e.dma_start(out=QTT[r].t[:], in_=env.d('d2qt')[:, :, sl].rearrange("h k n -> k h n")),
                  reads=[env.b('d2qt', si)], writes=[QTT[r].b])
            p.dma('sp', lambda e: e.dma_start(out=KH[r].t[:], in_=env.d('d2kh')[sl, :]), reads=[env.b('d2kh', si)], writes=[KH[r].b])
            p.dma('sp', lambda e: e.dma_start(out=VT[r].t[:], in_=env.d('vtok')[sl, :]), reads=[env.b('vtok', si)], writes=[VT[r].b])
            p.dma('sp', lambda e: e.dma_start(out=SG[r].t[:], in_=env.d('sgtok')[sl, :]), reads=[env.b('sgtok', si)], writes=[SG[r].b])
            p.dma('sp', lambda e: e.dma_start(out=EBt[r].t[:], in_=env.d('d2eb')[:, :, 2 * t:2 * t + 2]),
                  reads=[env.b('d2eb', si)], writes=[EBt[r].b])
            p.dma('sp', lambda e: e.dma_start(out=O1[r].t[:], in_=env.d('o1')[sl, :]), reads=[env.b('o1', t)], writes=[O1[r].b])
            p.dma('sp', lambda e: e.dma_start(out=XB[r].t[:], in_=env.d('xb')[sl, :]), reads=[env.b('xb', t)], writes=[XB[r].b])

        loads(0)

        def do_tile(n, t):
            r = n % R
            if n + 1 < len(order):
                loads(n + 1)
            for par in (1, 0):
                pbs = 64 * par
                qt = lambda hd, par=par: (QTT[r].t[:, hd, par * 64:(par + 1) * 64], QTT[r].b)
                qe = lambda hd, par=par: (QE[r].t[:, hd, par * 64:(par + 1) * 64], QE[r].b)
                kn = lambda hd, par=par: (KN[r].t[:, hd, par * 64:(par + 1) * 64], KN[r].b)
                kh = lambda hd, pbs=pbs: (KH[r].t[pbs:pbs + 64, hd * 128:(hd + 1) * 128], KH[r].b)
                vt = lambda hd, pbs=pbs: (VT[r].t[pbs:pbs + 64, hd * 128:(hd + 1) * 128], VT[r].b)
                eb = lambda hd, par=par: (EBt[r].t[:, hd, par:par + 1], EBt[r].b)
                scan_chunk(p, nc, ps, pb, (7, 5, 6, 2, 3), qt, qe, kn, kh, vt, eb, S32, Sbf, msk, par, False, {'aT': aT, 'o': o2})
            p.op('pool', lambda e: e.tensor_tensor(out=o2.t[:], in0=o2.t[:], in1=O1[r].t[:], op=ALU.add),
                 reads=[o2.b, O1[r].b], writes=[o2.b])
            p.op('dve', lambda e: e.tensor_tensor(out=sq.t[:], in0=o2.t[:], in1=o2.t[:], op=ALU.mult), reads=[o2.b], writes=[sq.b])
            p.op('dve', lambda e: e.tensor_reduce(out=hs.t[:, 0:8], in_=sq.t[:].rearrange("p (a b) -> p a b", b=128), axis=AX.X,
                                                  op=ALU.add), reads=[sq.b], writes=[hs.b])
            p.op('dve', lambda e: e.tensor_scalar(out=hs.t[:, 0:8], in0=hs.t[:, 0:8], scalar1=1.0 / 128, scalar2=EPS, op0=ALU.mult,
                                                  op1=ALU.add), reads=[hs.b], writes=[hs.b])
            p.op('act', lambda e: e.activation(out=hs.t[:, 0:8], in_=hs.t[:, 0:8], func=AF.Ln), reads=[hs.b], writes=[hs.b])
            p.op('act', lambda e: e.activation(out=hs.t[:, 8:16], in_=hs.t[:, 0:8], func=AF.Exp, scale=-0.5), reads=[hs.b], writes=[hs.b])
            p.op('dve', lambda e: e.tensor_tensor(
                out=sq.t[:].rearrange("p (a b) -> p a b", b=128), in0=o2.t[:].rearrange("p (a b) -> p a b", b=128),
                in1=hs.t[:, 8:16].unsqueeze(2).to_broadcast([128, 8, 128]), op=ALU.mult), reads=[o2.b, hs.b], writes=[sq.b])
            p.op('pool', lambda e: e.tensor_tensor(out=sq.t[:], in0=sq.t[:], in1=ng.t[:], op=ALU.mult), reads=[sq.b, ng.b], writes=[sq.b])
            p.op('dve', lambda e: e.tensor_tensor(out=yb.t[:], in0=sq.t[:], in1=SG[r].t[:], op=ALU.mult),
                 reads=[sq.b, SG[r].b], writes=[yb.b])
            for j in range(8):
                p.op('pe', lambda e, j=j: e.transpose(psb2[:, j * 128:(j + 1) * 128], yb.t[:, j * 128:(j + 1) * 128], identb.t[:]),
                     reads=[yb.b, identb.b], writes=[pb[4]], inc=(j == 7))
            p.op('act', lambda e: e.activation(out=oT.t[:].rearrange("p a b -> p (a b)"), in_=psb2[:, 0:1024], func=AF.Identity),
                 reads=[pb[4]], writes=[oT.b])
            for nb in range(2):
                bank = nb
                for kc in range(8):
                    p.op('pe', lambda e, bank=bank, kc=kc, nb=nb: e.matmul(
                        ps[:, bank, :], lhsT=oT.t[:, kc, :], rhs=wo[:, kc, nb * 512:(nb + 1) * 512], start=(kc == 0), stop=(kc == 7)),
                        reads=[oT.b, wo_b[nb]], writes=[pb[bank]], inc=(kc == 7))
                p.op('dve', lambda e, bank=bank, nb=nb: e.tensor_tensor(
                    out=tmp.t[:, nb * 512:(nb + 1) * 512], in0=ps[:, bank, :], in1=G.t[:, nb * 512:(nb + 1) * 512], op=ALU.mult),
                    reads=[pb[bank], G.b], writes=[tmp.b])
                p.op('pool', lambda e, nb=nb: e.tensor_tensor(
                    out=tmp.t[:, nb * 512:(nb + 1) * 512], in0=tmp.t[:, nb * 512:(nb + 1) * 512],
                    in1=XB[r].t[:, nb * 512:(nb + 1) * 512], op=ALU.add), reads=[tmp.b, XB[r].b], writes=[tmp.b])
            dst = env.d('xc')[t * 128:(t + 1) * 128, :]
            p.dma('sp', lambda e, dst=dst: e.dma_start(out=dst, in_=tmp.t[:]), reads=[tmp.b], writes=[env.b('xc', t)])

        for n, t in enumerate(order):
            do_tile(n, t)
    p.barrier()


def ph_xchg(p, nc, env, ps, pb):
    with ExitStack() as st:
        sb = lambda n, s, d: st.enter_context(nc.sbuf_tensor(_uniq(n), s, d))
        snd = env.d('s_send')
        gat = env.d('s_gath')
        p.collective(lambda e: e.collective_compute("AllGather", ALU.bypass, replica_groups=[[0, 1], [2, 3], [4, 5], [6, 7]],
                                                    ins=[snd.opt()], outs=[gat.opt()]),
                     reads=[env.b('s_send')], writes=[env.b('s_gath')])
        g = SB(sb('xg', [128, 2, D], F32))
        sel = SB(sb('xsel', [128, 2], F32))
        r = SB(sb('xr', [128, D], F32))
        p.dma('sp', lambda e: e.dma_start(out=g.t[:], in_=gat.rearrange("(r p) n -> p r n", p=128)), reads=[env.b('s_gath')],
              writes=[g.b])
        p.dma('sp', lambda e: e.dma_start(out=sel.t[:], in_=env.d('sel')[:, :]), writes=[sel.b])
        p.op('dve', lambda e: e.tensor_scalar(out=r.t[:], in0=g.t[:, 0, :], scalar1=sel.t[:, 0:1], scalar2=None, op0=ALU.mult),
             reads=[g.b, sel.b], writes=[r.b])
        p.op('dve', lambda e: e.scalar_tensor_tensor(out=r.t[:], in0=g.t[:, 1, :], scalar=sel.t[:, 1:2], in1=r.t[:], op0=ALU.mult,
                                                     op1=ALU.add), reads=[g.b, sel.b, r.b], writes=[r.b])
        p.dma('sp', lambda e: e.dma_start(out=env.d('s_recv')[:, :], in_=r.t[:]), reads=[r.b], writes=[env.b('s_recv')])
    p.barrier()


def _coords(tile, half):
    l = tile * 128 + np.arange(128)
    g = l if half == 0 else 8191 - l
    return g // 64, g % 64


def rpb_tables(rpb, half):
    flat = np.concatenate([rpb.reshape(16, 15 * 31), np.full((16, 1), PADV, np.float32)], axis=1)

    def table(t, kts):
        qr, qc = _coords(t, half)
        out = np.empty((128, 16, len(kts), 128), np.float32)
        rs = np.clip(qr - 4, 0, 120)
        cs = np.clip(qc - 8, 0, 48)
        for a, kt in enumerate(kts):
            kr, kc = _coords(kt, half)
            valid = ((kr[:, None] >= rs[None, :]) & (kr[:, None] <= rs[None, :] + 7) &
                     (kc[:, None] >= cs[None, :]) & (kc[:, None] <= cs[None, :] + 15))
            idx = (kr[:, None] - qr[None, :] + 7) * 31 + (kc[:, None] - qc[None, :] + 15)
            idx = np.where(valid, idx, 465)
            out[:, :, a, :] = np.transpose(flat[:, idx], (1, 0, 2))
        return out.reshape(128, -1)

    ebg = table(10, [8, 9, 10, 11, 12])
    ebs = np.stack([table(0, [0, 1, 2, 3]), table(1, [0, 1, 2, 3])], axis=0)
    return ebg, ebs


def host_consts():
    tri = np.zeros((128, 256), np.float32)
    for blk in range(2):
        s = np.arange(64)[:, None]
        t = np.arange(64)[None, :]
        tri[blk * 64:(blk + 1) * 64, blk * 64:(blk + 1) * 64] = (s <= t)
        tri[blk * 64:(blk + 1) * 64, 128 + blk * 64:128 + (blk + 1) * 64] = (s >= t)
    rst = np.ones((128, 512), np.float32)
    rst[:, 0::64] = 0.0
    return tri, rst


def prep_inputs(inp):
    f = lambda a: np.ascontiguousarray(a, dtype=np.float32)
    shared = {
        'ada_w': f(inp['ada_w']),
        'ada_b': f(inp['ada_b'].reshape(1, -1)),
        'n1g': f(inp['norm1_g'].reshape(2, 8, 128).transpose(2, 0, 1).reshape(128, 16)),
        'n2g': f(inp['norm2_g'].reshape(2, 8, 128).transpose(2, 0, 1).reshape(128, 16)),
        'identf': np.eye(128, dtype=np.float32),
        'w_qkv': f(inp['na_w_qkv'][0]),
        'w_o0': f(inp['na_w_o'][0]),
        'qkgain': f(np.stack([np.tile(inp['na_q_gain'][0], 2), np.tile(inp['na_k_gain'][0], 2)], axis=1)),
        'ffn_w_in': f(inp['ffn_w_in']),
        'ffn_w_out': f(inp['ffn_w_out']),
        'hg_w_o': f(inp['hg_w_o'][0]),
        'hg_ng': f(np.tile(inp['hg_norm_g'][0], 8).reshape(1, D)),
    }
    shared['trimask'], shared['rstmask'] = host_consts()
    hw = inp['hg_w_in'][0]
    hw_sw = f(np.concatenate([hw[:, 0:3 * D], hw[:, 4 * D:5 * D], hw[:, 3 * D:4 * D]], axis=1))
    low = inp['hg_lower'].reshape(2, 2, 8, 128)
    per_half = []
    for half in range(2):
        ebg, ebs = rpb_tables(f(inp['na_rpb'][0]), half)
        lo = low if half == 0 else low[:, ::-1]
        per_half.append({
            'ebg': ebg, 'ebs': ebs,
            'hg_w_in': f(hw) if half == 0 else hw_sw,
            'hg_low': f(lo.transpose(3, 0, 1, 2).reshape(128, 32)),
        })
    maps = []
    for b in range(4):
        for half in range(2):
            xs = inp['x'][b] if half == 0 else inp['x'][b, ::-1]
            cs = inp['ctx'][b] if half == 0 else inp['ctx'][b, ::-1]
            m = dict(shared)
            m.update(per_half[half])
            m['xin'] = f(xs[0:NTK * 128])
            m['ctxin'] = f(cs)
            m['sel'] = f(np.tile(np.array([[1.0, 0.0]] if half == 1 else [[0.0, 1.0]], np.float32), (128, 1)))
            m['csil'] = f(np.stack([inp['c'][b].reshape(8, 128).T, inp['c_ctx'].reshape(8, 128).T], axis=2).reshape(128, 16))
            maps.append(m)
    return maps


PHASES = {'ada': ph_ada, 'att0': ph_att0, 'ffn0': make_ffn(0, 'xa', 'xb', True), 'ffn1': make_ffn(1, 'xc', 'out', False), 'hg1': ph_hg1, 'hg2': ph_hg2, 'xchg': ph_xchg}


def build(phases, ext_in, ext_out):
    nc = bass.Bass("TRN2", target_bir_lowering=False)
    env = Env(nc, set(ext_in), set(ext_out), SHAPES)
    with ExitStack() as st:
        p = Prog(nc, st)
        ps = st.enter_context(nc.psum_tensor("ps", [128, 8, 512], F32))
        pb = [Buf() for _ in range(8)]
        for ph in phases:
            PHASES[ph](p, nc, env, ps, pb)
        p.finish()
    return nc, env, p


def run_launch(phases, ext_in_maps, ext_out, cores):
    nc, env, p = build(phases, ext_in_maps[0].keys(), ext_out)
    used = set(env.t.keys())
    in_maps = [{k: v for k, v in m.items() if k in used} for m in ext_in_maps]
    res = run_bass_kernel_spmd(nc, in_maps, core_ids=list(range(len(cores))))
    return res


HOST_IN = ['xin', 'ctxin', 'sel', 'csil', 'ada_w', 'ada_b', 'n1g', 'n2g', 'identf', 'w_qkv', 'w_o0', 'qkgain', 'ebg', 'ebs',
           'ffn_w_in', 'ffn_w_out', 'hg_w_in', 'hg_w_o', 'hg_low', 'hg_ng', 'trimask', 'rstmask']
L1_OUT = ['modrow', 'xb', 'o1', 'vtok', 'sgtok', 'd2kh', 'd2qe', 'd2kn', 'd2qt', 'd2eb', 's_send']


def _launch(phases, maps, ext_out):
    nc, env, _ = build(phases, maps[0].keys(), ext_out)
    used = set(env.t.keys())
    in_maps = [{k: v for k, v in m.items() if k in used} for m in maps]
    res = run_bass_kernel_spmd(nc, in_maps, core_ids=list(range(len(maps))))
    return res.results


def kernel(**inputs):
    maps = prep_inputs(inputs)
    r = _launch(['ada', 'att0', 'ffn0', 'hg1', 'xchg', 'hg2', 'ffn1'], maps, ['out'])
    out = np.empty((4, 8192, D), np.float32)
    for b in range(4):
        out[b, 0:NTOK] = np.asarray(r[2 * b]['out'])
        out[b, NTOK:] = np.asarray(r[2 * b + 1]['out'])[::-1]
    return out
```

```python
import re
import numpy as np
from contextlib import ExitStack
import concourse.bass as bass
import concourse.mybir as mybir
from concourse.bass_utils import run_bass_kernel_spmd

F32 = mybir.dt.float32
BF16 = mybir.dt.bfloat16
AF = mybir.ActivationFunctionType
ALU = mybir.AluOpType
AX = mybir.AxisListType

D = 1024
EPS = 1e-6
NTQ = 32
NTK = 34
NTOK = NTQ * 128
DFF = 2816
PADV = -200.0


class Buf:
    __slots__ = ("w", "r", "excl")

    def __init__(self, excl=False):
        self.w = None
        self.r = {}
        self.excl = excl


NDSEM = 20


class Prog:
    ENG = {'pe': 'tensor', 'act': 'scalar', 'dve': 'vector', 'pool': 'gpsimd', 'sp': 'sync'}

    def __init__(self, nc, stack):
        self.nc = nc
        self.q = {e: [] for e in self.ENG}
        self.cnt = {e: 0 for e in self.ENG}
        self.sem = {e: stack.enter_context(nc.semaphore("s_" + e)) for e in self.ENG}
        self.cc_sem = stack.enter_context(nc.semaphore("s_cc"))
        self.cc_cnt = 0
        self.dsem = {}
        self.dcnt = {}
        self.drr = {}
        for qn in ('sp', 'act', 'pool'):
            self.dsem[qn] = [stack.enter_context(nc.semaphore(f"d_{qn}{i}")) for i in range(NDSEM)]
            self.dcnt[qn] = [0] * NDSEM
            self.drr[qn] = 0

    def _deps(self, eng, reads, writes):
        best = {}
        for b in reads:
            if b.w is not None:
                k, v = b.w
                if best.get(k, 0) < v:
                    best[k] = v
            if b.excl:
                for k, v in b.r.items():
                    if k != eng and best.get(k, 0) < v:
                        best[k] = v
        for b in writes:
            if b.w is not None:
                k, v = b.w
                if k != eng and best.get(k, 0) < v:
                    best[k] = v
            for k, v in b.r.items():
                if k == eng:
                    continue
                if best.get(k, 0) < v:
                    best[k] = v
        if eng == 'pe':
            best.pop('pe', None)
        return best

    def _mark(self, tok, reads, writes):
        k, v = tok
        for b in reads:
            if b.r.get(k, 0) < v:
                b.r[k] = v
        for b in writes:
            b.w = tok
            b.r = {}

    def op(self, eng, fn, reads=(), writes=(), inc=True):
        for k, v in self._deps(eng, reads, writes).items():
            self.q[eng].append(('wait', k, v))
        tok = (eng, self.cnt[eng] + 1)
        if inc:
            self.cnt[eng] += 1
        self.q[eng].append(('op', fn, inc))
        self._mark(tok, reads, writes)
        return tok

    def dma(self, qn, fn, reads=(), writes=()):
        for k, v in self._deps(qn + '_q', reads, writes).items():
            self.q[qn].append(('wait', k, v))
        i = self.drr[qn]
        self.drr[qn] = (i + 1) % NDSEM
        prev = self.dcnt[qn][i]
        key = f"d_{qn}{i}"
        if prev > 0:
            self.q[qn].append(('wait', key, prev))
        self.dcnt[qn][i] = prev + 16
        tok = (key, prev + 16)
        self.q[qn].append(('dma', fn, i))
        self._mark(tok, reads, writes)
        return tok

    def collective(self, fn, reads=(), writes=()):
        for k, v in self._deps('pool_q', reads, writes).items():
            self.q['pool'].append(('wait', k, v))
        self.cc_cnt += 1
        tok = ('cc', self.cc_cnt)
        self.q['pool'].append(('cc', fn))
        self._mark(tok, reads, writes)
        return tok

    def barrier(self):
        toks = [(e, c) for e, c in self.cnt.items() if c > 0]
        if self.cc_cnt > 0:
            toks.append(('cc', self.cc_cnt))
        for qn in self.dcnt:
            for i, c in enumerate(self.dcnt[qn]):
                if c > 0:
                    toks.append((f"d_{qn}{i}", c))
        for e in self.ENG:
            for k, v in toks:
                if k != e:
                    self.q[e].append(('wait', k, v))

    def semof(self, key):
        if key == 'cc':
            return self.cc_sem
        if key in self.sem:
            return self.sem[key]
        m = re.match(r"d_([a-z]+)(\d+)", key)
        return self.dsem[m.group(1)][int(m.group(2))]

    def finish(self):
        self.barrier()
        nc = self.nc
        with nc.Block() as block:
            for eng, attr in self.ENG.items():
                items = self.q[eng]

                def body(e, items=items, eng=eng):
                    known = {}
                    for it in items:
                        if it[0] == 'wait':
                            _, k, v = it
                            if known.get(k, 0) >= v:
                                continue
                            known[k] = v
                            e.wait_ge(self.semof(k), v)
                        elif it[0] == 'cc':
                            it[1](e).then_inc(self.cc_sem)
                        elif it[0] == 'op':
                            ins = it[1](e)
                            if it[2]:
                                ins.then_inc(self.sem[eng], 1)
                        else:
                            ins = it[1](e)
                            ins.then_inc(self.dsem[eng][it[2]], 16)
                getattr(block, attr)(body)


_UNIQ = [0]


def _uniq(n):
    _UNIQ[0] += 1
    return f"{n}_{_UNIQ[0]}"


class SB:
    def __init__(self, t):
        self.t = t
        self.b = Buf()
        self.b2 = Buf()

    @property
    def bb(self):
        return [self.b, self.b2]


class Env:
    def __init__(self, nc, ext_in, ext_out, shapes):
        self.nc = nc
        self.ext_in = ext_in
        self.ext_out = ext_out
        self.shapes = shapes
        self.t = {}
        self.bufs = {}

    def d(self, name):
        if name not in self.t:
            shape, dt = self.shapes[name]
            kind = "ExternalInput" if name in self.ext_in else ("ExternalOutput" if name in self.ext_out else "Internal")
            self.t[name] = self.nc.dram_tensor(name, list(shape), dt, kind=kind).ap()
        return self.t[name]

    def b(self, name, idx=0):
        key = (name, idx)
        if key not in self.bufs:
            self.bufs[key] = Buf()
        return self.bufs[key]


SHAPES = {
    'xin': ((NTK * 128, D), F32),
    'ctxin': ((256, D), F32),
    'csil': ((128, 16), F32),
    'ada_w': ((2, D, 6 * D), F32),
    'ada_b': ((1, 2 * 6 * D), F32),
    'n1g': ((128, 16), F32),
    'n2g': ((128, 16), F32),
    'identf': ((128, 128), F32),
    'w_qkv': ((D, 3 * D), F32),
    'w_o0': ((D, D), F32),
    'qkgain': ((128, 2), F32),
    'ebg': ((128, 16 * 5 * 128), F32),
    'ebs': ((2, 128, 16 * 4 * 128), F32),
    'ffn_w_in': ((2, D, 2 * DFF), F32),
    'ffn_w_out': ((2, DFF, D), F32),
    'hg_w_in': ((D, 5 * D), F32),
    'hg_w_o': ((D, D), F32),
    'hg_low': ((128, 32), F32),
    'hg_ng': ((1, D), F32),
    'trimask': ((128, 256), F32),
    'rstmask': ((128, 512), F32),
    'modrow': ((1, 4 * 6 * D), F32),
    'xa': ((NTOK, D), F32), 'xa_ctx': ((256, D), F32),
    'xb': ((NTOK, D), F32), 'xb_ctx': ((256, D), F32),
    'xc': ((NTOK, D), F32),
    'out': ((NTOK, D), F32),
    'o1': ((NTOK, D), F32),
    'vtok': ((NTOK, D), BF16), 'sgtok': ((NTOK, D), BF16), 'd2kh': ((NTOK, D), BF16),
    'd2qe': ((8, 128, NTOK), BF16), 'd2kn': ((8, 128, NTOK), BF16), 'd2qt': ((8, 128, NTOK), BF16),
    'd2eb': ((128, 8, NTOK // 64), F32),
    's_send': ((128, D), F32), 's_recv': ((128, D), F32), 's_gath': ((256, D), F32), 'sel': ((128, 2), F32),
}


def cast_load_w(p, nc, dst, dram2d, K, N, bufs, q='pool', split=1):
    kc = K // 128
    step = N // split
    for s in range(split):
        src = dram2d[:, s * step:(s + 1) * step].rearrange("(c p) n -> p c n", p=128)
        p.dma(q, lambda e, s=s, src=src: e.dma_start(out=dst[:, :, s * step:(s + 1) * step], in_=src),
              writes=[bufs[s]])


def ph_ada(p, nc, env, ps, pb):
    with ExitStack() as st:
        sb = lambda n, s, d: st.enter_context(nc.sbuf_tensor(_uniq(n), s, d))
        cs_f = SB(sb('cs_f', [128, 16], F32))
        cs_b = SB(sb('cs_b', [128, 16], BF16))
        brow = SB(sb('brow', [1, 2 * 6 * D], F32))
        mrow = SB(sb('mrow', [1, 4 * 6 * D], F32))
        wr = [SB(sb(f'adaw{i}', [128, 8, 512], BF16)) for i in range(3)]
        p.dma('sp', lambda e: e.dma_start(out=cs_f.t[:], in_=env.d('csil')[:, :]), writes=[cs_f.b])
        p.dma('sp', lambda e: e.dma_start(out=brow.t[:], in_=env.d('ada_b')[:, :]), writes=[brow.b])
        p.op('act', lambda e: e.activation(out=cs_b.t[:], in_=cs_f.t[:], func=AF.Silu), reads=[cs_f.b], writes=[cs_b.b])
        aw = env.d('ada_w')
        n = 0
        for i in range(2):
            for nb in range(12):
                w = wr[n % 3]
                cast_load_w(p, nc, w.t, aw[i][:, nb * 512:(nb + 1) * 512], D, 512, [w.b])
                for m in range(2):
                    bank = (n * 2 + m) % 8
                    for kc in range(8):
                        p.op('pe', lambda e, bank=bank, kc=kc, m=m, w=w: e.matmul(
                            ps[0:1, bank, :], lhsT=cs_b.t[:, 2 * kc + m:2 * kc + m + 1], rhs=w.t[:, kc, :],
                            start=(kc == 0), stop=(kc == 7)),
                            reads=[cs_b.b, w.b], writes=[pb[bank]], inc=(kc == 7))
                    o0 = (i * 2 + m) * 6 * D + nb * 512
                    b0 = i * 6 * D + nb * 512
                    p.op('dve', lambda e, bank=bank, o0=o0, b0=b0: e.tensor_tensor(
                        out=mrow.t[0:1, o0:o0 + 512], in0=ps[0:1, bank, :], in1=brow.t[0:1, b0:b0 + 512], op=ALU.add),
                        reads=[pb[bank], brow.b], writes=[mrow.b])
                n += 1
        p.dma('sp', lambda e: e.dma_start(out=env.d('modrow')[:, :], in_=mrow.t[0:1, :]),
              reads=[mrow.b], writes=[env.b('modrow')])
    p.barrier()


def load_modP(p, nc, env, dst, col0, layer, m, ncols=1):
    base = (layer * 2 + m) * 6 * D + col0
    src = env.d('modrow')[0, base:base + D].rearrange("(j p) -> p j", p=128)
    p.dma('sp', lambda e: e.dma_start(out=dst.t[:, 0:8], in_=src, allow_slow_non_contiguous=True),
          reads=[env.b('modrow')], writes=[dst.b])


def load_modB(p, nc, env, dst, col0, layer, m):
    base = (layer * 2 + m) * 6 * D + col0
    src = env.d('modrow')[0:1, base:base + D].partition_broadcast(128)
    p.dma('sp', lambda e: e.dma_start(out=dst.t[:], in_=src), reads=[env.b('modrow')], writes=[dst.b])


def rstd_from_ss(p, s, c_in, c_tmp, c_out, scale, n=1):
    p.op('act', lambda e: e.activation(out=s.t[:, c_tmp:c_tmp + n], in_=s.t[:, c_in:c_in + n], func=AF.Ln, scale=scale, bias=EPS),
         reads=[s.b], writes=[s.b])
    p.op('act', lambda e: e.activation(out=s.t[:, c_out:c_out + n], in_=s.t[:, c_tmp:c_tmp + n], func=AF.Exp, scale=-0.5),
         reads=[s.b], writes=[s.b])


class NormMod:
    def __init__(self, p, nc, st, env, ps, pb, layer, which, ms, banks, tag, nxn=2, junk=None):
        self.p, self.nc, self.ps, self.pb, self.banks = p, nc, ps, pb, banks
        sb = lambda n, s, d: st.enter_context(nc.sbuf_tensor(_uniq(n + tag), s, d))
        self.identf = SB(sb('identf', [128, 128], F32))
        p.dma('sp', lambda e: e.dma_start(out=self.identf.t[:], in_=env.d('identf')[:, :]), writes=[self.identf.b])
        if junk is None:
            self.junk_t = sb('junk', [128, D], BF16)
            self.junk_b = Buf()
        else:
            self.junk_t, self.junk_b = junk
        self.nxn = nxn
        self.xn = [SB(sb(f'xn{i}', [128, D], F32)) for i in range(nxn)]
        self.st = [SB(sb(f'st{i}', [128, 4], F32)) for i in range(2)]
        self.k = 0
        gP = SB(sb('gP', [128, 16], F32))
        p.dma('sp', lambda e: e.dma_start(out=gP.t[:], in_=env.d('n1g' if which == 0 else 'n2g')[:, :]), writes=[gP.b])
        self.geff = {}
        self.sh = {}
        for m in ms:
            shP = SB(sb(f'shP{m}', [128, 8], F32))
            scP = SB(sb(f'scP{m}', [128, 8], F32))
            ge = SB(sb(f'geff{m}', [128, 8], F32))
            load_modP(p, nc, env, shP, (0 if which == 0 else 3 * D), layer, m)
            load_modP(p, nc, env, scP, (D if which == 0 else 4 * D), layer, m)
            p.op('dve', lambda e, scP=scP, ge=ge: e.scalar_tensor_tensor(
                out=ge.t[:], in0=scP.t[:], scalar=1.0, in1=gP.t[:, layer * 8:layer * 8 + 8], op0=ALU.add, op1=ALU.mult),
                reads=[scP.b, gP.b], writes=[ge.b])
            self.geff[m] = ge
            self.sh[m] = shP

    def part1(self, x):
        p = self.p
        k = self.k
        self.k += 1
        xn = self.xn[k % self.nxn]
        s = self.st[k % 2]
        p.op('act', lambda e: e.activation(out=self.junk_t[:], in_=x.t[:], func=AF.Square, accum_out=s.t[:, 0:1]),
             reads=[x.b], writes=[self.junk_b, s.b])
        rstd_from_ss(p, s, 0, 1, 2, 1.0 / D)
        p.op('pool', lambda e: e.tensor_scalar(out=xn.t[:], in0=x.t[:], scalar1=s.t[:, 2:3], scalar2=None, op0=ALU.mult),
             reads=[x.b, s.b], writes=[xn.b])
        return xn

    def part2(self, xn, m, hT, col0):
        p, ps, pb = self.p, self.ps, self.pb
        b0, b1 = self.banks
        for j in (0, 2, 4, 6, 1, 3, 5, 7):
            bank = b0 if j % 2 == 0 else b1
            p.op('pe', lambda e, j=j, bank=bank: e.transpose(ps[:, bank, (j // 2) * 128:(j // 2 + 1) * 128],
                                                             xn.t[:, j * 128:(j + 1) * 128], self.identf.t[:]),
                 reads=[xn.b, self.identf.b], writes=[pb[bank]], inc=(j >= 6))
        ge, shP = self.geff[m], self.sh[m]
        for j in range(8):
            bank = b0 if j % 2 == 0 else b1
            if j % 2 == 0:
                p.op('act', lambda e, j=j, bank=bank: e.activation(
                    out=hT.t[:, j, col0:col0 + 128], in_=ps[:, bank, (j // 2) * 128:(j // 2 + 1) * 128],
                    func=AF.Identity, scale=ge.t[:, j:j + 1], bias=shP.t[:, j:j + 1]),
                    reads=[pb[bank], ge.b, shP.b], writes=[hT.b])
            else:
                p.op('dve', lambda e, j=j, bank=bank: e.tensor_scalar(
                    out=hT.t[:, j, col0:col0 + 128], in0=ps[:, bank, (j // 2) * 128:(j // 2 + 1) * 128],
                    scalar1=ge.t[:, j:j + 1], scalar2=shP.t[:, j:j + 1], op0=ALU.mult, op1=ALU.add),
                    reads=[pb[bank], ge.b, shP.b], writes=[hT.b2])

    def run(self, x, m, hT, col0):
        self.part2(self.part1(x), m, hT, col0)


def interleave(gens):
    gens = list(gens)
    while gens:
        for g in list(gens):
            try:
                next(g)
            except StopIteration:
                gens.remove(g)


def ph_att0(p, nc, env, ps, pb):
    NK = 7
    NQ = 5
    with ExitStack() as st:
        sb = lambda n, s, d: st.enter_context(nc.sbuf_tensor(_uniq(n), s, d))
        wqkv = sb('wqkv', [128, 8, 3 * D], BF16)
        wq_b = [Buf() for _ in range(6)]
        cast_load_w(p, nc, wqkv, env.d('w_qkv'), D, 3 * D, wq_b, split=6)
        wo = sb('wo', [128, 8, D], BF16)
        wo_b = [Buf() for _ in range(2)]
        cast_load_w(p, nc, wo, env.d('w_o0'), D, D, wo_b, split=2)
        identb = SB(sb('identb', [128, 128], BF16))
        p.dma('pool', lambda e: e.dma_start(out=identb.t[:], in_=env.d('identf')[:, :]), writes=[identb.b])
        qkg = SB(sb('qkg', [128, 2], F32))
        p.dma('sp', lambda e: e.dma_start(out=qkg.t[:], in_=env.d('qkgain')[:, :]), writes=[qkg.b])
        G = SB(sb('G', [128, D], F32))
        sq = SB(sb('sq', [128, 512], F32))
        nm = NormMod(p, nc, st, env, ps, pb, 0, 0, [0, 1], (0, 1), 'a0', nxn=1, junk=(sq.t[:].bitcast(BF16), sq.b))
        EB = SB(sb('EB', [128, 16, 5, 128], BF16))
        EBS = SB(sb('EBS', [128, 16, 4, 128], BF16))
        stage = [SB(sb(f'ebst{i}', [128, 640], F32)) for i in range(2)]
        xr = [SB(sb(f'xr{i}', [128, D], F32)) for i in range(2)]
        hT = [SB(sb(f'hT{i}', [128, 8, 128], BF16)) for i in range(2)]
        KT = [SB(sb(f'KT{i}', [128, 8, 128], BF16)) for i in range(NK + 2)]
        QT = [SB(sb(f'QT{i}', [128, 8, 128], BF16)) for i in range(NQ)]
        V = [SB(sb(f'V{i}', [128, 16, 65], BF16)) for i in range(NK + 2)]
        qf = SB(sb('qf', [128, 2 * D], F32))
        hs = [SB(sb(f'hs{i}', [128, 32], F32)) for i in range(2)]
        qkt = SB(sb('qkt', [128, 2 * D], BF16))
        PT = [SB(sb(f'PT{i}', [128, 896], BF16)) for i in range(2)]
        otok = SB(sb('otok', [128, D], BF16))
        oT = SB(sb('oT', [128, 8, 128], BF16))
        rc = SB(sb('rc', [128, 4], F32))
        tmp = SB(sb('tmp', [128, D], F32))
        xres = SB(sb('xres', [128, D], F32))
        psb2 = ps[:, 2, :].bitcast(BF16)

        for v in V:
            p.op('pool', lambda e, v=v: e.memset(v.t[:, :, 64:65], 1.0), writes=[v.b])

        def load_table(dst, src2d, ndt):
            for h in range(16):
                sg = stage[h % 2]
                w = ndt * 128
                p.dma('sp', lambda e, h=h, sg=sg, w=w: e.dma_start(out=sg.t[:, 0:w], in_=src2d[:, h * w:(h + 1) * w]),
                      writes=[sg.b])
                p.op('act', lambda e, h=h, sg=sg, w=w: e.activation(
                    out=dst.t[:, h, :, :], in_=sg.t[:, 0:w].rearrange("p (a b) -> p a b", b=128), func=AF.Exp),
                    reads=[sg.b], writes=[dst.b])

        load_table(EB, env.d('ebg'), 5)

        gbank = [0]

        def next_bank():
            gbank[0] ^= 1
            return gbank[0]

        qtasks = [('ctx', 0), ('ctx', 1)] + [('lat', i) for i in range(NTK)]

        def xsrc(task):
            kind, i = task
            return (env.d('ctxin') if kind == 'ctx' else env.d('xin'))[i * 128:(i + 1) * 128, :]

        def issue_xload(pos):
            x = xr[pos % 2]
            src = xsrc(qtasks[pos])
            p.dma('sp', lambda e: e.dma_start(out=x.t[:], in_=src), writes=[x.b])

        def kslot(kind, i):
            return NK + i if kind == 'ctx' else i % NK

        def gen_qkv(pos):
            kind, i = qtasks[pos]
            m = 1 if kind == 'ctx' else 0
            if pos + 1 < len(qtasks):
                issue_xload(pos + 1)
            x = xr[pos % 2]
            h = hT[pos % 2]
            xn_ = nm.part1(x)
            yield
            yield
            yield
            yield
            nm.part2(xn_, m, h, 0)
            yield
            need_q = not (kind == 'lat' and i >= NTQ)
            ks = kslot(kind, i)
            blocks = ([0, 1] if need_q else []) + [2, 3, 4, 5]
            for nb in blocks:
                bank = next_bank()
                for kc in range(8):
                    p.op('pe', lambda e, bank=bank, kc=kc, nb=nb: e.matmul(
                        ps[:, bank, :], lhsT=h.t[:, kc, :], rhs=wqkv[:, kc, nb * 512:(nb + 1) * 512],
                        start=(kc == 0), stop=(kc == 7)),
                        reads=[h.b, h.b2, wq_b[nb]], writes=[pb[bank]], inc=(kc == 7))
                if nb < 4:
                    p.op('act', lambda e, bank=bank, nb=nb: e.activation(
                        out=qf.t[:, nb * 512:(nb + 1) * 512], in_=ps[:, bank, :], func=AF.Identity),
                        reads=[pb[bank]], writes=[qf.b])
                else:
                    vv = V[ks]
                    p.op('dve', lambda e, bank=bank, nb=nb, vv=vv: e.tensor_copy(
                        out=vv.t[:, (nb - 4) * 8:(nb - 4) * 8 + 8, 0:64],
                        in_=ps[:, bank, :].rearrange("p (a b) -> p a b", b=64)),
                        reads=[pb[bank]], writes=[vv.b])
                yield
            s = hs[pos % 2]
            c0 = 0 if need_q else 2
            for nb in range(c0, 4):
                p.op('act', lambda e, nb=nb: e.activation(out=sq.t[:], in_=qf.t[:, nb * 512:(nb + 1) * 512], func=AF.Square),
                     reads=[qf.b], writes=[sq.b])
                p.op('dve', lambda e, nb=nb: e.tensor_reduce(out=s.t[:, nb * 8:nb * 8 + 8],
                                                             in_=sq.t[:].rearrange("p (a b) -> p a b", b=64), axis=AX.X, op=ALU.add),
                     reads=[sq.b], writes=[s.b])
            yield
            yield
            rstd_from_ss(p, s, c0 * 8, c0 * 8, c0 * 8, 1.0 / 64, n=(4 - c0) * 8)
            yield
            yield
            for nb in range(c0, 4):
                cc = nb * 8
                p.op('dve', lambda e, nb=nb, cc=cc: e.tensor_tensor(
                    out=qkt.t[:, nb * 512:(nb + 1) * 512].rearrange("p (a b) -> p a b", b=64),
                    in0=qf.t[:, nb * 512:(nb + 1) * 512].rearrange("p (a b) -> p a b", b=64),
                    in1=s.t[:, cc:cc + 8].unsqueeze(2).to_broadcast([128, 8, 64]), op=ALU.mult),
                    reads=[qf.b, s.b], writes=[qkt.b])
            yield
            yield
            for which in ([0, 1] if need_q else [1]):
                dst = (QTc[i] if kind == 'ctx' else QT[i % NQ]) if which == 0 else KT[ks]
                for j in range(8):
                    p.op('pe', lambda e, j=j, which=which: e.transpose(
                        psb2[:, j * 128:(j + 1) * 128], qkt.t[:, which * D + j * 128:which * D + (j + 1) * 128], identb.t[:]),
                        reads=[qkt.b, identb.b], writes=[pb[2]], inc=(j == 7))
                if which == 0:
                    p.op('dve', lambda e, dst=dst: e.tensor_scalar(
                        out=dst.t[:].rearrange("p a b -> p (a b)"), in0=psb2[:, 0:1024], scalar1=qkg.t[:, 0:1], scalar2=0.125,
                        op0=ALU.mult, op1=ALU.mult), reads=[pb[2], qkg.b], writes=[dst.b])
                else:
                    p.op('dve', lambda e, dst=dst: e.tensor_scalar(
                        out=dst.t[:].rearrange("p a b -> p (a b)"), in0=psb2[:, 0:1024], scalar1=qkg.t[:, 1:2], scalar2=None,
                        op0=ALU.mult), reads=[pb[2], qkg.b], writes=[dst.b])
                yield

        hcount = [0]

        def gen_att(task):
            kind, t = task
            m = 1 if kind == 'ctx' else 0
            src = xsrc(task)
            p.dma('sp', lambda e: e.dma_start(out=xres.t[:], in_=src), writes=[xres.b])
            if (kind, t) in (('ctx', 0), ('lat', 0)):
                load_modB(p, nc, env, G, 2 * D, 0, m)
            if kind == 'lat' and t in (0, 1):
                load_table(EBS, env.d('ebs')[t], 4)
            if kind == 'ctx':
                chunks = [('ctx', 0), ('ctx', 1)]
                nloc = 0
                tab = None
                q = KT
                qt = None
            else:
                if t in (0, 1):
                    kts = [0, 1, 2, 3]
                    tab = EBS
                else:
                    kts = [t - 2, t - 1, t, t + 1, t + 2]
                    tab = EB
                nloc = len(kts)
                chunks = [('lat', k) for k in kts] + [('ctx', 0), ('ctx', 1)]
            if kind == 'ctx':
                qsrc = QTc[t]
            else:
                qsrc = QT[t % NQ]
            nch = len(chunks)

            def ppos(ci):
                return (0, ci * 128) if ci < 4 else (1, (ci - 4) * 128)
            hn0 = hcount[0]
            hcount[0] += 16

            def emit_qk(h):
                hp, pbs = h // 2, 64 * (h % 2)
                hn = hn0 + h
                sbank = (3 + 2 * (hn % 2), 4 + 2 * (hn % 2))
                for ci, (ck, kt) in enumerate(chunks):
                    bi, col = ppos(ci)
                    bank = sbank[bi]
                    kk = KT[kslot(ck, kt)]
                    last = (ci == nch - 1) or (ci == 3)
                    p.op('pe', lambda e, bank=bank, col=col, kk=kk, hp=hp, pbs=pbs: e.matmul(
                        ps[:, bank, col:col + 128], lhsT=kk.t[pbs:pbs + 64, hp, :], rhs=qsrc.t[pbs:pbs + 64, hp, :],
                        start=True, stop=True),
                        reads=[kk.b, qsrc.b], writes=[pb[bank]], inc=last)

            def emit_soft(h):
                hn = hn0 + h
                sbank = (3 + 2 * (hn % 2), 4 + 2 * (hn % 2))
                pt = PT[hn % 2]
                nA = min(4, nch) * 128
                nB = (nch - 4) * 128
                p.op('act', lambda e, sbank=sbank, pt=pt, nA=nA: e.activation(
                    out=pt.t[:, 0:nA], in_=ps[:, sbank[0], 0:nA], func=AF.Exp), reads=[pb[sbank[0]]], writes=[pt.b])
                if nB > 0:
                    p.op('act', lambda e, sbank=sbank, pt=pt, nB=nB: e.activation(
                        out=pt.t[:, 512:512 + nB], in_=ps[:, sbank[1], 0:nB], func=AF.Exp), reads=[pb[sbank[1]]], writes=[pt.b])
                if nloc > 0:
                    na = min(4, nloc)
                    p.op('dve', lambda e, pt=pt, na=na, h=h, tab=tab: e.tensor_tensor(
                        out=pt.t[:, 0:na * 128], in0=pt.t[:, 0:na * 128],
                        in1=tab.t[:, h, 0:na, :].rearrange("p a b -> p (a b)"), op=ALU.mult),
                        reads=[pt.b, tab.b], writes=[pt.b])
                    if nloc > 4:
                        p.op('dve', lambda e, pt=pt, h=h, tab=tab: e.tensor_tensor(
                            out=pt.t[:, 512:640], in0=pt.t[:, 512:640], in1=tab.t[:, h, 4, :], op=ALU.mult),
                            reads=[pt.b, tab.b], writes=[pt.b])

            def emit_pv(h):
                hn = hn0 + h
                pt = PT[hn % 2]
                g = h % 4
                for ci, (ck, kt) in enumerate(chunks):
                    bi, col = ppos(ci)
                    off = bi * 512 + col
                    vv = V[kslot(ck, kt)]
                    p.op('pe', lambda e, off=off, vv=vv, g=g, h=h, pt=pt, ci=ci: e.matmul(
                        ps[:, 7, g * 65:(g + 1) * 65], lhsT=pt.t[:, off:off + 128], rhs=vv.t[:, h, :],
                        start=(ci == 0), stop=(ci == nch - 1)),
                        reads=[pt.b, vv.b], writes=[pb[7]], inc=(ci == nch - 1))
                if g == 3:
                    o4 = ps[:, 7, 0:260].rearrange("p (g c) -> p g c", c=65)
                    p.op('dve', lambda e, o4=o4: e.reciprocal(out=rc.t[:, 0:4].unsqueeze(2), in_=o4[:, :, 64:65]),
                         reads=[pb[7]], writes=[rc.b])
                    p.op('dve', lambda e, o4=o4, h=h: e.tensor_tensor(
                        out=otok.t[:, (h - 3) * 64:(h + 1) * 64].rearrange("p (a b) -> p a b", b=64),
                        in0=o4[:, :, 0:64], in1=rc.t[:, 0:4].unsqueeze(2).to_broadcast([128, 4, 64]), op=ALU.mult),
                        reads=[pb[7], rc.b], writes=[otok.b])

            emit_qk(0)
            for h in range(16):
                if h + 1 < 16:
                    emit_qk(h + 1)
                emit_soft(h)
                emit_pv(h)
                yield
            for j in range(8):
                p.op('pe', lambda e, j=j: e.transpose(psb2[:, j * 128:(j + 1) * 128], otok.t[:, j * 128:(j + 1) * 128],
                                                      identb.t[:]),
                     reads=[otok.b, identb.b], writes=[pb[2]], inc=(j == 7))
            p.op('act', lambda e: e.activation(out=oT.t[:].rearrange("p a b -> p (a b)"), in_=psb2[:, 0:1024],
                                               func=AF.Identity), reads=[pb[2]], writes=[oT.b])
            yield
            for nb in range(2):
                bank = next_bank()
                for kc in range(8):
                    p.op('pe', lambda e, bank=bank, kc=kc, nb=nb: e.matmul(
                        ps[:, bank, :], lhsT=oT.t[:, kc, :], rhs=wo[:, kc, nb * 512:(nb + 1) * 512],
                        start=(kc == 0), stop=(kc == 7)),
                        reads=[oT.b, wo_b[nb]], writes=[pb[bank]], inc=(kc == 7))
                p.op('dve', lambda e, bank=bank, nb=nb: e.tensor_tensor(
                    out=tmp.t[:, nb * 512:(nb + 1) * 512], in0=ps[:, bank, :], in1=G.t[:, nb * 512:(nb + 1) * 512], op=ALU.mult),
                    reads=[pb[bank], G.b], writes=[tmp.b])
                p.op('pool', lambda e, nb=nb: e.tensor_tensor(
                    out=tmp.t[:, nb * 512:(nb + 1) * 512], in0=tmp.t[:, nb * 512:(nb + 1) * 512],
                    in1=xres.t[:, nb * 512:(nb + 1) * 512], op=ALU.add),
                    reads=[tmp.b, xres.b], writes=[tmp.b])
                yield
            dname = 'xa_ctx' if kind == 'ctx' else 'xa'
            dst = env.d(dname)[t * 128:(t + 1) * 128, :]
            p.dma('sp', lambda e: e.dma_start(out=dst, in_=tmp.t[:]), reads=[tmp.b], writes=[env.b(dname, t)])
            yield

        QTc = [SB(sb(f'QTc{i}', [128, 8, 128], BF16)) for i in range(2)]
        atasks = [(('ctx', 0), 1), (('ctx', 1), 1)] + [(('lat', t), 2 + min(t + 3, NTK - 1)) for t in range(NTQ)]
        issue_xload(0)
        ai = 0
        for pos in range(len(qtasks)):
            gens = [gen_qkv(pos)]
            if ai < len(atasks) and atasks[ai][1] <= pos - 1:
                gens.append(gen_att(atasks[ai][0]))
                ai += 1
            interleave(gens)
        while ai < len(atasks):
            interleave([gen_att(atasks[ai][0])])
            ai += 1
    p.barrier()


def make_ffn(layer, src, dst, with_ctx):
    def ph(p, nc, env, ps, pb):
        with ExitStack() as st:
            sb = lambda n, s, d: st.enter_context(nc.sbuf_tensor(_uniq(n), s, d))
            win = sb('win', [128, 8, 2 * DFF], BF16)
            win_b = [Buf() for _ in range(11)]
            wsrc = env.d('ffn_w_in')[layer]
            for s_ in [0, 5, 6, 1, 7, 2, 8, 3, 9, 4, 10]:
                srcw = wsrc[:, s_ * 512:(s_ + 1) * 512].rearrange("(c p) n -> p c n", p=128)
                p.dma('pool', lambda e, s_=s_, srcw=srcw: e.dma_start(out=win[:, :, s_ * 512:(s_ + 1) * 512], in_=srcw),
                      writes=[win_b[s_]])
            wout = sb('wout', [128, 22, D], BF16)
            wout_b = [Buf() for _ in range(2)]
            cast_load_w(p, nc, wout, env.d('ffn_w_out')[layer], DFF, D, wout_b, split=2)
            sg = [SB(sb(f'sg{i}', [128, 512], F32)) for i in range(2)]
            nm = NormMod(p, nc, st, env, ps, pb, layer, 1, [0, 1] if with_ctx else [0], (0, 1), 'f%d' % layer, nxn=1,
                         junk=(sg[0].t[:].bitcast(BF16), sg[0].b))
            hid = SB(sb('hid', [128, 22, 512], BF16))
            h2T = [SB(sb(f'h2T{i}', [128, 8, 512], BF16)) for i in range(2)]
            xr = [SB(sb(f'xr{i}', [128, D], F32)) for i in range(2)]
            tmp = SB(sb('tmp', [128, D], F32))
            xe = SB(sb('xe', [128, D], F32))
            G = SB(sb('G', [128, D], F32))
            sts = []
            if with_ctx:
                sts.append([(src + '_ctx', dst + '_ctx', 0, 1), (src + '_ctx', dst + '_ctx', 1, 1)])
            for s_ in range(NTQ // 4):
                sts.append([(src, dst, s_ * 4 + j, 0) for j in range(4)])
            xcnt = [0]

            def gen_A(si):
                for j, (sn, dn, ti, m) in enumerate(sts[si]):
                    x = xr[xcnt[0] % 2]
                    xcnt[0] += 1
                    srcx = env.d(sn)[ti * 128:(ti + 1) * 128, :]
                    p.dma('sp', lambda e, x=x, srcx=srcx: e.dma_start(out=x.t[:], in_=srcx), reads=[env.b(sn, ti)], writes=[x.b])
                    nm.run(x, m, h2T[si % 2], j * 128)
                    yield

            pair = [0]

            def gen_BC(si):
                tiles = sts[si]
                ntok = len(tiles) * 128
                h = h2T[si % 2]
                m = tiles[0][3]
                if si == 0 or (with_ctx and si == 1):
                    load_modB(p, nc, env, G, 5 * D, layer, m)
                for oc in range(22):
                    bA = 2 + 2 * (pair[0] % 3)
                    bU = bA + 1
                    pair[0] += 1
                    for (bank, c0) in ((bA, oc * 128), (bU, DFF + oc * 128)):
                        for kc in range(8):
                            p.op('pe', lambda e, bank=bank, c0=c0, kc=kc: e.matmul(
                                ps[:, bank, 0:ntok], lhsT=win[:, kc, c0:c0 + 128], rhs=h.t[:, kc, 0:ntok],
                                start=(kc == 0), stop=(kc == 7)),
                                reads=[h.b, h.b2, win_b[c0 // 512]], writes=[pb[bank]], inc=(kc == 7))
                    s2 = sg[oc % 2]
                    p.op('act', lambda e, bA=bA, s2=s2: e.activation(out=s2.t[:, 0:ntok], in_=ps[:, bA, 0:ntok], func=AF.Silu),
                         reads=[pb[bA]], writes=[s2.b])
                    p.op('dve', lambda e, bU=bU, s2=s2, oc=oc: e.tensor_tensor(
                        out=hid.t[:, oc, 0:ntok], in0=s2.t[:, 0:ntok], in1=ps[:, bU, 0:ntok], op=ALU.mult),
                        reads=[s2.b, pb[bU]], writes=[hid.b])
                    yield
                for j, (sn, dn, ti, m) in enumerate(tiles):
                    srcx = env.d(sn)[ti * 128:(ti + 1) * 128, :]
                    p.dma('sp', lambda e, srcx=srcx: e.dma_start(out=xe.t[:], in_=srcx), reads=[env.b(sn, ti)], writes=[xe.b])
                    for nb in range(2):
                        bank = nb
                        for oc in range(22):
                            p.op('pe', lambda e, bank=bank, oc=oc, nb=nb, j=j: e.matmul(
                                ps[:, bank, :], lhsT=hid.t[:, oc, j * 128:(j + 1) * 128], rhs=wout[:, oc, nb * 512:(nb + 1) * 512],
                                start=(oc == 0), stop=(oc == 21)),
                                reads=[hid.b, wout_b[nb]], writes=[pb[bank]], inc=(oc == 21))
                        p.op('dve', lambda e, bank=bank, nb=nb: e.tensor_tensor(
                            out=tmp.t[:, nb * 512:(nb + 1) * 512], in0=ps[:, bank, :], in1=G.t[:, nb * 512:(nb + 1) * 512],
                            op=ALU.mult), reads=[pb[bank], G.b], writes=[tmp.b])
                        p.op('pool', lambda e, nb=nb: e.tensor_tensor(
                            out=tmp.t[:, nb * 512:(nb + 1) * 512], in0=tmp.t[:, nb * 512:(nb + 1) * 512],
                            in1=xe.t[:, nb * 512:(nb + 1) * 512], op=ALU.add), reads=[tmp.b, xe.b], writes=[tmp.b])
                        yield
                    dstx = env.d(dn)[ti * 128:(ti + 1) * 128, :]
                    p.dma('sp', lambda e, dstx=dstx: e.dma_start(out=dstx, in_=tmp.t[:]), reads=[tmp.b], writes=[env.b(dn, ti)])

            interleave([gen_A(0)])
            for si in range(len(sts)):
                gens = [gen_BC(si)]
                if si + 1 < len(sts):
                    gens.append(gen_A(si + 1))
                interleave(gens)
        p.barrier()
    return ph


def scan_chunk(p, nc, ps, pb, banks, qt, qe, kn, kh, vt, ebend, S32, Sbf, msk, c, fwd, o_out):
    bA, bO0, bO1, bU0, bU1 = banks
    par = c % 2
    pbs = 64 * par
    mcol = pbs if fwd else 128 + pbs
    if o_out is not None:
        aT = o_out['aT']
        for hd in range(8):
            (q_ap, q_b), (k_ap, k_b) = qt(hd), kn(hd)
            p.op('pe', lambda e, hd=hd, q_ap=q_ap, k_ap=k_ap: e.matmul(
                ps[pbs:pbs + 64, bA, hd * 64:(hd + 1) * 64], lhsT=k_ap, rhs=q_ap, start=True, stop=True),
                reads=[q_b, k_b], writes=[pb[bA]], inc=(hd == 7))
        p.op('dve', lambda e: e.copy_predicated(
            out=aT.t[pbs:pbs + 64, :].rearrange("p (a b) -> p a b", b=64),
            mask=msk.t[pbs:pbs + 64, mcol:mcol + 64].bitcast(mybir.dt.uint32).unsqueeze(1).to_broadcast([64, 8, 64]),
            data=ps[pbs:pbs + 64, bA, :].rearrange("p (a b) -> p a b", b=64)),
            reads=[pb[bA], msk.b], writes=[aT.b])
        for hd in range(8):
            bank = bO0 if hd < 4 else bO1
            (q_ap, q_b) = qe(hd)
            v_ap, v_b = vt(hd)
            oap = ps[pbs:pbs + 64, bank, (hd % 4) * 128:(hd % 4 + 1) * 128]
            p.op('pe', lambda e, hd=hd, oap=oap, v_ap=v_ap: e.matmul(
                oap, lhsT=aT.t[pbs:pbs + 64, hd * 64:(hd + 1) * 64], rhs=v_ap, start=True, stop=False),
                reads=[aT.b, v_b], writes=[pb[bank]], inc=False)
            p.op('pe', lambda e, hd=hd, oap=oap, q_ap=q_ap: e.matmul(
                oap, lhsT=q_ap, rhs=Sbf.t[:, hd, :], start=False, stop=True),
                reads=[q_b, Sbf.b], writes=[pb[bank]], inc=(hd % 4 == 3))
        ot = o_out['o']
        for half, bank in ((0, bO0), (1, bO1)):
            p.op('act', lambda e, half=half, bank=bank: e.activation(
                out=ot.t[pbs:pbs + 64, half * 512:(half + 1) * 512], in_=ps[pbs:pbs + 64, bank, :], func=AF.Identity),
                reads=[pb[bank]], writes=[ot.b])
    for hd in range(8):
        bank = bU0 if hd < 4 else bU1
        kh_ap, kh_b = kh(hd)
        v_ap, v_b = vt(hd)
        p.op('pe', lambda e, hd=hd, bank=bank, kh_ap=kh_ap, v_ap=v_ap: e.matmul(
            ps[:, bank, (hd % 4) * 128:(hd % 4 + 1) * 128], lhsT=kh_ap, rhs=v_ap, start=True, stop=True),
            reads=[kh_b, v_b], writes=[pb[bank]], inc=(hd % 4 == 3))
    for hd in range(8):
        bank = bU0 if hd < 4 else bU1
        eb_ap, eb_b = ebend(hd)
        p.op('dve', lambda e, hd=hd, bank=bank, eb_ap=eb_ap: e.scalar_tensor_tensor(
            out=S32.t[:, hd, :], in0=S32.t[:, hd, :], scalar=eb_ap, in1=ps[:, bank, (hd % 4) * 128:(hd % 4 + 1) * 128],
            op0=ALU.mult, op1=ALU.add), reads=[S32.b, eb_b, pb[bank]], writes=[S32.b])
    p.op('pool', lambda e: e.tensor_copy(out=Sbf.t[:], in_=S32.t[:]), reads=[S32.b], writes=[Sbf.b])


def ph_hg1(p, nc, env, ps, pb):
    with ExitStack() as st:
        sb = lambda n, s, d: st.enter_context(nc.sbuf_tensor(_uniq(n), s, d))
        T = 512
        w = sb('hw', [128, 8, 5 * D], BF16)
        w_b = [Buf() for _ in range(10)]
        cast_load_w(p, nc, w, env.d('hg_w_in'), D, 5 * D, w_b, split=10)
        identb = SB(sb('identb', [128, 128], BF16))
        p.dma('pool', lambda e: e.dma_start(out=identb.t[:], in_=env.d('identf')[:, :]), writes=[identb.b])
        msk = SB(sb('msk', [128, 256], F32))
        p.dma('sp', lambda e: e.dma_start(out=msk.t[:], in_=env.d('trimask')[:, :]), writes=[msk.b])
        rst = SB(sb('rst', [128, 512], F32))
        p.dma('sp', lambda e: e.dma_start(out=rst.t[:], in_=env.d('rstmask')[:, :]), writes=[rst.b])
        low = SB(sb('low', [128, 32], F32))
        p.dma('sp', lambda e: e.dma_start(out=low.t[:], in_=env.d('hg_low')[:, :]), writes=[low.b])
        lbt = SB(sb('lbt', [128, 48], F32))
        p.op('dve', lambda e: e.tensor_tensor(out=lbt.t[:, 0:16], in0=low.t[:, 0:16], in1=low.t[:, 16:32], op=ALU.subtract),
             reads=[low.b], writes=[lbt.b])
        p.op('act', lambda e: e.activation(out=lbt.t[:, 16:32], in_=lbt.t[:, 0:16], func=AF.Sigmoid, scale=-1.0),
             reads=[lbt.b], writes=[lbt.b])
        p.op('act', lambda e: e.activation(out=lbt.t[:, 32:48], in_=lbt.t[:, 0:16], func=AF.Sigmoid),
             reads=[lbt.b], writes=[lbt.b])
        jk = SB(sb('jk', [128, 512], F32))
        nm = NormMod(p, nc, st, env, ps, pb, 1, 0, [0, 1], (0, 1), 'h1', nxn=1, junk=(jk.t[:].bitcast(BF16), jk.b))
        xr = [SB(sb(f'xr{i}', [128, D], F32)) for i in range(2)]
        hT = SB(sb('hT', [128, 8, T], BF16))
        tq = [SB(sb(f'tq{i}', [128, T], F32)) for i in range(2)]
        ta = [SB(sb(f'ta{i}', [128, T], F32)) for i in range(2)]
        tb = [SB(sb(f'tb{i}', [128, T], F32)) for i in range(2)]
        tc = [SB(sb(f'tc{i}', [128, T], F32)) for i in range(2)]
        td = [SB(sb(f'td{i}', [128, T], F32)) for i in range(2)]
        te = [SB(sb(f'te{i}', [128, T], F32)) for i in range(2)]
        tkh = [SB(sb(f'tkh{i}', [128, T], BF16)) for i in range(2)]
        QE1 = SB(sb('QE1', [128, 8, T], BF16))
        QT1 = SB(sb('QT1', [128, 8, T], BF16))
        KN1 = SB(sb('KN1', [128, 8, T], BF16))
        KH1 = SB(sb('KH1', [128, 4, D], BF16))
        EB1 = SB(sb('EB1', [128, 8, 8], F32))
        s2q = [SB(sb(f's2q{i}', [128, T], BF16)) for i in range(2)]
        s2k = [SB(sb(f's2k{i}', [128, T], BF16)) for i in range(2)]
        s2t = [SB(sb(f's2t{i}', [128, T], BF16)) for i in range(2)]
        KH2 = SB(sb('KH2', [128, 4, D], BF16))
        EB2 = SB(sb('EB2', [128, 8, 8], F32))
        Vt = SB(sb('Vt', [128, 4, D], BF16))
        SGt = SB(sb('SGt', [128, 4, D], BF16))
        S32 = SB(sb('S32', [128, 8, 128], F32))
        Sbf = SB(sb('Sbf', [128, 8, 128], BF16))
        aT = SB(sb('aT', [128, 512], BF16))
        o1t = [SB(sb(f'o1t{i}', [128, D], F32)) for i in range(1)]
        p.op('pool', lambda e: e.memset(aT.t[:], 0.0), writes=[aT.b])
        p.op('pool', lambda e: e.memset(S32.t[:], 0.0), writes=[S32.b])
        p.op('pool', lambda e: e.memset(Sbf.t[:], 0.0), writes=[Sbf.b])
        psb = lambda bank: ps[:, bank, :].bitcast(BF16)

        sts = [[('xb_ctx', 0, 1), ('xb_ctx', 1, 1)]] + [[('xb', s_ * 4 + j, 0) for j in range(4)] for s_ in range(NTQ // 4)]
        xcnt = [0]
        rot = [0]

        def nbank():
            rot[0] = (rot[0] + 1) % 3
            return 2 + rot[0]

        def do_supertile(si, tiles):
            nt = len(tiles)
            Tn = nt * 128
            nch = Tn // 64
            is_ctx = (si == 0)
            tok0 = None if is_ctx else tiles[0][1] * 128
            for j, (sn, ti, m) in enumerate(tiles):
                x = xr[xcnt[0] % 2]
                xcnt[0] += 1
                srcx = env.d(sn)[ti * 128:(ti + 1) * 128, :]
                p.dma('sp', lambda e, x=x, srcx=srcx: e.dma_start(out=x.t[:], in_=srcx), reads=[env.b(sn, ti)], writes=[x.b])
                nm.run(x, m, hT, j * 128)
            for j in range(nt):
                for nb in range(4):
                    bank = nbank()
                    c0 = D + nb * 512
                    for kc in range(8):
                        p.op('pe', lambda e, bank=bank, kc=kc, c0=c0, j=j: e.matmul(
                            ps[:, bank, :], lhsT=hT.t[:, kc, j * 128:(j + 1) * 128], rhs=w[:, kc, c0:c0 + 512],
                            start=(kc == 0), stop=(kc == 7)), reads=[hT.b, hT.b2, w_b[c0 // 512]], writes=[pb[bank]], inc=(kc == 7))
                    if nb < 2:
                        p.op('dve', lambda e, bank=bank, j=j, nb=nb: e.tensor_copy(
                            out=Vt.t[:, j, nb * 512:(nb + 1) * 512], in_=ps[:, bank, :]), reads=[pb[bank]], writes=[Vt.b])
                    elif not is_ctx:
                        p.op('act', lambda e, bank=bank, j=j, nb=nb: e.activation(
                            out=SGt.t[:, j, (nb - 2) * 512:(nb - 1) * 512], in_=ps[:, bank, :], func=AF.Silu),
                            reads=[pb[bank]], writes=[SGt.b])
            if not is_ctx:
                for (nm_, tl) in (('vtok', Vt), ('sgtok', SGt)):
                    dstd = env.d(nm_)[tok0:tok0 + Tn, :].rearrange("(j p) n -> p j n", p=128)
                    p.dma('sp', lambda e, dstd=dstd, tl=tl: e.dma_start(out=dstd, in_=tl.t[:, 0:nt, :]), reads=[tl.b],
                          writes=[env.b(nm_, si)])
            for hd in range(8):
                i2 = hd % 2
                bq, bf1, bf2 = 2, 3, 4
                for (bank, c0) in ((bq, hd * 128), (bf1, 3 * D + hd * 128), (bf2, 4 * D + hd * 128)):
                    for kc in range(8):
                        p.op('pe', lambda e, bank=bank, kc=kc, c0=c0: e.matmul(
                            ps[:, bank, 0:Tn], lhsT=w[:, kc, c0:c0 + 128], rhs=hT.t[:, kc, 0:Tn],
                            start=(kc == 0), stop=(kc == 7)), reads=[hT.b, hT.b2, w_b[c0 // 512]], writes=[pb[bank]], inc=(kc == 7))
                q_, a_, b_, c_, d_, e_, kh_ = tq[i2], ta[i2], tb[i2], tc[i2], td[i2], te[i2], tkh[i2]
                p.op('act', lambda e, q_=q_: e.activation(out=q_.t[:, 0:Tn], in_=ps[:, bq, 0:Tn], func=AF.Silu),
                     reads=[pb[bq]], writes=[q_.b])
                for d in (0, 1):
                    if is_ctx and d == 1:
                        continue
                    bank = bf1 if d == 0 else bf2
                    lcol = d * 8 + hd
                    lb_ap = lbt.t[:, 16 + lcol:17 + lcol]
                    oml_ap = lbt.t[:, 32 + lcol:33 + lcol]
                    p.op('act', lambda e, a_=a_, bank=bank: e.activation(out=a_.t[:, 0:Tn], in_=ps[:, bank, 0:Tn], func=AF.Sigmoid),
                         reads=[pb[bank]], writes=[a_.b])
                    p.op('act', lambda e, b_=b_, bank=bank: e.activation(out=b_.t[:, 0:Tn], in_=ps[:, bank, 0:Tn], func=AF.Sigmoid,
                                                                       scale=-1.0), reads=[pb[bank]], writes=[b_.b])
                    p.op('act', lambda e, a_=a_, lb_ap=lb_ap, oml_ap=oml_ap: e.activation(
                        out=a_.t[:, 0:Tn], in_=a_.t[:, 0:Tn], func=AF.Ln, scale=oml_ap, bias=lb_ap),
                        reads=[a_.b, lbt.b], writes=[a_.b])
                    p.op('dve', lambda e, a_=a_, c_=c_: e.tensor_tensor_scan(
                        out=c_.t[:, 0:Tn], data0=rst.t[:, 0:Tn], data1=a_.t[:, 0:Tn], initial=0.0, op0=ALU.mult, op1=ALU.add),
                        reads=[rst.b, a_.b], writes=[c_.b])
                    c3 = c_.t[:, 0:Tn].rearrange("p (c t) -> p c t", t=64)
                    bend = c3[:, :, 63:64]
                    p.op('dve', lambda e, d_=d_, c3=c3, bend=bend: e.tensor_tensor(
                        out=d_.t[:, 0:Tn].rearrange("p (c t) -> p c t", t=64), in0=bend.to_broadcast([128, nch, 64]), in1=c3,
                        op=ALU.subtract), reads=[c_.b], writes=[d_.b])
                    ebt = EB1 if d == 0 else EB2
                    p.op('act', lambda e, ebt=ebt, bend=bend, hd=hd: e.activation(
                        out=ebt.t[:, hd, 0:nch].unsqueeze(2), in_=bend, func=AF.Exp), reads=[c_.b], writes=[ebt.b])
                    if d == 0:
                        p.op('act', lambda e, e_=e_, c_=c_: e.activation(out=e_.t[:, 0:Tn], in_=c_.t[:, 0:Tn], func=AF.Exp),
                             reads=[c_.b], writes=[e_.b])
                        p.op('dve', lambda e, e_=e_, q_=q_, hd=hd: e.tensor_tensor(
                            out=QE1.t[:, hd, 0:Tn], in0=q_.t[:, 0:Tn], in1=e_.t[:, 0:Tn], op=ALU.mult),
                            reads=[q_.b, e_.b], writes=[QE1.b])
                        p.op('dve', lambda e, e_=e_, c3=c3: e.tensor_tensor(
                            out=e_.t[:, 0:Tn].rearrange("p (c t) -> p c t", t=64), in0=c3,
                            in1=c3[:, :, 31:32].to_broadcast([128, nch, 64]), op=ALU.subtract), reads=[c_.b], writes=[e_.b])
                        p.op('act', lambda e, e_=e_, a_=a_: e.activation(out=a_.t[:, 0:Tn], in_=e_.t[:, 0:Tn], func=AF.Exp),
                             reads=[e_.b], writes=[a_.b])
                        p.op('dve', lambda e, a_=a_, q_=q_, hd=hd: e.tensor_tensor(
                            out=QT1.t[:, hd, 0:Tn], in0=q_.t[:, 0:Tn], in1=a_.t[:, 0:Tn], op=ALU.mult),
                            reads=[q_.b, a_.b], writes=[QT1.b])
                        p.op('act', lambda e, e_=e_: e.activation(out=e_.t[:, 0:Tn], in_=e_.t[:, 0:Tn], func=AF.Exp, scale=-1.0),
                             reads=[e_.b], writes=[e_.b])
                        p.op('dve', lambda e, e_=e_, b_=b_, hd=hd, oml_ap=oml_ap: e.scalar_tensor_tensor(
                            out=KN1.t[:, hd, 0:Tn], in0=b_.t[:, 0:Tn], scalar=oml_ap, in1=e_.t[:, 0:Tn], op0=ALU.mult, op1=ALU.mult),
                            reads=[b_.b, e_.b, lbt.b], writes=[KN1.b])
                        p.op('act', lambda e, d_=d_: e.activation(out=d_.t[:, 0:Tn], in_=d_.t[:, 0:Tn], func=AF.Exp),
                             reads=[d_.b], writes=[d_.b])
                        p.op('dve', lambda e, d_=d_, b_=b_, kh_=kh_, oml_ap=oml_ap: e.scalar_tensor_tensor(
                            out=kh_.t[:, 0:Tn], in0=b_.t[:, 0:Tn], scalar=oml_ap, in1=d_.t[:, 0:Tn], op0=ALU.mult, op1=ALU.mult),
                            reads=[b_.b, d_.b, lbt.b], writes=[kh_.b])
                        KH = KH1
                    else:
                        p.op('dve', lambda e, d_=d_, a_=a_: e.tensor_tensor(out=d_.t[:, 0:Tn], in0=d_.t[:, 0:Tn], in1=a_.t[:, 0:Tn],
                                                                        op=ALU.add), reads=[d_.b, a_.b], writes=[d_.b])
                        sq_, sk_ = s2q[i2], s2k[i2]
                        p.op('act', lambda e, e_=e_, d_=d_: e.activation(out=e_.t[:, 0:Tn], in_=d_.t[:, 0:Tn], func=AF.Exp),
                             reads=[d_.b], writes=[e_.b])
                        p.op('dve', lambda e, e_=e_, q_=q_, sq_=sq_: e.tensor_tensor(
                            out=sq_.t[:, 0:Tn], in0=q_.t[:, 0:Tn], in1=e_.t[:, 0:Tn], op=ALU.mult),
                            reads=[q_.b, e_.b], writes=[sq_.b])
                        st_ = s2t[i2]
                        d3 = d_.t[:, 0:Tn].rearrange("p (c t) -> p c t", t=64)
                        p.op('dve', lambda e, e_=e_, d3=d3: e.tensor_tensor(
                            out=e_.t[:, 0:Tn].rearrange("p (c t) -> p c t", t=64), in0=d3,
                            in1=d3[:, :, 32:33].to_broadcast([128, nch, 64]), op=ALU.subtract), reads=[d_.b], writes=[e_.b])
                        p.op('act', lambda e, e_=e_, d_=d_: e.activation(out=d_.t[:, 0:Tn], in_=e_.t[:, 0:Tn], func=AF.Exp),
                             reads=[e_.b], writes=[d_.b])
                        p.op('dve', lambda e, d_=d_, q_=q_, st_=st_: e.tensor_tensor(
                            out=st_.t[:, 0:Tn], in0=q_.t[:, 0:Tn], in1=d_.t[:, 0:Tn], op=ALU.mult),
                            reads=[q_.b, d_.b], writes=[st_.b])
                        p.op('act', lambda e, e_=e_: e.activation(out=e_.t[:, 0:Tn], in_=e_.t[:, 0:Tn], func=AF.Exp, scale=-1.0),
                             reads=[e_.b], writes=[e_.b])
                        p.op('dve', lambda e, e_=e_, b_=b_, sk_=sk_, oml_ap=oml_ap: e.scalar_tensor_tensor(
                            out=sk_.t[:, 0:Tn], in0=b_.t[:, 0:Tn], scalar=oml_ap, in1=e_.t[:, 0:Tn], op0=ALU.mult, op1=ALU.mult),
                            reads=[b_.b, e_.b, lbt.b], writes=[sk_.b])
                        for (nm_, tl) in (('d2qe', sq_), ('d2kn', sk_), ('d2qt', st_)):
                            dstd = env.d(nm_)[hd, :, tok0:tok0 + Tn]
                            p.dma('sp', lambda e, dstd=dstd, tl=tl: e.dma_start(out=dstd, in_=tl.t[:, 0:Tn]), reads=[tl.b],
                                  writes=[env.b(nm_, si)])
                        p.op('dve', lambda e, d_=d_, c_=c_, a_=a_: e.tensor_tensor(
                            out=d_.t[:, 0:Tn], in0=c_.t[:, 0:Tn], in1=a_.t[:, 0:Tn], op=ALU.subtract),
                            reads=[c_.b, a_.b], writes=[d_.b])
                        p.op('act', lambda e, d_=d_: e.activation(out=d_.t[:, 0:Tn], in_=d_.t[:, 0:Tn], func=AF.Exp),
                             reads=[d_.b], writes=[d_.b])
                        p.op('dve', lambda e, d_=d_, b_=b_, kh_=kh_, oml_ap=oml_ap: e.scalar_tensor_tensor(
                            out=kh_.t[:, 0:Tn], in0=b_.t[:, 0:Tn], scalar=oml_ap, in1=d_.t[:, 0:Tn], op0=ALU.mult, op1=ALU.mult),
                            reads=[b_.b, d_.b, lbt.b], writes=[kh_.b])
                        KH = KH2
                    tbk = 5 + d
                    for j in range(nt):
                        p.op('pe', lambda e, j=j, tbk=tbk, kh_=kh_: e.transpose(
                            psb(tbk)[:, j * 128:(j + 1) * 128], kh_.t[:, j * 128:(j + 1) * 128], identb.t[:]),
                            reads=[kh_.b, identb.b], writes=[pb[tbk]], inc=(j == nt - 1))
                    p.op('dve', lambda e, tbk=tbk, KH=KH, hd=hd: e.tensor_copy(
                        out=KH.t[:, 0:nt, hd * 128:(hd + 1) * 128],
                        in_=psb(tbk)[:, 0:nt * 128].rearrange("p (j k) -> p j k", k=128)),
                        reads=[pb[tbk]], writes=[KH.b])
            if not is_ctx:
                dstd = env.d('d2kh')[tok0:tok0 + Tn, :].rearrange("(j p) n -> p j n", p=128)
                p.dma('sp', lambda e, dstd=dstd: e.dma_start(out=dstd, in_=KH2.t[:, 0:nt, :]), reads=[KH2.b],
                      writes=[env.b('d2kh', si)])
                ch0 = tok0 // 64
                dstd = env.d('d2eb')[:, :, ch0:ch0 + nch]
                p.dma('sp', lambda e, dstd=dstd: e.dma_start(out=dstd, in_=EB2.t[:, :, 0:nch]), reads=[EB2.b],
                      writes=[env.b('d2eb', si)])
            for c in range(nch):
                j = c // 2
                pbs = 64 * (c % 2)
                qt = lambda hd, c=c: (QT1.t[:, hd, c * 64:(c + 1) * 64], QT1.b)
                qe = lambda hd, c=c: (QE1.t[:, hd, c * 64:(c + 1) * 64], QE1.b)
                kn = lambda hd, c=c: (KN1.t[:, hd, c * 64:(c + 1) * 64], KN1.b)
                kh = lambda hd, j=j, pbs=pbs: (KH1.t[pbs:pbs + 64, j, hd * 128:(hd + 1) * 128], KH1.b)
                vt = lambda hd, j=j, pbs=pbs: (Vt.t[pbs:pbs + 64, j, hd * 128:(hd + 1) * 128], Vt.b)
                eb = lambda hd, c=c: (EB1.t[:, hd, c:c + 1], EB1.b)
                oo = None
                if not is_ctx:
                    ot = o1t[0]
                    oo = {'aT': aT, 'o': ot}
                scan_chunk(p, nc, ps, pb, (7, 5, 6, 2, 3), qt, qe, kn, kh, vt, eb, S32, Sbf, msk, c, True, oo)
                if oo is not None and c % 2 == 1:
                    ti = tiles[j][1]
                    dstd = env.d('o1')[ti * 128:(ti + 1) * 128, :]
                    p.dma('sp', lambda e, dstd=dstd, ot=ot: e.dma_start(out=dstd, in_=ot.t[:]), reads=[ot.b],
                          writes=[env.b('o1', ti)])
        for si, tiles in enumerate(sts):
            do_supertile(si, tiles)
        p.dma('sp', lambda e: e.dma_start(out=env.d('s_send')[:, :], in_=S32.t[:].rearrange("p a b -> p (a b)")),
              reads=[S32.b], writes=[env.b('s_send')])
    p.barrier()


def ph_hg2(p, nc, env, ps, pb):
    with ExitStack() as st:
        sb = lambda n, s, d: st.enter_context(nc.sbuf_tensor(_uniq(n), s, d))
        wo = sb('hwo', [128, 8, D], BF16)
        wo_b = [Buf() for _ in range(2)]
        cast_load_w(p, nc, wo, env.d('hg_w_o'), D, D, wo_b, split=2)
        identb = SB(sb('identb', [128, 128], BF16))
        p.dma('pool', lambda e: e.dma_start(out=identb.t[:], in_=env.d('identf')[:, :]), writes=[identb.b])
        msk = SB(sb('msk', [128, 256], F32))
        p.dma('sp', lambda e: e.dma_start(out=msk.t[:], in_=env.d('trimask')[:, :]), writes=[msk.b])
        ng = SB(sb('ng', [128, D], F32))
        p.dma('sp', lambda e: e.dma_start(out=ng.t[:], in_=env.d('hg_ng')[0:1, :].partition_broadcast(128)), writes=[ng.b])
        G = SB(sb('G', [128, D], F32))
        load_modB(p, nc, env, G, 2 * D, 1, 0)
        S32 = SB(sb('S32', [128, 8, 128], F32))
        Sbf = SB(sb('Sbf', [128, 8, 128], BF16))
        p.dma('sp', lambda e: e.dma_start(out=S32.t[:].rearrange("p a b -> p (a b)"), in_=env.d('s_recv')[:, :]),
              reads=[env.b('s_recv')], writes=[S32.b])
        p.op('pool', lambda e: e.tensor_copy(out=Sbf.t[:], in_=S32.t[:]), reads=[S32.b], writes=[Sbf.b])
        R = 2
        QE = [SB(sb(f'QE{i}', [128, 8, 128], BF16)) for i in range(R)]
        KN = [SB(sb(f'KN{i}', [128, 8, 128], BF16)) for i in range(R)]
        QTT = [SB(sb(f'QTT{i}', [128, 8, 128], BF16)) for i in range(R)]
        KH = [SB(sb(f'KH{i}', [128, D], BF16)) for i in range(R)]
        VT = [SB(sb(f'VT{i}', [128, D], BF16)) for i in range(R)]
        SG = [SB(sb(f'SG{i}', [128, D], BF16)) for i in range(R)]
        EBt = [SB(sb(f'EBt{i}', [128, 8, 2], F32)) for i in range(R)]
        O1 = [SB(sb(f'O1{i}', [128, D], F32)) for i in range(R)]
        XB = [SB(sb(f'XB{i}', [128, D], F32)) for i in range(R)]
        o2 = SB(sb('o2', [128, D], F32))
        sq = SB(sb('sq', [128, D], F32))
        hs = SB(sb('hs', [128, 16], F32))
        yb = SB(sb('yb', [128, D], BF16))
        oT = SB(sb('oT', [128, 8, 128], BF16))
        aT = SB(sb('aT', [128, 512], BF16))
        tmp = SB(sb('tmp', [128, D], F32))
        p.op('pool', lambda e: e.memset(aT.t[:], 0.0), writes=[aT.b])
        psb2 = ps[:, 4, :].bitcast(BF16)
        order = list(range(NTQ - 1, -1, -1))

        def loads(n):
            t = order[n]
            r = n % R
            sl = slice(t * 128, (t + 1) * 128)
            si = t // 4
            p.dma('sp', lambda e: e.dma_start(out=QE[r].t[:], in_=env.d('d2qe')[:, :, sl].rearrange("h k n -> k h n")),
                  reads=[env.b('d2qe', si)], writes=[QE[r].b])
            p.dma('sp', lambda e: e.dma_start(out=KN[r].t[:], in_=env.d('d2kn')[:, :, sl].rearrange("h k n -> k h n")),
                  reads=[env.b('d2kn', si)], writes=[KN[r].b])
            p.dma('sp', lambda e: e.dma_start(out=QTT[r].t[:], in_=env.d('d2qt')[:, :, sl].rearrange("h k n -> k h n")),
                  reads=[env.b('d2qt', si)], writes=[QTT[r].b])
            p.dma('sp', lambda e: e.dma_start(out=KH[r].t[:], in_=env.d('d2kh')[sl, :]), reads=[env.b('d2kh', si)], writes=[KH[r].b])
            p.dma('sp', lambda e: e.dma_start(out=VT[r].t[:], in_=env.d('vtok')[sl, :]), reads=[env.b('vtok', si)], writes=[VT[r].b])
            p.dma('sp', lambda e: e.dma_start(out=SG[r].t[:], in_=env.d('sgtok')[sl, :]), reads=[env.b('sgtok', si)], writes=[SG[r].b])
            p.dma('sp', lambda e: e.dma_start(out=EBt[r].t[:], in_=env.d('d2eb')[:, :, 2 * t:2 * t + 2]),
                  reads=[env.b('d2eb', si)], writes=[EBt[r].b])
            p.dma('sp', lambda e: e.dma_start(out=O1[r].t[:], in_=env.d('o1')[sl, :]), reads=[env.b('o1', t)], writes=[O1[r].b])
            p.dma('sp', lambda e: e.dma_start(out=XB[r].t[:], in_=env.d('xb')[sl, :]), reads=[env.b('xb', t)], writes=[XB[r].b])

        loads(0)

        def do_tile(n, t):
            r = n % R
            if n + 1 < len(order):
                loads(n + 1)
            for par in (1, 0):
                pbs = 64 * par
                qt = lambda hd, par=par: (QTT[r].t[:, hd, par * 64:(par + 1) * 64], QTT[r].b)
                qe = lambda hd, par=par: (QE[r].t[:, hd, par * 64:(par + 1) * 64], QE[r].b)
                kn = lambda hd, par=par: (KN[r].t[:, hd, par * 64:(par + 1) * 64], KN[r].b)
                kh = lambda hd, pbs=pbs: (KH[r].t[pbs:pbs + 64, hd * 128:(hd + 1) * 128], KH[r].b)
                vt = lambda hd, pbs=pbs: (VT[r].t[pbs:pbs + 64, hd * 128:(hd + 1) * 128], VT[r].b)
                eb = lambda hd, par=par: (EBt[r].t[:, hd, par:par + 1], EBt[r].b)
                scan_chunk(p, nc, ps, pb, (7, 5, 6, 2, 3), qt, qe, kn, kh, vt, eb, S32, Sbf, msk, par, False, {'aT': aT, 'o': o2})
            p.op('pool', lambda e: e.tensor_tensor(out=o2.t[:], in0=o2.t[:], in1=O1[r].t[:], op=ALU.add),
                 reads=[o2.b, O1[r].b], writes=[o2.b])
            p.op('dve', lambda e: e.tensor_tensor(out=sq.t[:], in0=o2.t[:], in1=o2.t[:], op=ALU.mult), reads=[o2.b], writes=[sq.b])
            p.op('dve', lambda e: e.tensor_reduce(out=hs.t[:, 0:8], in_=sq.t[:].rearrange("p (a b) -> p a b", b=128), axis=AX.X,
                                                  op=ALU.add), reads=[sq.b], writes=[hs.b])
            p.op('dve', lambda e: e.tensor_scalar(out=hs.t[:, 0:8], in0=hs.t[:, 0:8], scalar1=1.0 / 128, scalar2=EPS, op0=ALU.mult,
                                                  op1=ALU.add), reads=[hs.b], writes=[hs.b])
            p.op('act', lambda e: e.activation(out=hs.t[:, 0:8], in_=hs.t[:, 0:8], func=AF.Ln), reads=[hs.b], writes=[hs.b])
            p.op('act', lambda e: e.activation(out=hs.t[:, 8:16], in_=hs.t[:, 0:8], func=AF.Exp, scale=-0.5), reads=[hs.b], writes=[hs.b])
            p.op('dve', lambda e: e.tensor_tensor(
                out=sq.t[:].rearrange("p (a b) -> p a b", b=128), in0=o2.t[:].rearrange("p (a b) -> p a b", b=128),
                in1=hs.t[:, 8:16].unsqueeze(2).to_broadcast([128, 8, 128]), op=ALU.mult), reads=[o2.b, hs.b], writes=[sq.b])
            p.op('pool', lambda e: e.tensor_tensor(out=sq.t[:], in0=sq.t[:], in1=ng.t[:], op=ALU.mult), reads=[sq.b, ng.b], writes=[sq.b])
            p.op('dve', lambda e: e.tensor_tensor(out=yb.t[:], in0=sq.t[:], in1=SG[r].t[:], op=ALU.mult),
                 reads=[sq.b, SG[r].b], writes=[yb.b])
            for j in range(8):
                p.op('pe', lambda e, j=j: e.transpose(psb2[:, j * 128:(j + 1) * 128], yb.t[:, j * 128:(j + 1) * 128], identb.t[:]),
                     reads=[yb.b, identb.b], writes=[pb[4]], inc=(j == 7))
            p.op('act', lambda e: e.activation(out=oT.t[:].rearrange("p a b -> p (a b)"), in_=psb2[:, 0:1024], func=AF.Identity),
                 reads=[pb[4]], writes=[oT.b])
            for nb in range(2):
                bank = nb
                for kc in range(8):
                    p.op('pe', lambda e, bank=bank, kc=kc, nb=nb: e.matmul(
                        ps[:, bank, :], lhsT=oT.t[:, kc, :], rhs=wo[:, kc, nb * 512:(nb + 1) * 512], start=(kc == 0), stop=(kc == 7)),
                        reads=[oT.b, wo_b[nb]], writes=[pb[bank]], inc=(kc == 7))
                p.op('dve', lambda e, bank=bank, nb=nb: e.tensor_tensor(
                    out=tmp.t[:, nb * 512:(nb + 1) * 512], in0=ps[:, bank, :], in1=G.t[:, nb * 512:(nb + 1) * 512], op=ALU.mult),
                    reads=[pb[bank], G.b], writes=[tmp.b])
                p.op('pool', lambda e, nb=nb: e.tensor_tensor(
                    out=tmp.t[:, nb * 512:(nb + 1) * 512], in0=tmp.t[:, nb * 512:(nb + 1) * 512],
                    in1=XB[r].t[:, nb * 512:(nb + 1) * 512], op=ALU.add), reads=[tmp.b, XB[r].b], writes=[tmp.b])
            dst = env.d('xc')[t * 128:(t + 1) * 128, :]
            p.dma('sp', lambda e, dst=dst: e.dma_start(out=dst, in_=tmp.t[:]), reads=[tmp.b], writes=[env.b('xc', t)])

        for n, t in enumerate(order):
            do_tile(n, t)
    p.barrier()


def ph_xchg(p, nc, env, ps, pb):
    with ExitStack() as st:
        sb = lambda n, s, d: st.enter_context(nc.sbuf_tensor(_uniq(n), s, d))
        snd = env.d('s_send')
        gat = env.d('s_gath')
        p.collective(lambda e: e.collective_compute("AllGather", ALU.bypass, replica_groups=[[0, 1], [2, 3], [4, 5], [6, 7]],
                                                    ins=[snd.opt()], outs=[gat.opt()]),
                     reads=[env.b('s_send')], writes=[env.b('s_gath')])
        g = SB(sb('xg', [128, 2, D], F32))
        sel = SB(sb('xsel', [128, 2], F32))
        r = SB(sb('xr', [128, D], F32))
        p.dma('sp', lambda e: e.dma_start(out=g.t[:], in_=gat.rearrange("(r p) n -> p r n", p=128)), reads=[env.b('s_gath')],
              writes=[g.b])
        p.dma('sp', lambda e: e.dma_start(out=sel.t[:], in_=env.d('sel')[:, :]), writes=[sel.b])
        p.op('dve', lambda e: e.tensor_scalar(out=r.t[:], in0=g.t[:, 0, :], scalar1=sel.t[:, 0:1], scalar2=None, op0=ALU.mult),
             reads=[g.b, sel.b], writes=[r.b])
        p.op('dve', lambda e: e.scalar_tensor_tensor(out=r.t[:], in0=g.t[:, 1, :], scalar=sel.t[:, 1:2], in1=r.t[:], op0=ALU.mult,
                                                     op1=ALU.add), reads=[g.b, sel.b, r.b], writes=[r.b])
        p.dma('sp', lambda e: e.dma_start(out=env.d('s_recv')[:, :], in_=r.t[:]), reads=[r.b], writes=[env.b('s_recv')])
    p.barrier()


def _coords(tile, half):
    l = tile * 128 + np.arange(128)
    g = l if half == 0 else 8191 - l
    return g // 64, g % 64


def rpb_tables(rpb, half):
    flat = np.concatenate([rpb.reshape(16, 15 * 31), np.full((16, 1), PADV, np.float32)], axis=1)

    def table(t, kts):
        qr, qc = _coords(t, half)
        out = np.empty((128, 16, len(kts), 128), np.float32)
        rs = np.clip(qr - 4, 0, 120)
        cs = np.clip(qc - 8, 0, 48)
        for a, kt in enumerate(kts):
            kr, kc = _coords(kt, half)
            valid = ((kr[:, None] >= rs[None, :]) & (kr[:, None] <= rs[None, :] + 7) &
                     (kc[:, None] >= cs[None, :]) & (kc[:, None] <= cs[None, :] + 15))
            idx = (kr[:, None] - qr[None, :] + 7) * 31 + (kc[:, None] - qc[None, :] + 15)
            idx = np.where(valid, idx, 465)
            out[:, :, a, :] = np.transpose(flat[:, idx], (1, 0, 2))
        return out.reshape(128, -1)

    ebg = table(10, [8, 9, 10, 11, 12])
    ebs = np.stack([table(0, [0, 1, 2, 3]), table(1, [0, 1, 2, 3])], axis=0)
    return ebg, ebs


def host_consts():
    tri = np.zeros((128, 256), np.float32)
    for blk in range(2):
        s = np.arange(64)[:, None]
        t = np.arange(64)[None, :]
        tri[blk * 64:(blk + 1) * 64, blk * 64:(blk + 1) * 64] = (s <= t)
        tri[blk * 64:(blk + 1) * 64, 128 + blk * 64:128 + (blk + 1) * 64] = (s >= t)
    rst = np.ones((128, 512), np.float32)
    rst[:, 0::64] = 0.0
    return tri, rst


def prep_inputs(inp):
    f = lambda a: np.ascontiguousarray(a, dtype=np.float32)
    shared = {
        'ada_w': f(inp['ada_w']),
        'ada_b': f(inp['ada_b'].reshape(1, -1)),
        'n1g': f(inp['norm1_g'].reshape(2, 8, 128).transpose(2, 0, 1).reshape(128, 16)),
        'n2g': f(inp['norm2_g'].reshape(2, 8, 128).transpose(2, 0, 1).reshape(128, 16)),
        'identf': np.eye(128, dtype=np.float32),
        'w_qkv': f(inp['na_w_qkv'][0]),
        'w_o0': f(inp['na_w_o'][0]),
        'qkgain': f(np.stack([np.tile(inp['na_q_gain'][0], 2), np.tile(inp['na_k_gain'][0], 2)], axis=1)),
        'ffn_w_in': f(inp['ffn_w_in']),
        'ffn_w_out': f(inp['ffn_w_out']),
        'hg_w_o': f(inp['hg_w_o'][0]),
        'hg_ng': f(np.tile(inp['hg_norm_g'][0], 8).reshape(1, D)),
    }
    shared['trimask'], shared['rstmask'] = host_consts()
    hw = inp['hg_w_in'][0]
    hw_sw = f(np.concatenate([hw[:, 0:3 * D], hw[:, 4 * D:5 * D], hw[:, 3 * D:4 * D]], axis=1))
    low = inp['hg_lower'].reshape(2, 2, 8, 128)
    per_half = []
    for half in range(2):
        ebg, ebs = rpb_tables(f(inp['na_rpb'][0]), half)
        lo = low if half == 0 else low[:, ::-1]
        per_half.append({
            'ebg': ebg, 'ebs': ebs,
            'hg_w_in': f(hw) if half == 0 else hw_sw,
            'hg_low': f(lo.transpose(3, 0, 1, 2).reshape(128, 32)),
        })
    maps = []
    for b in range(4):
        for half in range(2):
            xs = inp['x'][b] if half == 0 else inp['x'][b, ::-1]
            cs = inp['ctx'][b] if half == 0 else inp['ctx'][b, ::-1]
            m = dict(shared)
            m.update(per_half[half])
            m['xin'] = f(xs[0:NTK * 128])
            m['ctxin'] = f(cs)
            m['sel'] = f(np.tile(np.array([[1.0, 0.0]] if half == 1 else [[0.0, 1.0]], np.float32), (128, 1)))
            m['csil'] = f(np.stack([inp['c'][b].reshape(8, 128).T, inp['c_ctx'].reshape(8, 128).T], axis=2).reshape(128, 16))
            maps.append(m)
    return maps


PHASES = {'ada': ph_ada, 'att0': ph_att0, 'ffn0': make_ffn(0, 'xa', 'xb', True), 'ffn1': make_ffn(1, 'xc', 'out', False), 'hg1': ph_hg1, 'hg2': ph_hg2, 'xchg': ph_xchg}


def build(phases, ext_in, ext_out):
    nc = bass.Bass("TRN2", target_bir_lowering=False)
    env = Env(nc, set(ext_in), set(ext_out), SHAPES)
    with ExitStack() as st:
        p = Prog(nc, st)
        ps = st.enter_context(nc.psum_tensor("ps", [128, 8, 512], F32))
        pb = [Buf(excl=True) for _ in range(8)]
        for ph in phases:
            PHASES[ph](p, nc, env, ps, pb)
        p.finish()
    return nc, env, p


def run_launch(phases, ext_in_maps, ext_out, cores):
    nc, env, p = build(phases, ext_in_maps[0].keys(), ext_out)
    used = set(env.t.keys())
    in_maps = [{k: v for k, v in m.items() if k in used} for m in ext_in_maps]
    res = run_bass_kernel_spmd(nc, in_maps, core_ids=list(range(len(cores))))
    return res


HOST_IN = ['xin', 'ctxin', 'sel', 'csil', 'ada_w', 'ada_b', 'n1g', 'n2g', 'identf', 'w_qkv', 'w_o0', 'qkgain', 'ebg', 'ebs',
           'ffn_w_in', 'ffn_w_out', 'hg_w_in', 'hg_w_o', 'hg_low', 'hg_ng', 'trimask', 'rstmask']
L1_OUT = ['modrow', 'xb', 'o1', 'vtok', 'sgtok', 'd2kh', 'd2qe', 'd2kn', 'd2qt', 'd2eb', 's_send']


def _launch(phases, maps, ext_out):
    nc, env, _ = build(phases, maps[0].keys(), ext_out)
    used = set(env.t.keys())
    in_maps = [{k: v for k, v in m.items() if k in used} for m in maps]
    res = run_bass_kernel_spmd(nc, in_maps, core_ids=list(range(len(maps))))
    return res.results


def kernel(**inputs):
    maps = prep_inputs(inputs)
    r = _launch(['ada', 'att0', 'ffn0', 'hg1', 'xchg', 'hg2', 'ffn1'], maps, ['out'])
    out = np.empty((4, 8192, D), np.float32)
    for b in range(4):
        out[b, 0:NTOK] = np.asarray(r[2 * b]['out'])
        out[b, NTOK:] = np.asarray(r[2 * b + 1]['out'])[::-1]
    return out
```

```python
import re
import numpy as np
from contextlib import ExitStack
import concourse.bass as bass
import concourse.mybir as mybir
from concourse.bass_utils import run_bass_kernel_spmd

F32 = mybir.dt.float32
BF16 = mybir.dt.bfloat16
AF = mybir.ActivationFunctionType
ALU = mybir.AluOpType
AX = mybir.AxisListType

D = 1024
EPS = 1e-6
NTQ = 32
NTK = 34
NTOK = NTQ * 128
DFF = 2816
PADV = -200.0


class Buf:
    __slots__ = ("w", "r", "excl")

    def __init__(self, excl=False):
        self.w = None
        self.r = {}
        self.excl = excl


NDSEM = 20


class Prog:
    ENG = {'pe': 'tensor', 'act': 'scalar', 'dve': 'vector', 'pool': 'gpsimd', 'sp': 'sync'}

    def __init__(self, nc, stack):
        self.nc = nc
        self.q = {e: [] for e in self.ENG}
        self.cnt = {e: 0 for e in self.ENG}
        self.sem = {e: stack.enter_context(nc.semaphore("s_" + e)) for e in self.ENG}
        self.cc_sem = stack.enter_context(nc.semaphore("s_cc"))
        self.cc_cnt = 0
        self.dsem = {}
        self.dcnt = {}
        self.drr = {}
        for qn in ('sp', 'act', 'pool'):
            self.dsem[qn] = [stack.enter_context(nc.semaphore(f"d_{qn}{i}")) for i in range(NDSEM)]
            self.dcnt[qn] = [0] * NDSEM
            self.drr[qn] = 0

    def _deps(self, eng, reads, writes):
        best = {}
        for b in reads:
            if b.w is not None:
                k, v = b.w
                if best.get(k, 0) < v:
                    best[k] = v
            if b.excl:
                for k, v in b.r.items():
                    if k != eng and best.get(k, 0) < v:
                        best[k] = v
        for b in writes:
            if b.w is not None:
                k, v = b.w
                if k != eng and best.get(k, 0) < v:
                    best[k] = v
            for k, v in b.r.items():
                if k == eng:
                    continue
                if best.get(k, 0) < v:
                    best[k] = v
        if eng == 'pe':
            best.pop('pe', None)
        return best

    def _mark(self, tok, reads, writes):
        k, v = tok
        for b in reads:
            if b.r.get(k, 0) < v:
                b.r[k] = v
        for b in writes:
            b.w = tok
            b.r = {}

    def op(self, eng, fn, reads=(), writes=(), inc=True):
        for k, v in self._deps(eng, reads, writes).items():
            self.q[eng].append(('wait', k, v))
        tok = (eng, self.cnt[eng] + 1)
        if inc:
            self.cnt[eng] += 1
        self.q[eng].append(('op', fn, inc))
        self._mark(tok, reads, writes)
        return tok

    def dma(self, qn, fn, reads=(), writes=()):
        for k, v in self._deps(qn + '_q', reads, writes).items():
            self.q[qn].append(('wait', k, v))
        i = self.drr[qn]
        self.drr[qn] = (i + 1) % NDSEM
        prev = self.dcnt[qn][i]
        key = f"d_{qn}{i}"
        if prev > 0:
            self.q[qn].append(('wait', key, prev))
        self.dcnt[qn][i] = prev + 16
        tok = (key, prev + 16)
        self.q[qn].append(('dma', fn, i))
        self._mark(tok, reads, writes)
        return tok

    def collective(self, fn, reads=(), writes=()):
        for k, v in self._deps('pool_q', reads, writes).items():
            self.q['pool'].append(('wait', k, v))
        self.cc_cnt += 1
        tok = ('cc', self.cc_cnt)
        self.q['pool'].append(('cc', fn))
        self._mark(tok, reads, writes)
        return tok

    def barrier(self):
        toks = [(e, c) for e, c in self.cnt.items() if c > 0]
        if self.cc_cnt > 0:
            toks.append(('cc', self.cc_cnt))
        for qn in self.dcnt:
            for i, c in enumerate(self.dcnt[qn]):
                if c > 0:
                    toks.append((f"d_{qn}{i}", c))
        for e in self.ENG:
            for k, v in toks:
                if k != e:
                    self.q[e].append(('wait', k, v))

    def semof(self, key):
        if key == 'cc':
            return self.cc_sem
        if key in self.sem:
            return self.sem[key]
        m = re.match(r"d_([a-z]+)(\d+)", key)
        return self.dsem[m.group(1)][int(m.group(2))]

    def finish(self):
        self.barrier()
        nc = self.nc
        with nc.Block() as block:
            for eng, attr in self.ENG.items():
                items = self.q[eng]

                def body(e, items=items, eng=eng):
                    known = {}
                    for it in items:
                        if it[0] == 'wait':
                            _, k, v = it
                            if known.get(k, 0) >= v:
                                continue
                            known[k] = v
                            e.wait_ge(self.semof(k), v)
                        elif it[0] == 'cc':
                            it[1](e).then_inc(self.cc_sem)
                        elif it[0] == 'op':
                            ins = it[1](e)
                            if it[2]:
                                ins.then_inc(self.sem[eng], 1)
                        else:
                            ins = it[1](e)
                            ins.then_inc(self.dsem[eng][it[2]], 16)
                getattr(block, attr)(body)


_UNIQ = [0]


def _uniq(n):
    _UNIQ[0] += 1
    return f"{n}_{_UNIQ[0]}"


class SB:
    def __init__(self, t):
        self.t = t
        self.b = Buf()
        self.b2 = Buf()

    @property
    def bb(self):
        return [self.b, self.b2]


class Env:
    def __init__(self, nc, ext_in, ext_out, shapes):
        self.nc = nc
        self.ext_in = ext_in
        self.ext_out = ext_out
        self.shapes = shapes
        self.t = {}
        self.bufs = {}

    def d(self, name):
        if name not in self.t:
            shape, dt = self.shapes[name]
            kind = "ExternalInput" if name in self.ext_in else ("ExternalOutput" if name in self.ext_out else "Internal")
            self.t[name] = self.nc.dram_tensor(name, list(shape), dt, kind=kind).ap()
        return self.t[name]

    def b(self, name, idx=0):
        key = (name, idx)
        if key not in self.bufs:
            self.bufs[key] = Buf()
        return self.bufs[key]


SHAPES = {
    'xin': ((NTK * 128, D), F32),
    'ctxin': ((256, D), F32),
    'csil': ((128, 16), F32),
    'ada_w': ((2, D, 6 * D), F32),
    'ada_b': ((1, 2 * 6 * D), F32),
    'n1g': ((128, 16), F32),
    'n2g': ((128, 16), F32),
    'identf': ((128, 128), F32),
    'w_qkv': ((D, 3 * D), F32),
    'w_o0': ((D, D), F32),
    'qkgain': ((128, 2), F32),
    'ebg': ((128, 16 * 5 * 128), F32),
    'ebs': ((2, 128, 16 * 4 * 128), F32),
    'ffn_w_in': ((2, D, 2 * DFF), F32),
    'ffn_w_out': ((2, DFF, D), F32),
    'hg_w_in': ((D, 5 * D), F32),
    'hg_w_o': ((D, D), F32),
    'hg_low': ((128, 32), F32),
    'hg_ng': ((1, D), F32),
    'trimask': ((128, 256), F32),
    'rstmask': ((128, 512), F32),
    'modrow': ((1, 4 * 6 * D), F32),
    'xa': ((NTOK, D), F32), 'xa_ctx': ((256, D), F32),
    'xb': ((NTOK, D), F32), 'xb_ctx': ((256, D), F32),
    'xc': ((NTOK, D), F32),
    'out': ((NTOK, D), F32),
    'o1': ((NTOK, D), F32),
    'vtok': ((NTOK, D), BF16), 'sgtok': ((NTOK, D), BF16), 'd2kh': ((NTOK, D), BF16),
    'd2qe': ((8, 128, NTOK), BF16), 'd2kn': ((8, 128, NTOK), BF16), 'd2qt': ((8, 128, NTOK), BF16),
    'd2eb': ((128, 8, NTOK // 64), F32),
    's_send': ((128, D), F32), 's_recv': ((128, D), F32), 's_gath': ((256, D), F32), 'sel': ((128, 2), F32),
}


def cast_load_w(p, nc, dst, dram2d, K, N, bufs, q='pool', split=1):
    kc = K // 128
    step = N // split
    for s in range(split):
        src = dram2d[:, s * step:(s + 1) * step].rearrange("(c p) n -> p c n", p=128)
        p.dma(q, lambda e, s=s, src=src: e.dma_start(out=dst[:, :, s * step:(s + 1) * step], in_=src),
              writes=[bufs[s]])


def ph_ada(p, nc, env, ps, pb):
    with ExitStack() as st:
        sb = lambda n, s, d: st.enter_context(nc.sbuf_tensor(_uniq(n), s, d))
        cs_f = SB(sb('cs_f', [128, 16], F32))
        cs_b = SB(sb('cs_b', [128, 16], BF16))
        brow = SB(sb('brow', [1, 2 * 6 * D], F32))
        mrow = SB(sb('mrow', [1, 4 * 6 * D], F32))
        wr = [SB(sb(f'adaw{i}', [128, 8, 512], BF16)) for i in range(3)]
        p.dma('sp', lambda e: e.dma_start(out=cs_f.t[:], in_=env.d('csil')[:, :]), writes=[cs_f.b])
        p.dma('sp', lambda e: e.dma_start(out=brow.t[:], in_=env.d('ada_b')[:, :]), writes=[brow.b])
        p.op('act', lambda e: e.activation(out=cs_b.t[:], in_=cs_f.t[:], func=AF.Silu), reads=[cs_f.b], writes=[cs_b.b])
        aw = env.d('ada_w')
        n = 0
        for i in range(2):
            for nb in range(12):
                w = wr[n % 3]
                cast_load_w(p, nc, w.t, aw[i][:, nb * 512:(nb + 1) * 512], D, 512, [w.b])
                for m in range(2):
                    bank = (n * 2 + m) % 8
                    for kc in range(8):
                        p.op('pe', lambda e, bank=bank, kc=kc, m=m, w=w: e.matmul(
                            ps[0:1, bank, :], lhsT=cs_b.t[:, 2 * kc + m:2 * kc + m + 1], rhs=w.t[:, kc, :],
                            start=(kc == 0), stop=(kc == 7)),
                            reads=[cs_b.b, w.b], writes=[pb[bank]], inc=(kc == 7))
                    o0 = (i * 2 + m) * 6 * D + nb * 512
                    b0 = i * 6 * D + nb * 512
                    p.op('dve', lambda e, bank=bank, o0=o0, b0=b0: e.tensor_tensor(
                        out=mrow.t[0:1, o0:o0 + 512], in0=ps[0:1, bank, :], in1=brow.t[0:1, b0:b0 + 512], op=ALU.add),
                        reads=[pb[bank], brow.b], writes=[mrow.b])
                n += 1
        p.dma('sp', lambda e: e.dma_start(out=env.d('modrow')[:, :], in_=mrow.t[0:1, :]),
              reads=[mrow.b], writes=[env.b('modrow')])
    p.barrier()


def load_modP(p, nc, env, dst, col0, layer, m, ncols=1):
    base = (layer * 2 + m) * 6 * D + col0
    src = env.d('modrow')[0, base:base + D].rearrange("(j p) -> p j", p=128)
    p.dma('sp', lambda e: e.dma_start(out=dst.t[:, 0:8], in_=src, allow_slow_non_contiguous=True),
          reads=[env.b('modrow')], writes=[dst.b])


def load_modB(p, nc, env, dst, col0, layer, m):
    base = (layer * 2 + m) * 6 * D + col0
    src = env.d('modrow')[0:1, base:base + D].partition_broadcast(128)
    p.dma('sp', lambda e: e.dma_start(out=dst.t[:], in_=src), reads=[env.b('modrow')], writes=[dst.b])


def rstd_from_ss(p, s, c_in, c_tmp, c_out, scale, n=1):
    p.op('act', lambda e: e.activation(out=s.t[:, c_tmp:c_tmp + n], in_=s.t[:, c_in:c_in + n], func=AF.Ln, scale=scale, bias=EPS),
         reads=[s.b], writes=[s.b])
    p.op('act', lambda e: e.activation(out=s.t[:, c_out:c_out + n], in_=s.t[:, c_tmp:c_tmp + n], func=AF.Exp, scale=-0.5),
         reads=[s.b], writes=[s.b])


class NormMod:
    def __init__(self, p, nc, st, env, ps, pb, layer, which, ms, banks, tag, nxn=2, junk=None):
        self.p, self.nc, self.ps, self.pb, self.banks = p, nc, ps, pb, banks
        sb = lambda n, s, d: st.enter_context(nc.sbuf_tensor(_uniq(n + tag), s, d))
        self.identf = SB(sb('identf', [128, 128], F32))
        p.dma('sp', lambda e: e.dma_start(out=self.identf.t[:], in_=env.d('identf')[:, :]), writes=[self.identf.b])
        if junk is None:
            self.junk_t = sb('junk', [128, D], BF16)
            self.junk_b = Buf()
        else:
            self.junk_t, self.junk_b = junk
        self.nxn = nxn
        self.xn = [SB(sb(f'xn{i}', [128, D], F32)) for i in range(nxn)]
        self.st = [SB(sb(f'st{i}', [128, 4], F32)) for i in range(2)]
        self.k = 0
        gP = SB(sb('gP', [128, 16], F32))
        p.dma('sp', lambda e: e.dma_start(out=gP.t[:], in_=env.d('n1g' if which == 0 else 'n2g')[:, :]), writes=[gP.b])
        self.geff = {}
        self.sh = {}
        for m in ms:
            shP = SB(sb(f'shP{m}', [128, 8], F32))
            scP = SB(sb(f'scP{m}', [128, 8], F32))
            ge = SB(sb(f'geff{m}', [128, 8], F32))
            load_modP(p, nc, env, shP, (0 if which == 0 else 3 * D), layer, m)
            load_modP(p, nc, env, scP, (D if which == 0 else 4 * D), layer, m)
            p.op('dve', lambda e, scP=scP, ge=ge: e.scalar_tensor_tensor(
                out=ge.t[:], in0=scP.t[:], scalar=1.0, in1=gP.t[:, layer * 8:layer * 8 + 8], op0=ALU.add, op1=ALU.mult),
                reads=[scP.b, gP.b], writes=[ge.b])
            self.geff[m] = ge
            self.sh[m] = shP

    def part1(self, x):
        p = self.p
        k = self.k
        self.k += 1
        xn = self.xn[k % self.nxn]
        s = self.st[k % 2]
        p.op('act', lambda e: e.activation(out=self.junk_t[:], in_=x.t[:], func=AF.Square, accum_out=s.t[:, 0:1]),
             reads=[x.b], writes=[self.junk_b, s.b])
        rstd_from_ss(p, s, 0, 1, 2, 1.0 / D)
        p.op('pool', lambda e: e.tensor_scalar(out=xn.t[:], in0=x.t[:], scalar1=s.t[:, 2:3], scalar2=None, op0=ALU.mult),
             reads=[x.b, s.b], writes=[xn.b])
        return xn

    def part2(self, xn, m, hT, col0):
        p, ps, pb = self.p, self.ps, self.pb
        b0, b1 = self.banks
        for j in (0, 2, 4, 6, 1, 3, 5, 7):
            bank = b0 if j % 2 == 0 else b1
            p.op('pe', lambda e, j=j, bank=bank: e.transpose(ps[:, bank, (j // 2) * 128:(j // 2 + 1) * 128],
                                                             xn.t[:, j * 128:(j + 1) * 128], self.identf.t[:]),
                 reads=[xn.b, self.identf.b], writes=[pb[bank]], inc=(j >= 6))
        ge, shP = self.geff[m], self.sh[m]
        for j in range(8):
            bank = b0 if j % 2 == 0 else b1
            if j % 2 == 0:
                p.op('act', lambda e, j=j, bank=bank: e.activation(
                    out=hT.t[:, j, col0:col0 + 128], in_=ps[:, bank, (j // 2) * 128:(j // 2 + 1) * 128],
                    func=AF.Identity, scale=ge.t[:, j:j + 1], bias=shP.t[:, j:j + 1]),
                    reads=[pb[bank], ge.b, shP.b], writes=[hT.b])
            else:
                p.op('dve', lambda e, j=j, bank=bank: e.tensor_scalar(
                    out=hT.t[:, j, col0:col0 + 128], in0=ps[:, bank, (j // 2) * 128:(j // 2 + 1) * 128],
                    scalar1=ge.t[:, j:j + 1], scalar2=shP.t[:, j:j + 1], op0=ALU.mult, op1=ALU.add),
                    reads=[pb[bank], ge.b, shP.b], writes=[hT.b2])

    def run(self, x, m, hT, col0):
        self.part2(self.part1(x), m, hT, col0)


def interleave(gens):
    gens = list(gens)
    while gens:
        for g in list(gens):
            try:
                next(g)
            except StopIteration:
                gens.remove(g)


def ph_att0(p, nc, env, ps, pb):
    NK = 7
    NQ = 5
    with ExitStack() as st:
        sb = lambda n, s, d: st.enter_context(nc.sbuf_tensor(_uniq(n), s, d))
        wqkv = sb('wqkv', [128, 8, 3 * D], BF16)
        wq_b = [Buf() for _ in range(6)]
        cast_load_w(p, nc, wqkv, env.d('w_qkv'), D, 3 * D, wq_b, split=6)
        wo = sb('wo', [128, 8, D], BF16)
        wo_b = [Buf() for _ in range(2)]
        cast_load_w(p, nc, wo, env.d('w_o0'), D, D, wo_b, split=2)
        identb = SB(sb('identb', [128, 128], BF16))
        p.dma('pool', lambda e: e.dma_start(out=identb.t[:], in_=env.d('identf')[:, :]), writes=[identb.b])
        qkg = SB(sb('qkg', [128, 2], F32))
        p.dma('sp', lambda e: e.dma_start(out=qkg.t[:], in_=env.d('qkgain')[:, :]), writes=[qkg.b])
        G = SB(sb('G', [128, D], F32))
        sq = SB(sb('sq', [128, 512], F32))
        nm = NormMod(p, nc, st, env, ps, pb, 0, 0, [0, 1], (0, 1), 'a0', nxn=1, junk=(sq.t[:].bitcast(BF16), sq.b))
        EB = SB(sb('EB', [128, 16, 5, 128], BF16))
        EBS = SB(sb('EBS', [128, 16, 4, 128], BF16))
        stage = [SB(sb(f'ebst{i}', [128, 640], F32)) for i in range(2)]
        xr = [SB(sb(f'xr{i}', [128, D], F32)) for i in range(2)]
        hT = [SB(sb(f'hT{i}', [128, 8, 128], BF16)) for i in range(2)]
        KT = [SB(sb(f'KT{i}', [128, 8, 128], BF16)) for i in range(NK + 2)]
        QT = [SB(sb(f'QT{i}', [128, 8, 128], BF16)) for i in range(NQ)]
        V = [SB(sb(f'V{i}', [128, 16, 65], BF16)) for i in range(NK + 2)]
        qf = SB(sb('qf', [128, 2 * D], F32))
        hs = [SB(sb(f'hs{i}', [128, 32], F32)) for i in range(2)]
        qkt = SB(sb('qkt', [128, 2 * D], BF16))
        PT = [SB(sb(f'PT{i}', [128, 896], BF16)) for i in range(2)]
        otok = SB(sb('otok', [128, D], BF16))
        oT = SB(sb('oT', [128, 8, 128], BF16))
        rc = SB(sb('rc', [128, 4], F32))
        tmp = SB(sb('tmp', [128, D], F32))
        xres = SB(sb('xres', [128, D], F32))
        psb2 = ps[:, 2, :].bitcast(BF16)

        for v in V:
            p.op('pool', lambda e, v=v: e.memset(v.t[:, :, 64:65], 1.0), writes=[v.b])

        def load_table(dst, src2d, ndt):
            for h in range(16):
                sg = stage[h % 2]
                w = ndt * 128
                p.dma('sp', lambda e, h=h, sg=sg, w=w: e.dma_start(out=sg.t[:, 0:w], in_=src2d[:, h * w:(h + 1) * w]),
                      writes=[sg.b])
                p.op('act', lambda e, h=h, sg=sg, w=w: e.activation(
                    out=dst.t[:, h, :, :], in_=sg.t[:, 0:w].rearrange("p (a b) -> p a b", b=128), func=AF.Exp),
                    reads=[sg.b], writes=[dst.b])

        load_table(EB, env.d('ebg'), 5)

        gbank = [0]

        def next_bank():
            gbank[0] ^= 1
            return gbank[0]

        qtasks = [('ctx', 0), ('ctx', 1)] + [('lat', i) for i in range(NTK)]

        def xsrc(task):
            kind, i = task
            return (env.d('ctxin') if kind == 'ctx' else env.d('xin'))[i * 128:(i + 1) * 128, :]

        def issue_xload(pos):
            x = xr[pos % 2]
            src = xsrc(qtasks[pos])
            p.dma('sp', lambda e: e.dma_start(out=x.t[:], in_=src), writes=[x.b])

        def kslot(kind, i):
            return NK + i if kind == 'ctx' else i % NK

        def gen_qkv(pos):
            kind, i = qtasks[pos]
            m = 1 if kind == 'ctx' else 0
            if pos + 1 < len(qtasks):
                issue_xload(pos + 1)
            x = xr[pos % 2]
            h = hT[pos % 2]
            xn_ = nm.part1(x)
            yield
            yield
            yield
            yield
            nm.part2(xn_, m, h, 0)
            yield
            need_q = not (kind == 'lat' and i >= NTQ)
            ks = kslot(kind, i)
            blocks = ([0, 1] if need_q else []) + [2, 3, 4, 5]
            for nb in blocks:
                bank = next_bank()
                for kc in range(8):
                    p.op('pe', lambda e, bank=bank, kc=kc, nb=nb: e.matmul(
                        ps[:, bank, :], lhsT=h.t[:, kc, :], rhs=wqkv[:, kc, nb * 512:(nb + 1) * 512],
                        start=(kc == 0), stop=(kc == 7)),
                        reads=[h.b, h.b2, wq_b[nb]], writes=[pb[bank]], inc=(kc == 7))
                if nb < 4:
                    p.op('act', lambda e, bank=bank, nb=nb: e.activation(
                        out=qf.t[:, nb * 512:(nb + 1) * 512], in_=ps[:, bank, :], func=AF.Identity),
                        reads=[pb[bank]], writes=[qf.b])
                else:
                    vv = V[ks]
                    p.op('dve', lambda e, bank=bank, nb=nb, vv=vv: e.tensor_copy(
                        out=vv.t[:, (nb - 4) * 8:(nb - 4) * 8 + 8, 0:64],
                        in_=ps[:, bank, :].rearrange("p (a b) -> p a b", b=64)),
                        reads=[pb[bank]], writes=[vv.b])
                yield
            s = hs[pos % 2]
            c0 = 0 if need_q else 2
            for nb in range(c0, 4):
                p.op('act', lambda e, nb=nb: e.activation(out=sq.t[:], in_=qf.t[:, nb * 512:(nb + 1) * 512], func=AF.Square),
                     reads=[qf.b], writes=[sq.b])
                p.op('dve', lambda e, nb=nb: e.tensor_reduce(out=s.t[:, nb * 8:nb * 8 + 8],
                                                             in_=sq.t[:].rearrange("p (a b) -> p a b", b=64), axis=AX.X, op=ALU.add),
                     reads=[sq.b], writes=[s.b])
            yield
            yield
            rstd_from_ss(p, s, c0 * 8, c0 * 8, c0 * 8, 1.0 / 64, n=(4 - c0) * 8)
            yield
            yield
            for nb in range(c0, 4):
                cc = nb * 8
                p.op('dve', lambda e, nb=nb, cc=cc: e.tensor_tensor(
                    out=qkt.t[:, nb * 512:(nb + 1) * 512].rearrange("p (a b) -> p a b", b=64),
                    in0=qf.t[:, nb * 512:(nb + 1) * 512].rearrange("p (a b) -> p a b", b=64),
                    in1=s.t[:, cc:cc + 8].unsqueeze(2).to_broadcast([128, 8, 64]), op=ALU.mult),
                    reads=[qf.b, s.b], writes=[qkt.b])
            yield
            yield
            for which in ([0, 1] if need_q else [1]):
                dst = (QTc[i] if kind == 'ctx' else QT[i % NQ]) if which == 0 else KT[ks]
                for j in range(8):
                    p.op('pe', lambda e, j=j, which=which: e.transpose(
                        psb2[:, j * 128:(j + 1) * 128], qkt.t[:, which * D + j * 128:which * D + (j + 1) * 128], identb.t[:]),
                        reads=[qkt.b, identb.b], writes=[pb[2]], inc=(j == 7))
                if which == 0:
                    p.op('dve', lambda e, dst=dst: e.tensor_scalar(
                        out=dst.t[:].rearrange("p a b -> p (a b)"), in0=psb2[:, 0:1024], scalar1=qkg.t[:, 0:1], scalar2=0.125,
                        op0=ALU.mult, op1=ALU.mult), reads=[pb[2], qkg.b], writes=[dst.b])
                else:
                    p.op('dve', lambda e, dst=dst: e.tensor_scalar(
                        out=dst.t[:].rearrange("p a b -> p (a b)"), in0=psb2[:, 0:1024], scalar1=qkg.t[:, 1:2], scalar2=None,
                        op0=ALU.mult), reads=[pb[2], qkg.b], writes=[dst.b])
                yield

        hcount = [0]

        def gen_att(task):
            kind, t = task
            m = 1 if kind == 'ctx' else 0
            src = xsrc(task)
            p.dma('sp', lambda e: e.dma_start(out=xres.t[:], in_=src), writes=[xres.b])
            if (kind, t) in (('ctx', 0), ('lat', 0)):
                load_modB(p, nc, env, G, 2 * D, 0, m)
            if kind == 'lat' and t in (0, 1):
                load_table(EBS, env.d('ebs')[t], 4)
            if kind == 'ctx':
                chunks = [('ctx', 0), ('ctx', 1)]
                nloc = 0
                tab = None
                q = KT
                qt = None
            else:
                if t in (0, 1):
                    kts = [0, 1, 2, 3]
                    tab = EBS
                else:
                    kts = [t - 2, t - 1, t, t + 1, t + 2]
                    tab = EB
                nloc = len(kts)
                chunks = [('lat', k) for k in kts] + [('ctx', 0), ('ctx', 1)]
            if kind == 'ctx':
                qsrc = QTc[t]
            else:
                qsrc = QT[t % NQ]
            nch = len(chunks)

            def ppos(ci):
                return (0, ci * 128) if ci < 4 else (1, (ci - 4) * 128)
            hn0 = hcount[0]
            hcount[0] += 16

            def emit_qk(h):
                hp, pbs = h // 2, 64 * (h % 2)
                hn = hn0 + h
                sbank = (3 + 2 * (hn % 2), 4 + 2 * (hn % 2))
                for ci, (ck, kt) in enumerate(chunks):
                    bi, col = ppos(ci)
                    bank = sbank[bi]
                    kk = KT[kslot(ck, kt)]
                    last = (ci == nch - 1) or (ci == 3)
                    p.op('pe', lambda e, bank=bank, col=col, kk=kk, hp=hp, pbs=pbs: e.matmul(
                        ps[:, bank, col:col + 128], lhsT=kk.t[pbs:pbs + 64, hp, :], rhs=qsrc.t[pbs:pbs + 64, hp, :],
                        start=True, stop=True),
                        reads=[kk.b, qsrc.b], writes=[pb[bank]], inc=last)

            def emit_soft(h):
                hn = hn0 + h
                sbank = (3 + 2 * (hn % 2), 4 + 2 * (hn % 2))
                pt = PT[hn % 2]
                nA = min(4, nch) * 128
                nB = (nch - 4) * 128
                p.op('act', lambda e, sbank=sbank, pt=pt, nA=nA: e.activation(
                    out=pt.t[:, 0:nA], in_=ps[:, sbank[0], 0:nA], func=AF.Exp), reads=[pb[sbank[0]]], writes=[pt.b])
                if nB > 0:
                    p.op('act', lambda e, sbank=sbank, pt=pt, nB=nB: e.activation(
                        out=pt.t[:, 512:512 + nB], in_=ps[:, sbank[1], 0:nB], func=AF.Exp), reads=[pb[sbank[1]]], writes=[pt.b])
                if nloc > 0:
                    na = min(4, nloc)
                    p.op('dve', lambda e, pt=pt, na=na, h=h, tab=tab: e.tensor_tensor(
                        out=pt.t[:, 0:na * 128], in0=pt.t[:, 0:na * 128],
                        in1=tab.t[:, h, 0:na, :].rearrange("p a b -> p (a b)"), op=ALU.mult),
                        reads=[pt.b, tab.b], writes=[pt.b])
                    if nloc > 4:
                        p.op('dve', lambda e, pt=pt, h=h, tab=tab: e.tensor_tensor(
                            out=pt.t[:, 512:640], in0=pt.t[:, 512:640], in1=tab.t[:, h, 4, :], op=ALU.mult),
                            reads=[pt.b, tab.b], writes=[pt.b])

            def emit_pv(h):
                hn = hn0 + h
                pt = PT[hn % 2]
                g = h % 4
                for ci, (ck, kt) in enumerate(chunks):
                    bi, col = ppos(ci)
                    off = bi * 512 + col
                    vv = V[kslot(ck, kt)]
                    p.op('pe', lambda e, off=off, vv=vv, g=g, h=h, pt=pt, ci=ci: e.matmul(
                        ps[:, 7, g * 65:(g + 1) * 65], lhsT=pt.t[:, off:off + 128], rhs=vv.t[:, h, :],
                        start=(ci == 0), stop=(ci == nch - 1)),
                        reads=[pt.b, vv.b], writes=[pb[7]], inc=(ci == nch - 1))
                if g == 3:
                    o4 = ps[:, 7, 0:260].rearrange("p (g c) -> p g c", c=65)
                    p.op('dve', lambda e, o4=o4: e.reciprocal(out=rc.t[:, 0:4].unsqueeze(2), in_=o4[:, :, 64:65]),
                         reads=[pb[7]], writes=[rc.b])
                    p.op('dve', lambda e, o4=o4, h=h: e.tensor_tensor(
                        out=otok.t[:, (h - 3) * 64:(h + 1) * 64].rearrange("p (a b) -> p a b", b=64),
                        in0=o4[:, :, 0:64], in1=rc.t[:, 0:4].unsqueeze(2).to_broadcast([128, 4, 64]), op=ALU.mult),
                        reads=[pb[7], rc.b], writes=[otok.b])

            emit_qk(0)
            for h in range(16):
                if h + 1 < 16:
                    emit_qk(h + 1)
                emit_soft(h)
                emit_pv(h)
                yield
            for j in range(8):
                p.op('pe', lambda e, j=j: e.transpose(psb2[:, j * 128:(j + 1) * 128], otok.t[:, j * 128:(j + 1) * 128],
                                                      identb.t[:]),
                     reads=[otok.b, identb.b], writes=[pb[2]], inc=(j == 7))
            p.op('act', lambda e: e.activation(out=oT.t[:].rearrange("p a b -> p (a b)"), in_=psb2[:, 0:1024],
                                               func=AF.Identity), reads=[pb[2]], writes=[oT.b])
            yield
            for nb in range(2):
                bank = next_bank()
                for kc in range(8):
                    p.op('pe', lambda e, bank=bank, kc=kc, nb=nb: e.matmul(
                        ps[:, bank, :], lhsT=oT.t[:, kc, :], rhs=wo[:, kc, nb * 512:(nb + 1) * 512],
                        start=(kc == 0), stop=(kc == 7)),
                        reads=[oT.b, wo_b[nb]], writes=[pb[bank]], inc=(kc == 7))
                p.op('dve', lambda e, bank=bank, nb=nb: e.tensor_tensor(
                    out=tmp.t[:, nb * 512:(nb + 1) * 512], in0=ps[:, bank, :], in1=G.t[:, nb * 512:(nb + 1) * 512], op=ALU.mult),
                    reads=[pb[bank], G.b], writes=[tmp.b])
                p.op('pool', lambda e, nb=nb: e.tensor_tensor(
                    out=tmp.t[:, nb * 512:(nb + 1) * 512], in0=tmp.t[:, nb * 512:(nb + 1) * 512],
                    in1=xres.t[:, nb * 512:(nb + 1) * 512], op=ALU.add),
                    reads=[tmp.b, xres.b], writes=[tmp.b])
                yield
            dname = 'xa_ctx' if kind == 'ctx' else 'xa'
            dst = env.d(dname)[t * 128:(t + 1) * 128, :]
            p.dma('sp', lambda e: e.dma_start(out=dst, in_=tmp.t[:]), reads=[tmp.b], writes=[env.b(dname, t)])
            yield

        QTc = [SB(sb(f'QTc{i}', [128, 8, 128], BF16)) for i in range(2)]
        atasks = [(('ctx', 0), 1), (('ctx', 1), 1)] + [(('lat', t), 2 + min(t + 3, NTK - 1)) for t in range(NTQ)]
        issue_xload(0)
        ai = 0
        for pos in range(len(qtasks)):
            gens = [gen_qkv(pos)]
            if ai < len(atasks) and atasks[ai][1] <= pos - 1:
                gens.append(gen_att(atasks[ai][0]))
                ai += 1
            interleave(gens)
        while ai < len(atasks):
            interleave([gen_att(atasks[ai][0])])
            ai += 1
    p.barrier()


def make_ffn(layer, src, dst, with_ctx):
    def ph(p, nc, env, ps, pb):
        with ExitStack() as st:
            sb = lambda n, s, d: st.enter_context(nc.sbuf_tensor(_uniq(n), s, d))
            win = sb('win', [128, 8, 2 * DFF], BF16)
            win_b = [Buf() for _ in range(11)]
            wsrc = env.d('ffn_w_in')[layer]
            for s_ in [0, 5, 6, 1, 7, 2, 8, 3, 9, 4, 10]:
                srcw = wsrc[:, s_ * 512:(s_ + 1) * 512].rearrange("(c p) n -> p c n", p=128)
                p.dma('pool', lambda e, s_=s_, srcw=srcw: e.dma_start(out=win[:, :, s_ * 512:(s_ + 1) * 512], in_=srcw),
                      writes=[win_b[s_]])
            wout = sb('wout', [128, 22, D], BF16)
            wout_b = [Buf() for _ in range(2)]
            cast_load_w(p, nc, wout, env.d('ffn_w_out')[layer], DFF, D, wout_b, split=2)
            sg = [SB(sb(f'sg{i}', [128, 512], F32)) for i in range(2)]
            nm = NormMod(p, nc, st, env, ps, pb, layer, 1, [0, 1] if with_ctx else [0], (0, 1), 'f%d' % layer, nxn=2,
                         junk=(sg[0].t[:].bitcast(BF16), sg[0].b))
            hid = SB(sb('hid', [128, 22, 512], BF16))
            h2T = [SB(sb(f'h2T{i}', [128, 8, 512], BF16)) for i in range(2)]
            xr = [SB(sb(f'xr{i}', [128, D], F32)) for i in range(2)]
            tmp = SB(sb('tmp', [128, D], F32))
            xe = SB(sb('xe', [128, D], F32))
            G = SB(sb('G', [128, D], F32))
            sts = []
            if with_ctx:
                sts.append([(src + '_ctx', dst + '_ctx', 0, 1), (src + '_ctx', dst + '_ctx', 1, 1)])
            for s_ in range(NTQ // 4):
                sts.append([(src, dst, s_ * 4 + j, 0) for j in range(4)])
            xcnt = [0]

            def gen_A(si):
                tl = sts[si]
                xns = {}

                def p1(j):
                    sn, dn, ti, m = tl[j]
                    x = xr[xcnt[0] % 2]
                    xcnt[0] += 1
                    srcx = env.d(sn)[ti * 128:(ti + 1) * 128, :]
                    p.dma('sp', lambda e, x=x, srcx=srcx: e.dma_start(out=x.t[:], in_=srcx), reads=[env.b(sn, ti)], writes=[x.b])
                    xns[j] = nm.part1(x)
                p1(0)
                yield
                yield
                for j in range(len(tl)):
                    if j + 1 < len(tl):
                        p1(j + 1)
                    yield
                    yield
                    yield
                    nm.part2(xns[j], tl[j][3], h2T[si % 2], j * 128)
                    yield

            pair = [0]

            def gen_BC(si):
                tiles = sts[si]
                ntok = len(tiles) * 128
                h = h2T[si % 2]
                m = tiles[0][3]
                if si == 0 or (with_ctx and si == 1):
                    load_modB(p, nc, env, G, 5 * D, layer, m)
                for oc in range(22):
                    bA = 2 + 2 * (pair[0] % 3)
                    bU = bA + 1
                    pair[0] += 1
                    for (bank, c0) in ((bA, oc * 128), (bU, DFF + oc * 128)):
                        for kc in range(8):
                            p.op('pe', lambda e, bank=bank, c0=c0, kc=kc: e.matmul(
                                ps[:, bank, 0:ntok], lhsT=win[:, kc, c0:c0 + 128], rhs=h.t[:, kc, 0:ntok],
                                start=(kc == 0), stop=(kc == 7)),
                                reads=[h.b, h.b2, win_b[c0 // 512]], writes=[pb[bank]], inc=(kc == 7))
                    s2 = sg[oc % 2]
                    p.op('act', lambda e, bA=bA, s2=s2: e.activation(out=s2.t[:, 0:ntok], in_=ps[:, bA, 0:ntok], func=AF.Silu),
                         reads=[pb[bA]], writes=[s2.b])
                    p.op('dve', lambda e, bU=bU, s2=s2, oc=oc: e.tensor_tensor(
                        out=hid.t[:, oc, 0:ntok], in0=s2.t[:, 0:ntok], in1=ps[:, bU, 0:ntok], op=ALU.mult),
                        reads=[s2.b, pb[bU]], writes=[hid.b])
                    yield
                for j, (sn, dn, ti, m) in enumerate(tiles):
                    srcx = env.d(sn)[ti * 128:(ti + 1) * 128, :]
                    p.dma('sp', lambda e, srcx=srcx: e.dma_start(out=xe.t[:], in_=srcx), reads=[env.b(sn, ti)], writes=[xe.b])
                    for nb in range(2):
                        bank = nb
                        for oc in range(22):
                            p.op('pe', lambda e, bank=bank, oc=oc, nb=nb, j=j: e.matmul(
                                ps[:, bank, :], lhsT=hid.t[:, oc, j * 128:(j + 1) * 128], rhs=wout[:, oc, nb * 512:(nb + 1) * 512],
                                start=(oc == 0), stop=(oc == 21)),
                                reads=[hid.b, wout_b[nb]], writes=[pb[bank]], inc=(oc == 21))
                        p.op('dve', lambda e, bank=bank, nb=nb: e.tensor_tensor(
                            out=tmp.t[:, nb * 512:(nb + 1) * 512], in0=ps[:, bank, :], in1=G.t[:, nb * 512:(nb + 1) * 512],
                            op=ALU.mult), reads=[pb[bank], G.b], writes=[tmp.b])
                        p.op('pool', lambda e, nb=nb: e.tensor_tensor(
                            out=tmp.t[:, nb * 512:(nb + 1) * 512], in0=tmp.t[:, nb * 512:(nb + 1) * 512],
                            in1=xe.t[:, nb * 512:(nb + 1) * 512], op=ALU.add), reads=[tmp.b, xe.b], writes=[tmp.b])
                        yield
                    dstx = env.d(dn)[ti * 128:(ti + 1) * 128, :]
                    p.dma('sp', lambda e, dstx=dstx: e.dma_start(out=dstx, in_=tmp.t[:]), reads=[tmp.b], writes=[env.b(dn, ti)])

            interleave([gen_A(0)])
            for si in range(len(sts)):
                gens = [gen_BC(si)]
                if si + 1 < len(sts):
                    gens.append(gen_A(si + 1))
                interleave(gens)
        p.barrier()
    return ph


class ScanState:
    def __init__(self, sbf, tag):
        self.S32 = sbf('S32' + tag, [128, 8, 128], F32)
        self.Sbf = [sbf(f'Sbf{i}' + tag, [128, 8, 128], BF16) for i in range(2)]
        self.b32 = [Buf() for _ in range(8)]
        self.bbf = [[Buf() for _ in range(8)] for _ in range(2)]
        self.n = 0


def scan_chunk(p, nc, ps, pb, banks, qt, qe, kn, kh, vt, ebend, ST, msk, c, fwd, o_out):
    bA, bO0, bO1, bU0, bU1 = banks
    par = c % 2
    pbs = 64 * par
    mcol = pbs if fwd else 128 + pbs
    cur = ST.n % 2
    nxt = 1 - cur
    if o_out is not None:
        aT = o_out['aT']
        for hd in range(8):
            (q_ap, q_b), (k_ap, k_b) = qt(hd), kn(hd)
            p.op('pe', lambda e, hd=hd, q_ap=q_ap, k_ap=k_ap: e.matmul(
                ps[pbs:pbs + 64, bA, hd * 64:(hd + 1) * 64], lhsT=k_ap, rhs=q_ap, start=True, stop=True),
                reads=[q_b, k_b], writes=[pb[bA]], inc=(hd == 7))
    for hd in range(8):
        bank = bU0 if hd < 4 else bU1
        kh_ap, kh_b = kh(hd)
        v_ap, v_b = vt(hd)
        p.op('pe', lambda e, hd=hd, bank=bank, kh_ap=kh_ap, v_ap=v_ap: e.matmul(
            ps[:, bank, (hd % 4) * 128:(hd % 4 + 1) * 128], lhsT=kh_ap, rhs=v_ap, start=True, stop=True),
            reads=[kh_b, v_b], writes=[pb[bank]], inc=(hd % 4 == 3))
    if o_out is not None:
        p.op('dve', lambda e: e.copy_predicated(
            out=aT.t[pbs:pbs + 64, :].rearrange("p (a b) -> p a b", b=64),
            mask=msk.t[pbs:pbs + 64, mcol:mcol + 64].bitcast(mybir.dt.uint32).unsqueeze(1).to_broadcast([64, 8, 64]),
            data=ps[pbs:pbs + 64, bA, :].rearrange("p (a b) -> p a b", b=64)),
            reads=[pb[bA], msk.b], writes=[aT.b])
        for hd in range(8):
            bank = bO0 if hd < 4 else bO1
            (q_ap, q_b) = qe(hd)
            v_ap, v_b = vt(hd)
            oap = ps[pbs:pbs + 64, bank, (hd % 4) * 128:(hd % 4 + 1) * 128]
            p.op('pe', lambda e, hd=hd, oap=oap, v_ap=v_ap: e.matmul(
                oap, lhsT=aT.t[pbs:pbs + 64, hd * 64:(hd + 1) * 64], rhs=v_ap, start=True, stop=False),
                reads=[aT.b, v_b], writes=[pb[bank]], inc=False)
            p.op('pe', lambda e, hd=hd, oap=oap, q_ap=q_ap: e.matmul(
                oap, lhsT=q_ap, rhs=ST.Sbf[cur][:, hd, :], start=False, stop=True),
                reads=[q_b, ST.bbf[cur][hd]], writes=[pb[bank]], inc=(hd % 4 == 3))
        ot = o_out['o']
        add = o_out.get('add')
        for half, bank in ((0, bO0), (1, bO1)):
            if add is None:
                p.op('act', lambda e, half=half, bank=bank: e.activation(
                    out=ot.t[pbs:pbs + 64, half * 512:(half + 1) * 512], in_=ps[pbs:pbs + 64, bank, :], func=AF.Identity),
                    reads=[pb[bank]], writes=[ot.b])
            else:
                p.op('dve', lambda e, half=half, bank=bank: e.tensor_tensor(
                    out=ot.t[pbs:pbs + 64, half * 512:(half + 1) * 512], in0=ps[pbs:pbs + 64, bank, :],
                    in1=add.t[pbs:pbs + 64, half * 512:(half + 1) * 512], op=ALU.add),
                    reads=[pb[bank], add.b], writes=[ot.b])
    for hd in range(8):
        bank = bU0 if hd < 4 else bU1
        eb_ap, eb_b = ebend(hd)
        p.op('dve', lambda e, hd=hd, bank=bank, eb_ap=eb_ap: e.scalar_tensor_tensor(
            out=ST.S32[:, hd, :], in0=ST.S32[:, hd, :], scalar=eb_ap, in1=ps[:, bank, (hd % 4) * 128:(hd % 4 + 1) * 128],
            op0=ALU.mult, op1=ALU.add), reads=[ST.b32[hd], eb_b, pb[bank]], writes=[ST.b32[hd]])
        p.op('act', lambda e, hd=hd: e.activation(out=ST.Sbf[nxt][:, hd, :], in_=ST.S32[:, hd, :], func=AF.Identity),
             reads=[ST.b32[hd]], writes=[ST.bbf[nxt][hd]])
    ST.n += 1


def ph_hg1(p, nc, env, ps, pb):
    with ExitStack() as st:
        sb = lambda n, s, d: st.enter_context(nc.sbuf_tensor(_uniq(n), s, d))
        T = 512
        w = sb('hw', [128, 8, 5 * D], BF16)
        w_b = [Buf() for _ in range(10)]
        cast_load_w(p, nc, w, env.d('hg_w_in'), D, 5 * D, w_b, split=10)
        identb = SB(sb('identb', [128, 128], BF16))
        p.dma('pool', lambda e: e.dma_start(out=identb.t[:], in_=env.d('identf')[:, :]), writes=[identb.b])
        msk = SB(sb('msk', [128, 256], F32))
        p.dma('sp', lambda e: e.dma_start(out=msk.t[:], in_=env.d('trimask')[:, :]), writes=[msk.b])
        rst = SB(sb('rst', [128, 512], BF16))
        p.dma('pool', lambda e: e.dma_start(out=rst.t[:], in_=env.d('rstmask')[:, :]), writes=[rst.b])
        low = SB(sb('low', [128, 32], F32))
        p.dma('sp', lambda e: e.dma_start(out=low.t[:], in_=env.d('hg_low')[:, :]), writes=[low.b])
        lbt = SB(sb('lbt', [128, 48], F32))
        p.op('dve', lambda e: e.tensor_tensor(out=lbt.t[:, 0:16], in0=low.t[:, 0:16], in1=low.t[:, 16:32], op=ALU.subtract),
             reads=[low.b], writes=[lbt.b])
        p.op('act', lambda e: e.activation(out=lbt.t[:, 16:32], in_=lbt.t[:, 0:16], func=AF.Sigmoid, scale=-1.0),
             reads=[lbt.b], writes=[lbt.b])
        p.op('act', lambda e: e.activation(out=lbt.t[:, 32:48], in_=lbt.t[:, 0:16], func=AF.Sigmoid),
             reads=[lbt.b], writes=[lbt.b])
        jk = SB(sb('jk', [128, 512], F32))
        nm = NormMod(p, nc, st, env, ps, pb, 1, 0, [0, 1], (0, 1), 'h1', nxn=1, junk=(jk.t[:].bitcast(BF16), jk.b))
        xr = [SB(sb(f'xr{i}', [128, D], F32)) for i in range(2)]
        hT = SB(sb('hT', [128, 8, T], BF16))
        tq = [SB(sb(f'tq{i}', [128, T], F32)) for i in range(2)]
        ta = [SB(sb(f'ta{i}', [128, T], F32)) for i in range(2)]
        tb = [SB(sb(f'tb{i}', [128, T], F32)) for i in range(2)]
        tc = [SB(sb(f'tc{i}', [128, T], F32)) for i in range(2)]
        td = [SB(sb(f'td{i}', [128, T], F32)) for i in range(2)]
        te = [SB(sb(f'te{i}', [128, T], F32)) for i in range(2)]
        tkh = [SB(sb(f'tkh{i}', [128, T], BF16)) for i in range(2)]
        QE1 = SB(sb('QE1', [128, 8, T], BF16))
        QT1 = SB(sb('QT1', [128, 8, T], BF16))
        KN1 = SB(sb('KN1', [128, 8, T], BF16))
        KH1 = SB(sb('KH1', [128, 4, D], BF16))
        EB1 = SB(sb('EB1', [128, 8, 8], F32))
        s2q = [SB(sb(f's2q{i}', [128, T], BF16)) for i in range(2)]
        s2k = [SB(sb(f's2k{i}', [128, T], BF16)) for i in range(2)]
        s2t = [SB(sb(f's2t{i}', [128, T], BF16)) for i in range(2)]
        KH2 = SB(sb('KH2', [128, 4, D], BF16))
        EB2 = SB(sb('EB2', [128, 8, 8], F32))
        Vt = SB(sb('Vt', [128, 4, D], BF16))
        SGt = SB(sb('SGt', [128, 4, D], BF16))
        ST = ScanState(sb, 'h1')
        aT = SB(sb('aT', [128, 512], BF16))
        o1t = [SB(sb(f'o1t{i}', [128, D], F32)) for i in range(1)]
        p.op('pool', lambda e: e.memset(aT.t[:], 0.0), writes=[aT.b])
        p.op('pool', lambda e: e.memset(ST.S32[:], 0.0), writes=ST.b32)
        p.op('pool', lambda e: e.memset(ST.Sbf[0][:], 0.0), writes=ST.bbf[0])
        psb = lambda bank: ps[:, bank, :].bitcast(BF16)

        sts = [[('xb_ctx', 0, 1), ('xb_ctx', 1, 1)]] + [[('xb', s_ * 4 + j, 0) for j in range(4)] for s_ in range(NTQ // 4)]
        xcnt = [0]
        rot = [0]

        def nbank():
            rot[0] = (rot[0] + 1) % 3
            return 2 + rot[0]

        def do_supertile(si, tiles):
            nt = len(tiles)
            Tn = nt * 128
            nch = Tn // 64
            is_ctx = (si == 0)
            tok0 = None if is_ctx else tiles[0][1] * 128
            for j, (sn, ti, m) in enumerate(tiles):
                x = xr[xcnt[0] % 2]
                xcnt[0] += 1
                srcx = env.d(sn)[ti * 128:(ti + 1) * 128, :]
                p.dma('sp', lambda e, x=x, srcx=srcx: e.dma_start(out=x.t[:], in_=srcx), reads=[env.b(sn, ti)], writes=[x.b])
                nm.run(x, m, hT, j * 128)
            for j in range(nt):
                for nb in range(4):
                    bank = nbank()
                    c0 = D + nb * 512
                    for kc in range(8):
                        p.op('pe', lambda e, bank=bank, kc=kc, c0=c0, j=j: e.matmul(
                            ps[:, bank, :], lhsT=hT.t[:, kc, j * 128:(j + 1) * 128], rhs=w[:, kc, c0:c0 + 512],
                            start=(kc == 0), stop=(kc == 7)), reads=[hT.b, hT.b2, w_b[c0 // 512]], writes=[pb[bank]], inc=(kc == 7))
                    if nb < 2:
                        p.op('dve', lambda e, bank=bank, j=j, nb=nb: e.tensor_copy(
                            out=Vt.t[:, j, nb * 512:(nb + 1) * 512], in_=ps[:, bank, :]), reads=[pb[bank]], writes=[Vt.b])
                    elif not is_ctx:
                        p.op('act', lambda e, bank=bank, j=j, nb=nb: e.activation(
                            out=SGt.t[:, j, (nb - 2) * 512:(nb - 1) * 512], in_=ps[:, bank, :], func=AF.Silu),
                            reads=[pb[bank]], writes=[SGt.b])
            if not is_ctx:
                for (nm_, tl) in (('vtok', Vt), ('sgtok', SGt)):
                    dstd = env.d(nm_)[tok0:tok0 + Tn, :].rearrange("(j p) n -> p j n", p=128)
                    p.dma('sp', lambda e, dstd=dstd, tl=tl: e.dma_start(out=dstd, in_=tl.t[:, 0:nt, :]), reads=[tl.b],
                          writes=[env.b(nm_, si)])
            def chain(hd, d, q_):
                a_, b_, c_, d_, e_, kh_ = ta[d], tb[d], tc[d], td[d], te[d], tkh[d]
                bank = 3 if d == 0 else 4
                lcol = d * 8 + hd
                lb_ap = lbt.t[:, 16 + lcol:17 + lcol]
                oml_ap = lbt.t[:, 32 + lcol:33 + lcol]
                p.op('act', lambda e: e.activation(out=a_.t[:, 0:Tn], in_=ps[:, bank, 0:Tn], func=AF.Sigmoid),
                     reads=[pb[bank]], writes=[a_.b])
                p.op('act', lambda e: e.activation(out=b_.t[:, 0:Tn], in_=ps[:, bank, 0:Tn], func=AF.Sigmoid, scale=-1.0),
                     reads=[pb[bank]], writes=[b_.b])
                p.op('act', lambda e: e.activation(out=a_.t[:, 0:Tn], in_=a_.t[:, 0:Tn], func=AF.Ln, scale=oml_ap, bias=lb_ap),
                     reads=[a_.b, lbt.b], writes=[a_.b])
                yield
                p.op('dve', lambda e: e.tensor_tensor_scan(
                    out=c_.t[:, 0:Tn], data0=rst.t[:, 0:Tn], data1=a_.t[:, 0:Tn], initial=0.0, op0=ALU.mult, op1=ALU.add),
                    reads=[rst.b, a_.b], writes=[c_.b])
                c3 = c_.t[:, 0:Tn].rearrange("p (c t) -> p c t", t=64)
                bend = c3[:, :, 63:64]
                p.op('dve', lambda e: e.tensor_tensor(
                    out=d_.t[:, 0:Tn].rearrange("p (c t) -> p c t", t=64), in0=bend.to_broadcast([128, nch, 64]), in1=c3,
                    op=ALU.subtract), reads=[c_.b], writes=[d_.b])
                ebt = EB1 if d == 0 else EB2
                yield
                p.op('act', lambda e: e.activation(out=ebt.t[:, hd, 0:nch].unsqueeze(2), in_=bend, func=AF.Exp),
                     reads=[c_.b], writes=[ebt.b])
                if d == 0:
                    p.op('act', lambda e: e.activation(out=e_.t[:, 0:Tn], in_=c_.t[:, 0:Tn], func=AF.Exp), reads=[c_.b], writes=[e_.b])
                    yield
                    p.op('dve', lambda e: e.tensor_tensor(out=QE1.t[:, hd, 0:Tn], in0=q_.t[:, 0:Tn], in1=e_.t[:, 0:Tn], op=ALU.mult),
                         reads=[q_.b, e_.b], writes=[QE1.b])
                    p.op('dve', lambda e: e.tensor_tensor(
                        out=e_.t[:, 0:Tn].rearrange("p (c t) -> p c t", t=64), in0=c3,
                        in1=c3[:, :, 31:32].to_broadcast([128, nch, 64]), op=ALU.subtract), reads=[c_.b], writes=[e_.b])
                    yield
                    p.op('act', lambda e: e.activation(out=a_.t[:, 0:Tn], in_=e_.t[:, 0:Tn], func=AF.Exp), reads=[e_.b], writes=[a_.b])
                    p.op('act', lambda e: e.activation(out=e_.t[:, 0:Tn], in_=e_.t[:, 0:Tn], func=AF.Exp, scale=-1.0),
                         reads=[e_.b], writes=[e_.b])
                    p.op('act', lambda e: e.activation(out=d_.t[:, 0:Tn], in_=d_.t[:, 0:Tn], func=AF.Exp), reads=[d_.b], writes=[d_.b])
                    yield
                    p.op('dve', lambda e: e.tensor_tensor(out=QT1.t[:, hd, 0:Tn], in0=q_.t[:, 0:Tn], in1=a_.t[:, 0:Tn], op=ALU.mult),
                         reads=[q_.b, a_.b], writes=[QT1.b])
                    p.op('dve', lambda e: e.scalar_tensor_tensor(
                        out=KN1.t[:, hd, 0:Tn], in0=b_.t[:, 0:Tn], scalar=oml_ap, in1=e_.t[:, 0:Tn], op0=ALU.mult, op1=ALU.mult),
                        reads=[b_.b, e_.b, lbt.b], writes=[KN1.b])
                    p.op('dve', lambda e: e.scalar_tensor_tensor(
                        out=kh_.t[:, 0:Tn], in0=b_.t[:, 0:Tn], scalar=oml_ap, in1=d_.t[:, 0:Tn], op0=ALU.mult, op1=ALU.mult),
                        reads=[b_.b, d_.b, lbt.b], writes=[kh_.b])
                    KH = KH1
                else:
                    p.op('dve', lambda e: e.tensor_tensor(out=d_.t[:, 0:Tn], in0=d_.t[:, 0:Tn], in1=a_.t[:, 0:Tn], op=ALU.add),
                         reads=[d_.b, a_.b], writes=[d_.b])
                    sq_, sk_, st_ = s2q[hd % 2], s2k[hd % 2], s2t[hd % 2]
                    d3 = d_.t[:, 0:Tn].rearrange("p (c t) -> p c t", t=64)
                    p.op('dve', lambda e: e.tensor_tensor(
                        out=e_.t[:, 0:Tn].rearrange("p (c t) -> p c t", t=64), in0=d3,
                        in1=d3[:, :, 32:33].to_broadcast([128, nch, 64]), op=ALU.subtract), reads=[d_.b], writes=[e_.b])
                    p.op('dve', lambda e: e.tensor_tensor(out=c_.t[:, 0:Tn], in0=c_.t[:, 0:Tn], in1=a_.t[:, 0:Tn], op=ALU.subtract),
                         reads=[c_.b, a_.b], writes=[c_.b])
                    yield
                    p.op('act', lambda e: e.activation(out=d_.t[:, 0:Tn], in_=d_.t[:, 0:Tn], func=AF.Exp), reads=[d_.b], writes=[d_.b])
                    p.op('act', lambda e: e.activation(out=a_.t[:, 0:Tn], in_=e_.t[:, 0:Tn], func=AF.Exp), reads=[e_.b], writes=[a_.b])
                    p.op('act', lambda e: e.activation(out=e_.t[:, 0:Tn], in_=e_.t[:, 0:Tn], func=AF.Exp, scale=-1.0),
                         reads=[e_.b], writes=[e_.b])
                    p.op('act', lambda e: e.activation(out=c_.t[:, 0:Tn], in_=c_.t[:, 0:Tn], func=AF.Exp), reads=[c_.b], writes=[c_.b])
                    yield
                    p.op('dve', lambda e: e.tensor_tensor(out=sq_.t[:, 0:Tn], in0=q_.t[:, 0:Tn], in1=d_.t[:, 0:Tn], op=ALU.mult),
                         reads=[q_.b, d_.b], writes=[sq_.b])
                    p.op('dve', lambda e: e.tensor_tensor(out=st_.t[:, 0:Tn], in0=q_.t[:, 0:Tn], in1=a_.t[:, 0:Tn], op=ALU.mult),
                         reads=[q_.b, a_.b], writes=[st_.b])
                    p.op('dve', lambda e: e.scalar_tensor_tensor(
                        out=sk_.t[:, 0:Tn], in0=b_.t[:, 0:Tn], scalar=oml_ap, in1=e_.t[:, 0:Tn], op0=ALU.mult, op1=ALU.mult),
                        reads=[b_.b, e_.b, lbt.b], writes=[sk_.b])
                    p.op('dve', lambda e: e.scalar_tensor_tensor(
                        out=kh_.t[:, 0:Tn], in0=b_.t[:, 0:Tn], scalar=oml_ap, in1=c_.t[:, 0:Tn], op0=ALU.mult, op1=ALU.mult),
                        reads=[b_.b, c_.b, lbt.b], writes=[kh_.b])
                    for (nm_, tl) in (('d2qe', sq_), ('d2kn', sk_), ('d2qt', st_)):
                        dstd = env.d(nm_)[hd, :, tok0:tok0 + Tn]
                        p.dma('sp', lambda e, dstd=dstd, tl=tl: e.dma_start(out=dstd, in_=tl.t[:, 0:Tn]), reads=[tl.b],
                              writes=[env.b(nm_, si)])
                    KH = KH2
                yield
                tbk = 5 + d
                for j in range(nt):
                    p.op('pe', lambda e, j=j: e.transpose(
                        psb(tbk)[:, j * 128:(j + 1) * 128], kh_.t[:, j * 128:(j + 1) * 128], identb.t[:]),
                        reads=[kh_.b, identb.b], writes=[pb[tbk]], inc=(j == nt - 1))
                p.op('dve', lambda e: e.tensor_copy(
                    out=KH.t[:, 0:nt, hd * 128:(hd + 1) * 128],
                    in_=psb(tbk)[:, 0:nt * 128].rearrange("p (j k) -> p j k", k=128)),
                    reads=[pb[tbk]], writes=[KH.b])

            for hd in range(8):
                bq, bf1, bf2 = 2, 3, 4
                for (bank, c0) in ((bq, hd * 128), (bf1, 3 * D + hd * 128), (bf2, 4 * D + hd * 128)):
                    if is_ctx and bank == bf2:
                        continue
                    for kc in range(8):
                        p.op('pe', lambda e, bank=bank, kc=kc, c0=c0: e.matmul(
                            ps[:, bank, 0:Tn], lhsT=w[:, kc, c0:c0 + 128], rhs=hT.t[:, kc, 0:Tn],
                            start=(kc == 0), stop=(kc == 7)), reads=[hT.b, hT.b2, w_b[c0 // 512]], writes=[pb[bank]], inc=(kc == 7))
                q_ = tq[hd % 2]
                p.op('act', lambda e, q_=q_: e.activation(out=q_.t[:, 0:Tn], in_=ps[:, bq, 0:Tn], func=AF.Silu),
                     reads=[pb[bq]], writes=[q_.b])
                gens = [chain(hd, 0, q_)]
                if not is_ctx:
                    gens.append(chain(hd, 1, q_))
                interleave(gens)
            if not is_ctx:
                dstd = env.d('d2kh')[tok0:tok0 + Tn, :].rearrange("(j p) n -> p j n", p=128)
                p.dma('sp', lambda e, dstd=dstd: e.dma_start(out=dstd, in_=KH2.t[:, 0:nt, :]), reads=[KH2.b],
                      writes=[env.b('d2kh', si)])
                ch0 = tok0 // 64
                dstd = env.d('d2eb')[:, :, ch0:ch0 + nch]
                p.dma('sp', lambda e, dstd=dstd: e.dma_start(out=dstd, in_=EB2.t[:, :, 0:nch]), reads=[EB2.b],
                      writes=[env.b('d2eb', si)])
            for c in range(nch):
                j = c // 2
                pbs = 64 * (c % 2)
                qt = lambda hd, c=c: (QT1.t[:, hd, c * 64:(c + 1) * 64], QT1.b)
                qe = lambda hd, c=c: (QE1.t[:, hd, c * 64:(c + 1) * 64], QE1.b)
                kn = lambda hd, c=c: (KN1.t[:, hd, c * 64:(c + 1) * 64], KN1.b)
                kh = lambda hd, j=j, pbs=pbs: (KH1.t[pbs:pbs + 64, j, hd * 128:(hd + 1) * 128], KH1.b)
                vt = lambda hd, j=j, pbs=pbs: (Vt.t[pbs:pbs + 64, j, hd * 128:(hd + 1) * 128], Vt.b)
                eb = lambda hd, c=c: (EB1.t[:, hd, c:c + 1], EB1.b)
                oo = None
                if not is_ctx:
                    ot = o1t[0]
                    oo = {'aT': aT, 'o': ot}
                scan_chunk(p, nc, ps, pb, (7, 5, 6, 2, 3), qt, qe, kn, kh, vt, eb, ST, msk, c, True, oo)
                if oo is not None and c % 2 == 1:
                    ti = tiles[j][1]
                    dstd = env.d('o1')[ti * 128:(ti + 1) * 128, :]
                    p.dma('sp', lambda e, dstd=dstd, ot=ot: e.dma_start(out=dstd, in_=ot.t[:]), reads=[ot.b],
                          writes=[env.b('o1', ti)])
        for si, tiles in enumerate(sts):
            do_supertile(si, tiles)
        p.dma('sp', lambda e: e.dma_start(out=env.d('s_send')[:, :], in_=ST.S32[:].rearrange("p a b -> p (a b)")),
              reads=ST.b32, writes=[env.b('s_send')])
    p.barrier()


def ph_hg2(p, nc, env, ps, pb):
    with ExitStack() as st:
        sb = lambda n, s, d: st.enter_context(nc.sbuf_tensor(_uniq(n), s, d))
        wo = sb('hwo', [128, 8, D], BF16)
        wo_b = [Buf() for _ in range(2)]
        cast_load_w(p, nc, wo, env.d('hg_w_o'), D, D, wo_b, split=2)
        identb = SB(sb('identb', [128, 128], BF16))
        p.dma('pool', lambda e: e.dma_start(out=identb.t[:], in_=env.d('identf')[:, :]), writes=[identb.b])
        msk = SB(sb('msk', [128, 256], F32))
        p.dma('sp', lambda e: e.dma_start(out=msk.t[:], in_=env.d('trimask')[:, :]), writes=[msk.b])
        ng = SB(sb('ng', [128, D], F32))
        p.dma('sp', lambda e: e.dma_start(out=ng.t[:], in_=env.d('hg_ng')[0:1, :].partition_broadcast(128)), writes=[ng.b])
        G = SB(sb('G', [128, D], F32))
        load_modB(p, nc, env, G, 2 * D, 1, 0)
        ST = ScanState(sb, 'h2')
        p.dma('sp', lambda e: e.dma_start(out=ST.S32[:].rearrange("p a b -> p (a b)"), in_=env.d('s_recv')[:, :]),
              reads=[env.b('s_recv')], writes=ST.b32)
        p.op('pool', lambda e: e.tensor_copy(out=ST.Sbf[0][:], in_=ST.S32[:]), reads=ST.b32, writes=ST.bbf[0])
        R = 2
        QE = [SB(sb(f'QE{i}', [128, 8, 128], BF16)) for i in range(R)]
        KN = [SB(sb(f'KN{i}', [128, 8, 128], BF16)) for i in range(R)]
        QTT = [SB(sb(f'QTT{i}', [128, 8, 128], BF16)) for i in range(R)]
        KH = [SB(sb(f'KH{i}', [128, D], BF16)) for i in range(R)]
        VT = [SB(sb(f'VT{i}', [128, D], BF16)) for i in range(R)]
        SG = [SB(sb(f'SG{i}', [128, D], BF16)) for i in range(R)]
        EBt = [SB(sb(f'EBt{i}', [128, 8, 2], F32)) for i in range(R)]
        O1 = [SB(sb(f'O1{i}', [128, D], F32)) for i in range(R)]
        XB = [SB(sb(f'XB{i}', [128, D], F32)) for i in range(3)]
        o2 = SB(sb('o2', [128, D], F32))
        sq = SB(sb('sq', [128, D], F32))
        hs = SB(sb('hs', [128, 16], F32))
        yb = SB(sb('yb', [128, D], BF16))
        oT = SB(sb('oT', [128, 8, 128], BF16))
        aT = SB(sb('aT', [128, 512], BF16))
        tmp = SB(sb('tmp', [128, D], F32))
        p.op('pool', lambda e: e.memset(aT.t[:], 0.0), writes=[aT.b])
        psb2 = ps[:, 4, :].bitcast(BF16)
        order = list(range(NTQ - 1, -1, -1))

        def loads(n):
            t = order[n]
            r = n % R
            sl = slice(t * 128, (t + 1) * 128)
            si = t // 4
            p.dma('sp', lambda e: e.dma_start(out=QE[r].t[:], in_=env.d('d2qe')[:, :, sl].rearrange("h k n -> k h n")),
                  reads=[env.b('d2qe', si)], writes=[QE[r].b])
            p.dma('sp', lambda e: e.dma_start(out=KN[r].t[:], in_=env.d('d2kn')[:, :, sl].rearrange("h k n -> k h n")),
                  reads=[env.b('d2kn', si)], writes=[KN[r].b])
            p.dma('sp', lambda e: e.dma_start(out=QTT[r].t[:], in_=env.d('d2qt')[:, :, sl].rearrange("h k n -> k h n")),
                  reads=[env.b('d2qt', si)], writes=[QTT[r].b])
            p.dma('sp', lambda e: e.dma_start(out=KH[r].t[:], in_=env.d('d2kh')[sl, :]), reads=[env.b('d2kh', si)], writes=[KH[r].b])
            p.dma('sp', lambda e: e.dma_start(out=VT[r].t[:], in_=env.d('vtok')[sl, :]), reads=[env.b('vtok', si)], writes=[VT[r].b])
            p.dma('sp', lambda e: e.dma_start(out=SG[r].t[:], in_=env.d('sgtok')[sl, :]), reads=[env.b('sgtok', si)], writes=[SG[r].b])
            p.dma('sp', lambda e: e.dma_start(out=EBt[r].t[:], in_=env.d('d2eb')[:, :, 2 * t:2 * t + 2]),
                  reads=[env.b('d2eb', si)], writes=[EBt[r].b])
            p.dma('sp', lambda e: e.dma_start(out=O1[r].t[:], in_=env.d('o1')[sl, :]), reads=[env.b('o1', t)], writes=[O1[r].b])
            xb_ = XB[n % 3]
            p.dma('sp', lambda e: e.dma_start(out=xb_.t[:], in_=env.d('xb')[sl, :]), reads=[env.b('xb', t)], writes=[xb_.b])

        loads(0)

        NG = [SB(sb(f'NG{i}', [128, D], F32)) for i in range(R)]
        pend = []

        def pe_readout(t, xb_):
            for j in range(8):
                p.op('pe', lambda e, j=j: e.transpose(psb2[:, j * 128:(j + 1) * 128], yb.t[:, j * 128:(j + 1) * 128], identb.t[:]),
                     reads=[yb.b, identb.b], writes=[pb[4]], inc=(j == 7))
            p.op('act', lambda e: e.activation(out=oT.t[:].rearrange("p a b -> p (a b)"), in_=psb2[:, 0:1024], func=AF.Identity),
                 reads=[pb[4]], writes=[oT.b])
            for nb in range(2):
                bank = nb
                for kc in range(8):
                    p.op('pe', lambda e, bank=bank, kc=kc, nb=nb: e.matmul(
                        ps[:, bank, :], lhsT=oT.t[:, kc, :], rhs=wo[:, kc, nb * 512:(nb + 1) * 512], start=(kc == 0), stop=(kc == 7)),
                        reads=[oT.b, wo_b[nb]], writes=[pb[bank]], inc=(kc == 7))
                p.op('dve', lambda e, bank=bank, nb=nb: e.tensor_tensor(
                    out=tmp.t[:, nb * 512:(nb + 1) * 512], in0=ps[:, bank, :], in1=G.t[:, nb * 512:(nb + 1) * 512], op=ALU.mult),
                    reads=[pb[bank], G.b], writes=[tmp.b])
                p.op('pool', lambda e, nb=nb: e.tensor_tensor(
                    out=tmp.t[:, nb * 512:(nb + 1) * 512], in0=tmp.t[:, nb * 512:(nb + 1) * 512],
                    in1=xb_.t[:, nb * 512:(nb + 1) * 512], op=ALU.add), reads=[tmp.b, xb_.b], writes=[tmp.b])
            dst = env.d('xc')[t * 128:(t + 1) * 128, :]
            p.dma('sp', lambda e, dst=dst: e.dma_start(out=dst, in_=tmp.t[:]), reads=[tmp.b], writes=[env.b('xc', t)])

        def do_tile(n, t):
            r = n % R
            if n + 1 < len(order):
                loads(n + 1)
            p.op('pool', lambda e: e.tensor_tensor(out=NG[r].t[:], in0=SG[r].t[:], in1=ng.t[:], op=ALU.mult),
                 reads=[SG[r].b, ng.b], writes=[NG[r].b])
            for par in (1, 0):
                pbs = 64 * par
                qt = lambda hd, par=par: (QTT[r].t[:, hd, par * 64:(par + 1) * 64], QTT[r].b)
                qe = lambda hd, par=par: (QE[r].t[:, hd, par * 64:(par + 1) * 64], QE[r].b)
                kn = lambda hd, par=par: (KN[r].t[:, hd, par * 64:(par + 1) * 64], KN[r].b)
                kh = lambda hd, pbs=pbs: (KH[r].t[pbs:pbs + 64, hd * 128:(hd + 1) * 128], KH[r].b)
                vt = lambda hd, pbs=pbs: (VT[r].t[pbs:pbs + 64, hd * 128:(hd + 1) * 128], VT[r].b)
                eb = lambda hd, par=par: (EBt[r].t[:, hd, par:par + 1], EBt[r].b)
                scan_chunk(p, nc, ps, pb, (7, 5, 6, 2, 3), qt, qe, kn, kh, vt, eb, ST, msk, par, False,
                           {'aT': aT, 'o': o2, 'add': O1[r]})
                if par == 1 and pend:
                    pe_readout(*pend.pop())
            for hd in range(8):
                p.op('act', lambda e, hd=hd: e.activation(out=sq.t[:, hd * 128:(hd + 1) * 128], in_=o2.t[:, hd * 128:(hd + 1) * 128],
                                                          func=AF.Square, accum_out=hs.t[:, hd:hd + 1]),
                     reads=[o2.b], writes=[sq.b, hs.b])
            p.op('act', lambda e: e.activation(out=hs.t[:, 0:8], in_=hs.t[:, 0:8], func=AF.Ln, scale=1.0 / 128, bias=EPS),
                 reads=[hs.b], writes=[hs.b])
            p.op('act', lambda e: e.activation(out=hs.t[:, 8:16], in_=hs.t[:, 0:8], func=AF.Exp, scale=-0.5), reads=[hs.b], writes=[hs.b])
            p.op('dve', lambda e: e.tensor_tensor(
                out=sq.t[:].rearrange("p (a b) -> p a b", b=128), in0=o2.t[:].rearrange("p (a b) -> p a b", b=128),
                in1=hs.t[:, 8:16].unsqueeze(2).to_broadcast([128, 8, 128]), op=ALU.mult), reads=[o2.b, hs.b], writes=[sq.b])
            p.op('dve', lambda e: e.tensor_tensor(out=yb.t[:], in0=sq.t[:], in1=NG[r].t[:], op=ALU.mult),
                 reads=[sq.b, NG[r].b], writes=[yb.b])
            pend.append((t, XB[n % 3]))

        for n, t in enumerate(order):
            do_tile(n, t)
        while pend:
            pe_readout(*pend.pop())
    p.barrier()


def ph_xchg(p, nc, env, ps, pb):
    with ExitStack() as st:
        sb = lambda n, s, d: st.enter_context(nc.sbuf_tensor(_uniq(n), s, d))
        snd = env.d('s_send')
        gat = env.d('s_gath')
        p.collective(lambda e: e.collective_compute("AllGather", ALU.bypass, replica_groups=[[0, 1], [2, 3], [4, 5], [6, 7]],
                                                    ins=[snd.opt()], outs=[gat.opt()]),
                     reads=[env.b('s_send')], writes=[env.b('s_gath')])
        g = SB(sb('xg', [128, 2, D], F32))
        sel = SB(sb('xsel', [128, 2], F32))
        r = SB(sb('xr', [128, D], F32))
        p.dma('sp', lambda e: e.dma_start(out=g.t[:], in_=gat.rearrange("(r p) n -> p r n", p=128)), reads=[env.b('s_gath')],
              writes=[g.b])
        p.dma('sp', lambda e: e.dma_start(out=sel.t[:], in_=env.d('sel')[:, :]), writes=[sel.b])
        p.op('dve', lambda e: e.tensor_scalar(out=r.t[:], in0=g.t[:, 0, :], scalar1=sel.t[:, 0:1], scalar2=None, op0=ALU.mult),
             reads=[g.b, sel.b], writes=[r.b])
        p.op('dve', lambda e: e.scalar_tensor_tensor(out=r.t[:], in0=g.t[:, 1, :], scalar=sel.t[:, 1:2], in1=r.t[:], op0=ALU.mult,
                                                     op1=ALU.add), reads=[g.b, sel.b, r.b], writes=[r.b])
        p.dma('sp', lambda e: e.dma_start(out=env.d('s_recv')[:, :], in_=r.t[:]), reads=[r.b], writes=[env.b('s_recv')])
    p.barrier()


def _coords(tile, half):
    l = tile * 128 + np.arange(128)
    g = l if half == 0 else 8191 - l
    return g // 64, g % 64


def rpb_tables(rpb, half):
    flat = np.concatenate([rpb.reshape(16, 15 * 31), np.full((16, 1), PADV, np.float32)], axis=1)

    def table(t, kts):
        qr, qc = _coords(t, half)
        out = np.empty((128, 16, len(kts), 128), np.float32)
        rs = np.clip(qr - 4, 0, 120)
        cs = np.clip(qc - 8, 0, 48)
        for a, kt in enumerate(kts):
            kr, kc = _coords(kt, half)
            valid = ((kr[:, None] >= rs[None, :]) & (kr[:, None] <= rs[None, :] + 7) &
                     (kc[:, None] >= cs[None, :]) & (kc[:, None] <= cs[None, :] + 15))
            idx = (kr[:, None] - qr[None, :] + 7) * 31 + (kc[:, None] - qc[None, :] + 15)
            idx = np.where(valid, idx, 465)
            out[:, :, a, :] = np.transpose(flat[:, idx], (1, 0, 2))
        return out.reshape(128, -1)

    ebg = table(10, [8, 9, 10, 11, 12])
    ebs = np.stack([table(0, [0, 1, 2, 3]), table(1, [0, 1, 2, 3])], axis=0)
    return ebg, ebs


def host_consts():
    tri = np.zeros((128, 256), np.float32)
    for blk in range(2):
        s = np.arange(64)[:, None]
        t = np.arange(64)[None, :]
        tri[blk * 64:(blk + 1) * 64, blk * 64:(blk + 1) * 64] = (s <= t)
        tri[blk * 64:(blk + 1) * 64, 128 + blk * 64:128 + (blk + 1) * 64] = (s >= t)
    rst = np.ones((128, 512), np.float32)
    rst[:, 0::64] = 0.0
    return tri, rst


def prep_inputs(inp):
    f = lambda a: np.ascontiguousarray(a, dtype=np.float32)
    shared = {
        'ada_w': f(inp['ada_w']),
        'ada_b': f(inp['ada_b'].reshape(1, -1)),
        'n1g': f(inp['norm1_g'].reshape(2, 8, 128).transpose(2, 0, 1).reshape(128, 16)),
        'n2g': f(inp['norm2_g'].reshape(2, 8, 128).transpose(2, 0, 1).reshape(128, 16)),
        'identf': np.eye(128, dtype=np.float32),
        'w_qkv': f(inp['na_w_qkv'][0]),
        'w_o0': f(inp['na_w_o'][0]),
        'qkgain': f(np.stack([np.tile(inp['na_q_gain'][0], 2), np.tile(inp['na_k_gain'][0], 2)], axis=1)),
        'ffn_w_in': f(inp['ffn_w_in']),
        'ffn_w_out': f(inp['ffn_w_out']),
        'hg_w_o': f(inp['hg_w_o'][0]),
        'hg_ng': f(np.tile(inp['hg_norm_g'][0], 8).reshape(1, D)),
    }
    shared['trimask'], shared['rstmask'] = host_consts()
    hw = inp['hg_w_in'][0]
    hw_sw = f(np.concatenate([hw[:, 0:3 * D], hw[:, 4 * D:5 * D], hw[:, 3 * D:4 * D]], axis=1))
    low = inp['hg_lower'].reshape(2, 2, 8, 128)
    per_half = []
    for half in range(2):
        ebg, ebs = rpb_tables(f(inp['na_rpb'][0]), half)
        lo = low if half == 0 else low[:, ::-1]
        per_half.append({
            'ebg': ebg, 'ebs': ebs,
            'hg_w_in': f(hw) if half == 0 else hw_sw,
            'hg_low': f(lo.transpose(3, 0, 1, 2).reshape(128, 32)),
        })
    maps = []
    for b in range(4):
        for half in range(2):
            xs = inp['x'][b] if half == 0 else inp['x'][b, ::-1]
            cs = inp['ctx'][b] if half == 0 else inp['ctx'][b, ::-1]
            m = dict(shared)
            m.update(per_half[half])
            m['xin'] = f(xs[0:NTK * 128])
            m['ctxin'] = f(cs)
            m['sel'] = f(np.tile(np.array([[1.0, 0.0]] if half == 1 else [[0.0, 1.0]], np.float32), (128, 1)))
            m['csil'] = f(np.stack([inp['c'][b].reshape(8, 128).T, inp['c_ctx'].reshape(8, 128).T], axis=2).reshape(128, 16))
            maps.append(m)
    return maps


PHASES = {'ada': ph_ada, 'att0': ph_att0, 'ffn0': make_ffn(0, 'xa', 'xb', True), 'ffn1': make_ffn(1, 'xc', 'out', False), 'hg1': ph_hg1, 'hg2': ph_hg2, 'xchg': ph_xchg}


def build(phases, ext_in, ext_out):
    nc = bass.Bass("TRN2", target_bir_lowering=False)
    env = Env(nc, set(ext_in), set(ext_out), SHAPES)
    with ExitStack() as st:
        p = Prog(nc, st)
        ps = st.enter_context(nc.psum_tensor("ps", [128, 8, 512], F32))
        pb = [Buf(excl=True) for _ in range(8)]
        for ph in phases:
            PHASES[ph](p, nc, env, ps, pb)
        p.finish()
    return nc, env, p


def run_launch(phases, ext_in_maps, ext_out, cores):
    nc, env, p = build(phases, ext_in_maps[0].keys(), ext_out)
    used = set(env.t.keys())
    in_maps = [{k: v for k, v in m.items() if k in used} for m in ext_in_maps]
    res = run_bass_kernel_spmd(nc, in_maps, core_ids=list(range(len(cores))))
    return res


HOST_IN = ['xin', 'ctxin', 'sel', 'csil', 'ada_w', 'ada_b', 'n1g', 'n2g', 'identf', 'w_qkv', 'w_o0', 'qkgain', 'ebg', 'ebs',
           'ffn_w_in', 'ffn_w_out', 'hg_w_in', 'hg_w_o', 'hg_low', 'hg_ng', 'trimask', 'rstmask']
L1_OUT = ['modrow', 'xb', 'o1', 'vtok', 'sgtok', 'd2kh', 'd2qe', 'd2kn', 'd2qt', 'd2eb', 's_send']


def _launch(phases, maps, ext_out):
    nc, env, _ = build(phases, maps[0].keys(), ext_out)
    used = set(env.t.keys())
    in_maps = [{k: v for k, v in m.items() if k in used} for m in maps]
    res = run_bass_kernel_spmd(nc, in_maps, core_ids=list(range(len(maps))))
    return res.results


def kernel(**inputs):
    maps = prep_inputs(inputs)
    r = _launch(['ada', 'att0', 'ffn0', 'hg1', 'xchg', 'hg2', 'ffn1'], maps, ['out'])
    out = np.empty((4, 8192, D), np.float32)
    for b in range(4):
        out[b, 0:NTOK] = np.asarray(r[2 * b]['out'])
        out[b, NTOK:] = np.asarray(r[2 * b + 1]['out'])[::-1]
    return out
```

```python
import re
import numpy as np
from contextlib import ExitStack
import concourse.bass as bass
import concourse.mybir as mybir
from concourse.bass_utils import run_bass_kernel_spmd

F32 = mybir.dt.float32
BF16 = mybir.dt.bfloat16
AF = mybir.ActivationFunctionType
ALU = mybir.AluOpType
AX = mybir.AxisListType

D = 1024
EPS = 1e-6
NTQ = 32
NTK = 34
NTOK = NTQ * 128
DFF = 2816
PADV = -200.0


class Buf:
    __slots__ = ("w", "r", "excl")

    def __init__(self, excl=False):
        self.w = None
        self.r = {}
        self.excl = excl


NDSEM = 20


class Prog:
    ENG = {'pe': 'tensor', 'act': 'scalar', 'dve': 'vector', 'pool': 'gpsimd', 'sp': 'sync'}

    def __init__(self, nc, stack):
        self.nc = nc
        self.q = {e: [] for e in self.ENG}
        self.cnt = {e: 0 for e in self.ENG}
        self.sem = {e: stack.enter_context(nc.semaphore("s_" + e)) for e in self.ENG}
        self.cc_sem = stack.enter_context(nc.semaphore("s_cc"))
        self.cc_cnt = 0
        self.dsem = {}
        self.dcnt = {}
        self.drr = {}
        for qn in ('sp', 'act', 'pool'):
            self.dsem[qn] = [stack.enter_context(nc.semaphore(f"d_{qn}{i}")) for i in range(NDSEM)]
            self.dcnt[qn] = [0] * NDSEM
            self.drr[qn] = 0

    def _deps(self, eng, reads, writes):
        best = {}
        for b in reads:
            if b.w is not None:
                k, v = b.w
                if best.get(k, 0) < v:
                    best[k] = v
            if b.excl:
                for k, v in b.r.items():
                    if k != eng and best.get(k, 0) < v:
                        best[k] = v
        for b in writes:
            if b.w is not None:
                k, v = b.w
                if k != eng and best.get(k, 0) < v:
                    best[k] = v
            for k, v in b.r.items():
                if k == eng:
                    continue
                if best.get(k, 0) < v:
                    best[k] = v
        if eng == 'pe':
            best.pop('pe', None)
        return best

    def _mark(self, tok, reads, writes):
        k, v = tok
        for b in reads:
            if b.r.get(k, 0) < v:
                b.r[k] = v
        for b in writes:
            b.w = tok
            b.r = {}

    def op(self, eng, fn, reads=(), writes=(), inc=True):
        for k, v in self._deps(eng, reads, writes).items():
            self.q[eng].append(('wait', k, v))
        tok = (eng, self.cnt[eng] + 1)
        if inc:
            self.cnt[eng] += 1
        self.q[eng].append(('op', fn, inc))
        self._mark(tok, reads, writes)
        return tok

    def dma(self, qn, fn, reads=(), writes=()):
        for k, v in self._deps(qn + '_q', reads, writes).items():
            self.q[qn].append(('wait', k, v))
        i = self.drr[qn]
        self.drr[qn] = (i + 1) % NDSEM
        prev = self.dcnt[qn][i]
        key = f"d_{qn}{i}"
        if prev > 0:
            self.q[qn].append(('wait', key, prev))
        self.dcnt[qn][i] = prev + 16
        tok = (key, prev + 16)
        self.q[qn].append(('dma', fn, i))
        self._mark(tok, reads, writes)
        return tok

    def collective(self, fn, reads=(), writes=()):
        for k, v in self._deps('pool_q', reads, writes).items():
            self.q['pool'].append(('wait', k, v))
        self.cc_cnt += 1
        tok = ('cc', self.cc_cnt)
        self.q['pool'].append(('cc', fn))
        self._mark(tok, reads, writes)
        return tok

    def barrier(self):
        toks = [(e, c) for e, c in self.cnt.items() if c > 0]
        if self.cc_cnt > 0:
            toks.append(('cc', self.cc_cnt))
        for qn in self.dcnt:
            for i, c in enumerate(self.dcnt[qn]):
                if c > 0:
                    toks.append((f"d_{qn}{i}", c))
        for e in self.ENG:
            for k, v in toks:
                if k != e:
                    self.q[e].append(('wait', k, v))

    def semof(self, key):
        if key == 'cc':
            return self.cc_sem
        if key in self.sem:
            return self.sem[key]
        m = re.match(r"d_([a-z]+)(\d+)", key)
        return self.dsem[m.group(1)][int(m.group(2))]

    def finish(self):
        self.barrier()
        nc = self.nc
        with nc.Block() as block:
            for eng, attr in self.ENG.items():
                items = self.q[eng]

                def body(e, items=items, eng=eng):
                    known = {}
                    for it in items:
                        if it[0] == 'wait':
                            _, k, v = it
                            if known.get(k, 0) >= v:
                                continue
                            known[k] = v
                            e.wait_ge(self.semof(k), v)
                        elif it[0] == 'cc':
                            it[1](e).then_inc(self.cc_sem)
                        elif it[0] == 'op':
                            ins = it[1](e)
                            if it[2]:
                                ins.then_inc(self.sem[eng], 1)
                        else:
                            ins = it[1](e)
                            ins.then_inc(self.dsem[eng][it[2]], 16)
                getattr(block, attr)(body)


_UNIQ = [0]


def _uniq(n):
    _UNIQ[0] += 1
    return f"{n}_{_UNIQ[0]}"


class SB:
    def __init__(self, t):
        self.t = t
        self.b = Buf()
        self.b2 = Buf()

    @property
    def bb(self):
        return [self.b, self.b2]


class Env:
    def __init__(self, nc, ext_in, ext_out, shapes):
        self.nc = nc
        self.ext_in = ext_in
        self.ext_out = ext_out
        self.shapes = shapes
        self.t = {}
        self.bufs = {}
        self.pre = {}

    def d(self, name):
        if name not in self.t:
            shape, dt = self.shapes[name]
            kind = "ExternalInput" if name in self.ext_in else ("ExternalOutput" if name in self.ext_out else "Internal")
            self.t[name] = self.nc.dram_tensor(name, list(shape), dt, kind=kind).ap()
        return self.t[name]

    def b(self, name, idx=0):
        key = (name, idx)
        if key not in self.bufs:
            self.bufs[key] = Buf()
        return self.bufs[key]


SHAPES = {
    'xin': ((NTK * 128, D), F32),
    'ctxin': ((256, D), F32),
    'csil': ((128, 16), F32),
    'ada_w': ((2, D, 6 * D), F32),
    'ada_b': ((1, 2 * 6 * D), F32),
    'n1g': ((128, 16), F32),
    'n2g': ((128, 16), F32),
    'identf': ((128, 128), F32),
    'w_qkv': ((D, 3 * D), F32),
    'w_o0': ((D, D), F32),
    'qkgain': ((128, 2), F32),
    'ebg': ((128, 16 * 5 * 128), F32),
    'ebs': ((2, 128, 16 * 4 * 128), F32),
    'ffn_w_in': ((2, D, 2 * DFF), F32),
    'ffn_w_out': ((2, DFF, D), F32),
    'hg_w_in': ((D, 5 * D), F32),
    'hg_w_o': ((D, D), F32),
    'hg_low': ((128, 32), F32),
    'hg_ng': ((1, D), F32),
    'trimask': ((128, 256), F32),
    'rstmask': ((128, 512), F32),
    'modrow': ((1, 4 * 6 * D), F32),
    'xa': ((NTOK, D), F32), 'xa_ctx': ((256, D), F32),
    'xb': ((NTOK, D), F32), 'xb_ctx': ((256, D), F32),
    'xc': ((NTOK, D), F32),
    'out': ((NTOK, D), F32),
    'o1': ((NTOK, D), F32),
    'vtok': ((NTOK, D), BF16), 'sgtok': ((NTOK, D), BF16), 'd2kh': ((NTOK, D), BF16),
    'd2qe': ((8, 128, NTOK), BF16), 'd2kn': ((8, 128, NTOK), BF16), 'd2qt': ((8, 128, NTOK), BF16),
    'd2eb': ((128, 8, NTOK // 64), F32),
    's_send': ((128, D), F32), 's_recv': ((128, D), F32), 's_gath': ((256, D), F32), 'sel': ((128, 2), F32),
}


def cast_load_w(p, nc, dst, dram2d, K, N, bufs, q='pool', split=1):
    kc = K // 128
    step = N // split
    for s in range(split):
        src = dram2d[:, s * step:(s + 1) * step].rearrange("(c p) n -> p c n", p=128)
        p.dma(q, lambda e, s=s, src=src: e.dma_start(out=dst[:, :, s * step:(s + 1) * step], in_=src),
              writes=[bufs[s]])


def ph_ada(p, nc, env, ps, pb):
    with ExitStack() as st:
        sb = lambda n, s, d: st.enter_context(nc.sbuf_tensor(_uniq(n), s, d))
        cs_f = SB(sb('cs_f', [128, 16], F32))
        cs_b = SB(sb('cs_b', [128, 16], BF16))
        brow = SB(sb('brow', [2, 2 * 6 * D // 2], F32))
        mrow = SB(sb('mrow', [2, 2 * 6 * D // 2], F32))
        NW = 3
        wr = [SB(sb(f'adaw{i}', [128, 8, 512], F32)) for i in range(NW)]
        wb = [SB(sb(f'adab{i}', [128, 8, 512], BF16)) for i in range(2)]
        p.dma('sp', lambda e: e.dma_start(out=cs_f.t[:], in_=env.d('csil')[:, :]), writes=[cs_f.b])
        p.op('act', lambda e: e.activation(out=cs_b.t[:], in_=cs_f.t[:], func=AF.Silu), reads=[cs_f.b], writes=[cs_b.b])
        aw = env.d('ada_w')
        n = 0
        for i in range(2):
            p.dma('sp', lambda e, i=i: e.dma_start(out=brow.t[:], in_=env.d('ada_b')[0:1, i * 6 * D:(i + 1) * 6 * D].partition_broadcast(2)),
                  writes=[brow.b])
            for nb in range(12):
                w = wr[n % NW]
                w2 = wb[n % 2]
                src = aw[i][:, nb * 512:(nb + 1) * 512].rearrange("(c p) n -> p c n", p=128)
                q = 'sp' if n % 2 == 0 else 'act'
                p.dma(q, lambda e, w=w, src=src: e.dma_start(out=w.t[:], in_=src), writes=[w.b])
                p.op('act', lambda e, w=w, w2=w2: e.activation(out=w2.t[:, 0:4, :], in_=w.t[:, 0:4, :], func=AF.Identity),
                     reads=[w.b], writes=[w2.b])
                p.op('dve', lambda e, w=w, w2=w2: e.tensor_copy(out=w2.t[:, 4:8, :], in_=w.t[:, 4:8, :]),
                     reads=[w.b], writes=[w2.b2])
                bank = n % 8
                for kc in range(8):
                    p.op('pe', lambda e, bank=bank, kc=kc, w2=w2: e.matmul(
                        ps[0:2, bank, :], lhsT=cs_b.t[:, 2 * kc:2 * kc + 2], rhs=w2.t[:, kc, :],
                        start=(kc == 0), stop=(kc == 7)),
                        reads=[cs_b.b, w2.b, w2.b2], writes=[pb[bank]], inc=(kc == 7))
                o0 = nb * 512
                p.op('dve', lambda e, bank=bank, o0=o0: e.tensor_tensor(
                    out=mrow.t[0:2, o0:o0 + 512], in0=ps[0:2, bank, :], in1=brow.t[0:2, o0:o0 + 512], op=ALU.add),
                    reads=[pb[bank], brow.b], writes=[mrow.b])
                n += 1
            dst = env.d('modrow')[0, i * 2 * 6 * D:(i + 1) * 2 * 6 * D].rearrange("(m c) -> m c", m=2)
            p.dma('sp', lambda e, dst=dst: e.dma_start(out=dst, in_=mrow.t[0:2, :]), reads=[mrow.b], writes=[env.b('modrow')])
    p.barrier()


def load_modP(p, nc, env, dst, col0, layer, m, ncols=1):
    base = (layer * 2 + m) * 6 * D + col0
    src = env.d('modrow')[0, base:base + D].rearrange("(j p) -> p j", p=128)
    p.dma('sp', lambda e: e.dma_start(out=dst.t[:, 0:8], in_=src, allow_slow_non_contiguous=True),
          reads=[env.b('modrow')], writes=[dst.b])


def load_modB(p, nc, env, dst, col0, layer, m):
    base = (layer * 2 + m) * 6 * D + col0
    src = env.d('modrow')[0:1, base:base + D].partition_broadcast(128)
    p.dma('sp', lambda e: e.dma_start(out=dst.t[:], in_=src), reads=[env.b('modrow')], writes=[dst.b])


def rstd_from_ss(p, s, c_in, c_tmp, c_out, scale, n=1):
    p.op('act', lambda e: e.activation(out=s.t[:, c_tmp:c_tmp + n], in_=s.t[:, c_in:c_in + n], func=AF.Ln, scale=scale, bias=EPS),
         reads=[s.b], writes=[s.b])
    p.op('act', lambda e: e.activation(out=s.t[:, c_out:c_out + n], in_=s.t[:, c_tmp:c_tmp + n], func=AF.Exp, scale=-0.5),
         reads=[s.b], writes=[s.b])


class NormMod:
    def __init__(self, p, nc, st, env, ps, pb, layer, which, ms, banks, tag, nxn=2, junk=None):
        self.p, self.nc, self.ps, self.pb, self.banks = p, nc, ps, pb, banks
        sb = lambda n, s, d: st.enter_context(nc.sbuf_tensor(_uniq(n + tag), s, d))
        self.identf = SB(sb('identf', [128, 128], F32))
        p.dma('sp', lambda e: e.dma_start(out=self.identf.t[:], in_=env.d('identf')[:, :]), writes=[self.identf.b])
        if junk is None:
            self.junk_t = sb('junk', [128, D], BF16)
            self.junk_b = Buf()
        else:
            self.junk_t, self.junk_b = junk
        self.nxn = nxn
        self.xn = [SB(sb(f'xn{i}', [128, D], F32)) for i in range(nxn)]
        self.st = [SB(sb(f'st{i}', [128, 4], F32)) for i in range(2)]
        self.k = 0
        gP = SB(sb('gP', [128, 16], F32))
        p.dma('sp', lambda e: e.dma_start(out=gP.t[:], in_=env.d('n1g' if which == 0 else 'n2g')[:, :]), writes=[gP.b])
        self.geff = {}
        self.sh = {}
        for m in ms:
            shP = SB(sb(f'shP{m}', [128, 8], F32))
            scP = SB(sb(f'scP{m}', [128, 8], F32))
            ge = SB(sb(f'geff{m}', [128, 8], F32))
            load_modP(p, nc, env, shP, (0 if which == 0 else 3 * D), layer, m)
            load_modP(p, nc, env, scP, (D if which == 0 else 4 * D), layer, m)
            p.op('dve', lambda e, scP=scP, ge=ge: e.scalar_tensor_tensor(
                out=ge.t[:], in0=scP.t[:], scalar=1.0, in1=gP.t[:, layer * 8:layer * 8 + 8], op0=ALU.add, op1=ALU.mult),
                reads=[scP.b, gP.b], writes=[ge.b])
            self.geff[m] = ge
            self.sh[m] = shP

    def part1(self, x):
        p = self.p
        k = self.k
        self.k += 1
        xn = self.xn[k % self.nxn]
        s = self.st[k % 2]
        p.op('act', lambda e: e.activation(out=self.junk_t[:], in_=x.t[:], func=AF.Square, accum_out=s.t[:, 0:1]),
             reads=[x.b], writes=[self.junk_b, s.b])
        rstd_from_ss(p, s, 0, 1, 2, 1.0 / D)
        p.op('pool', lambda e: e.tensor_scalar(out=xn.t[:], in0=x.t[:], scalar1=s.t[:, 2:3], scalar2=None, op0=ALU.mult),
             reads=[x.b, s.b], writes=[xn.b])
        return xn

    def part2(self, xn, m, hT, col0):
        self.part2_pe(xn)
        self.part2_evac(m, hT, col0)

    def part2_pe(self, xn):
        p, ps, pb = self.p, self.ps, self.pb
        b0, b1 = self.banks
        for j in (0, 2, 4, 6, 1, 3, 5, 7):
            bank = b0 if j % 2 == 0 else b1
            p.op('pe', lambda e, j=j, bank=bank: e.transpose(ps[:, bank, (j // 2) * 128:(j // 2 + 1) * 128],
                                                             xn.t[:, j * 128:(j + 1) * 128], self.identf.t[:]),
                 reads=[xn.b, self.identf.b], writes=[pb[bank]], inc=(j >= 6))

    def part2_evac(self, m, hT, col0):
        p, ps, pb = self.p, self.ps, self.pb
        b0, b1 = self.banks
        ge, shP = self.geff[m], self.sh[m]
        for j in range(8):
            bank = b0 if j % 2 == 0 else b1
            if j % 2 == 0:
                p.op('act', lambda e, j=j, bank=bank: e.activation(
                    out=hT.t[:, j, col0:col0 + 128], in_=ps[:, bank, (j // 2) * 128:(j // 2 + 1) * 128],
                    func=AF.Identity, scale=ge.t[:, j:j + 1], bias=shP.t[:, j:j + 1]),
                    reads=[pb[bank], ge.b, shP.b], writes=[hT.b])
            else:
                p.op('dve', lambda e, j=j, bank=bank: e.tensor_scalar(
                    out=hT.t[:, j, col0:col0 + 128], in0=ps[:, bank, (j // 2) * 128:(j // 2 + 1) * 128],
                    scalar1=ge.t[:, j:j + 1], scalar2=shP.t[:, j:j + 1], op0=ALU.mult, op1=ALU.add),
                    reads=[pb[bank], ge.b, shP.b], writes=[hT.b2])

    def run(self, x, m, hT, col0):
        self.part2(self.part1(x), m, hT, col0)


def interleave(gens):
    gens = list(gens)
    while gens:
        for g in list(gens):
            try:
                next(g)
            except StopIteration:
                gens.remove(g)


def att0_prefetch(p, nc, env, st):
    sb = lambda n, s, d: st.enter_context(nc.sbuf_tensor(_uniq(n), s, d))
    wqkv = sb('wqkv', [128, 8, 3 * D], BF16)
    wq_b = [Buf() for _ in range(6)]
    cast_load_w(p, nc, wqkv, env.d('w_qkv'), D, 3 * D, wq_b, split=6)
    wo = sb('wo', [128, 8, D], BF16)
    wo_b = [Buf() for _ in range(2)]
    cast_load_w(p, nc, wo, env.d('w_o0'), D, D, wo_b, split=2)
    return wqkv, wq_b, wo, wo_b


def ph_att0(p, nc, env, ps, pb):
    NK = 7
    NQ = 5
    with ExitStack() as st:
        sb = lambda n, s, d: st.enter_context(nc.sbuf_tensor(_uniq(n), s, d))
        if 'att0_pre' in env.pre:
            wqkv, wq_b, wo, wo_b = env.pre.pop('att0_pre')
        else:
            wqkv, wq_b, wo, wo_b = att0_prefetch(p, nc, env, st)
        identb = SB(sb('identb', [128, 128], BF16))
        p.dma('pool', lambda e: e.dma_start(out=identb.t[:], in_=env.d('identf')[:, :]), writes=[identb.b])
        qkg = SB(sb('qkg', [128, 2], F32))
        p.dma('sp', lambda e: e.dma_start(out=qkg.t[:], in_=env.d('qkgain')[:, :]), writes=[qkg.b])
        G = SB(sb('G', [128, D], F32))
        sq = SB(sb('sq', [128, 512], F32))
        nm = NormMod(p, nc, st, env, ps, pb, 0, 0, [0, 1], (0, 1), 'a0', nxn=1, junk=(sq.t[:].bitcast(BF16), sq.b))
        EB = SB(sb('EB', [128, 16, 5, 128], BF16))
        EBS = SB(sb('EBS', [128, 16, 4, 128], BF16))
        xr = [SB(sb(f'xr{i}', [128, D], F32)) for i in range(2)]
        hT = [SB(sb(f'hT{i}', [128, 8, 128], BF16)) for i in range(2)]
        KT = [SB(sb(f'KT{i}', [128, 8, 128], BF16)) for i in range(NK + 2)]
        QT = [SB(sb(f'QT{i}', [128, 8, 128], BF16)) for i in range(NQ)]
        V = [SB(sb(f'V{i}', [128, 16, 65], BF16)) for i in range(NK + 2)]
        qf = SB(sb('qf', [128, 2 * D], F32))
        hs = [SB(sb(f'hs{i}', [128, 32], F32)) for i in range(2)]
        qkt = SB(sb('qkt', [128, 2 * D], BF16))
        PT = [SB(sb(f'PT{i}', [128, 896], BF16)) for i in range(3)]
        otok = SB(sb('otok', [128, D], BF16))
        oT = SB(sb('oT', [128, 8, 128], BF16))
        rc = SB(sb('rc', [128, 4], F32))
        tmp = SB(sb('tmp', [128, D], F32))
        xres = SB(sb('xres', [128, D], F32))
        stage = [SB(tmp.t[:, 0:640]), SB(xres.t[:, 0:640])]
        stage[0].b = tmp.b
        stage[1].b = xres.b
        psb2 = ps[:, 2, :].bitcast(BF16)

        for v in V:
            p.op('pool', lambda e, v=v: e.memset(v.t[:, :, 64:65], 1.0), writes=[v.b])

        def load_table(dst, src2d, ndt):
            for h in range(16):
                sg = stage[h % 2]
                w = ndt * 128
                p.dma('sp', lambda e, h=h, sg=sg, w=w: e.dma_start(out=sg.t[:, 0:w], in_=src2d[:, h * w:(h + 1) * w]),
                      writes=[sg.b])
                p.op('act', lambda e, h=h, sg=sg, w=w: e.activation(
                    out=dst.t[:, h, :, :], in_=sg.t[:, 0:w].rearrange("p (a b) -> p a b", b=128), func=AF.Exp),
                    reads=[sg.b], writes=[dst.b])

        load_table(EB, env.d('ebg'), 5)

        gbank = [0]

        def next_bank():
            gbank[0] ^= 1
            return gbank[0]

        qtasks = [('ctx', 0), ('ctx', 1)] + [('lat', i) for i in range(NTK)]

        def xsrc(task):
            kind, i = task
            return (env.d('ctxin') if kind == 'ctx' else env.d('xin'))[i * 128:(i + 1) * 128, :]

        def issue_xload(pos):
            x = xr[pos % 2]
            src = xsrc(qtasks[pos])
            p.dma('sp', lambda e: e.dma_start(out=x.t[:], in_=src), writes=[x.b])

        def kslot(kind, i):
            return NK + i if kind == 'ctx' else i % NK

        def gen_qkv(pos):
            kind, i = qtasks[pos]
            m = 1 if kind == 'ctx' else 0
            x = xr[pos % 2]
            h = hT[pos % 2]
            need_q = not (kind == 'lat' and i >= NTQ)
            ks = kslot(kind, i)
            s = hs[pos % 2]
            c0 = 0 if need_q else 2
            sched = {}

            def at(step, fn):
                sched.setdefault(step, []).append(fn)
            st8 = {}

            def f_p1():
                if pos + 1 < len(qtasks):
                    issue_xload(pos + 1)
                st8['xn'] = nm.part1(x)
            at(0, f_p1)
            at(4, lambda: nm.part2(st8['xn'], m, h, 0))
            blocks = ([0, 1] if need_q else []) + [2, 3, 4, 5]

            def f_blk_pe(nb):
                bank = next_bank()
                st8[('bank', nb)] = bank
                for kc in range(8):
                    p.op('pe', lambda e, bank=bank, kc=kc, nb=nb: e.matmul(
                        ps[:, bank, :], lhsT=h.t[:, kc, :], rhs=wqkv[:, kc, nb * 512:(nb + 1) * 512],
                        start=(kc == 0), stop=(kc == 7)),
                        reads=[h.b, h.b2, wq_b[nb]], writes=[pb[bank]], inc=(kc == 7))

            def f_blk_evac(nb):
                bank = st8[('bank', nb)]
                if nb < 4:
                    p.op('act', lambda e: e.activation(
                        out=qf.t[:, nb * 512:(nb + 1) * 512], in_=ps[:, bank, :], func=AF.Identity),
                        reads=[pb[bank]], writes=[qf.b])
                else:
                    vv = V[ks]
                    p.op('dve', lambda e: e.tensor_copy(
                        out=vv.t[:, (nb - 4) * 8:(nb - 4) * 8 + 8, 0:64],
                        in_=ps[:, bank, :].rearrange("p (a b) -> p a b", b=64)),
                        reads=[pb[bank]], writes=[vv.b])
            for k_, nb in enumerate(blocks):
                at(6 + k_, lambda nb=nb: f_blk_pe(nb))
                at(7 + k_, lambda nb=nb: f_blk_evac(nb))

            def f_sq():
                for nb in range(c0, 4):
                    p.op('act', lambda e, nb=nb: e.activation(out=sq.t[:], in_=qf.t[:, nb * 512:(nb + 1) * 512], func=AF.Square),
                         reads=[qf.b], writes=[sq.b])
                    p.op('dve', lambda e, nb=nb: e.tensor_reduce(out=s.t[:, nb * 8:nb * 8 + 8],
                                                                 in_=sq.t[:].rearrange("p (a b) -> p a b", b=64), axis=AX.X, op=ALU.add),
                         reads=[sq.b], writes=[s.b])
            nblk = len(blocks)
            at(6 + nblk, f_sq)
            at(8 + nblk, lambda: rstd_from_ss(p, s, c0 * 8, c0 * 8, c0 * 8, 1.0 / 64, n=(4 - c0) * 8))

            def f_mul():
                for nb in range(c0, 4):
                    cc = nb * 8
                    p.op('dve', lambda e, nb=nb, cc=cc: e.tensor_tensor(
                        out=qkt.t[:, nb * 512:(nb + 1) * 512].rearrange("p (a b) -> p a b", b=64),
                        in0=qf.t[:, nb * 512:(nb + 1) * 512].rearrange("p (a b) -> p a b", b=64),
                        in1=s.t[:, cc:cc + 8].unsqueeze(2).to_broadcast([128, 8, 64]), op=ALU.mult),
                        reads=[qf.b, s.b], writes=[qkt.b])
            at(9 + nblk, f_mul)

            def f_T_pe(which):
                for j in range(8):
                    p.op('pe', lambda e, j=j: e.transpose(
                        psb2[:, j * 128:(j + 1) * 128], qkt.t[:, which * D + j * 128:which * D + (j + 1) * 128], identb.t[:]),
                        reads=[qkt.b, identb.b], writes=[pb[2]], inc=(j == 7))

            def f_T_evac(which):
                dst = (QTc[i] if kind == 'ctx' else QT[i % NQ]) if which == 0 else KT[ks]
                if which == 0:
                    p.op('dve', lambda e: e.tensor_scalar(
                        out=dst.t[:].rearrange("p a b -> p (a b)"), in0=psb2[:, 0:1024], scalar1=qkg.t[:, 0:1], scalar2=0.125,
                        op0=ALU.mult, op1=ALU.mult), reads=[pb[2], qkg.b], writes=[dst.b])
                else:
                    p.op('dve', lambda e: e.tensor_scalar(
                        out=dst.t[:].rearrange("p a b -> p (a b)"), in0=psb2[:, 0:1024], scalar1=qkg.t[:, 1:2], scalar2=None,
                        op0=ALU.mult), reads=[pb[2], qkg.b], writes=[dst.b])
            t0_ = 10 + nblk
            for which in ([0, 1] if need_q else [1]):
                at(t0_, lambda which=which: f_T_pe(which))
                at(t0_, lambda which=which: f_T_evac(which))
                t0_ += 2
            for st_ in range(max(sched) + 1):
                for fn in sched.get(st_, []):
                    fn()
                yield

        hcount = [0]

        def gen_att(task):
            kind, t = task
            m = 1 if kind == 'ctx' else 0
            src = xsrc(task)
            if (kind, t) in (('ctx', 0), ('lat', 0)):
                load_modB(p, nc, env, G, 2 * D, 0, m)
            if kind == 'lat' and t in (0, 1):
                load_table(EBS, env.d('ebs')[t], 4)
            if kind == 'ctx':
                chunks = [('ctx', 0), ('ctx', 1)]
                nloc = 0
                tab = None
                q = KT
                qt = None
            else:
                if t in (0, 1):
                    kts = [0, 1, 2, 3]
                    tab = EBS
                else:
                    kts = [t - 2, t - 1, t, t + 1, t + 2]
                    tab = EB
                nloc = len(kts)
                chunks = [('lat', k) for k in kts] + [('ctx', 0), ('ctx', 1)]
            if kind == 'ctx':
                qsrc = QTc[t]
            else:
                qsrc = QT[t % NQ]
            nch = len(chunks)

            def ppos(ci):
                return (0, ci * 128) if ci < 4 else (1, (ci - 4) * 128)
            hn0 = hcount[0]
            hcount[0] += 16

            def emit_qk(h):
                hp, pbs = h // 2, 64 * (h % 2)
                hn = hn0 + h
                sbank = (3 + 2 * (hn % 2), 4 + 2 * (hn % 2))
                for ci, (ck, kt) in enumerate(chunks):
                    bi, col = ppos(ci)
                    bank = sbank[bi]
                    kk = KT[kslot(ck, kt)]
                    last = (ci == nch - 1) or (ci == 3)
                    p.op('pe', lambda e, bank=bank, col=col, kk=kk, hp=hp, pbs=pbs: e.matmul(
                        ps[:, bank, col:col + 128], lhsT=kk.t[pbs:pbs + 64, hp, :], rhs=qsrc.t[pbs:pbs + 64, hp, :],
                        start=True, stop=True),
                        reads=[kk.b, qsrc.b], writes=[pb[bank]], inc=last)

            def emit_exp(h):
                hn = hn0 + h
                sb0 = 3 + 2 * (hn % 2)
                pt = PT[hn % 3]
                S2 = ps[:, sb0:sb0 + 2, :].rearrange("p b c -> p (b c)")
                rb = [pb[sb0]] + ([pb[sb0 + 1]] if nch > 4 else [])
                p.op('act', lambda e: e.activation(out=pt.t[:, 0:nch * 128], in_=S2[:, 0:nch * 128], func=AF.Exp),
                     reads=rb, writes=[pt.b])

            def emit_mult(h):
                hn = hn0 + h
                pt = PT[hn % 3]
                if nloc > 0:
                    p.op('dve', lambda e: e.tensor_tensor(
                        out=pt.t[:, 0:nloc * 128], in0=pt.t[:, 0:nloc * 128],
                        in1=tab.t[:, h, 0:nloc, :].rearrange("p a b -> p (a b)"), op=ALU.mult),
                        reads=[pt.b, tab.b], writes=[pt.b])

            def emit_pv(h):
                hn = hn0 + h
                pt = PT[hn % 3]
                g = h % 4
                for ci, (ck, kt) in enumerate(chunks):
                    off = ci * 128
                    vv = V[kslot(ck, kt)]
                    p.op('pe', lambda e, off=off, vv=vv, g=g, h=h, pt=pt, ci=ci: e.matmul(
                        ps[:, 7, g * 65:(g + 1) * 65], lhsT=pt.t[:, off:off + 128], rhs=vv.t[:, h, :],
                        start=(ci == 0), stop=(ci == nch - 1)),
                        reads=[pt.b, vv.b], writes=[pb[7]], inc=(ci == nch - 1))
                if g == 3:
                    o4 = ps[:, 7, 0:260].rearrange("p (g c) -> p g c", c=65)
                    p.op('dve', lambda e, o4=o4: e.reciprocal(out=rc.t[:, 0:4].unsqueeze(2), in_=o4[:, :, 64:65]),
                         reads=[pb[7]], writes=[rc.b])
                    p.op('dve', lambda e, o4=o4, h=h: e.tensor_tensor(
                        out=otok.t[:, (h - 3) * 64:(h + 1) * 64].rearrange("p (a b) -> p a b", b=64),
                        in0=o4[:, :, 0:64], in1=rc.t[:, 0:4].unsqueeze(2).to_broadcast([128, 4, 64]), op=ALU.mult),
                        reads=[pb[7], rc.b], writes=[otok.b])

            emit_qk(0)
            for s_ in range(18):
                if s_ < 16:
                    emit_exp(s_)
                if 1 <= s_ <= 16:
                    emit_mult(s_ - 1)
                if s_ + 1 < 16:
                    emit_qk(s_ + 1)
                if s_ >= 2:
                    emit_pv(s_ - 2)
                yield
            for j in range(8):
                p.op('pe', lambda e, j=j: e.transpose(psb2[:, j * 128:(j + 1) * 128], otok.t[:, j * 128:(j + 1) * 128],
                                                      identb.t[:]),
                     reads=[otok.b, identb.b], writes=[pb[2]], inc=(j == 7))
            p.op('act', lambda e: e.activation(out=oT.t[:].rearrange("p a b -> p (a b)"), in_=psb2[:, 0:1024],
                                               func=AF.Identity), reads=[pb[2]], writes=[oT.b])
            yield
            p.dma('sp', lambda e: e.dma_start(out=xres.t[:], in_=src), writes=[xres.b])
            for nb in range(2):
                bank = next_bank()
                for kc in range(8):
                    p.op('pe', lambda e, bank=bank, kc=kc, nb=nb: e.matmul(
                        ps[:, bank, :], lhsT=oT.t[:, kc, :], rhs=wo[:, kc, nb * 512:(nb + 1) * 512],
                        start=(kc == 0), stop=(kc == 7)),
                        reads=[oT.b, wo_b[nb]], writes=[pb[bank]], inc=(kc == 7))
                p.op('dve', lambda e, bank=bank, nb=nb: e.tensor_tensor(
                    out=tmp.t[:, nb * 512:(nb + 1) * 512], in0=ps[:, bank, :], in1=G.t[:, nb * 512:(nb + 1) * 512], op=ALU.mult),
                    reads=[pb[bank], G.b], writes=[tmp.b])
                p.op('pool', lambda e, nb=nb: e.tensor_tensor(
                    out=tmp.t[:, nb * 512:(nb + 1) * 512], in0=tmp.t[:, nb * 512:(nb + 1) * 512],
                    in1=xres.t[:, nb * 512:(nb + 1) * 512], op=ALU.add),
                    reads=[tmp.b, xres.b], writes=[tmp.b])
                yield
            dname = 'xa_ctx' if kind == 'ctx' else 'xa'
            dst = env.d(dname)[t * 128:(t + 1) * 128, :]
            p.dma('sp', lambda e: e.dma_start(out=dst, in_=tmp.t[:]), reads=[tmp.b], writes=[env.b(dname, t)])
            yield

        QTc = [SB(sb(f'QTc{i}', [128, 8, 128], BF16)) for i in range(2)]
        atasks = [(('ctx', 0), 1), (('ctx', 1), 1)] + [(('lat', t), 2 + min(t + 3, NTK - 1)) for t in range(NTQ)]
        issue_xload(0)
        ai = 0
        for pos in range(len(qtasks)):
            gens = [gen_qkv(pos)]
            if ai < len(atasks) and atasks[ai][1] <= pos - 1:
                gens.append(gen_att(atasks[ai][0]))
                ai += 1
            interleave(gens)
        while ai < len(atasks):
            interleave([gen_att(atasks[ai][0])])
            ai += 1
    p.barrier()


def make_ffn(layer, src, dst, with_ctx):
    def ph(p, nc, env, ps, pb):
        with ExitStack() as st:
            sb = lambda n, s, d: st.enter_context(nc.sbuf_tensor(_uniq(n), s, d))
            win = sb('win', [128, 8, 2 * DFF], BF16)
            win_b = [Buf() for _ in range(11)]
            wsrc = env.d('ffn_w_in')[layer]
            for s_ in [0, 5, 6, 1, 7, 2, 8, 3, 9, 4, 10]:
                srcw = wsrc[:, s_ * 512:(s_ + 1) * 512].rearrange("(c p) n -> p c n", p=128)
                p.dma('pool', lambda e, s_=s_, srcw=srcw: e.dma_start(out=win[:, :, s_ * 512:(s_ + 1) * 512], in_=srcw),
                      writes=[win_b[s_]])
            wout = sb('wout', [128, 22, D], BF16)
            wout_b = [Buf() for _ in range(2)]
            cast_load_w(p, nc, wout, env.d('ffn_w_out')[layer], DFF, D, wout_b, split=2)
            sg = [SB(sb(f'sg{i}', [128, 512], F32)) for i in range(2)]
            nm = NormMod(p, nc, st, env, ps, pb, layer, 1, [0, 1] if with_ctx else [0], (0, 1), 'f%d' % layer, nxn=2,
                         junk=(sg[0].t[:].bitcast(BF16), sg[0].b))
            hid = SB(sb('hid', [128, 22, 512], BF16))
            h2T = [SB(sb(f'h2T{i}', [128, 8, 512], BF16)) for i in range(2)]
            xr = [SB(sb(f'xr{i}', [128, D], F32)) for i in range(2)]
            tmp = SB(sb('tmp', [128, D], F32))
            xe = SB(sb('xe', [128, D], F32))
            G = SB(sb('G', [128, D], F32))
            sts = []
            if with_ctx:
                sts.append([(src + '_ctx', dst + '_ctx', 0, 1), (src + '_ctx', dst + '_ctx', 1, 1)])
            for s_ in range(NTQ // 4):
                sts.append([(src, dst, s_ * 4 + j, 0) for j in range(4)])
            xcnt = [0]

            def gen_A(si):
                tl = sts[si]
                xns = {}

                def p1(j):
                    sn, dn, ti, m = tl[j]
                    x = xr[xcnt[0] % 2]
                    xcnt[0] += 1
                    srcx = env.d(sn)[ti * 128:(ti + 1) * 128, :]
                    p.dma('sp', lambda e, x=x, srcx=srcx: e.dma_start(out=x.t[:], in_=srcx), reads=[env.b(sn, ti)], writes=[x.b])
                    xns[j] = nm.part1(x)
                p1(0)
                yield
                yield
                for j in range(len(tl)):
                    if j + 1 < len(tl):
                        p1(j + 1)
                    yield
                    yield
                    yield
                    nm.part2(xns[j], tl[j][3], h2T[si % 2], j * 128)
                    yield

            pair = [0]

            def gen_BC(si):
                tiles = sts[si]
                ntok = len(tiles) * 128
                h = h2T[si % 2]
                m = tiles[0][3]
                if si == 0 or (with_ctx and si == 1):
                    load_modB(p, nc, env, G, 5 * D, layer, m)
                for oc in range(22):
                    bA = 2 + 2 * (pair[0] % 3)
                    bU = bA + 1
                    pair[0] += 1
                    for (bank, c0) in ((bA, oc * 128), (bU, DFF + oc * 128)):
                        for kc in range(8):
                            p.op('pe', lambda e, bank=bank, c0=c0, kc=kc: e.matmul(
                                ps[:, bank, 0:ntok], lhsT=win[:, kc, c0:c0 + 128], rhs=h.t[:, kc, 0:ntok],
                                start=(kc == 0), stop=(kc == 7)),
                                reads=[h.b, h.b2, win_b[c0 // 512]], writes=[pb[bank]], inc=(kc == 7))
                    s2 = sg[oc % 2]
                    p.op('act', lambda e, bA=bA, s2=s2: e.activation(out=s2.t[:, 0:ntok], in_=ps[:, bA, 0:ntok], func=AF.Silu),
                         reads=[pb[bA]], writes=[s2.b])
                    p.op('dve', lambda e, bU=bU, s2=s2, oc=oc: e.tensor_tensor(
                        out=hid.t[:, oc, 0:ntok], in0=s2.t[:, 0:ntok], in1=ps[:, bU, 0:ntok], op=ALU.mult),
                        reads=[s2.b, pb[bU]], writes=[hid.b])
                    yield
                for j, (sn, dn, ti, m) in enumerate(tiles):
                    srcx = env.d(sn)[ti * 128:(ti + 1) * 128, :]
                    p.dma('sp', lambda e, srcx=srcx: e.dma_start(out=xe.t[:], in_=srcx), reads=[env.b(sn, ti)], writes=[xe.b])
                    for nb in range(2):
                        bank = nb
                        for oc in range(22):
                            p.op('pe', lambda e, bank=bank, oc=oc, nb=nb, j=j: e.matmul(
                                ps[:, bank, :], lhsT=hid.t[:, oc, j * 128:(j + 1) * 128], rhs=wout[:, oc, nb * 512:(nb + 1) * 512],
                                start=(oc == 0), stop=(oc == 21)),
                                reads=[hid.b, wout_b[nb]], writes=[pb[bank]], inc=(oc == 21))
                        p.op('dve', lambda e, bank=bank, nb=nb: e.tensor_tensor(
                            out=tmp.t[:, nb * 512:(nb + 1) * 512], in0=ps[:, bank, :], in1=G.t[:, nb * 512:(nb + 1) * 512],
                            op=ALU.mult), reads=[pb[bank], G.b], writes=[tmp.b])
                        p.op('pool', lambda e, nb=nb: e.tensor_tensor(
                            out=tmp.t[:, nb * 512:(nb + 1) * 512], in0=tmp.t[:, nb * 512:(nb + 1) * 512],
                            in1=xe.t[:, nb * 512:(nb + 1) * 512], op=ALU.add), reads=[tmp.b, xe.b], writes=[tmp.b])
                        yield
                    dstx = env.d(dn)[ti * 128:(ti + 1) * 128, :]
                    p.dma('sp', lambda e, dstx=dstx: e.dma_start(out=dstx, in_=tmp.t[:]), reads=[tmp.b], writes=[env.b(dn, ti)])

            interleave([gen_A(0)])
            for si in range(len(sts)):
                gens = [gen_BC(si)]
                if si + 1 < len(sts):
                    gens.append(gen_A(si + 1))
                interleave(gens)
        p.barrier()
    return ph


class ScanState:
    def __init__(self, sbf, tag):
        self.S32 = sbf('S32' + tag, [128, 8, 128], F32)
        self.Sbf = [sbf(f'Sbf{i}' + tag, [128, 8, 128], BF16) for i in range(2)]
        self.b32 = [Buf() for _ in range(8)]
        self.bbf = [[Buf() for _ in range(8)] for _ in range(2)]
        self.n = 0


def scan_chunk(p, nc, ps, pb, banks, qt, qe, kn, kh, vt, ebend, ST, msk, c, fwd, o_out):
    bA, bO0, bO1, bU0, bU1 = banks
    par = c % 2
    pbs = 64 * par
    mcol = pbs if fwd else 128 + pbs
    cur = ST.n % 2
    nxt = 1 - cur
    if o_out is not None:
        aT = o_out['aT']
        for hd in range(8):
            (q_ap, q_b), (k_ap, k_b) = qt(hd), kn(hd)
            p.op('pe', lambda e, hd=hd, q_ap=q_ap, k_ap=k_ap: e.matmul(
                ps[pbs:pbs + 64, bA, hd * 64:(hd + 1) * 64], lhsT=k_ap, rhs=q_ap, start=True, stop=True),
                reads=[q_b, k_b], writes=[pb[bA]], inc=(hd == 7))
    for hd in range(8):
        bank = bU0 if hd < 4 else bU1
        kh_ap, kh_b = kh(hd)
        v_ap, v_b = vt(hd)
        p.op('pe', lambda e, hd=hd, bank=bank, kh_ap=kh_ap, v_ap=v_ap: e.matmul(
            ps[:, bank, (hd % 4) * 128:(hd % 4 + 1) * 128], lhsT=kh_ap, rhs=v_ap, start=True, stop=True),
            reads=[kh_b, v_b], writes=[pb[bank]], inc=(hd % 4 == 3))
    if o_out is not None:
        p.op('dve', lambda e: e.copy_predicated(
            out=aT.t[pbs:pbs + 64, :].rearrange("p (a b) -> p a b", b=64),
            mask=msk.t[pbs:pbs + 64, mcol:mcol + 64].bitcast(mybir.dt.uint32).unsqueeze(1).to_broadcast([64, 8, 64]),
            data=ps[pbs:pbs + 64, bA, :].rearrange("p (a b) -> p a b", b=64)),
            reads=[pb[bA], msk.b], writes=[aT.b])
        for hd in range(8):
            bank = bO0 if hd < 4 else bO1
            (q_ap, q_b) = qe(hd)
            v_ap, v_b = vt(hd)
            oap = ps[pbs:pbs + 64, bank, (hd % 4) * 128:(hd % 4 + 1) * 128]
            p.op('pe', lambda e, hd=hd, oap=oap, v_ap=v_ap: e.matmul(
                oap, lhsT=aT.t[pbs:pbs + 64, hd * 64:(hd + 1) * 64], rhs=v_ap, start=True, stop=False),
                reads=[aT.b, v_b], writes=[pb[bank]], inc=False)
            p.op('pe', lambda e, hd=hd, oap=oap, q_ap=q_ap: e.matmul(
                oap, lhsT=q_ap, rhs=ST.Sbf[cur][:, hd, :], start=False, stop=True),
                reads=[q_b, ST.bbf[cur][hd]], writes=[pb[bank]], inc=(hd % 4 == 3))
        ot = o_out['o']
        add = o_out.get('add')
        for half, bank in ((0, bO0), (1, bO1)):
            if add is None:
                p.op('act', lambda e, half=half, bank=bank: e.activation(
                    out=ot.t[pbs:pbs + 64, half * 512:(half + 1) * 512], in_=ps[pbs:pbs + 64, bank, :], func=AF.Identity),
                    reads=[pb[bank]], writes=[ot.b])
            else:
                p.op('dve', lambda e, half=half, bank=bank: e.tensor_tensor(
                    out=ot.t[pbs:pbs + 64, half * 512:(half + 1) * 512], in0=ps[pbs:pbs + 64, bank, :],
                    in1=add.t[pbs:pbs + 64, half * 512:(half + 1) * 512], op=ALU.add),
                    reads=[pb[bank], add.b], writes=[ot.b])
    for hd in range(8):
        bank = bU0 if hd < 4 else bU1
        eb_ap, eb_b = ebend(hd)
        p.op('dve', lambda e, hd=hd, bank=bank, eb_ap=eb_ap: e.scalar_tensor_tensor(
            out=ST.S32[:, hd, :], in0=ST.S32[:, hd, :], scalar=eb_ap, in1=ps[:, bank, (hd % 4) * 128:(hd % 4 + 1) * 128],
            op0=ALU.mult, op1=ALU.add), reads=[ST.b32[hd], eb_b, pb[bank]], writes=[ST.b32[hd]])
        p.op('act', lambda e, hd=hd: e.activation(out=ST.Sbf[nxt][:, hd, :], in_=ST.S32[:, hd, :], func=AF.Identity),
             reads=[ST.b32[hd]], writes=[ST.bbf[nxt][hd]])
    ST.n += 1


def ph_hg1(p, nc, env, ps, pb):
    with ExitStack() as st:
        sb = lambda n, s, d: st.enter_context(nc.sbuf_tensor(_uniq(n), s, d))
        T = 512
        w = sb('hw', [128, 8, 5 * D], BF16)
        w_b = [Buf() for _ in range(10)]
        cast_load_w(p, nc, w, env.d('hg_w_in'), D, 5 * D, w_b, split=10)
        identb = SB(sb('identb', [128, 128], BF16))
        p.dma('pool', lambda e: e.dma_start(out=identb.t[:], in_=env.d('identf')[:, :]), writes=[identb.b])
        msk = SB(sb('msk', [128, 256], F32))
        p.dma('sp', lambda e: e.dma_start(out=msk.t[:], in_=env.d('trimask')[:, :]), writes=[msk.b])
        rst = SB(sb('rst', [128, 512], BF16))
        p.dma('pool', lambda e: e.dma_start(out=rst.t[:], in_=env.d('rstmask')[:, :]), writes=[rst.b])
        low = SB(sb('low', [128, 32], F32))
        p.dma('sp', lambda e: e.dma_start(out=low.t[:], in_=env.d('hg_low')[:, :]), writes=[low.b])
        lbt = SB(sb('lbt', [128, 48], F32))
        p.op('dve', lambda e: e.tensor_tensor(out=lbt.t[:, 0:16], in0=low.t[:, 0:16], in1=low.t[:, 16:32], op=ALU.subtract),
             reads=[low.b], writes=[lbt.b])
        p.op('act', lambda e: e.activation(out=lbt.t[:, 16:32], in_=lbt.t[:, 0:16], func=AF.Sigmoid, scale=-1.0),
             reads=[lbt.b], writes=[lbt.b])
        p.op('act', lambda e: e.activation(out=lbt.t[:, 32:48], in_=lbt.t[:, 0:16], func=AF.Sigmoid),
             reads=[lbt.b], writes=[lbt.b])
        jk = SB(sb('jk', [128, 512], F32))
        nm = NormMod(p, nc, st, env, ps, pb, 1, 0, [0, 1], (0, 1), 'h1', nxn=1, junk=(jk.t[:].bitcast(BF16), jk.b))
        xr = [SB(sb(f'xr{i}', [128, D], F32)) for i in range(2)]
        hT = SB(sb('hT', [128, 8, T], BF16))
        tq = [SB(sb(f'tq{i}', [128, T], F32)) for i in range(2)]
        ta = [SB(sb(f'ta{i}', [128, T], F32)) for i in range(2)]
        tb = [SB(sb(f'tb{i}', [128, T], F32)) for i in range(2)]
        tc = [SB(sb(f'tc{i}', [128, T], F32)) for i in range(2)]
        td = [SB(sb(f'td{i}', [128, T], F32)) for i in range(2)]
        te = [SB(sb(f'te{i}', [128, T], F32)) for i in range(2)]
        tkh = [SB(sb(f'tkh{i}', [128, T], BF16)) for i in range(2)]
        QE1 = SB(sb('QE1', [128, 8, T], BF16))
        QT1 = SB(sb('QT1', [128, 8, T], BF16))
        KN1 = SB(sb('KN1', [128, 8, T], BF16))
        KH1 = SB(sb('KH1', [128, 4, D], BF16))
        EB1 = SB(sb('EB1', [128, 8, 8], F32))
        s2q = [SB(sb(f's2q{i}', [128, T], BF16)) for i in range(2)]
        s2k = [SB(sb(f's2k{i}', [128, T], BF16)) for i in range(2)]
        s2t = [SB(sb(f's2t{i}', [128, T], BF16)) for i in range(2)]
        KH2 = SB(sb('KH2', [128, 4, D], BF16))
        EB2 = SB(sb('EB2', [128, 8, 8], F32))
        Vt = SB(sb('Vt', [128, 4, D], BF16))
        SGt = SB(sb('SGt', [128, 4, D], BF16))
        ST = ScanState(sb, 'h1')
        aT = SB(sb('aT', [128, 512], BF16))
        o1t = [SB(sb(f'o1t{i}', [128, D], F32)) for i in range(1)]
        p.op('pool', lambda e: e.memset(aT.t[:], 0.0), writes=[aT.b])
        p.op('pool', lambda e: e.memset(ST.S32[:], 0.0), writes=ST.b32)
        p.op('pool', lambda e: e.memset(ST.Sbf[0][:], 0.0), writes=ST.bbf[0])
        psb = lambda bank: ps[:, bank, :].bitcast(BF16)

        sts = [[('xb_ctx', 0, 1), ('xb_ctx', 1, 1)]] + [[('xb', s_ * 4 + j, 0) for j in range(4)] for s_ in range(NTQ // 4)]
        xcnt = [0]
        rot = [0]

        def nbank():
            rot[0] = (rot[0] + 1) % 3
            return 2 + rot[0]

        def do_supertile(si, tiles):
            nt = len(tiles)
            Tn = nt * 128
            nch = Tn // 64
            is_ctx = (si == 0)
            tok0 = None if is_ctx else tiles[0][1] * 128
            for j, (sn, ti, m) in enumerate(tiles):
                x = xr[xcnt[0] % 2]
                xcnt[0] += 1
                srcx = env.d(sn)[ti * 128:(ti + 1) * 128, :]
                p.dma('sp', lambda e, x=x, srcx=srcx: e.dma_start(out=x.t[:], in_=srcx), reads=[env.b(sn, ti)], writes=[x.b])
                nm.run(x, m, hT, j * 128)
            for j in range(nt):
                for nb in range(4):
                    bank = nbank()
                    c0 = D + nb * 512
                    for kc in range(8):
                        p.op('pe', lambda e, bank=bank, kc=kc, c0=c0, j=j: e.matmul(
                            ps[:, bank, :], lhsT=hT.t[:, kc, j * 128:(j + 1) * 128], rhs=w[:, kc, c0:c0 + 512],
                            start=(kc == 0), stop=(kc == 7)), reads=[hT.b, hT.b2, w_b[c0 // 512]], writes=[pb[bank]], inc=(kc == 7))
                    if nb < 2:
                        p.op('dve', lambda e, bank=bank, j=j, nb=nb: e.tensor_copy(
                            out=Vt.t[:, j, nb * 512:(nb + 1) * 512], in_=ps[:, bank, :]), reads=[pb[bank]], writes=[Vt.b])
                    elif not is_ctx:
                        p.op('act', lambda e, bank=bank, j=j, nb=nb: e.activation(
                            out=SGt.t[:, j, (nb - 2) * 512:(nb - 1) * 512], in_=ps[:, bank, :], func=AF.Silu),
                            reads=[pb[bank]], writes=[SGt.b])
            if not is_ctx:
                for (nm_, tl) in (('vtok', Vt), ('sgtok', SGt)):
                    dstd = env.d(nm_)[tok0:tok0 + Tn, :].rearrange("(j p) n -> p j n", p=128)
                    p.dma('sp', lambda e, dstd=dstd, tl=tl: e.dma_start(out=dstd, in_=tl.t[:, 0:nt, :]), reads=[tl.b],
                          writes=[env.b(nm_, si)])
            def chain(hd, d, q_):
                a_, b_, c_, d_, e_, kh_ = ta[d], tb[d], tc[d], td[d], te[d], tkh[d]
                bank = 3 if d == 0 else 4
                lcol = d * 8 + hd
                lb_ap = lbt.t[:, 16 + lcol:17 + lcol]
                oml_ap = lbt.t[:, 32 + lcol:33 + lcol]
                p.op('act', lambda e: e.activation(out=a_.t[:, 0:Tn], in_=ps[:, bank, 0:Tn], func=AF.Sigmoid),
                     reads=[pb[bank]], writes=[a_.b])
                p.op('act', lambda e: e.activation(out=b_.t[:, 0:Tn], in_=ps[:, bank, 0:Tn], func=AF.Sigmoid, scale=-1.0),
                     reads=[pb[bank]], writes=[b_.b])
                p.op('act', lambda e: e.activation(out=a_.t[:, 0:Tn], in_=a_.t[:, 0:Tn], func=AF.Ln, scale=oml_ap, bias=lb_ap),
                     reads=[a_.b, lbt.b], writes=[a_.b])
                yield
                p.op('dve', lambda e: e.tensor_tensor_scan(
                    out=c_.t[:, 0:Tn], data0=rst.t[:, 0:Tn], data1=a_.t[:, 0:Tn], initial=0.0, op0=ALU.mult, op1=ALU.add),
                    reads=[rst.b, a_.b], writes=[c_.b])
                c3 = c_.t[:, 0:Tn].rearrange("p (c t) -> p c t", t=64)
                bend = c3[:, :, 63:64]
                p.op('dve', lambda e: e.tensor_tensor(
                    out=d_.t[:, 0:Tn].rearrange("p (c t) -> p c t", t=64), in0=bend.to_broadcast([128, nch, 64]), in1=c3,
                    op=ALU.subtract), reads=[c_.b], writes=[d_.b])
                ebt = EB1 if d == 0 else EB2
                yield
                p.op('act', lambda e: e.activation(out=ebt.t[:, hd, 0:nch].unsqueeze(2), in_=bend, func=AF.Exp),
                     reads=[c_.b], writes=[ebt.b])
                if d == 0:
                    p.op('act', lambda e: e.activation(out=e_.t[:, 0:Tn], in_=c_.t[:, 0:Tn], func=AF.Exp), reads=[c_.b], writes=[e_.b])
                    yield
                    p.op('dve', lambda e: e.tensor_tensor(out=QE1.t[:, hd, 0:Tn], in0=q_.t[:, 0:Tn], in1=e_.t[:, 0:Tn], op=ALU.mult),
                         reads=[q_.b, e_.b], writes=[QE1.b])
                    p.op('dve', lambda e: e.tensor_tensor(
                        out=e_.t[:, 0:Tn].rearrange("p (c t) -> p c t", t=64), in0=c3,
                        in1=c3[:, :, 31:32].to_broadcast([128, nch, 64]), op=ALU.subtract), reads=[c_.b], writes=[e_.b])
                    yield
                    p.op('act', lambda e: e.activation(out=a_.t[:, 0:Tn], in_=e_.t[:, 0:Tn], func=AF.Exp), reads=[e_.b], writes=[a_.b])
                    p.op('act', lambda e: e.activation(out=e_.t[:, 0:Tn], in_=e_.t[:, 0:Tn], func=AF.Exp, scale=-1.0),
                         reads=[e_.b], writes=[e_.b])
                    p.op('act', lambda e: e.activation(out=d_.t[:, 0:Tn], in_=d_.t[:, 0:Tn], func=AF.Exp), reads=[d_.b], writes=[d_.b])
                    yield
                    p.op('dve', lambda e: e.tensor_tensor(out=QT1.t[:, hd, 0:Tn], in0=q_.t[:, 0:Tn], in1=a_.t[:, 0:Tn], op=ALU.mult),
                         reads=[q_.b, a_.b], writes=[QT1.b])
                    p.op('dve', lambda e: e.scalar_tensor_tensor(
                        out=KN1.t[:, hd, 0:Tn], in0=b_.t[:, 0:Tn], scalar=oml_ap, in1=e_.t[:, 0:Tn], op0=ALU.mult, op1=ALU.mult),
                        reads=[b_.b, e_.b, lbt.b], writes=[KN1.b])
                    p.op('dve', lambda e: e.scalar_tensor_tensor(
                        out=kh_.t[:, 0:Tn], in0=b_.t[:, 0:Tn], scalar=oml_ap, in1=d_.t[:, 0:Tn], op0=ALU.mult, op1=ALU.mult),
                        reads=[b_.b, d_.b, lbt.b], writes=[kh_.b])
                    KH = KH1
                else:
                    p.op('dve', lambda e: e.tensor_tensor(out=d_.t[:, 0:Tn], in0=d_.t[:, 0:Tn], in1=a_.t[:, 0:Tn], op=ALU.add),
                         reads=[d_.b, a_.b], writes=[d_.b])
                    sq_, sk_, st_ = s2q[hd % 2], s2k[hd % 2], s2t[hd % 2]
                    d3 = d_.t[:, 0:Tn].rearrange("p (c t) -> p c t", t=64)
                    p.op('dve', lambda e: e.tensor_tensor(
                        out=e_.t[:, 0:Tn].rearrange("p (c t) -> p c t", t=64), in0=d3,
                        in1=d3[:, :, 32:33].to_broadcast([128, nch, 64]), op=ALU.subtract), reads=[d_.b], writes=[e_.b])
                    p.op('dve', lambda e: e.tensor_tensor(out=c_.t[:, 0:Tn], in0=c_.t[:, 0:Tn], in1=a_.t[:, 0:Tn], op=ALU.subtract),
                         reads=[c_.b, a_.b], writes=[c_.b])
                    yield
                    p.op('act', lambda e: e.activation(out=d_.t[:, 0:Tn], in_=d_.t[:, 0:Tn], func=AF.Exp), reads=[d_.b], writes=[d_.b])
                    p.op('act', lambda e: e.activation(out=a_.t[:, 0:Tn], in_=e_.t[:, 0:Tn], func=AF.Exp), reads=[e_.b], writes=[a_.b])
                    p.op('act', lambda e: e.activation(out=e_.t[:, 0:Tn], in_=e_.t[:, 0:Tn], func=AF.Exp, scale=-1.0),
                         reads=[e_.b], writes=[e_.b])
                    p.op('act', lambda e: e.activation(out=c_.t[:, 0:Tn], in_=c_.t[:, 0:Tn], func=AF.Exp), reads=[c_.b], writes=[c_.b])
                    yield
                    p.op('dve', lambda e: e.tensor_tensor(out=sq_.t[:, 0:Tn], in0=q_.t[:, 0:Tn], in1=d_.t[:, 0:Tn], op=ALU.mult),
                         reads=[q_.b, d_.b], writes=[sq_.b])
                    p.op('dve', lambda e: e.tensor_tensor(out=st_.t[:, 0:Tn], in0=q_.t[:, 0:Tn], in1=a_.t[:, 0:Tn], op=ALU.mult),
                         reads=[q_.b, a_.b], writes=[st_.b])
                    p.op('dve', lambda e: e.scalar_tensor_tensor(
                        out=sk_.t[:, 0:Tn], in0=b_.t[:, 0:Tn], scalar=oml_ap, in1=e_.t[:, 0:Tn], op0=ALU.mult, op1=ALU.mult),
                        reads=[b_.b, e_.b, lbt.b], writes=[sk_.b])
                    p.op('dve', lambda e: e.scalar_tensor_tensor(
                        out=kh_.t[:, 0:Tn], in0=b_.t[:, 0:Tn], scalar=oml_ap, in1=c_.t[:, 0:Tn], op0=ALU.mult, op1=ALU.mult),
                        reads=[b_.b, c_.b, lbt.b], writes=[kh_.b])
                    for (nm_, tl) in (('d2qe', sq_), ('d2kn', sk_), ('d2qt', st_)):
                        dstd = env.d(nm_)[hd, :, tok0:tok0 + Tn]
                        p.dma('sp', lambda e, dstd=dstd, tl=tl: e.dma_start(out=dstd, in_=tl.t[:, 0:Tn]), reads=[tl.b],
                              writes=[env.b(nm_, si)])
                    KH = KH2
                yield
                tbk = 5 + d
                for j in range(nt):
                    p.op('pe', lambda e, j=j: e.transpose(
                        psb(tbk)[:, j * 128:(j + 1) * 128], kh_.t[:, j * 128:(j + 1) * 128], identb.t[:]),
                        reads=[kh_.b, identb.b], writes=[pb[tbk]], inc=(j == nt - 1))
                p.op('dve', lambda e: e.tensor_copy(
                    out=KH.t[:, 0:nt, hd * 128:(hd + 1) * 128],
                    in_=psb(tbk)[:, 0:nt * 128].rearrange("p (j k) -> p j k", k=128)),
                    reads=[pb[tbk]], writes=[KH.b])

            for hd in range(8):
                bq, bf1, bf2 = 2, 3, 4
                for (bank, c0) in ((bq, hd * 128), (bf1, 3 * D + hd * 128), (bf2, 4 * D + hd * 128)):
                    if is_ctx and bank == bf2:
                        continue
                    for kc in range(8):
                        p.op('pe', lambda e, bank=bank, kc=kc, c0=c0: e.matmul(
                            ps[:, bank, 0:Tn], lhsT=w[:, kc, c0:c0 + 128], rhs=hT.t[:, kc, 0:Tn],
                            start=(kc == 0), stop=(kc == 7)), reads=[hT.b, hT.b2, w_b[c0 // 512]], writes=[pb[bank]], inc=(kc == 7))
                q_ = tq[hd % 2]
                p.op('act', lambda e, q_=q_: e.activation(out=q_.t[:, 0:Tn], in_=ps[:, bq, 0:Tn], func=AF.Silu),
                     reads=[pb[bq]], writes=[q_.b])
                gens = [chain(hd, 0, q_)]
                if not is_ctx:
                    gens.append(chain(hd, 1, q_))
                interleave(gens)
            if not is_ctx:
                dstd = env.d('d2kh')[tok0:tok0 + Tn, :].rearrange("(j p) n -> p j n", p=128)
                p.dma('sp', lambda e, dstd=dstd: e.dma_start(out=dstd, in_=KH2.t[:, 0:nt, :]), reads=[KH2.b],
                      writes=[env.b('d2kh', si)])
                ch0 = tok0 // 64
                dstd = env.d('d2eb')[:, :, ch0:ch0 + nch]
                p.dma('sp', lambda e, dstd=dstd: e.dma_start(out=dstd, in_=EB2.t[:, :, 0:nch]), reads=[EB2.b],
                      writes=[env.b('d2eb', si)])
            for c in range(nch):
                j = c // 2
                pbs = 64 * (c % 2)
                qt = lambda hd, c=c: (QT1.t[:, hd, c * 64:(c + 1) * 64], QT1.b)
                qe = lambda hd, c=c: (QE1.t[:, hd, c * 64:(c + 1) * 64], QE1.b)
                kn = lambda hd, c=c: (KN1.t[:, hd, c * 64:(c + 1) * 64], KN1.b)
                kh = lambda hd, j=j, pbs=pbs: (KH1.t[pbs:pbs + 64, j, hd * 128:(hd + 1) * 128], KH1.b)
                vt = lambda hd, j=j, pbs=pbs: (Vt.t[pbs:pbs + 64, j, hd * 128:(hd + 1) * 128], Vt.b)
                eb = lambda hd, c=c: (EB1.t[:, hd, c:c + 1], EB1.b)
                oo = None
                if not is_ctx:
                    ot = o1t[0]
                    oo = {'aT': aT, 'o': ot}
                scan_chunk(p, nc, ps, pb, (7, 5, 6, 2, 3), qt, qe, kn, kh, vt, eb, ST, msk, c, True, oo)
                if oo is not None and c % 2 == 1:
                    ti = tiles[j][1]
                    dstd = env.d('o1')[ti * 128:(ti + 1) * 128, :]
                    p.dma('sp', lambda e, dstd=dstd, ot=ot: e.dma_start(out=dstd, in_=ot.t[:]), reads=[ot.b],
                          writes=[env.b('o1', ti)])
        for si, tiles in enumerate(sts):
            do_supertile(si, tiles)
        p.dma('sp', lambda e: e.dma_start(out=env.d('s_send')[:, :], in_=ST.S32[:].rearrange("p a b -> p (a b)")),
              reads=ST.b32, writes=[env.b('s_send')])
    p.barrier()


def ph_hg2(p, nc, env, ps, pb):
    with ExitStack() as st:
        sb = lambda n, s, d: st.enter_context(nc.sbuf_tensor(_uniq(n), s, d))
        wo = sb('hwo', [128, 8, D], BF16)
        wo_b = [Buf() for _ in range(2)]
        cast_load_w(p, nc, wo, env.d('hg_w_o'), D, D, wo_b, split=2)
        identb = SB(sb('identb', [128, 128], BF16))
        p.dma('pool', lambda e: e.dma_start(out=identb.t[:], in_=env.d('identf')[:, :]), writes=[identb.b])
        msk = SB(sb('msk', [128, 256], F32))
        p.dma('sp', lambda e: e.dma_start(out=msk.t[:], in_=env.d('trimask')[:, :]), writes=[msk.b])
        ng = SB(sb('ng', [128, D], F32))
        p.dma('sp', lambda e: e.dma_start(out=ng.t[:], in_=env.d('hg_ng')[0:1, :].partition_broadcast(128)), writes=[ng.b])
        G = SB(sb('G', [128, D], F32))
        load_modB(p, nc, env, G, 2 * D, 1, 0)
        ST = ScanState(sb, 'h2')
        p.dma('sp', lambda e: e.dma_start(out=ST.S32[:].rearrange("p a b -> p (a b)"), in_=env.d('s_recv')[:, :]),
              reads=[env.b('s_recv')], writes=ST.b32)
        p.op('pool', lambda e: e.tensor_copy(out=ST.Sbf[0][:], in_=ST.S32[:]), reads=ST.b32, writes=ST.bbf[0])
        R = 2
        QE = [SB(sb(f'QE{i}', [128, 8, 128], BF16)) for i in range(R)]
        KN = [SB(sb(f'KN{i}', [128, 8, 128], BF16)) for i in range(R)]
        QTT = [SB(sb(f'QTT{i}', [128, 8, 128], BF16)) for i in range(R)]
        KH = [SB(sb(f'KH{i}', [128, D], BF16)) for i in range(R)]
        VT = [SB(sb(f'VT{i}', [128, D], BF16)) for i in range(R)]
        SG = [SB(sb(f'SG{i}', [128, D], BF16)) for i in range(R)]
        EBt = [SB(sb(f'EBt{i}', [128, 8, 2], F32)) for i in range(R)]
        O1 = [SB(sb(f'O1{i}', [128, D], F32)) for i in range(R)]
        XB = [SB(sb(f'XB{i}', [128, D], F32)) for i in range(3)]
        o2 = SB(sb('o2', [128, D], F32))
        sq = SB(sb('sq', [128, D], F32))
        hs = SB(sb('hs', [128, 16], F32))
        yb = SB(sb('yb', [128, D], BF16))
        oT = SB(sb('oT', [128, 8, 128], BF16))
        aT = SB(sb('aT', [128, 512], BF16))
        tmp = SB(sb('tmp', [128, D], F32))
        p.op('pool', lambda e: e.memset(aT.t[:], 0.0), writes=[aT.b])
        psb2 = ps[:, 4, :].bitcast(BF16)
        order = list(range(NTQ - 1, -1, -1))

        def loads(n):
            t = order[n]
            r = n % R
            sl = slice(t * 128, (t + 1) * 128)
            si = t // 4
            p.dma('sp', lambda e: e.dma_start(out=QE[r].t[:], in_=env.d('d2qe')[:, :, sl].rearrange("h k n -> k h n")),
                  reads=[env.b('d2qe', si)], writes=[QE[r].b])
            p.dma('sp', lambda e: e.dma_start(out=KN[r].t[:], in_=env.d('d2kn')[:, :, sl].rearrange("h k n -> k h n")),
                  reads=[env.b('d2kn', si)], writes=[KN[r].b])
            p.dma('sp', lambda e: e.dma_start(out=QTT[r].t[:], in_=env.d('d2qt')[:, :, sl].rearrange("h k n -> k h n")),
                  reads=[env.b('d2qt', si)], writes=[QTT[r].b])
            p.dma('sp', lambda e: e.dma_start(out=KH[r].t[:], in_=env.d('d2kh')[sl, :]), reads=[env.b('d2kh', si)], writes=[KH[r].b])
            p.dma('sp', lambda e: e.dma_start(out=VT[r].t[:], in_=env.d('vtok')[sl, :]), reads=[env.b('vtok', si)], writes=[VT[r].b])
            p.dma('sp', lambda e: e.dma_start(out=SG[r].t[:], in_=env.d('sgtok')[sl, :]), reads=[env.b('sgtok', si)], writes=[SG[r].b])
            p.dma('sp', lambda e: e.dma_start(out=EBt[r].t[:], in_=env.d('d2eb')[:, :, 2 * t:2 * t + 2]),
                  reads=[env.b('d2eb', si)], writes=[EBt[r].b])
            p.dma('sp', lambda e: e.dma_start(out=O1[r].t[:], in_=env.d('o1')[sl, :]), reads=[env.b('o1', t)], writes=[O1[r].b])
            xb_ = XB[n % 3]
            p.dma('sp', lambda e: e.dma_start(out=xb_.t[:], in_=env.d('xb')[sl, :]), reads=[env.b('xb', t)], writes=[xb_.b])

        loads(0)

        NG = [SB(sb(f'NG{i}', [128, D], F32)) for i in range(R)]
        pend = []

        def pe_readout(t, xb_):
            for j in range(8):
                p.op('pe', lambda e, j=j: e.transpose(psb2[:, j * 128:(j + 1) * 128], yb.t[:, j * 128:(j + 1) * 128], identb.t[:]),
                     reads=[yb.b, identb.b], writes=[pb[4]], inc=(j == 7))
            p.op('act', lambda e: e.activation(out=oT.t[:].rearrange("p a b -> p (a b)"), in_=psb2[:, 0:1024], func=AF.Identity),
                 reads=[pb[4]], writes=[oT.b])
            for nb in range(2):
                bank = nb
                for kc in range(8):
                    p.op('pe', lambda e, bank=bank, kc=kc, nb=nb: e.matmul(
                        ps[:, bank, :], lhsT=oT.t[:, kc, :], rhs=wo[:, kc, nb * 512:(nb + 1) * 512], start=(kc == 0), stop=(kc == 7)),
                        reads=[oT.b, wo_b[nb]], writes=[pb[bank]], inc=(kc == 7))
                p.op('dve', lambda e, bank=bank, nb=nb: e.tensor_tensor(
                    out=tmp.t[:, nb * 512:(nb + 1) * 512], in0=ps[:, bank, :], in1=G.t[:, nb * 512:(nb + 1) * 512], op=ALU.mult),
                    reads=[pb[bank], G.b], writes=[tmp.b])
                p.op('pool', lambda e, nb=nb: e.tensor_tensor(
                    out=tmp.t[:, nb * 512:(nb + 1) * 512], in0=tmp.t[:, nb * 512:(nb + 1) * 512],
                    in1=xb_.t[:, nb * 512:(nb + 1) * 512], op=ALU.add), reads=[tmp.b, xb_.b], writes=[tmp.b])
            dst = env.d('xc')[t * 128:(t + 1) * 128, :]
            p.dma('sp', lambda e, dst=dst: e.dma_start(out=dst, in_=tmp.t[:]), reads=[tmp.b], writes=[env.b('xc', t)])

        def do_tile(n, t):
            r = n % R
            if n + 1 < len(order):
                loads(n + 1)
            p.op('pool', lambda e: e.tensor_tensor(out=NG[r].t[:], in0=SG[r].t[:], in1=ng.t[:], op=ALU.mult),
                 reads=[SG[r].b, ng.b], writes=[NG[r].b])
            for par in (1, 0):
                pbs = 64 * par
                qt = lambda hd, par=par: (QTT[r].t[:, hd, par * 64:(par + 1) * 64], QTT[r].b)
                qe = lambda hd, par=par: (QE[r].t[:, hd, par * 64:(par + 1) * 64], QE[r].b)
                kn = lambda hd, par=par: (KN[r].t[:, hd, par * 64:(par + 1) * 64], KN[r].b)
                kh = lambda hd, pbs=pbs: (KH[r].t[pbs:pbs + 64, hd * 128:(hd + 1) * 128], KH[r].b)
                vt = lambda hd, pbs=pbs: (VT[r].t[pbs:pbs + 64, hd * 128:(hd + 1) * 128], VT[r].b)
                eb = lambda hd, par=par: (EBt[r].t[:, hd, par:par + 1], EBt[r].b)
                scan_chunk(p, nc, ps, pb, (7, 5, 6, 2, 3), qt, qe, kn, kh, vt, eb, ST, msk, par, False,
                           {'aT': aT, 'o': o2, 'add': O1[r]})
                if par == 1 and pend:
                    pe_readout(*pend.pop())
            for hd in range(8):
                p.op('act', lambda e, hd=hd: e.activation(out=sq.t[:, hd * 128:(hd + 1) * 128], in_=o2.t[:, hd * 128:(hd + 1) * 128],
                                                          func=AF.Square, accum_out=hs.t[:, hd:hd + 1]),
                     reads=[o2.b], writes=[sq.b, hs.b])
            p.op('act', lambda e: e.activation(out=hs.t[:, 0:8], in_=hs.t[:, 0:8], func=AF.Ln, scale=1.0 / 128, bias=EPS),
                 reads=[hs.b], writes=[hs.b])
            p.op('act', lambda e: e.activation(out=hs.t[:, 8:16], in_=hs.t[:, 0:8], func=AF.Exp, scale=-0.5), reads=[hs.b], writes=[hs.b])
            p.op('dve', lambda e: e.tensor_tensor(
                out=sq.t[:].rearrange("p (a b) -> p a b", b=128), in0=o2.t[:].rearrange("p (a b) -> p a b", b=128),
                in1=hs.t[:, 8:16].unsqueeze(2).to_broadcast([128, 8, 128]), op=ALU.mult), reads=[o2.b, hs.b], writes=[sq.b])
            p.op('dve', lambda e: e.tensor_tensor(out=yb.t[:], in0=sq.t[:], in1=NG[r].t[:], op=ALU.mult),
                 reads=[sq.b, NG[r].b], writes=[yb.b])
            pend.append((t, XB[n % 3]))

        for n, t in enumerate(order):
            do_tile(n, t)
        while pend:
            pe_readout(*pend.pop())
    p.barrier()


def ph_xchg(p, nc, env, ps, pb):
    with ExitStack() as st:
        sb = lambda n, s, d: st.enter_context(nc.sbuf_tensor(_uniq(n), s, d))
        snd = env.d('s_send')
        gat = env.d('s_gath')
        p.collective(lambda e: e.collective_compute("AllGather", ALU.bypass, replica_groups=[[0, 1], [2, 3], [4, 5], [6, 7]],
                                                    ins=[snd.opt()], outs=[gat.opt()]),
                     reads=[env.b('s_send')], writes=[env.b('s_gath')])
        g = SB(sb('xg', [128, 2, D], F32))
        sel = SB(sb('xsel', [128, 2], F32))
        r = SB(sb('xr', [128, D], F32))
        p.dma('sp', lambda e: e.dma_start(out=g.t[:], in_=gat.rearrange("(r p) n -> p r n", p=128)), reads=[env.b('s_gath')],
              writes=[g.b])
        p.dma('sp', lambda e: e.dma_start(out=sel.t[:], in_=env.d('sel')[:, :]), writes=[sel.b])
        p.op('dve', lambda e: e.tensor_scalar(out=r.t[:], in0=g.t[:, 0, :], scalar1=sel.t[:, 0:1], scalar2=None, op0=ALU.mult),
             reads=[g.b, sel.b], writes=[r.b])
        p.op('dve', lambda e: e.scalar_tensor_tensor(out=r.t[:], in0=g.t[:, 1, :], scalar=sel.t[:, 1:2], in1=r.t[:], op0=ALU.mult,
                                                     op1=ALU.add), reads=[g.b, sel.b, r.b], writes=[r.b])
        p.dma('sp', lambda e: e.dma_start(out=env.d('s_recv')[:, :], in_=r.t[:]), reads=[r.b], writes=[env.b('s_recv')])
    p.barrier()


def _coords(tile, half):
    l = tile * 128 + np.arange(128)
    g = l if half == 0 else 8191 - l
    return g // 64, g % 64


def rpb_tables(rpb, half):
    flat = np.concatenate([rpb.reshape(16, 15 * 31), np.full((16, 1), PADV, np.float32)], axis=1)

    def table(t, kts):
        qr, qc = _coords(t, half)
        out = np.empty((128, 16, len(kts), 128), np.float32)
        rs = np.clip(qr - 4, 0, 120)
        cs = np.clip(qc - 8, 0, 48)
        for a, kt in enumerate(kts):
            kr, kc = _coords(kt, half)
            valid = ((kr[:, None] >= rs[None, :]) & (kr[:, None] <= rs[None, :] + 7) &
                     (kc[:, None] >= cs[None, :]) & (kc[:, None] <= cs[None, :] + 15))
            idx = (kr[:, None] - qr[None, :] + 7) * 31 + (kc[:, None] - qc[None, :] + 15)
            idx = np.where(valid, idx, 465)
            out[:, :, a, :] = np.transpose(flat[:, idx], (1, 0, 2))
        return out.reshape(128, -1)

    ebg = table(10, [8, 9, 10, 11, 12])
    ebs = np.stack([table(0, [0, 1, 2, 3]), table(1, [0, 1, 2, 3])], axis=0)
    return ebg, ebs


def host_consts():
    tri = np.zeros((128, 256), np.float32)
    for blk in range(2):
        s = np.arange(64)[:, None]
        t = np.arange(64)[None, :]
        tri[blk * 64:(blk + 1) * 64, blk * 64:(blk + 1) * 64] = (s <= t)
        tri[blk * 64:(blk + 1) * 64, 128 + blk * 64:128 + (blk + 1) * 64] = (s >= t)
    rst = np.ones((128, 512), np.float32)
    rst[:, 0::64] = 0.0
    return tri, rst


def prep_inputs(inp):
    f = lambda a: np.ascontiguousarray(a, dtype=np.float32)
    shared = {
        'ada_w': f(inp['ada_w']),
        'ada_b': f(inp['ada_b'].reshape(1, -1)),
        'n1g': f(inp['norm1_g'].reshape(2, 8, 128).transpose(2, 0, 1).reshape(128, 16)),
        'n2g': f(inp['norm2_g'].reshape(2, 8, 128).transpose(2, 0, 1).reshape(128, 16)),
        'identf': np.eye(128, dtype=np.float32),
        'w_qkv': f(inp['na_w_qkv'][0]),
        'w_o0': f(inp['na_w_o'][0]),
        'qkgain': f(np.stack([np.tile(inp['na_q_gain'][0], 2), np.tile(inp['na_k_gain'][0], 2)], axis=1)),
        'ffn_w_in': f(inp['ffn_w_in']),
        'ffn_w_out': f(inp['ffn_w_out']),
        'hg_w_o': f(inp['hg_w_o'][0]),
        'hg_ng': f(np.tile(inp['hg_norm_g'][0], 8).reshape(1, D)),
    }
    shared['trimask'], shared['rstmask'] = host_consts()
    hw = inp['hg_w_in'][0]
    hw_sw = f(np.concatenate([hw[:, 0:3 * D], hw[:, 4 * D:5 * D], hw[:, 3 * D:4 * D]], axis=1))
    low = inp['hg_lower'].reshape(2, 2, 8, 128)
    per_half = []
    for half in range(2):
        ebg, ebs = rpb_tables(f(inp['na_rpb'][0]), half)
        lo = low if half == 0 else low[:, ::-1]
        per_half.append({
            'ebg': ebg, 'ebs': ebs,
            'hg_w_in': f(hw) if half == 0 else hw_sw,
            'hg_low': f(lo.transpose(3, 0, 1, 2).reshape(128, 32)),
        })
    maps = []
    for b in range(4):
        for half in range(2):
            xs = inp['x'][b] if half == 0 else inp['x'][b, ::-1]
            cs = inp['ctx'][b] if half == 0 else inp['ctx'][b, ::-1]
            m = dict(shared)
            m.update(per_half[half])
            m['xin'] = f(xs[0:NTK * 128])
            m['ctxin'] = f(cs)
            m['sel'] = f(np.tile(np.array([[1.0, 0.0]] if half == 1 else [[0.0, 1.0]], np.float32), (128, 1)))
            m['csil'] = f(np.stack([inp['c'][b].reshape(8, 128).T, inp['c_ctx'].reshape(8, 128).T], axis=2).reshape(128, 16))
            maps.append(m)
    return maps


PHASES = {'ada': ph_ada, 'att0': ph_att0, 'ffn0': make_ffn(0, 'xa', 'xb', True), 'ffn1': make_ffn(1, 'xc', 'out', False), 'hg1': ph_hg1, 'hg2': ph_hg2, 'xchg': ph_xchg}


def build(phases, ext_in, ext_out):
    nc = bass.Bass("TRN2", target_bir_lowering=False)
    env = Env(nc, set(ext_in), set(ext_out), SHAPES)
    with ExitStack() as st:
        p = Prog(nc, st)
        ps = st.enter_context(nc.psum_tensor("ps", [128, 8, 512], F32))
        pb = [Buf(excl=True) for _ in range(8)]
        pre_st = None
        for i, ph in enumerate(phases):
            if ph == 'ada' and i + 1 < len(phases) and phases[i + 1] == 'att0':
                pre_st = ExitStack()
                env.pre['att0_pre'] = att0_prefetch(p, nc, env, pre_st)
            PHASES[ph](p, nc, env, ps, pb)
            if ph == 'att0' and pre_st is not None:
                pre_st.close()
                pre_st = None
        p.finish()
    return nc, env, p


def run_launch(phases, ext_in_maps, ext_out, cores):
    nc, env, p = build(phases, ext_in_maps[0].keys(), ext_out)
    used = set(env.t.keys())
    in_maps = [{k: v for k, v in m.items() if k in used} for m in ext_in_maps]
    res = run_bass_kernel_spmd(nc, in_maps, core_ids=list(range(len(cores))))
    return res


HOST_IN = ['xin', 'ctxin', 'sel', 'csil', 'ada_w', 'ada_b', 'n1g', 'n2g', 'identf', 'w_qkv', 'w_o0', 'qkgain', 'ebg', 'ebs',
           'ffn_w_in', 'ffn_w_out', 'hg_w_in', 'hg_w_o', 'hg_low', 'hg_ng', 'trimask', 'rstmask']
L1_OUT = ['modrow', 'xb', 'o1', 'vtok', 'sgtok', 'd2kh', 'd2qe', 'd2kn', 'd2qt', 'd2eb', 's_send']


def _launch(phases, maps, ext_out):
    nc, env, _ = build(phases, maps[0].keys(), ext_out)
    used = set(env.t.keys())
    in_maps = [{k: v for k, v in m.items() if k in used} for m in maps]
    res = run_bass_kernel_spmd(nc, in_maps, core_ids=list(range(len(maps))))
    return res.results


def kernel(**inputs):
    maps = prep_inputs(inputs)
    r = _launch(['ada', 'att0', 'ffn0', 'hg1', 'xchg', 'hg2', 'ffn1'], maps, ['out'])
    out = np.empty((4, 8192, D), np.float32)
    for b in range(4):
        out[b, 0:NTOK] = np.asarray(r[2 * b]['out'])
        out[b, NTOK:] = np.asarray(r[2 * b + 1]['out'])[::-1]
    return out
```

```python
import re
import numpy as np
from contextlib import ExitStack
import concourse.bass as bass
import concourse.mybir as mybir
from concourse.bass_utils import run_bass_kernel_spmd

F32 = mybir.dt.float32
BF16 = mybir.dt.bfloat16
AF = mybir.ActivationFunctionType
ALU = mybir.AluOpType
AX = mybir.AxisListType

D = 1024
EPS = 1e-6
NTQ = 32
NTK = 34
NTOK = NTQ * 128
DFF = 2816
PADV = -200.0


class Buf:
    __slots__ = ("w", "r", "excl")

    def __init__(self, excl=False):
        self.w = None
        self.r = {}
        self.excl = excl


NDSEM = 20


class Prog:
    ENG = {'pe': 'tensor', 'act': 'scalar', 'dve': 'vector', 'pool': 'gpsimd', 'sp': 'sync'}

    def __init__(self, nc, stack):
        self.nc = nc
        self.q = {e: [] for e in self.ENG}
        self.cnt = {e: 0 for e in self.ENG}
        self.sem = {e: stack.enter_context(nc.semaphore("s_" + e)) for e in self.ENG}
        self.cc_sem = stack.enter_context(nc.semaphore("s_cc"))
        self.cc_cnt = 0
        self.dsem = {}
        self.dcnt = {}
        self.drr = {}
        for qn in ('sp', 'act', 'pool'):
            self.dsem[qn] = [stack.enter_context(nc.semaphore(f"d_{qn}{i}")) for i in range(NDSEM)]
            self.dcnt[qn] = [0] * NDSEM
            self.drr[qn] = 0

    def _deps(self, eng, reads, writes):
        best = {}
        for b in reads:
            if b.w is not None:
                k, v = b.w
                if best.get(k, 0) < v:
                    best[k] = v
            if b.excl:
                for k, v in b.r.items():
                    if k != eng and best.get(k, 0) < v:
                        best[k] = v
        for b in writes:
            if b.w is not None:
                k, v = b.w
                if k != eng and best.get(k, 0) < v:
                    best[k] = v
            for k, v in b.r.items():
                if k == eng:
                    continue
                if best.get(k, 0) < v:
                    best[k] = v
        if eng == 'pe':
            best.pop('pe', None)
        return best

    def _mark(self, tok, reads, writes):
        k, v = tok
        for b in reads:
            if b.r.get(k, 0) < v:
                b.r[k] = v
        for b in writes:
            b.w = tok
            b.r = {}

    def op(self, eng, fn, reads=(), writes=(), inc=True):
        for k, v in self._deps(eng, reads, writes).items():
            self.q[eng].append(('wait', k, v))
        tok = (eng, self.cnt[eng] + 1)
        if inc:
            self.cnt[eng] += 1
        self.q[eng].append(('op', fn, inc))
        self._mark(tok, reads, writes)
        return tok

    def dma(self, qn, fn, reads=(), writes=()):
        for k, v in self._deps(qn + '_q', reads, writes).items():
            self.q[qn].append(('wait', k, v))
        i = self.drr[qn]
        self.drr[qn] = (i + 1) % NDSEM
        prev = self.dcnt[qn][i]
        key = f"d_{qn}{i}"
        if prev > 0:
            self.q[qn].append(('wait', key, prev))
        self.dcnt[qn][i] = prev + 16
        tok = (key, prev + 16)
        self.q[qn].append(('dma', fn, i))
        self._mark(tok, reads, writes)
        return tok

    def collective(self, fn, reads=(), writes=()):
        for k, v in self._deps('pool_q', reads, writes).items():
            self.q['pool'].append(('wait', k, v))
        self.cc_cnt += 1
        tok = ('cc', self.cc_cnt)
        self.q['pool'].append(('cc', fn))
        self._mark(tok, reads, writes)
        return tok

    def barrier(self):
        toks = [(e, c) for e, c in self.cnt.items() if c > 0]
        if self.cc_cnt > 0:
            toks.append(('cc', self.cc_cnt))
        for qn in self.dcnt:
            for i, c in enumerate(self.dcnt[qn]):
                if c > 0:
                    toks.append((f"d_{qn}{i}", c))
        for e in self.ENG:
            for k, v in toks:
                if k != e:
                    self.q[e].append(('wait', k, v))

    def semof(self, key):
        if key == 'cc':
            return self.cc_sem
        if key in self.sem:
            return self.sem[key]
        m = re.match(r"d_([a-z]+)(\d+)", key)
        return self.dsem[m.group(1)][int(m.group(2))]

    def finish(self):
        self.barrier()
        nc = self.nc
        with nc.Block() as block:
            for eng, attr in self.ENG.items():
                items = self.q[eng]

                def body(e, items=items, eng=eng):
                    known = {}
                    for it in items:
                        if it[0] == 'wait':
                            _, k, v = it
                            if known.get(k, 0) >= v:
                                continue
                            known[k] = v
                            e.wait_ge(self.semof(k), v)
                        elif it[0] == 'cc':
                            it[1](e).then_inc(self.cc_sem)
                        elif it[0] == 'op':
                            ins = it[1](e)
                            if it[2]:
                                ins.then_inc(self.sem[eng], 1)
                        else:
                            ins = it[1](e)
                            ins.then_inc(self.dsem[eng][it[2]], 16)
                getattr(block, attr)(body)


_UNIQ = [0]


def _uniq(n):
    _UNIQ[0] += 1
    return f"{n}_{_UNIQ[0]}"


class SB:
    def __init__(self, t):
        self.t = t
        self.b = Buf()
        self.b2 = Buf()

    @property
    def bb(self):
        return [self.b, self.b2]


class Env:
    def __init__(self, nc, ext_in, ext_out, shapes):
        self.nc = nc
        self.ext_in = ext_in
        self.ext_out = ext_out
        self.shapes = shapes
        self.t = {}
        self.bufs = {}
        self.pre = {}

    def d(self, name):
        if name not in self.t:
            shape, dt = self.shapes[name]
            kind = "ExternalInput" if name in self.ext_in else ("ExternalOutput" if name in self.ext_out else "Internal")
            self.t[name] = self.nc.dram_tensor(name, list(shape), dt, kind=kind).ap()
        return self.t[name]

    def b(self, name, idx=0):
        key = (name, idx)
        if key not in self.bufs:
            self.bufs[key] = Buf()
        return self.bufs[key]


SHAPES = {
    'xin': ((NTK * 128, D), F32),
    'ctxin': ((256, D), F32),
    'csil': ((128, 16), F32),
    'ada_w': ((2, D, 6 * D), F32),
    'ada_b': ((1, 2 * 6 * D), F32),
    'n1g': ((128, 16), F32),
    'n2g': ((128, 16), F32),
    'identf': ((128, 128), F32),
    'w_qkv': ((D, 3 * D), F32),
    'w_o0': ((D, D), F32),
    'qkgain': ((128, 2), F32),
    'ebg': ((128, 16 * 5 * 128), F32),
    'ebs': ((2, 128, 16 * 4 * 128), F32),
    'ffn_w_in': ((2, D, 2 * DFF), F32),
    'ffn_w_out': ((2, DFF, D), F32),
    'hg_w_in': ((D, 5 * D), F32),
    'hg_w_o': ((D, D), F32),
    'hg_low': ((128, 32), F32),
    'hg_ng': ((1, D), F32),
    'trimask': ((128, 256), F32),
    'rstmask': ((128, 512), F32),
    'modrow': ((1, 4 * 6 * D), F32),
    'xa': ((NTOK, D), F32), 'xa_ctx': ((256, D), F32),
    'xb': ((NTOK, D), F32), 'xb_ctx': ((256, D), F32),
    'xc': ((NTOK, D), F32),
    'out': ((NTOK, D), F32),
    'o1': ((NTOK, D), F32),
    'vtok': ((NTOK, D), BF16), 'sgtok': ((NTOK, D), BF16), 'd2kh': ((NTOK, D), BF16),
    'd2qe': ((8, 128, NTOK), BF16), 'd2kn': ((8, 128, NTOK), BF16), 'd2qt': ((8, 128, NTOK), BF16),
    'd2eb': ((128, 8, NTOK // 64), F32),
    's_send': ((128, D), F32), 's_recv': ((128, D), F32), 's_gath': ((256, D), F32), 'sel': ((128, 2), F32),
}


def cast_load_w(p, nc, dst, dram2d, K, N, bufs, q='pool', split=1):
    kc = K // 128
    step = N // split
    for s in range(split):
        src = dram2d[:, s * step:(s + 1) * step].rearrange("(c p) n -> p c n", p=128)
        p.dma(q, lambda e, s=s, src=src: e.dma_start(out=dst[:, :, s * step:(s + 1) * step], in_=src),
              writes=[bufs[s]])


def ph_ada(p, nc, env, ps, pb):
    with ExitStack() as st:
        sb = lambda n, s, d: st.enter_context(nc.sbuf_tensor(_uniq(n), s, d))
        cs_f = SB(sb('cs_f', [128, 16], F32))
        cs_b = SB(sb('cs_b', [128, 16], BF16))
        brow = SB(sb('brow', [2, 2 * 6 * D // 2], F32))
        mrow = SB(sb('mrow', [2, 2 * 6 * D // 2], F32))
        NW = 3
        wr = [SB(sb(f'adaw{i}', [128, 8, 512], F32)) for i in range(NW)]
        wb = [SB(sb(f'adab{i}', [128, 8, 512], BF16)) for i in range(2)]
        p.dma('sp', lambda e: e.dma_start(out=cs_f.t[:], in_=env.d('csil')[:, :]), writes=[cs_f.b])
        p.op('act', lambda e: e.activation(out=cs_b.t[:], in_=cs_f.t[:], func=AF.Silu), reads=[cs_f.b], writes=[cs_b.b])
        aw = env.d('ada_w')
        n = 0
        for i in range(2):
            p.dma('sp', lambda e, i=i: e.dma_start(out=brow.t[:], in_=env.d('ada_b')[0:1, i * 6 * D:(i + 1) * 6 * D].partition_broadcast(2)),
                  writes=[brow.b])
            for nb in range(12):
                w = wr[n % NW]
                w2 = wb[n % 2]
                src = aw[i][:, nb * 512:(nb + 1) * 512].rearrange("(c p) n -> p c n", p=128)
                q = 'sp' if n % 2 == 0 else 'act'
                p.dma(q, lambda e, w=w, src=src: e.dma_start(out=w.t[:], in_=src), writes=[w.b])
                p.op('act', lambda e, w=w, w2=w2: e.activation(out=w2.t[:, 0:4, :], in_=w.t[:, 0:4, :], func=AF.Identity),
                     reads=[w.b], writes=[w2.b])
                p.op('dve', lambda e, w=w, w2=w2: e.tensor_copy(out=w2.t[:, 4:8, :], in_=w.t[:, 4:8, :]),
                     reads=[w.b], writes=[w2.b2])
                bank = n % 8
                for kc in range(8):
                    p.op('pe', lambda e, bank=bank, kc=kc, w2=w2: e.matmul(
                        ps[0:2, bank, :], lhsT=cs_b.t[:, 2 * kc:2 * kc + 2], rhs=w2.t[:, kc, :],
                        start=(kc == 0), stop=(kc == 7)),
                        reads=[cs_b.b, w2.b, w2.b2], writes=[pb[bank]], inc=(kc == 7))
                o0 = nb * 512
                p.op('dve', lambda e, bank=bank, o0=o0: e.tensor_tensor(
                    out=mrow.t[0:2, o0:o0 + 512], in0=ps[0:2, bank, :], in1=brow.t[0:2, o0:o0 + 512], op=ALU.add),
                    reads=[pb[bank], brow.b], writes=[mrow.b])
                n += 1
            dst = env.d('modrow')[0, i * 2 * 6 * D:(i + 1) * 2 * 6 * D].rearrange("(m c) -> m c", m=2)
            p.dma('sp', lambda e, dst=dst: e.dma_start(out=dst, in_=mrow.t[0:2, :]), reads=[mrow.b], writes=[env.b('modrow')])
    p.barrier()


def load_modP(p, nc, env, dst, col0, layer, m, ncols=1):
    base = (layer * 2 + m) * 6 * D + col0
    src = env.d('modrow')[0, base:base + D].rearrange("(j p) -> p j", p=128)
    p.dma('sp', lambda e: e.dma_start(out=dst.t[:, 0:8], in_=src, allow_slow_non_contiguous=True),
          reads=[env.b('modrow')], writes=[dst.b])


def load_modB(p, nc, env, dst, col0, layer, m):
    base = (layer * 2 + m) * 6 * D + col0
    src = env.d('modrow')[0:1, base:base + D].partition_broadcast(128)
    p.dma('sp', lambda e: e.dma_start(out=dst.t[:], in_=src), reads=[env.b('modrow')], writes=[dst.b])


def rstd_from_ss(p, s, c_in, c_tmp, c_out, scale, n=1):
    p.op('act', lambda e: e.activation(out=s.t[:, c_tmp:c_tmp + n], in_=s.t[:, c_in:c_in + n], func=AF.Ln, scale=scale, bias=EPS),
         reads=[s.b], writes=[s.b])
    p.op('act', lambda e: e.activation(out=s.t[:, c_out:c_out + n], in_=s.t[:, c_tmp:c_tmp + n], func=AF.Exp, scale=-0.5),
         reads=[s.b], writes=[s.b])


class NormMod:
    def __init__(self, p, nc, st, env, ps, pb, layer, which, ms, banks, tag, nxn=2, junk=None):
        self.p, self.nc, self.ps, self.pb, self.banks = p, nc, ps, pb, banks
        sb = lambda n, s, d: st.enter_context(nc.sbuf_tensor(_uniq(n + tag), s, d))
        self.identf = SB(sb('identf', [128, 128], F32))
        p.dma('sp', lambda e: e.dma_start(out=self.identf.t[:], in_=env.d('identf')[:, :]), writes=[self.identf.b])
        if junk is None:
            self.junk_t = sb('junk', [128, D], BF16)
            self.junk_b = Buf()
        else:
            self.junk_t, self.junk_b = junk
        self.nxn = nxn
        self.xn = [SB(sb(f'xn{i}', [128, D], F32)) for i in range(nxn)]
        self.st = [SB(sb(f'st{i}', [128, 4], F32)) for i in range(2)]
        self.k = 0
        gP = SB(sb('gP', [128, 16], F32))
        p.dma('sp', lambda e: e.dma_start(out=gP.t[:], in_=env.d('n1g' if which == 0 else 'n2g')[:, :]), writes=[gP.b])
        self.geff = {}
        self.sh = {}
        for m in ms:
            shP = SB(sb(f'shP{m}', [128, 8], F32))
            scP = SB(sb(f'scP{m}', [128, 8], F32))
            ge = SB(sb(f'geff{m}', [128, 8], F32))
            load_modP(p, nc, env, shP, (0 if which == 0 else 3 * D), layer, m)
            load_modP(p, nc, env, scP, (D if which == 0 else 4 * D), layer, m)
            p.op('dve', lambda e, scP=scP, ge=ge: e.scalar_tensor_tensor(
                out=ge.t[:], in0=scP.t[:], scalar=1.0, in1=gP.t[:, layer * 8:layer * 8 + 8], op0=ALU.add, op1=ALU.mult),
                reads=[scP.b, gP.b], writes=[ge.b])
            self.geff[m] = ge
            self.sh[m] = shP

    def part1(self, x):
        p = self.p
        k = self.k
        self.k += 1
        xn = self.xn[k % self.nxn]
        s = self.st[k % 2]
        p.op('act', lambda e: e.activation(out=self.junk_t[:], in_=x.t[:], func=AF.Square, accum_out=s.t[:, 0:1]),
             reads=[x.b], writes=[self.junk_b, s.b])
        rstd_from_ss(p, s, 0, 1, 2, 1.0 / D)
        p.op('pool', lambda e: e.tensor_scalar(out=xn.t[:], in0=x.t[:], scalar1=s.t[:, 2:3], scalar2=None, op0=ALU.mult),
             reads=[x.b, s.b], writes=[xn.b])
        return xn

    def part2(self, xn, m, hT, col0):
        self.part2_pe(xn)
        self.part2_evac(m, hT, col0)

    def part2_pe(self, xn):
        p, ps, pb = self.p, self.ps, self.pb
        b0, b1 = self.banks
        for j in (0, 2, 4, 6, 1, 3, 5, 7):
            bank = b0 if j % 2 == 0 else b1
            p.op('pe', lambda e, j=j, bank=bank: e.transpose(ps[:, bank, (j // 2) * 128:(j // 2 + 1) * 128],
                                                             xn.t[:, j * 128:(j + 1) * 128], self.identf.t[:]),
                 reads=[xn.b, self.identf.b], writes=[pb[bank]], inc=(j >= 6))

    def part2_evac(self, m, hT, col0):
        p, ps, pb = self.p, self.ps, self.pb
        b0, b1 = self.banks
        ge, shP = self.geff[m], self.sh[m]
        for j in range(8):
            bank = b0 if j % 2 == 0 else b1
            if j % 2 == 0:
                p.op('act', lambda e, j=j, bank=bank: e.activation(
                    out=hT.t[:, j, col0:col0 + 128], in_=ps[:, bank, (j // 2) * 128:(j // 2 + 1) * 128],
                    func=AF.Identity, scale=ge.t[:, j:j + 1], bias=shP.t[:, j:j + 1]),
                    reads=[pb[bank], ge.b, shP.b], writes=[hT.b])
            else:
                p.op('dve', lambda e, j=j, bank=bank: e.tensor_scalar(
                    out=hT.t[:, j, col0:col0 + 128], in0=ps[:, bank, (j // 2) * 128:(j // 2 + 1) * 128],
                    scalar1=ge.t[:, j:j + 1], scalar2=shP.t[:, j:j + 1], op0=ALU.mult, op1=ALU.add),
                    reads=[pb[bank], ge.b, shP.b], writes=[hT.b2])

    def run(self, x, m, hT, col0):
        self.part2(self.part1(x), m, hT, col0)


def interleave(gens):
    gens = list(gens)
    while gens:
        for g in list(gens):
            try:
                next(g)
            except StopIteration:
                gens.remove(g)


def att0_prefetch(p, nc, env, st):
    sb = lambda n, s, d: st.enter_context(nc.sbuf_tensor(_uniq(n), s, d))
    wqkv = sb('wqkv', [128, 8, 3 * D], BF16)
    wq_b = [Buf() for _ in range(6)]
    cast_load_w(p, nc, wqkv, env.d('w_qkv'), D, 3 * D, wq_b, split=6)
    wo = sb('wo', [128, 8, D], BF16)
    wo_b = [Buf() for _ in range(2)]
    cast_load_w(p, nc, wo, env.d('w_o0'), D, D, wo_b, split=2)
    return wqkv, wq_b, wo, wo_b


def ph_att0(p, nc, env, ps, pb):
    NK = 7
    NQ = 5
    with ExitStack() as st:
        sb = lambda n, s, d: st.enter_context(nc.sbuf_tensor(_uniq(n), s, d))
        if 'att0_pre' in env.pre:
            wqkv, wq_b, wo, wo_b = env.pre.pop('att0_pre')
        else:
            wqkv, wq_b, wo, wo_b = att0_prefetch(p, nc, env, st)
        identb = SB(sb('identb', [128, 128], BF16))
        p.dma('pool', lambda e: e.dma_start(out=identb.t[:], in_=env.d('identf')[:, :]), writes=[identb.b])
        qkg = SB(sb('qkg', [128, 2], F32))
        p.dma('sp', lambda e: e.dma_start(out=qkg.t[:], in_=env.d('qkgain')[:, :]), writes=[qkg.b])
        G = SB(sb('G', [128, D], F32))
        sq = SB(sb('sq', [128, 512], F32))
        nm = NormMod(p, nc, st, env, ps, pb, 0, 0, [0, 1], (0, 1), 'a0', nxn=1, junk=(sq.t[:].bitcast(BF16), sq.b))
        EB = SB(sb('EB', [128, 16, 5, 128], BF16))
        EBS = SB(sb('EBS', [128, 16, 4, 128], BF16))
        xr = [SB(sb(f'xr{i}', [128, D], F32)) for i in range(2)]
        hT = [SB(sb(f'hT{i}', [128, 8, 128], BF16)) for i in range(2)]
        KT = [SB(sb(f'KT{i}', [128, 8, 128], BF16)) for i in range(NK + 2)]
        QT = [SB(sb(f'QT{i}', [128, 8, 128], BF16)) for i in range(NQ)]
        V = [SB(sb(f'V{i}', [128, 16, 65], BF16)) for i in range(NK + 2)]
        qf = SB(sb('qf', [128, 2 * D], F32))
        hs = [SB(sb(f'hs{i}', [128, 32], F32)) for i in range(2)]
        qkt = SB(sb('qkt', [128, 2 * D], BF16))
        PT = [SB(sb(f'PT{i}', [128, 896], BF16)) for i in range(3)]
        otok = SB(sb('otok', [128, D], BF16))
        oT = SB(sb('oT', [128, 8, 128], BF16))
        rc = SB(sb('rc', [128, 4], F32))
        tmp = SB(sb('tmp', [128, D], F32))
        xres = SB(sb('xres', [128, D], F32))
        stage = [SB(tmp.t[:, 0:640]), SB(xres.t[:, 0:640])]
        stage[0].b = tmp.b
        stage[1].b = xres.b
        psb2 = ps[:, 2, :].bitcast(BF16)

        for v in V:
            p.op('pool', lambda e, v=v: e.memset(v.t[:, :, 64:65], 1.0), writes=[v.b])

        def load_table(dst, src2d, ndt):
            for h in range(16):
                sg = stage[h % 2]
                w = ndt * 128
                p.dma('sp', lambda e, h=h, sg=sg, w=w: e.dma_start(out=sg.t[:, 0:w], in_=src2d[:, h * w:(h + 1) * w]),
                      writes=[sg.b])
                p.op('act', lambda e, h=h, sg=sg, w=w: e.activation(
                    out=dst.t[:, h, :, :], in_=sg.t[:, 0:w].rearrange("p (a b) -> p a b", b=128), func=AF.Exp),
                    reads=[sg.b], writes=[dst.b])

        load_table(EB, env.d('ebg'), 5)

        gbank = [0]

        def next_bank():
            gbank[0] ^= 1
            return gbank[0]

        qtasks = [('ctx', 0), ('ctx', 1)] + [('lat', i) for i in range(NTK)]

        def xsrc(task):
            kind, i = task
            return (env.d('ctxin') if kind == 'ctx' else env.d('xin'))[i * 128:(i + 1) * 128, :]

        def issue_xload(pos):
            x = xr[pos % 2]
            src = xsrc(qtasks[pos])
            p.dma('sp', lambda e: e.dma_start(out=x.t[:], in_=src), writes=[x.b])

        def kslot(kind, i):
            return NK + i if kind == 'ctx' else i % NK

        def gen_qkv(pos):
            kind, i = qtasks[pos]
            m = 1 if kind == 'ctx' else 0
            x = xr[pos % 2]
            h = hT[pos % 2]
            need_q = not (kind == 'lat' and i >= NTQ)
            ks = kslot(kind, i)
            s = hs[pos % 2]
            c0 = 0 if need_q else 2
            sched = {}

            def at(step, fn):
                sched.setdefault(step, []).append(fn)
            st8 = {}

            def f_p1():
                if pos + 1 < len(qtasks):
                    issue_xload(pos + 1)
                st8['xn'] = nm.part1(x)
            at(0, f_p1)
            at(4, lambda: nm.part2(st8['xn'], m, h, 0))
            blocks = ([0, 1] if need_q else []) + [2, 3, 4, 5]

            def f_blk_pe(nb):
                bank = next_bank()
                st8[('bank', nb)] = bank
                for kc in range(8):
                    p.op('pe', lambda e, bank=bank, kc=kc, nb=nb: e.matmul(
                        ps[:, bank, :], lhsT=h.t[:, kc, :], rhs=wqkv[:, kc, nb * 512:(nb + 1) * 512],
                        start=(kc == 0), stop=(kc == 7)),
                        reads=[h.b, h.b2, wq_b[nb]], writes=[pb[bank]], inc=(kc == 7))

            def f_blk_evac(nb):
                bank = st8[('bank', nb)]
                if nb < 4:
                    p.op('act', lambda e: e.activation(
                        out=qf.t[:, nb * 512:(nb + 1) * 512], in_=ps[:, bank, :], func=AF.Identity),
                        reads=[pb[bank]], writes=[qf.b])
                else:
                    vv = V[ks]
                    p.op('dve', lambda e: e.tensor_copy(
                        out=vv.t[:, (nb - 4) * 8:(nb - 4) * 8 + 8, 0:64],
                        in_=ps[:, bank, :].rearrange("p (a b) -> p a b", b=64)),
                        reads=[pb[bank]], writes=[vv.b])
            for k_, nb in enumerate(blocks):
                at(6 + k_, lambda nb=nb: f_blk_pe(nb))
                at(7 + k_, lambda nb=nb: f_blk_evac(nb))

            def f_sq():
                for nb in range(c0, 4):
                    p.op('act', lambda e, nb=nb: e.activation(out=sq.t[:], in_=qf.t[:, nb * 512:(nb + 1) * 512], func=AF.Square),
                         reads=[qf.b], writes=[sq.b])
                    p.op('dve', lambda e, nb=nb: e.tensor_reduce(out=s.t[:, nb * 8:nb * 8 + 8],
                                                                 in_=sq.t[:].rearrange("p (a b) -> p a b", b=64), axis=AX.X, op=ALU.add),
                         reads=[sq.b], writes=[s.b])
            nblk = len(blocks)
            at(6 + nblk, f_sq)
            at(8 + nblk, lambda: rstd_from_ss(p, s, c0 * 8, c0 * 8, c0 * 8, 1.0 / 64, n=(4 - c0) * 8))

            def f_mul():
                for nb in range(c0, 4):
                    cc = nb * 8
                    p.op('dve', lambda e, nb=nb, cc=cc: e.tensor_tensor(
                        out=qkt.t[:, nb * 512:(nb + 1) * 512].rearrange("p (a b) -> p a b", b=64),
                        in0=qf.t[:, nb * 512:(nb + 1) * 512].rearrange("p (a b) -> p a b", b=64),
                        in1=s.t[:, cc:cc + 8].unsqueeze(2).to_broadcast([128, 8, 64]), op=ALU.mult),
                        reads=[qf.b, s.b], writes=[qkt.b])
            at(9 + nblk, f_mul)

            def f_T_pe(which):
                for j in range(8):
                    p.op('pe', lambda e, j=j: e.transpose(
                        psb2[:, j * 128:(j + 1) * 128], qkt.t[:, which * D + j * 128:which * D + (j + 1) * 128], identb.t[:]),
                        reads=[qkt.b, identb.b], writes=[pb[2]], inc=(j == 7))

            def f_T_evac(which):
                dst = (QTc[i] if kind == 'ctx' else QT[i % NQ]) if which == 0 else KT[ks]
                if which == 0:
                    p.op('dve', lambda e: e.tensor_scalar(
                        out=dst.t[:].rearrange("p a b -> p (a b)"), in0=psb2[:, 0:1024], scalar1=qkg.t[:, 0:1], scalar2=0.125,
                        op0=ALU.mult, op1=ALU.mult), reads=[pb[2], qkg.b], writes=[dst.b])
                else:
                    p.op('dve', lambda e: e.tensor_scalar(
                        out=dst.t[:].rearrange("p a b -> p (a b)"), in0=psb2[:, 0:1024], scalar1=qkg.t[:, 1:2], scalar2=None,
                        op0=ALU.mult), reads=[pb[2], qkg.b], writes=[dst.b])
            t0_ = 10 + nblk
            for which in ([0, 1] if need_q else [1]):
                at(t0_, lambda which=which: f_T_pe(which))
                at(t0_, lambda which=which: f_T_evac(which))
                t0_ += 2
            for st_ in range(max(sched) + 1):
                for fn in sched.get(st_, []):
                    fn()
                yield

        hcount = [0]

        def gen_att(task):
            kind, t = task
            m = 1 if kind == 'ctx' else 0
            src = xsrc(task)
            if (kind, t) in (('ctx', 0), ('lat', 0)):
                load_modB(p, nc, env, G, 2 * D, 0, m)
            if kind == 'lat' and t in (0, 1):
                load_table(EBS, env.d('ebs')[t], 4)
            if kind == 'ctx':
                chunks = [('ctx', 0), ('ctx', 1)]
                nloc = 0
                tab = None
                q = KT
                qt = None
            else:
                if t in (0, 1):
                    kts = [0, 1, 2, 3]
                    tab = EBS
                else:
                    kts = [t - 2, t - 1, t, t + 1, t + 2]
                    tab = EB
                nloc = len(kts)
                chunks = [('lat', k) for k in kts] + [('ctx', 0), ('ctx', 1)]
            if kind == 'ctx':
                qsrc = QTc[t]
            else:
                qsrc = QT[t % NQ]
            nch = len(chunks)

            def ppos(ci):
                return (0, ci * 128) if ci < 4 else (1, (ci - 4) * 128)
            hn0 = hcount[0]
            hcount[0] += 16

            def emit_qk(h):
                hp, pbs = h // 2, 64 * (h % 2)
                hn = hn0 + h
                sbank = (3 + 2 * (hn % 2), 4 + 2 * (hn % 2))
                for ci, (ck, kt) in enumerate(chunks):
                    bi, col = ppos(ci)
                    bank = sbank[bi]
                    kk = KT[kslot(ck, kt)]
                    last = (ci == nch - 1) or (ci == 3)
                    p.op('pe', lambda e, bank=bank, col=col, kk=kk, hp=hp, pbs=pbs: e.matmul(
                        ps[:, bank, col:col + 128], lhsT=kk.t[pbs:pbs + 64, hp, :], rhs=qsrc.t[pbs:pbs + 64, hp, :],
                        start=True, stop=True),
                        reads=[kk.b, qsrc.b], writes=[pb[bank]], inc=last)

            def emit_exp(h):
                hn = hn0 + h
                sb0 = 3 + 2 * (hn % 2)
                pt = PT[hn % 3]
                S2 = ps[:, sb0:sb0 + 2, :].rearrange("p b c -> p (b c)")
                rb = [pb[sb0]] + ([pb[sb0 + 1]] if nch > 4 else [])
                p.op('act', lambda e: e.activation(out=pt.t[:, 0:nch * 128], in_=S2[:, 0:nch * 128], func=AF.Exp),
                     reads=rb, writes=[pt.b])

            def emit_mult(h):
                hn = hn0 + h
                pt = PT[hn % 3]
                if nloc > 0:
                    p.op('dve', lambda e: e.tensor_tensor(
                        out=pt.t[:, 0:nloc * 128], in0=pt.t[:, 0:nloc * 128],
                        in1=tab.t[:, h, 0:nloc, :].rearrange("p a b -> p (a b)"), op=ALU.mult),
                        reads=[pt.b, tab.b], writes=[pt.b])

            def emit_pv(h):
                hn = hn0 + h
                pt = PT[hn % 3]
                g = h % 4
                for ci, (ck, kt) in enumerate(chunks):
                    off = ci * 128
                    vv = V[kslot(ck, kt)]
                    p.op('pe', lambda e, off=off, vv=vv, g=g, h=h, pt=pt, ci=ci: e.matmul(
                        ps[:, 7, g * 65:(g + 1) * 65], lhsT=pt.t[:, off:off + 128], rhs=vv.t[:, h, :],
                        start=(ci == 0), stop=(ci == nch - 1)),
                        reads=[pt.b, vv.b], writes=[pb[7]], inc=(ci == nch - 1))
                if g == 3:
                    o4 = ps[:, 7, 0:260].rearrange("p (g c) -> p g c", c=65)
                    p.op('dve', lambda e, o4=o4: e.reciprocal(out=rc.t[:, 0:4].unsqueeze(2), in_=o4[:, :, 64:65]),
                         reads=[pb[7]], writes=[rc.b])
                    p.op('dve', lambda e, o4=o4, h=h: e.tensor_tensor(
                        out=otok.t[:, (h - 3) * 64:(h + 1) * 64].rearrange("p (a b) -> p a b", b=64),
                        in0=o4[:, :, 0:64], in1=rc.t[:, 0:4].unsqueeze(2).to_broadcast([128, 4, 64]), op=ALU.mult),
                        reads=[pb[7], rc.b], writes=[otok.b])

            emit_qk(0)
            for s_ in range(18):
                if s_ < 16:
                    emit_exp(s_)
                if 1 <= s_ <= 16:
                    emit_mult(s_ - 1)
                if s_ + 1 < 16:
                    emit_qk(s_ + 1)
                if s_ >= 2:
                    emit_pv(s_ - 2)
                yield
            for j in range(8):
                p.op('pe', lambda e, j=j: e.transpose(psb2[:, j * 128:(j + 1) * 128], otok.t[:, j * 128:(j + 1) * 128],
                                                      identb.t[:]),
                     reads=[otok.b, identb.b], writes=[pb[2]], inc=(j == 7))
            p.op('act', lambda e: e.activation(out=oT.t[:].rearrange("p a b -> p (a b)"), in_=psb2[:, 0:1024],
                                               func=AF.Identity), reads=[pb[2]], writes=[oT.b])
            yield
            p.dma('sp', lambda e: e.dma_start(out=xres.t[:], in_=src), writes=[xres.b])
            for nb in range(2):
                bank = next_bank()
                for kc in range(8):
                    p.op('pe', lambda e, bank=bank, kc=kc, nb=nb: e.matmul(
                        ps[:, bank, :], lhsT=oT.t[:, kc, :], rhs=wo[:, kc, nb * 512:(nb + 1) * 512],
                        start=(kc == 0), stop=(kc == 7)),
                        reads=[oT.b, wo_b[nb]], writes=[pb[bank]], inc=(kc == 7))
                p.op('dve', lambda e, bank=bank, nb=nb: e.tensor_tensor(
                    out=tmp.t[:, nb * 512:(nb + 1) * 512], in0=ps[:, bank, :], in1=G.t[:, nb * 512:(nb + 1) * 512], op=ALU.mult),
                    reads=[pb[bank], G.b], writes=[tmp.b])
                p.op('pool', lambda e, nb=nb: e.tensor_tensor(
                    out=tmp.t[:, nb * 512:(nb + 1) * 512], in0=tmp.t[:, nb * 512:(nb + 1) * 512],
                    in1=xres.t[:, nb * 512:(nb + 1) * 512], op=ALU.add),
                    reads=[tmp.b, xres.b], writes=[tmp.b])
                yield
            dname = 'xa_ctx' if kind == 'ctx' else 'xa'
            dst = env.d(dname)[t * 128:(t + 1) * 128, :]
            p.dma('sp', lambda e: e.dma_start(out=dst, in_=tmp.t[:]), reads=[tmp.b], writes=[env.b(dname, t)])
            yield

        QTc = [SB(sb(f'QTc{i}', [128, 8, 128], BF16)) for i in range(2)]
        atasks = [(('ctx', 0), 1), (('ctx', 1), 1)] + [(('lat', t), 2 + min(t + 3, NTK - 1)) for t in range(NTQ)]
        issue_xload(0)
        ai = 0
        for pos in range(len(qtasks)):
            gens = [gen_qkv(pos)]
            if ai < len(atasks) and atasks[ai][1] <= pos - 1:
                gens.append(gen_att(atasks[ai][0]))
                ai += 1
            interleave(gens)
        while ai < len(atasks):
            interleave([gen_att(atasks[ai][0])])
            ai += 1
    p.barrier()


def make_ffn(layer, src, dst, with_ctx):
    def ph(p, nc, env, ps, pb):
        with ExitStack() as st:
            sb = lambda n, s, d: st.enter_context(nc.sbuf_tensor(_uniq(n), s, d))
            win = sb('win', [128, 8, 2 * DFF], BF16)
            win_b = [Buf() for _ in range(11)]
            wsrc = env.d('ffn_w_in')[layer]
            for s_ in [0, 5, 6, 1, 7, 2, 8, 3, 9, 4, 10]:
                srcw = wsrc[:, s_ * 512:(s_ + 1) * 512].rearrange("(c p) n -> p c n", p=128)
                p.dma('pool', lambda e, s_=s_, srcw=srcw: e.dma_start(out=win[:, :, s_ * 512:(s_ + 1) * 512], in_=srcw),
                      writes=[win_b[s_]])
            wout = sb('wout', [128, 22, D], BF16)
            wout_b = [Buf() for _ in range(2)]
            cast_load_w(p, nc, wout, env.d('ffn_w_out')[layer], DFF, D, wout_b, split=2)
            sg = [SB(sb(f'sg{i}', [128, 512], F32)) for i in range(2)]
            nm = NormMod(p, nc, st, env, ps, pb, layer, 1, [0, 1] if with_ctx else [0], (0, 1), 'f%d' % layer, nxn=2,
                         junk=(sg[0].t[:].bitcast(BF16), sg[0].b))
            hid = SB(sb('hid', [128, 22, 512], BF16))
            h2T = [SB(sb(f'h2T{i}', [128, 8, 512], BF16)) for i in range(2)]
            xr = [SB(sb(f'xr{i}', [128, D], F32)) for i in range(2)]
            tmp = SB(sb('tmp', [128, D], F32))
            xe = SB(sb('xe', [128, D], F32))
            G = SB(sb('G', [128, D], F32))
            sts = []
            if with_ctx:
                sts.append([(src + '_ctx', dst + '_ctx', 0, 1), (src + '_ctx', dst + '_ctx', 1, 1)])
            for s_ in range(NTQ // 4):
                sts.append([(src, dst, s_ * 4 + j, 0) for j in range(4)])
            xcnt = [0]

            def gen_A(si):
                tl = sts[si]
                xns = {}

                def p1(j):
                    sn, dn, ti, m = tl[j]
                    x = xr[xcnt[0] % 2]
                    xcnt[0] += 1
                    srcx = env.d(sn)[ti * 128:(ti + 1) * 128, :]
                    p.dma('sp', lambda e, x=x, srcx=srcx: e.dma_start(out=x.t[:], in_=srcx), reads=[env.b(sn, ti)], writes=[x.b])
                    xns[j] = nm.part1(x)
                p1(0)
                yield
                yield
                for j in range(len(tl)):
                    if j + 1 < len(tl):
                        p1(j + 1)
                    yield
                    yield
                    yield
                    nm.part2(xns[j], tl[j][3], h2T[si % 2], j * 128)
                    yield

            pair = [0]

            def gen_BC(si):
                tiles = sts[si]
                ntok = len(tiles) * 128
                h = h2T[si % 2]
                m = tiles[0][3]
                if si == 0 or (with_ctx and si == 1):
                    load_modB(p, nc, env, G, 5 * D, layer, m)
                for oc in range(22):
                    bA = 2 + 2 * (pair[0] % 3)
                    bU = bA + 1
                    pair[0] += 1
                    for (bank, c0) in ((bA, oc * 128), (bU, DFF + oc * 128)):
                        for kc in range(8):
                            p.op('pe', lambda e, bank=bank, c0=c0, kc=kc: e.matmul(
                                ps[:, bank, 0:ntok], lhsT=win[:, kc, c0:c0 + 128], rhs=h.t[:, kc, 0:ntok],
                                start=(kc == 0), stop=(kc == 7)),
                                reads=[h.b, h.b2, win_b[c0 // 512]], writes=[pb[bank]], inc=(kc == 7))
                    s2 = sg[oc % 2]
                    p.op('act', lambda e, bA=bA, s2=s2: e.activation(out=s2.t[:, 0:ntok], in_=ps[:, bA, 0:ntok], func=AF.Silu),
                         reads=[pb[bA]], writes=[s2.b])
                    p.op('dve', lambda e, bU=bU, s2=s2, oc=oc: e.tensor_tensor(
                        out=hid.t[:, oc, 0:ntok], in0=s2.t[:, 0:ntok], in1=ps[:, bU, 0:ntok], op=ALU.mult),
                        reads=[s2.b, pb[bU]], writes=[hid.b])
                    yield
                for j, (sn, dn, ti, m) in enumerate(tiles):
                    srcx = env.d(sn)[ti * 128:(ti + 1) * 128, :]
                    p.dma('sp', lambda e, srcx=srcx: e.dma_start(out=xe.t[:], in_=srcx), reads=[env.b(sn, ti)], writes=[xe.b])
                    for nb in range(2):
                        bank = nb
                        for oc in range(22):
                            p.op('pe', lambda e, bank=bank, oc=oc, nb=nb, j=j: e.matmul(
                                ps[:, bank, :], lhsT=hid.t[:, oc, j * 128:(j + 1) * 128], rhs=wout[:, oc, nb * 512:(nb + 1) * 512],
                                start=(oc == 0), stop=(oc == 21)),
                                reads=[hid.b, wout_b[nb]], writes=[pb[bank]], inc=(oc == 21))
                        p.op('dve', lambda e, bank=bank, nb=nb: e.tensor_tensor(
                            out=tmp.t[:, nb * 512:(nb + 1) * 512], in0=ps[:, bank, :], in1=G.t[:, nb * 512:(nb + 1) * 512],
                            op=ALU.mult), reads=[pb[bank], G.b], writes=[tmp.b])
                        p.op('pool', lambda e, nb=nb: e.tensor_tensor(
                            out=tmp.t[:, nb * 512:(nb + 1) * 512], in0=tmp.t[:, nb * 512:(nb + 1) * 512],
                            in1=xe.t[:, nb * 512:(nb + 1) * 512], op=ALU.add), reads=[tmp.b, xe.b], writes=[tmp.b])
                        yield
                    dstx = env.d(dn)[ti * 128:(ti + 1) * 128, :]
                    p.dma('sp', lambda e, dstx=dstx: e.dma_start(out=dstx, in_=tmp.t[:]), reads=[tmp.b], writes=[env.b(dn, ti)])

            interleave([gen_A(0)])
            for si in range(len(sts)):
                gens = [gen_BC(si)]
                if si + 1 < len(sts):
                    gens.append(gen_A(si + 1))
                interleave(gens)
        p.barrier()
    return ph


class ScanState:
    def __init__(self, sbf, tag):
        self.S32 = sbf('S32' + tag, [128, 8, 128], F32)
        self.Sbf = [sbf(f'Sbf{i}' + tag, [128, 8, 128], BF16) for i in range(2)]
        self.b32 = [Buf() for _ in range(8)]
        self.bbf = [[Buf() for _ in range(8)] for _ in range(2)]
        self.n = 0


def scan_chunk(p, nc, ps, pb, banks, qt, qe, kn, kh, vt, ebend, ST, msk, c, fwd, o_out):
    bA, bO0, bO1, bU0, bU1 = banks
    par = c % 2
    pbs = 64 * par
    mcol = pbs if fwd else 128 + pbs
    cur = ST.n % 2
    nxt = 1 - cur
    if o_out is not None:
        aT = o_out['aT']
        for hd in range(8):
            (q_ap, q_b), (k_ap, k_b) = qt(hd), kn(hd)
            p.op('pe', lambda e, hd=hd, q_ap=q_ap, k_ap=k_ap: e.matmul(
                ps[pbs:pbs + 64, bA, hd * 64:(hd + 1) * 64], lhsT=k_ap, rhs=q_ap, start=True, stop=True),
                reads=[q_b, k_b], writes=[pb[bA]], inc=(hd == 7))
    for hd in range(8):
        bank = bU0 if hd < 4 else bU1
        kh_ap, kh_b = kh(hd)
        v_ap, v_b = vt(hd)
        p.op('pe', lambda e, hd=hd, bank=bank, kh_ap=kh_ap, v_ap=v_ap: e.matmul(
            ps[:, bank, (hd % 4) * 128:(hd % 4 + 1) * 128], lhsT=kh_ap, rhs=v_ap, start=True, stop=True),
            reads=[kh_b, v_b], writes=[pb[bank]], inc=(hd % 4 == 3))
    if o_out is not None:
        p.op('dve', lambda e: e.copy_predicated(
            out=aT.t[pbs:pbs + 64, :].rearrange("p (a b) -> p a b", b=64),
            mask=msk.t[pbs:pbs + 64, mcol:mcol + 64].bitcast(mybir.dt.uint32).unsqueeze(1).to_broadcast([64, 8, 64]),
            data=ps[pbs:pbs + 64, bA, :].rearrange("p (a b) -> p a b", b=64)),
            reads=[pb[bA], msk.b], writes=[aT.b])
        for hd in range(8):
            bank = bO0 if hd < 4 else bO1
            (q_ap, q_b) = qe(hd)
            v_ap, v_b = vt(hd)
            oap = ps[pbs:pbs + 64, bank, (hd % 4) * 128:(hd % 4 + 1) * 128]
            p.op('pe', lambda e, hd=hd, oap=oap, v_ap=v_ap: e.matmul(
                oap, lhsT=aT.t[pbs:pbs + 64, hd * 64:(hd + 1) * 64], rhs=v_ap, start=True, stop=False),
                reads=[aT.b, v_b], writes=[pb[bank]], inc=False)
            p.op('pe', lambda e, hd=hd, oap=oap, q_ap=q_ap: e.matmul(
                oap, lhsT=q_ap, rhs=ST.Sbf[cur][:, hd, :], start=False, stop=True),
                reads=[q_b, ST.bbf[cur][hd]], writes=[pb[bank]], inc=(hd % 4 == 3))
        ot = o_out['o']
        add = o_out.get('add')
        for half, bank in ((0, bO0), (1, bO1)):
            if add is None:
                p.op('act', lambda e, half=half, bank=bank: e.activation(
                    out=ot.t[pbs:pbs + 64, half * 512:(half + 1) * 512], in_=ps[pbs:pbs + 64, bank, :], func=AF.Identity),
                    reads=[pb[bank]], writes=[ot.b])
            else:
                p.op('dve', lambda e, half=half, bank=bank: e.tensor_tensor(
                    out=ot.t[pbs:pbs + 64, half * 512:(half + 1) * 512], in0=ps[pbs:pbs + 64, bank, :],
                    in1=add.t[pbs:pbs + 64, half * 512:(half + 1) * 512], op=ALU.add),
                    reads=[pb[bank], add.b], writes=[ot.b])
    for hd in range(8):
        bank = bU0 if hd < 4 else bU1
        eb_ap, eb_b = ebend(hd)
        p.op('dve', lambda e, hd=hd, bank=bank, eb_ap=eb_ap: e.scalar_tensor_tensor(
            out=ST.S32[:, hd, :], in0=ST.S32[:, hd, :], scalar=eb_ap, in1=ps[:, bank, (hd % 4) * 128:(hd % 4 + 1) * 128],
            op0=ALU.mult, op1=ALU.add), reads=[ST.b32[hd], eb_b, pb[bank]], writes=[ST.b32[hd]])
        p.op('act', lambda e, hd=hd: e.activation(out=ST.Sbf[nxt][:, hd, :], in_=ST.S32[:, hd, :], func=AF.Identity),
             reads=[ST.b32[hd]], writes=[ST.bbf[nxt][hd]])
    ST.n += 1


def ph_hg1(p, nc, env, ps, pb):
    with ExitStack() as st:
        sb = lambda n, s, d: st.enter_context(nc.sbuf_tensor(_uniq(n), s, d))
        T = 512
        w = sb('hw', [128, 8, 5 * D], BF16)
        w_b = [Buf() for _ in range(10)]
        cast_load_w(p, nc, w, env.d('hg_w_in'), D, 5 * D, w_b, split=10)
        identb = SB(sb('identb', [128, 128], BF16))
        p.dma('pool', lambda e: e.dma_start(out=identb.t[:], in_=env.d('identf')[:, :]), writes=[identb.b])
        msk = SB(sb('msk', [128, 256], F32))
        p.dma('sp', lambda e: e.dma_start(out=msk.t[:], in_=env.d('trimask')[:, :]), writes=[msk.b])
        rst = SB(sb('rst', [128, 512], BF16))
        p.dma('pool', lambda e: e.dma_start(out=rst.t[:], in_=env.d('rstmask')[:, :]), writes=[rst.b])
        low = SB(sb('low', [128, 32], F32))
        p.dma('sp', lambda e: e.dma_start(out=low.t[:], in_=env.d('hg_low')[:, :]), writes=[low.b])
        lbt = SB(sb('lbt', [128, 48], F32))
        p.op('dve', lambda e: e.tensor_tensor(out=lbt.t[:, 0:16], in0=low.t[:, 0:16], in1=low.t[:, 16:32], op=ALU.subtract),
             reads=[low.b], writes=[lbt.b])
        p.op('act', lambda e: e.activation(out=lbt.t[:, 16:32], in_=lbt.t[:, 0:16], func=AF.Sigmoid, scale=-1.0),
             reads=[lbt.b], writes=[lbt.b])
        p.op('act', lambda e: e.activation(out=lbt.t[:, 32:48], in_=lbt.t[:, 0:16], func=AF.Sigmoid),
             reads=[lbt.b], writes=[lbt.b])
        jk = SB(sb('jk', [128, 512], F32))
        nm = NormMod(p, nc, st, env, ps, pb, 1, 0, [0, 1], (0, 1), 'h1', nxn=1, junk=(jk.t[:].bitcast(BF16), jk.b))
        xr = [SB(sb(f'xr{i}', [128, D], F32)) for i in range(2)]
        hT = SB(sb('hT', [128, 8, T], BF16))
        tq = [SB(sb(f'tq{i}', [128, T], F32)) for i in range(2)]
        ta = [SB(sb(f'ta{i}', [128, T], F32)) for i in range(2)]
        tb = [SB(sb(f'tb{i}', [128, T], F32)) for i in range(2)]
        tc = [SB(sb(f'tc{i}', [128, T], F32)) for i in range(2)]
        td = [SB(sb(f'td{i}', [128, T], F32)) for i in range(2)]
        te = [SB(sb(f'te{i}', [128, T], F32)) for i in range(2)]
        tkh = [SB(sb(f'tkh{i}', [128, T], BF16)) for i in range(2)]
        QE1 = SB(sb('QE1', [128, 8, T], BF16))
        QT1 = SB(sb('QT1', [128, 8, T], BF16))
        KN1 = SB(sb('KN1', [128, 8, T], BF16))
        KH1 = SB(sb('KH1', [128, 4, D], BF16))
        EB1 = SB(sb('EB1', [128, 8, 8], F32))
        s2q = [SB(sb(f's2q{i}', [128, T], BF16)) for i in range(2)]
        s2k = [SB(sb(f's2k{i}', [128, T], BF16)) for i in range(2)]
        s2t = [SB(sb(f's2t{i}', [128, T], BF16)) for i in range(2)]
        KH2 = SB(sb('KH2', [128, 4, D], BF16))
        EB2 = SB(sb('EB2', [128, 8, 8], F32))
        Vt = SB(sb('Vt', [128, 4, D], BF16))
        SGt = SB(sb('SGt', [128, 4, D], BF16))
        ST = ScanState(sb, 'h1')
        aT = SB(sb('aT', [128, 512], BF16))
        o1t = [SB(sb(f'o1t{i}', [128, D], F32)) for i in range(1)]
        p.op('pool', lambda e: e.memset(aT.t[:], 0.0), writes=[aT.b])
        p.op('pool', lambda e: e.memset(ST.S32[:], 0.0), writes=ST.b32)
        p.op('pool', lambda e: e.memset(ST.Sbf[0][:], 0.0), writes=ST.bbf[0])
        psb = lambda bank: ps[:, bank, :].bitcast(BF16)

        sts = [[('xb_ctx', 0, 1), ('xb_ctx', 1, 1)]] + [[('xb', s_ * 4 + j, 0) for j in range(4)] for s_ in range(NTQ // 4)]
        xcnt = [0]
        rot = [0]

        def nbank():
            rot[0] = (rot[0] + 1) % 3
            return 2 + rot[0]

        b3cnt = [0]

        def gen_A(si):
            for j, (sn, ti, m) in enumerate(sts[si]):
                x = xr[xcnt[0] % 2]
                xcnt[0] += 1
                srcx = env.d(sn)[ti * 128:(ti + 1) * 128, :]
                p.dma('sp', lambda e, x=x, srcx=srcx: e.dma_start(out=x.t[:], in_=srcx), reads=[env.b(sn, ti)], writes=[x.b])
                xn_ = nm.part1(x)
                yield
                yield
                nm.part2(xn_, m, hT, j * 128)
                yield

        def do_supertile(si, tiles):
            nt = len(tiles)
            Tn = nt * 128
            nch = Tn // 64
            is_ctx = (si == 0)
            tok0 = None if is_ctx else tiles[0][1] * 128

            def emit_b3(j, nb):
                if True:
                    b3cnt[0] ^= 1
                    bank = b3cnt[0]
                    c0 = D + nb * 512
                    for kc in range(8):
                        p.op('pe', lambda e, bank=bank, kc=kc, c0=c0, j=j: e.matmul(
                            ps[:, bank, :], lhsT=hT.t[:, kc, j * 128:(j + 1) * 128], rhs=w[:, kc, c0:c0 + 512],
                            start=(kc == 0), stop=(kc == 7)), reads=[hT.b, hT.b2, w_b[c0 // 512]], writes=[pb[bank]], inc=(kc == 7))
                    if nb < 2:
                        p.op('dve', lambda e, bank=bank, j=j, nb=nb: e.tensor_copy(
                            out=Vt.t[:, j, nb * 512:(nb + 1) * 512], in_=ps[:, bank, :]), reads=[pb[bank]], writes=[Vt.b])
                    elif not is_ctx:
                        p.op('act', lambda e, bank=bank, j=j, nb=nb: e.activation(
                            out=SGt.t[:, j, (nb - 2) * 512:(nb - 1) * 512], in_=ps[:, bank, :], func=AF.Silu),
                            reads=[pb[bank]], writes=[SGt.b])
            b3_groups = [(j, nb) for j in range(nt) for nb in range(4)]
            def chain(hd, d, q_):
                a_, b_, c_, d_, e_, kh_ = ta[d], tb[d], tc[d], td[d], te[d], tkh[d]
                bank = 3 if d == 0 else 4
                lcol = d * 8 + hd
                lb_ap = lbt.t[:, 16 + lcol:17 + lcol]
                oml_ap = lbt.t[:, 32 + lcol:33 + lcol]
                p.op('act', lambda e: e.activation(out=a_.t[:, 0:Tn], in_=ps[:, bank, 0:Tn], func=AF.Sigmoid),
                     reads=[pb[bank]], writes=[a_.b])
                p.op('act', lambda e: e.activation(out=b_.t[:, 0:Tn], in_=ps[:, bank, 0:Tn], func=AF.Sigmoid, scale=-1.0),
                     reads=[pb[bank]], writes=[b_.b])
                p.op('act', lambda e: e.activation(out=a_.t[:, 0:Tn], in_=a_.t[:, 0:Tn], func=AF.Ln, scale=oml_ap, bias=lb_ap),
                     reads=[a_.b, lbt.b], writes=[a_.b])
                yield
                p.op('dve', lambda e: e.tensor_tensor_scan(
                    out=c_.t[:, 0:Tn], data0=rst.t[:, 0:Tn], data1=a_.t[:, 0:Tn], initial=0.0, op0=ALU.mult, op1=ALU.add),
                    reads=[rst.b, a_.b], writes=[c_.b])
                c3 = c_.t[:, 0:Tn].rearrange("p (c t) -> p c t", t=64)
                bend = c3[:, :, 63:64]
                p.op('dve', lambda e: e.tensor_tensor(
                    out=d_.t[:, 0:Tn].rearrange("p (c t) -> p c t", t=64), in0=bend.to_broadcast([128, nch, 64]), in1=c3,
                    op=ALU.subtract), reads=[c_.b], writes=[d_.b])
                ebt = EB1 if d == 0 else EB2
                yield
                p.op('act', lambda e: e.activation(out=ebt.t[:, hd, 0:nch].unsqueeze(2), in_=bend, func=AF.Exp),
                     reads=[c_.b], writes=[ebt.b])
                if d == 0:
                    p.op('act', lambda e: e.activation(out=e_.t[:, 0:Tn], in_=c_.t[:, 0:Tn], func=AF.Exp), reads=[c_.b], writes=[e_.b])
                    yield
                    p.op('dve', lambda e: e.tensor_tensor(out=QE1.t[:, hd, 0:Tn], in0=q_.t[:, 0:Tn], in1=e_.t[:, 0:Tn], op=ALU.mult),
                         reads=[q_.b, e_.b], writes=[QE1.b])
                    p.op('dve', lambda e: e.tensor_tensor(
                        out=e_.t[:, 0:Tn].rearrange("p (c t) -> p c t", t=64), in0=c3,
                        in1=c3[:, :, 31:32].to_broadcast([128, nch, 64]), op=ALU.subtract), reads=[c_.b], writes=[e_.b])
                    yield
                    p.op('act', lambda e: e.activation(out=a_.t[:, 0:Tn], in_=e_.t[:, 0:Tn], func=AF.Exp), reads=[e_.b], writes=[a_.b])
                    p.op('act', lambda e: e.activation(out=e_.t[:, 0:Tn], in_=e_.t[:, 0:Tn], func=AF.Exp, scale=-1.0),
                         reads=[e_.b], writes=[e_.b])
                    p.op('act', lambda e: e.activation(out=d_.t[:, 0:Tn], in_=d_.t[:, 0:Tn], func=AF.Exp), reads=[d_.b], writes=[d_.b])
                    yield
                    p.op('dve', lambda e: e.tensor_tensor(out=QT1.t[:, hd, 0:Tn], in0=q_.t[:, 0:Tn], in1=a_.t[:, 0:Tn], op=ALU.mult),
                         reads=[q_.b, a_.b], writes=[QT1.b])
                    p.op('dve', lambda e: e.scalar_tensor_tensor(
                        out=KN1.t[:, hd, 0:Tn], in0=b_.t[:, 0:Tn], scalar=oml_ap, in1=e_.t[:, 0:Tn], op0=ALU.mult, op1=ALU.mult),
                        reads=[b_.b, e_.b, lbt.b], writes=[KN1.b])
                    p.op('dve', lambda e: e.scalar_tensor_tensor(
                        out=kh_.t[:, 0:Tn], in0=b_.t[:, 0:Tn], scalar=oml_ap, in1=d_.t[:, 0:Tn], op0=ALU.mult, op1=ALU.mult),
                        reads=[b_.b, d_.b, lbt.b], writes=[kh_.b])
                    KH = KH1
                else:
                    p.op('dve', lambda e: e.tensor_tensor(out=d_.t[:, 0:Tn], in0=d_.t[:, 0:Tn], in1=a_.t[:, 0:Tn], op=ALU.add),
                         reads=[d_.b, a_.b], writes=[d_.b])
                    sq_, sk_, st_ = s2q[hd % 2], s2k[hd % 2], s2t[hd % 2]
                    d3 = d_.t[:, 0:Tn].rearrange("p (c t) -> p c t", t=64)
                    p.op('dve', lambda e: e.tensor_tensor(
                        out=e_.t[:, 0:Tn].rearrange("p (c t) -> p c t", t=64), in0=d3,
                        in1=d3[:, :, 32:33].to_broadcast([128, nch, 64]), op=ALU.subtract), reads=[d_.b], writes=[e_.b])
                    p.op('dve', lambda e: e.tensor_tensor(out=c_.t[:, 0:Tn], in0=c_.t[:, 0:Tn], in1=a_.t[:, 0:Tn], op=ALU.subtract),
                         reads=[c_.b, a_.b], writes=[c_.b])
                    yield
                    p.op('act', lambda e: e.activation(out=d_.t[:, 0:Tn], in_=d_.t[:, 0:Tn], func=AF.Exp), reads=[d_.b], writes=[d_.b])
                    p.op('act', lambda e: e.activation(out=a_.t[:, 0:Tn], in_=e_.t[:, 0:Tn], func=AF.Exp), reads=[e_.b], writes=[a_.b])
                    p.op('act', lambda e: e.activation(out=e_.t[:, 0:Tn], in_=e_.t[:, 0:Tn], func=AF.Exp, scale=-1.0),
                         reads=[e_.b], writes=[e_.b])
                    p.op('act', lambda e: e.activation(out=c_.t[:, 0:Tn], in_=c_.t[:, 0:Tn], func=AF.Exp), reads=[c_.b], writes=[c_.b])
                    yield
                    p.op('dve', lambda e: e.tensor_tensor(out=sq_.t[:, 0:Tn], in0=q_.t[:, 0:Tn], in1=d_.t[:, 0:Tn], op=ALU.mult),
                         reads=[q_.b, d_.b], writes=[sq_.b])
                    p.op('dve', lambda e: e.tensor_tensor(out=st_.t[:, 0:Tn], in0=q_.t[:, 0:Tn], in1=a_.t[:, 0:Tn], op=ALU.mult),
                         reads=[q_.b, a_.b], writes=[st_.b])
                    p.op('dve', lambda e: e.scalar_tensor_tensor(
                        out=sk_.t[:, 0:Tn], in0=b_.t[:, 0:Tn], scalar=oml_ap, in1=e_.t[:, 0:Tn], op0=ALU.mult, op1=ALU.mult),
                        reads=[b_.b, e_.b, lbt.b], writes=[sk_.b])
                    p.op('dve', lambda e: e.scalar_tensor_tensor(
                        out=kh_.t[:, 0:Tn], in0=b_.t[:, 0:Tn], scalar=oml_ap, in1=c_.t[:, 0:Tn], op0=ALU.mult, op1=ALU.mult),
                        reads=[b_.b, c_.b, lbt.b], writes=[kh_.b])
                    for (nm_, tl) in (('d2qe', sq_), ('d2kn', sk_), ('d2qt', st_)):
                        dstd = env.d(nm_)[hd, :, tok0:tok0 + Tn]
                        p.dma('sp', lambda e, dstd=dstd, tl=tl: e.dma_start(out=dstd, in_=tl.t[:, 0:Tn]), reads=[tl.b],
                              writes=[env.b(nm_, si)])
                    KH = KH2
                pendT.append(lambda: emit_T(hd, d, kh_, KH))
                yield

            def emit_T(hd, d, kh_, KH):
                tbk = 5 + d
                for j in range(nt):
                    p.op('pe', lambda e, j=j: e.transpose(
                        psb(tbk)[:, j * 128:(j + 1) * 128], kh_.t[:, j * 128:(j + 1) * 128], identb.t[:]),
                        reads=[kh_.b, identb.b], writes=[pb[tbk]], inc=(j == nt - 1))
                p.op('dve', lambda e: e.tensor_copy(
                    out=KH.t[:, 0:nt, hd * 128:(hd + 1) * 128],
                    in_=psb(tbk)[:, 0:nt * 128].rearrange("p (j k) -> p j k", k=128)),
                    reads=[pb[tbk]], writes=[KH.b])

            pendT = []
            for hd in range(8):
                bq, bf1, bf2 = 2, 3, 4
                for (bank, c0) in ((bq, hd * 128), (bf1, 3 * D + hd * 128), (bf2, 4 * D + hd * 128)):
                    if is_ctx and bank == bf2:
                        continue
                    for kc in range(8):
                        p.op('pe', lambda e, bank=bank, kc=kc, c0=c0: e.matmul(
                            ps[:, bank, 0:Tn], lhsT=w[:, kc, c0:c0 + 128], rhs=hT.t[:, kc, 0:Tn],
                            start=(kc == 0), stop=(kc == 7)), reads=[hT.b, hT.b2, w_b[c0 // 512]], writes=[pb[bank]], inc=(kc == 7))
                q_ = tq[hd % 2]
                p.op('act', lambda e, q_=q_: e.activation(out=q_.t[:, 0:Tn], in_=ps[:, bq, 0:Tn], func=AF.Silu),
                     reads=[pb[bq]], writes=[q_.b])
                ng_ = (len(b3_groups) + 7 - hd) // (8 - hd)
                for _ in range(min(ng_, len(b3_groups))):
                    emit_b3(*b3_groups.pop(0))
                for f in pendT:
                    f()
                pendT.clear()

                def delayed(g, n):
                    for _ in range(n):
                        yield
                    yield from g
                gens = [chain(hd, 0, q_)]
                if not is_ctx:
                    gens.append(delayed(chain(hd, 1, q_), 2))
                interleave(gens)
            for f in pendT:
                f()
            pendT.clear()
            while b3_groups:
                emit_b3(*b3_groups.pop(0))
            if not is_ctx:
                for (nm_, tl) in (('vtok', Vt), ('sgtok', SGt)):
                    dstd = env.d(nm_)[tok0:tok0 + Tn, :].rearrange("(j p) n -> p j n", p=128)
                    p.dma('sp', lambda e, dstd=dstd, tl=tl: e.dma_start(out=dstd, in_=tl.t[:, 0:nt, :]), reads=[tl.b],
                          writes=[env.b(nm_, si)])
            if not is_ctx:
                dstd = env.d('d2kh')[tok0:tok0 + Tn, :].rearrange("(j p) n -> p j n", p=128)
                p.dma('sp', lambda e, dstd=dstd: e.dma_start(out=dstd, in_=KH2.t[:, 0:nt, :]), reads=[KH2.b],
                      writes=[env.b('d2kh', si)])
                ch0 = tok0 // 64
                dstd = env.d('d2eb')[:, :, ch0:ch0 + nch]
                p.dma('sp', lambda e, dstd=dstd: e.dma_start(out=dstd, in_=EB2.t[:, :, 0:nch]), reads=[EB2.b],
                      writes=[env.b('d2eb', si)])
            return gen_C(si, tiles, nt, nch, is_ctx)

        def gen_C(si, tiles, nt, nch, is_ctx):
            for c in range(nch):
                j = c // 2
                pbs = 64 * (c % 2)
                qt = lambda hd, c=c: (QT1.t[:, hd, c * 64:(c + 1) * 64], QT1.b)
                qe = lambda hd, c=c: (QE1.t[:, hd, c * 64:(c + 1) * 64], QE1.b)
                kn = lambda hd, c=c: (KN1.t[:, hd, c * 64:(c + 1) * 64], KN1.b)
                kh = lambda hd, j=j, pbs=pbs: (KH1.t[pbs:pbs + 64, j, hd * 128:(hd + 1) * 128], KH1.b)
                vt = lambda hd, j=j, pbs=pbs: (Vt.t[pbs:pbs + 64, j, hd * 128:(hd + 1) * 128], Vt.b)
                eb = lambda hd, c=c: (EB1.t[:, hd, c:c + 1], EB1.b)
                oo = None
                if not is_ctx:
                    ot = o1t[0]
                    oo = {'aT': aT, 'o': ot}
                scan_chunk(p, nc, ps, pb, (7, 5, 6, 2, 3), qt, qe, kn, kh, vt, eb, ST, msk, c, True, oo)
                if oo is not None and c % 2 == 1:
                    ti = tiles[j][1]
                    dstd = env.d('o1')[ti * 128:(ti + 1) * 128, :]
                    p.dma('sp', lambda e, dstd=dstd, ot=ot: e.dma_start(out=dstd, in_=ot.t[:]), reads=[ot.b],
                          writes=[env.b('o1', ti)])
                yield

        interleave([gen_A(0)])
        for si, tiles in enumerate(sts):
            gC = do_supertile(si, tiles)
            gens = [gC]
            if si + 1 < len(sts):
                gens.append(gen_A(si + 1))
            interleave(gens)
        p.dma('sp', lambda e: e.dma_start(out=env.d('s_send')[:, :], in_=ST.S32[:].rearrange("p a b -> p (a b)")),
              reads=ST.b32, writes=[env.b('s_send')])
    p.barrier()


def ph_hg2(p, nc, env, ps, pb):
    with ExitStack() as st:
        sb = lambda n, s, d: st.enter_context(nc.sbuf_tensor(_uniq(n), s, d))
        wo = sb('hwo', [128, 8, D], BF16)
        wo_b = [Buf() for _ in range(2)]
        cast_load_w(p, nc, wo, env.d('hg_w_o'), D, D, wo_b, split=2)
        identb = SB(sb('identb', [128, 128], BF16))
        p.dma('pool', lambda e: e.dma_start(out=identb.t[:], in_=env.d('identf')[:, :]), writes=[identb.b])
        msk = SB(sb('msk', [128, 256], F32))
        p.dma('sp', lambda e: e.dma_start(out=msk.t[:], in_=env.d('trimask')[:, :]), writes=[msk.b])
        ng = SB(sb('ng', [128, D], F32))
        p.dma('sp', lambda e: e.dma_start(out=ng.t[:], in_=env.d('hg_ng')[0:1, :].partition_broadcast(128)), writes=[ng.b])
        G = SB(sb('G', [128, D], F32))
        load_modB(p, nc, env, G, 2 * D, 1, 0)
        ST = ScanState(sb, 'h2')
        p.dma('sp', lambda e: e.dma_start(out=ST.S32[:].rearrange("p a b -> p (a b)"), in_=env.d('s_recv')[:, :]),
              reads=[env.b('s_recv')], writes=ST.b32)
        p.op('pool', lambda e: e.tensor_copy(out=ST.Sbf[0][:], in_=ST.S32[:]), reads=ST.b32, writes=ST.bbf[0])
        R = 2
        QE = [SB(sb(f'QE{i}', [128, 8, 128], BF16)) for i in range(R)]
        KN = [SB(sb(f'KN{i}', [128, 8, 128], BF16)) for i in range(R)]
        QTT = [SB(sb(f'QTT{i}', [128, 8, 128], BF16)) for i in range(R)]
        KH = [SB(sb(f'KH{i}', [128, D], BF16)) for i in range(R)]
        VT = [SB(sb(f'VT{i}', [128, D], BF16)) for i in range(R)]
        SG = [SB(sb(f'SG{i}', [128, D], BF16)) for i in range(R)]
        EBt = [SB(sb(f'EBt{i}', [128, 8, 2], F32)) for i in range(R)]
        O1 = [SB(sb(f'O1{i}', [128, D], F32)) for i in range(R)]
        XB = [SB(sb(f'XB{i}', [128, D], F32)) for i in range(3)]
        o2 = SB(sb('o2', [128, D], F32))
        sq = SB(sb('sq', [128, D], F32))
        hs = SB(sb('hs', [128, 16], F32))
        yb = SB(sb('yb', [128, D], BF16))
        oT = SB(sb('oT', [128, 8, 128], BF16))
        aT = SB(sb('aT', [128, 512], BF16))
        tmp = SB(sb('tmp', [128, D], F32))
        p.op('pool', lambda e: e.memset(aT.t[:], 0.0), writes=[aT.b])
        psb2 = ps[:, 4, :].bitcast(BF16)
        order = list(range(NTQ - 1, -1, -1))

        def loads(n):
            t = order[n]
            r = n % R
            sl = slice(t * 128, (t + 1) * 128)
            si = t // 4
            p.dma('sp', lambda e: e.dma_start(out=QE[r].t[:], in_=env.d('d2qe')[:, :, sl].rearrange("h k n -> k h n")),
                  reads=[env.b('d2qe', si)], writes=[QE[r].b])
            p.dma('sp', lambda e: e.dma_start(out=KN[r].t[:], in_=env.d('d2kn')[:, :, sl].rearrange("h k n -> k h n")),
                  reads=[env.b('d2kn', si)], writes=[KN[r].b])
            p.dma('sp', lambda e: e.dma_start(out=QTT[r].t[:], in_=env.d('d2qt')[:, :, sl].rearrange("h k n -> k h n")),
                  reads=[env.b('d2qt', si)], writes=[QTT[r].b])
            p.dma('sp', lambda e: e.dma_start(out=KH[r].t[:], in_=env.d('d2kh')[sl, :]), reads=[env.b('d2kh', si)], writes=[KH[r].b])
            p.dma('sp', lambda e: e.dma_start(out=VT[r].t[:], in_=env.d('vtok')[sl, :]), reads=[env.b('vtok', si)], writes=[VT[r].b])
            p.dma('sp', lambda e: e.dma_start(out=SG[r].t[:], in_=env.d('sgtok')[sl, :]), reads=[env.b('sgtok', si)], writes=[SG[r].b])
            p.dma('sp', lambda e: e.dma_start(out=EBt[r].t[:], in_=env.d('d2eb')[:, :, 2 * t:2 * t + 2]),
                  reads=[env.b('d2eb', si)], writes=[EBt[r].b])
            p.dma('sp', lambda e: e.dma_start(out=O1[r].t[:], in_=env.d('o1')[sl, :]), reads=[env.b('o1', t)], writes=[O1[r].b])
            xb_ = XB[n % 3]
            p.dma('sp', lambda e: e.dma_start(out=xb_.t[:], in_=env.d('xb')[sl, :]), reads=[env.b('xb', t)], writes=[xb_.b])

        loads(0)

        NG = [SB(sb(f'NG{i}', [128, D], F32)) for i in range(R)]
        pend = []

        def pe_readout(t, xb_):
            for j in range(8):
                p.op('pe', lambda e, j=j: e.transpose(psb2[:, j * 128:(j + 1) * 128], yb.t[:, j * 128:(j + 1) * 128], identb.t[:]),
                     reads=[yb.b, identb.b], writes=[pb[4]], inc=(j == 7))
            p.op('act', lambda e: e.activation(out=oT.t[:].rearrange("p a b -> p (a b)"), in_=psb2[:, 0:1024], func=AF.Identity),
                 reads=[pb[4]], writes=[oT.b])
            for nb in range(2):
                bank = nb
                for kc in range(8):
                    p.op('pe', lambda e, bank=bank, kc=kc, nb=nb: e.matmul(
                        ps[:, bank, :], lhsT=oT.t[:, kc, :], rhs=wo[:, kc, nb * 512:(nb + 1) * 512], start=(kc == 0), stop=(kc == 7)),
                        reads=[oT.b, wo_b[nb]], writes=[pb[bank]], inc=(kc == 7))
                p.op('dve', lambda e, bank=bank, nb=nb: e.tensor_tensor(
                    out=tmp.t[:, nb * 512:(nb + 1) * 512], in0=ps[:, bank, :], in1=G.t[:, nb * 512:(nb + 1) * 512], op=ALU.mult),
                    reads=[pb[bank], G.b], writes=[tmp.b])
                p.op('pool', lambda e, nb=nb: e.tensor_tensor(
                    out=tmp.t[:, nb * 512:(nb + 1) * 512], in0=tmp.t[:, nb * 512:(nb + 1) * 512],
                    in1=xb_.t[:, nb * 512:(nb + 1) * 512], op=ALU.add), reads=[tmp.b, xb_.b], writes=[tmp.b])
            dst = env.d('xc')[t * 128:(t + 1) * 128, :]
            p.dma('sp', lambda e, dst=dst: e.dma_start(out=dst, in_=tmp.t[:]), reads=[tmp.b], writes=[env.b('xc', t)])

        def do_tile(n, t):
            r = n % R
            if n + 1 < len(order):
                loads(n + 1)
            p.op('pool', lambda e: e.tensor_tensor(out=NG[r].t[:], in0=SG[r].t[:], in1=ng.t[:], op=ALU.mult),
                 reads=[SG[r].b, ng.b], writes=[NG[r].b])
            for par in (1, 0):
                pbs = 64 * par
                qt = lambda hd, par=par: (QTT[r].t[:, hd, par * 64:(par + 1) * 64], QTT[r].b)
                qe = lambda hd, par=par: (QE[r].t[:, hd, par * 64:(par + 1) * 64], QE[r].b)
                kn = lambda hd, par=par: (KN[r].t[:, hd, par * 64:(par + 1) * 64], KN[r].b)
                kh = lambda hd, pbs=pbs: (KH[r].t[pbs:pbs + 64, hd * 128:(hd + 1) * 128], KH[r].b)
                vt = lambda hd, pbs=pbs: (VT[r].t[pbs:pbs + 64, hd * 128:(hd + 1) * 128], VT[r].b)
                eb = lambda hd, par=par: (EBt[r].t[:, hd, par:par + 1], EBt[r].b)
                scan_chunk(p, nc, ps, pb, (7, 5, 6, 2, 3), qt, qe, kn, kh, vt, eb, ST, msk, par, False,
                           {'aT': aT, 'o': o2, 'add': O1[r]})
                if par == 1 and pend:
                    pe_readout(*pend.pop())
            for hd in range(8):
                p.op('act', lambda e, hd=hd: e.activation(out=sq.t[:, hd * 128:(hd + 1) * 128], in_=o2.t[:, hd * 128:(hd + 1) * 128],
                                                          func=AF.Square, accum_out=hs.t[:, hd:hd + 1]),
                     reads=[o2.b], writes=[sq.b, hs.b])
            p.op('act', lambda e: e.activation(out=hs.t[:, 0:8], in_=hs.t[:, 0:8], func=AF.Ln, scale=1.0 / 128, bias=EPS),
                 reads=[hs.b], writes=[hs.b])
            p.op('act', lambda e: e.activation(out=hs.t[:, 8:16], in_=hs.t[:, 0:8], func=AF.Exp, scale=-0.5), reads=[hs.b], writes=[hs.b])
            p.op('dve', lambda e: e.tensor_tensor(
                out=sq.t[:].rearrange("p (a b) -> p a b", b=128), in0=o2.t[:].rearrange("p (a b) -> p a b", b=128),
                in1=hs.t[:, 8:16].unsqueeze(2).to_broadcast([128, 8, 128]), op=ALU.mult), reads=[o2.b, hs.b], writes=[sq.b])
            p.op('dve', lambda e: e.tensor_tensor(out=yb.t[:], in0=sq.t[:], in1=NG[r].t[:], op=ALU.mult),
                 reads=[sq.b, NG[r].b], writes=[yb.b])
            pend.append((t, XB[n % 3]))

        for n, t in enumerate(order):
            do_tile(n, t)
        while pend:
            pe_readout(*pend.pop())
    p.barrier()


def ph_xchg(p, nc, env, ps, pb):
    with ExitStack() as st:
        sb = lambda n, s, d: st.enter_context(nc.sbuf_tensor(_uniq(n), s, d))
        snd = env.d('s_send')
        gat = env.d('s_gath')
        p.collective(lambda e: e.collective_compute("AllGather", ALU.bypass, replica_groups=[[0, 1], [2, 3], [4, 5], [6, 7]],
                                                    ins=[snd.opt()], outs=[gat.opt()]),
                     reads=[env.b('s_send')], writes=[env.b('s_gath')])
        g = SB(sb('xg', [128, 2, D], F32))
        sel = SB(sb('xsel', [128, 2], F32))
        r = SB(sb('xr', [128, D], F32))
        p.dma('sp', lambda e: e.dma_start(out=g.t[:], in_=gat.rearrange("(r p) n -> p r n", p=128)), reads=[env.b('s_gath')],
              writes=[g.b])
        p.dma('sp', lambda e: e.dma_start(out=sel.t[:], in_=env.d('sel')[:, :]), writes=[sel.b])
        p.op('dve', lambda e: e.tensor_scalar(out=r.t[:], in0=g.t[:, 0, :], scalar1=sel.t[:, 0:1], scalar2=None, op0=ALU.mult),
             reads=[g.b, sel.b], writes=[r.b])
        p.op('dve', lambda e: e.scalar_tensor_tensor(out=r.t[:], in0=g.t[:, 1, :], scalar=sel.t[:, 1:2], in1=r.t[:], op0=ALU.mult,
                                                     op1=ALU.add), reads=[g.b, sel.b, r.b], writes=[r.b])
        p.dma('sp', lambda e: e.dma_start(out=env.d('s_recv')[:, :], in_=r.t[:]), reads=[r.b], writes=[env.b('s_recv')])
    p.barrier()


def _coords(tile, half):
    l = tile * 128 + np.arange(128)
    g = l if half == 0 else 8191 - l
    return g // 64, g % 64


def rpb_tables(rpb, half):
    flat = np.concatenate([rpb.reshape(16, 15 * 31), np.full((16, 1), PADV, np.float32)], axis=1)

    def table(t, kts):
        qr, qc = _coords(t, half)
        out = np.empty((128, 16, len(kts), 128), np.float32)
        rs = np.clip(qr - 4, 0, 120)
        cs = np.clip(qc - 8, 0, 48)
        for a, kt in enumerate(kts):
            kr, kc = _coords(kt, half)
            valid = ((kr[:, None] >= rs[None, :]) & (kr[:, None] <= rs[None, :] + 7) &
                     (kc[:, None] >= cs[None, :]) & (kc[:, None] <= cs[None, :] + 15))
            idx = (kr[:, None] - qr[None, :] + 7) * 31 + (kc[:, None] - qc[None, :] + 15)
            idx = np.where(valid, idx, 465)
            out[:, :, a, :] = np.transpose(flat[:, idx], (1, 0, 2))
        return out.reshape(128, -1)

    ebg = table(10, [8, 9, 10, 11, 12])
    ebs = np.stack([table(0, [0, 1, 2, 3]), table(1, [0, 1, 2, 3])], axis=0)
    return ebg, ebs


def host_consts():
    tri = np.zeros((128, 256), np.float32)
    for blk in range(2):
        s = np.arange(64)[:, None]
        t = np.arange(64)[None, :]
        tri[blk * 64:(blk + 1) * 64, blk * 64:(blk + 1) * 64] = (s <= t)
        tri[blk * 64:(blk + 1) * 64, 128 + blk * 64:128 + (blk + 1) * 64] = (s >= t)
    rst = np.ones((128, 512), np.float32)
    rst[:, 0::64] = 0.0
    return tri, rst


def prep_inputs(inp):
    f = lambda a: np.ascontiguousarray(a, dtype=np.float32)
    shared = {
        'ada_w': f(inp['ada_w']),
        'ada_b': f(inp['ada_b'].reshape(1, -1)),
        'n1g': f(inp['norm1_g'].reshape(2, 8, 128).transpose(2, 0, 1).reshape(128, 16)),
        'n2g': f(inp['norm2_g'].reshape(2, 8, 128).transpose(2, 0, 1).reshape(128, 16)),
        'identf': np.eye(128, dtype=np.float32),
        'w_qkv': f(inp['na_w_qkv'][0]),
        'w_o0': f(inp['na_w_o'][0]),
        'qkgain': f(np.stack([np.tile(inp['na_q_gain'][0], 2), np.tile(inp['na_k_gain'][0], 2)], axis=1)),
        'ffn_w_in': f(inp['ffn_w_in']),
        'ffn_w_out': f(inp['ffn_w_out']),
        'hg_w_o': f(inp['hg_w_o'][0]),
        'hg_ng': f(np.tile(inp['hg_norm_g'][0], 8).reshape(1, D)),
    }
    shared['trimask'], shared['rstmask'] = host_consts()
    hw = inp['hg_w_in'][0]
    hw_sw = f(np.concatenate([hw[:, 0:3 * D], hw[:, 4 * D:5 * D], hw[:, 3 * D:4 * D]], axis=1))
    low = inp['hg_lower'].reshape(2, 2, 8, 128)
    per_half = []
    for half in range(2):
        ebg, ebs = rpb_tables(f(inp['na_rpb'][0]), half)
        lo = low if half == 0 else low[:, ::-1]
        per_half.append({
            'ebg': ebg, 'ebs': ebs,
            'hg_w_in': f(hw) if half == 0 else hw_sw,
            'hg_low': f(lo.transpose(3, 0, 1, 2).reshape(128, 32)),
        })
    maps = []
    for b in range(4):
        for half in range(2):
            xs = inp['x'][b] if half == 0 else inp['x'][b, ::-1]
            cs = inp['ctx'][b] if half == 0 else inp['ctx'][b, ::-1]
            m = dict(shared)
            m.update(per_half[half])
            m['xin'] = f(xs[0:NTK * 128])
            m['ctxin'] = f(cs)
            m['sel'] = f(np.tile(np.array([[1.0, 0.0]] if half == 1 else [[0.0, 1.0]], np.float32), (128, 1)))
            m['csil'] = f(np.stack([inp['c'][b].reshape(8, 128).T, inp['c_ctx'].reshape(8, 128).T], axis=2).reshape(128, 16))
            maps.append(m)
    return maps


PHASES = {'ada': ph_ada, 'att0': ph_att0, 'ffn0': make_ffn(0, 'xa', 'xb', True), 'ffn1': make_ffn(1, 'xc', 'out', False), 'hg1': ph_hg1, 'hg2': ph_hg2, 'xchg': ph_xchg}


def build(phases, ext_in, ext_out):
    nc = bass.Bass("TRN2", target_bir_lowering=False)
    env = Env(nc, set(ext_in), set(ext_out), SHAPES)
    with ExitStack() as st:
        p = Prog(nc, st)
        ps = st.enter_context(nc.psum_tensor("ps", [128, 8, 512], F32))
        pb = [Buf(excl=True) for _ in range(8)]
        pre_st = None
        for i, ph in enumerate(phases):
            if ph == 'ada' and i + 1 < len(phases) and phases[i + 1] == 'att0':
                pre_st = ExitStack()
                env.pre['att0_pre'] = att0_prefetch(p, nc, env, pre_st)
            PHASES[ph](p, nc, env, ps, pb)
            if ph == 'att0' and pre_st is not None:
                pre_st.close()
                pre_st = None
        p.finish()
    return nc, env, p


def run_launch(phases, ext_in_maps, ext_out, cores):
    nc, env, p = build(phases, ext_in_maps[0].keys(), ext_out)
    used = set(env.t.keys())
    in_maps = [{k: v for k, v in m.items() if k in used} for m in ext_in_maps]
    res = run_bass_kernel_spmd(nc, in_maps, core_ids=list(range(len(cores))))
    return res


HOST_IN = ['xin', 'ctxin', 'sel', 'csil', 'ada_w', 'ada_b', 'n1g', 'n2g', 'identf', 'w_qkv', 'w_o0', 'qkgain', 'ebg', 'ebs',
           'ffn_w_in', 'ffn_w_out', 'hg_w_in', 'hg_w_o', 'hg_low', 'hg_ng', 'trimask', 'rstmask']
L1_OUT = ['modrow', 'xb', 'o1', 'vtok', 'sgtok', 'd2kh', 'd2qe', 'd2kn', 'd2qt', 'd2eb', 's_send']


def _launch(phases, maps, ext_out):
    nc, env, _ = build(phases, maps[0].keys(), ext_out)
    used = set(env.t.keys())
    in_maps = [{k: v for k, v in m.items() if k in used} for m in maps]
    res = run_bass_kernel_spmd(nc, in_maps, core_ids=list(range(len(maps))))
    return res.results


def kernel(**inputs):
    maps = prep_inputs(inputs)
    r = _launch(['ada', 'att0', 'ffn0', 'hg1', 'xchg', 'hg2', 'ffn1'], maps, ['out'])
    out = np.empty((4, 8192, D), np.float32)
    for b in range(4):
        out[b, 0:NTOK] = np.asarray(r[2 * b]['out'])
        out[b, NTOK:] = np.asarray(r[2 * b + 1]['out'])[::-1]
    return out
```

```python
import re
import numpy as np
from contextlib import ExitStack
import concourse.bass as bass
import concourse.mybir as mybir
from concourse.bass_utils import run_bass_kernel_spmd

F32 = mybir.dt.float32
BF16 = mybir.dt.bfloat16
AF = mybir.ActivationFunctionType
ALU = mybir.AluOpType
AX = mybir.AxisListType

D = 1024
EPS = 1e-6
NTQ = 32
NTK = 34
NTOK = NTQ * 128
DFF = 2816
PADV = -200.0


class Buf:
    __slots__ = ("w", "r", "excl")

    def __init__(self, excl=False):
        self.w = None
        self.r = {}
        self.excl = excl


NDSEM = 20


class Prog:
    ENG = {'pe': 'tensor', 'act': 'scalar', 'dve': 'vector', 'pool': 'gpsimd', 'sp': 'sync'}

    def __init__(self, nc, stack):
        self.nc = nc
        self.q = {e: [] for e in self.ENG}
        self.cnt = {e: 0 for e in self.ENG}
        self.sem = {e: stack.enter_context(nc.semaphore("s_" + e)) for e in self.ENG}
        self.cc_sem = stack.enter_context(nc.semaphore("s_cc"))
        self.cc_cnt = 0
        self.dsem = {}
        self.dcnt = {}
        self.drr = {}
        for qn in ('sp', 'act', 'pool'):
            self.dsem[qn] = [stack.enter_context(nc.semaphore(f"d_{qn}{i}")) for i in range(NDSEM)]
            self.dcnt[qn] = [0] * NDSEM
            self.drr[qn] = 0

    def _deps(self, eng, reads, writes):
        best = {}
        for b in reads:
            if b.w is not None:
                k, v = b.w
                if best.get(k, 0) < v:
                    best[k] = v
            if b.excl:
                for k, v in b.r.items():
                    if k != eng and best.get(k, 0) < v:
                        best[k] = v
        for b in writes:
            if b.w is not None:
                k, v = b.w
                if k != eng and best.get(k, 0) < v:
                    best[k] = v
            for k, v in b.r.items():
                if k == eng:
                    continue
                if best.get(k, 0) < v:
                    best[k] = v
        if eng == 'pe':
            best.pop('pe', None)
        return best

    def _mark(self, tok, reads, writes):
        k, v = tok
        for b in reads:
            if b.r.get(k, 0) < v:
                b.r[k] = v
        for b in writes:
            b.w = tok
            b.r = {}

    def op(self, eng, fn, reads=(), writes=(), inc=True):
        for k, v in self._deps(eng, reads, writes).items():
            self.q[eng].append(('wait', k, v))
        tok = (eng, self.cnt[eng] + 1)
        if inc:
            self.cnt[eng] += 1
        self.q[eng].append(('op', fn, inc))
        self._mark(tok, reads, writes)
        return tok

    def dma(self, qn, fn, reads=(), writes=()):
        for k, v in self._deps(qn + '_q', reads, writes).items():
            self.q[qn].append(('wait', k, v))
        i = self.drr[qn]
        self.drr[qn] = (i + 1) % NDSEM
        prev = self.dcnt[qn][i]
        key = f"d_{qn}{i}"
        if prev > 0:
            self.q[qn].append(('wait', key, prev))
        self.dcnt[qn][i] = prev + 16
        tok = (key, prev + 16)
        self.q[qn].append(('dma', fn, i))
        self._mark(tok, reads, writes)
        return tok

    def collective(self, fn, reads=(), writes=()):
        for k, v in self._deps('pool_q', reads, writes).items():
            self.q['pool'].append(('wait', k, v))
        self.cc_cnt += 1
        tok = ('cc', self.cc_cnt)
        self.q['pool'].append(('cc', fn))
        self._mark(tok, reads, writes)
        return tok

    def barrier(self):
        toks = [(e, c) for e, c in self.cnt.items() if c > 0]
        if self.cc_cnt > 0:
            toks.append(('cc', self.cc_cnt))
        for qn in self.dcnt:
            for i, c in enumerate(self.dcnt[qn]):
                if c > 0:
                    toks.append((f"d_{qn}{i}", c))
        for e in self.ENG:
            for k, v in toks:
                if k != e:
                    self.q[e].append(('wait', k, v))

    def semof(self, key):
        if key == 'cc':
            return self.cc_sem
        if key in self.sem:
            return self.sem[key]
        m = re.match(r"d_([a-z]+)(\d+)", key)
        return self.dsem[m.group(1)][int(m.group(2))]

    def finish(self):
        self.barrier()
        nc = self.nc
        with nc.Block() as block:
            for eng, attr in self.ENG.items():
                items = self.q[eng]

                def body(e, items=items, eng=eng):
                    known = {}
                    for it in items:
                        if it[0] == 'wait':
                            _, k, v = it
                            if known.get(k, 0) >= v:
                                continue
                            known[k] = v
                            e.wait_ge(self.semof(k), v)
                        elif it[0] == 'cc':
                            it[1](e).then_inc(self.cc_sem)
                        elif it[0] == 'op':
                            ins = it[1](e)
                            if it[2]:
                                ins.then_inc(self.sem[eng], 1)
                        else:
                            ins = it[1](e)
                            ins.then_inc(self.dsem[eng][it[2]], 16)
                getattr(block, attr)(body)


_UNIQ = [0]


def _uniq(n):
    _UNIQ[0] += 1
    return f"{n}_{_UNIQ[0]}"


class SB:
    def __init__(self, t):
        self.t = t
        self.b = Buf()
        self.b2 = Buf()

    @property
    def bb(self):
        return [self.b, self.b2]


class Env:
    def __init__(self, nc, ext_in, ext_out, shapes):
        self.nc = nc
        self.ext_in = ext_in
        self.ext_out = ext_out
        self.shapes = shapes
        self.t = {}
        self.bufs = {}
        self.pre = {}

    def d(self, name):
        if name not in self.t:
            shape, dt = self.shapes[name]
            kind = "ExternalInput" if name in self.ext_in else ("ExternalOutput" if name in self.ext_out else "Internal")
            self.t[name] = self.nc.dram_tensor(name, list(shape), dt, kind=kind).ap()
        return self.t[name]

    def b(self, name, idx=0):
        key = (name, idx)
        if key not in self.bufs:
            self.bufs[key] = Buf()
        return self.bufs[key]


SHAPES = {
    'xin': ((NTK * 128, D), F32),
    'ctxin': ((256, D), F32),
    'csil': ((128, 16), F32),
    'ada_w': ((2, D, 6 * D), F32),
    'ada_b': ((1, 2 * 6 * D), F32),
    'n1g': ((128, 16), F32),
    'n2g': ((128, 16), F32),
    'identf': ((128, 128), F32),
    'w_qkv': ((D, 3 * D), F32),
    'w_o0': ((D, D), F32),
    'qkgain': ((128, 2), F32),
    'ebg': ((128, 16 * 5 * 128), F32),
    'ebs': ((2, 128, 16 * 4 * 128), F32),
    'ffn_w_in': ((2, D, 2 * DFF), F32),
    'ffn_w_out': ((2, DFF, D), F32),
    'hg_w_in': ((D, 5 * D), F32),
    'hg_w_o': ((D, D), F32),
    'hg_low': ((128, 32), F32),
    'hg_ng': ((1, D), F32),
    'trimask': ((128, 256), F32),
    'rstmask': ((128, 512), F32),
    'modrow': ((1, 4 * 6 * D), F32),
    'xa': ((NTOK, D), F32), 'xa_ctx': ((256, D), F32),
    'xb': ((NTOK, D), F32), 'xb_ctx': ((256, D), F32),
    'xc': ((NTOK, D), F32),
    'out': ((NTOK, D), F32),
    'o1': ((NTOK, D), F32),
    'vtok': ((NTOK, D), BF16), 'sgtok': ((NTOK, D), BF16), 'd2kh': ((NTOK, D), BF16),
    'd2qe': ((8, 128, NTOK), BF16), 'd2kn': ((8, 128, NTOK), BF16), 'd2qt': ((8, 128, NTOK), BF16),
    'd2eb': ((128, 8, NTOK // 64), F32),
    's_send': ((128, D), F32), 's_recv': ((128, D), F32), 's_gath': ((256, D), F32), 'sel': ((128, 2), F32),
}


def cast_load_w(p, nc, dst, dram2d, K, N, bufs, q='pool', split=1):
    kc = K // 128
    step = N // split
    for s in range(split):
        src = dram2d[:, s * step:(s + 1) * step].rearrange("(c p) n -> p c n", p=128)
        p.dma(q, lambda e, s=s, src=src: e.dma_start(out=dst[:, :, s * step:(s + 1) * step], in_=src),
              writes=[bufs[s]])


def ph_ada(p, nc, env, ps, pb):
    with ExitStack() as st:
        sb = lambda n, s, d: st.enter_context(nc.sbuf_tensor(_uniq(n), s, d))
        cs_f = SB(sb('cs_f', [128, 16], F32))
        cs_b = SB(sb('cs_b', [128, 16], BF16))
        brow = SB(sb('brow', [2, 2 * 6 * D // 2], F32))
        mrow = SB(sb('mrow', [2, 2 * 6 * D // 2], F32))
        NW = 3
        wr = [SB(sb(f'adaw{i}', [128, 8, 512], F32)) for i in range(NW)]
        wb = [SB(sb(f'adab{i}', [128, 8, 512], BF16)) for i in range(2)]
        p.dma('sp', lambda e: e.dma_start(out=cs_f.t[:], in_=env.d('csil')[:, :]), writes=[cs_f.b])
        p.op('act', lambda e: e.activation(out=cs_b.t[:], in_=cs_f.t[:], func=AF.Silu), reads=[cs_f.b], writes=[cs_b.b])
        aw = env.d('ada_w')
        n = 0
        for i in range(2):
            p.dma('sp', lambda e, i=i: e.dma_start(out=brow.t[:], in_=env.d('ada_b')[0:1, i * 6 * D:(i + 1) * 6 * D].partition_broadcast(2)),
                  writes=[brow.b])
            for nb in range(12):
                w = wr[n % NW]
                w2 = wb[n % 2]
                src = aw[i][:, nb * 512:(nb + 1) * 512].rearrange("(c p) n -> p c n", p=128)
                q = 'sp' if n % 2 == 0 else 'act'
                p.dma(q, lambda e, w=w, src=src: e.dma_start(out=w.t[:], in_=src), writes=[w.b])
                p.op('act', lambda e, w=w, w2=w2: e.activation(out=w2.t[:, 0:4, :], in_=w.t[:, 0:4, :], func=AF.Identity),
                     reads=[w.b], writes=[w2.b])
                p.op('dve', lambda e, w=w, w2=w2: e.tensor_copy(out=w2.t[:, 4:8, :], in_=w.t[:, 4:8, :]),
                     reads=[w.b], writes=[w2.b2])
                bank = n % 8
                for kc in range(8):
                    p.op('pe', lambda e, bank=bank, kc=kc, w2=w2: e.matmul(
                        ps[0:2, bank, :], lhsT=cs_b.t[:, 2 * kc:2 * kc + 2], rhs=w2.t[:, kc, :],
                        start=(kc == 0), stop=(kc == 7)),
                        reads=[cs_b.b, w2.b, w2.b2], writes=[pb[bank]], inc=(kc == 7))
                o0 = nb * 512
                p.op('dve', lambda e, bank=bank, o0=o0: e.tensor_tensor(
                    out=mrow.t[0:2, o0:o0 + 512], in0=ps[0:2, bank, :], in1=brow.t[0:2, o0:o0 + 512], op=ALU.add),
                    reads=[pb[bank], brow.b], writes=[mrow.b])
                n += 1
            dst = env.d('modrow')[0, i * 2 * 6 * D:(i + 1) * 2 * 6 * D].rearrange("(m c) -> m c", m=2)
            p.dma('sp', lambda e, dst=dst: e.dma_start(out=dst, in_=mrow.t[0:2, :]), reads=[mrow.b], writes=[env.b('modrow')])
    p.barrier()


def load_modP(p, nc, env, dst, col0, layer, m, ncols=1):
    base = (layer * 2 + m) * 6 * D + col0
    src = env.d('modrow')[0, base:base + D].rearrange("(j p) -> p j", p=128)
    p.dma('sp', lambda e: e.dma_start(out=dst.t[:, 0:8], in_=src, allow_slow_non_contiguous=True),
          reads=[env.b('modrow')], writes=[dst.b])


def load_modB(p, nc, env, dst, col0, layer, m):
    base = (layer * 2 + m) * 6 * D + col0
    src = env.d('modrow')[0:1, base:base + D].partition_broadcast(128)
    p.dma('sp', lambda e: e.dma_start(out=dst.t[:], in_=src), reads=[env.b('modrow')], writes=[dst.b])


def rstd_from_ss(p, s, c_in, c_tmp, c_out, scale, n=1):
    p.op('act', lambda e: e.activation(out=s.t[:, c_tmp:c_tmp + n], in_=s.t[:, c_in:c_in + n], func=AF.Ln, scale=scale, bias=EPS),
         reads=[s.b], writes=[s.b])
    p.op('act', lambda e: e.activation(out=s.t[:, c_out:c_out + n], in_=s.t[:, c_tmp:c_tmp + n], func=AF.Exp, scale=-0.5),
         reads=[s.b], writes=[s.b])


class NormMod:
    def __init__(self, p, nc, st, env, ps, pb, layer, which, ms, banks, tag, nxn=2, junk=None):
        self.p, self.nc, self.ps, self.pb, self.banks = p, nc, ps, pb, banks
        sb = lambda n, s, d: st.enter_context(nc.sbuf_tensor(_uniq(n + tag), s, d))
        self.identf = SB(sb('identf', [128, 128], F32))
        p.dma('sp', lambda e: e.dma_start(out=self.identf.t[:], in_=env.d('identf')[:, :]), writes=[self.identf.b])
        if junk is None:
            self.junk_t = sb('junk', [128, D], BF16)
            self.junk_b = Buf()
        else:
            self.junk_t, self.junk_b = junk
        self.nxn = nxn
        self.xn = [SB(sb(f'xn{i}', [128, D], F32)) for i in range(nxn)]
        self.st = [SB(sb(f'st{i}', [128, 4], F32)) for i in range(2)]
        self.k = 0
        gP = SB(sb('gP', [128, 16], F32))
        p.dma('sp', lambda e: e.dma_start(out=gP.t[:], in_=env.d('n1g' if which == 0 else 'n2g')[:, :]), writes=[gP.b])
        self.geff = {}
        self.sh = {}
        for m in ms:
            shP = SB(sb(f'shP{m}', [128, 8], F32))
            scP = SB(sb(f'scP{m}', [128, 8], F32))
            ge = SB(sb(f'geff{m}', [128, 8], F32))
            load_modP(p, nc, env, shP, (0 if which == 0 else 3 * D), layer, m)
            load_modP(p, nc, env, scP, (D if which == 0 else 4 * D), layer, m)
            p.op('dve', lambda e, scP=scP, ge=ge: e.scalar_tensor_tensor(
                out=ge.t[:], in0=scP.t[:], scalar=1.0, in1=gP.t[:, layer * 8:layer * 8 + 8], op0=ALU.add, op1=ALU.mult),
                reads=[scP.b, gP.b], writes=[ge.b])
            self.geff[m] = ge
            self.sh[m] = shP

    def part1(self, x):
        p = self.p
        k = self.k
        self.k += 1
        xn = self.xn[k % self.nxn]
        s = self.st[k % 2]
        p.op('act', lambda e: e.activation(out=self.junk_t[:], in_=x.t[:], func=AF.Square, accum_out=s.t[:, 0:1]),
             reads=[x.b], writes=[self.junk_b, s.b])
        rstd_from_ss(p, s, 0, 1, 2, 1.0 / D)
        p.op('pool', lambda e: e.tensor_scalar(out=xn.t[:], in0=x.t[:], scalar1=s.t[:, 2:3], scalar2=None, op0=ALU.mult),
             reads=[x.b, s.b], writes=[xn.b])
        return xn

    def part2(self, xn, m, hT, col0):
        self.part2_pe(xn)
        self.part2_evac(m, hT, col0)

    def part2_pe(self, xn):
        p, ps, pb = self.p, self.ps, self.pb
        b0, b1 = self.banks
        for j in (0, 2, 4, 6, 1, 3, 5, 7):
            bank = b0 if j % 2 == 0 else b1
            p.op('pe', lambda e, j=j, bank=bank: e.transpose(ps[:, bank, (j // 2) * 128:(j // 2 + 1) * 128],
                                                             xn.t[:, j * 128:(j + 1) * 128], self.identf.t[:]),
                 reads=[xn.b, self.identf.b], writes=[pb[bank]], inc=(j >= 6))

    def part2_evac(self, m, hT, col0):
        p, ps, pb = self.p, self.ps, self.pb
        b0, b1 = self.banks
        ge, shP = self.geff[m], self.sh[m]
        for j in range(8):
            bank = b0 if j % 2 == 0 else b1
            if j % 2 == 0:
                p.op('act', lambda e, j=j, bank=bank: e.activation(
                    out=hT.t[:, j, col0:col0 + 128], in_=ps[:, bank, (j // 2) * 128:(j // 2 + 1) * 128],
                    func=AF.Identity, scale=ge.t[:, j:j + 1], bias=shP.t[:, j:j + 1]),
                    reads=[pb[bank], ge.b, shP.b], writes=[hT.b])
            else:
                p.op('dve', lambda e, j=j, bank=bank: e.tensor_scalar(
                    out=hT.t[:, j, col0:col0 + 128], in0=ps[:, bank, (j // 2) * 128:(j // 2 + 1) * 128],
                    scalar1=ge.t[:, j:j + 1], scalar2=shP.t[:, j:j + 1], op0=ALU.mult, op1=ALU.add),
                    reads=[pb[bank], ge.b, shP.b], writes=[hT.b2])

    def run(self, x, m, hT, col0):
        self.part2(self.part1(x), m, hT, col0)


def interleave(gens):
    gens = list(gens)
    while gens:
        for g in list(gens):
            try:
                next(g)
            except StopIteration:
                gens.remove(g)


def att0_prefetch(p, nc, env, st):
    sb = lambda n, s, d: st.enter_context(nc.sbuf_tensor(_uniq(n), s, d))
    wqkv = sb('wqkv', [128, 8, 3 * D], BF16)
    wq_b = [Buf() for _ in range(6)]
    cast_load_w(p, nc, wqkv, env.d('w_qkv'), D, 3 * D, wq_b, split=6)
    wo = sb('wo', [128, 8, D], BF16)
    wo_b = [Buf() for _ in range(2)]
    cast_load_w(p, nc, wo, env.d('w_o0'), D, D, wo_b, split=2)
    return wqkv, wq_b, wo, wo_b


def ph_att0(p, nc, env, ps, pb):
    NK = 7
    NQ = 5
    with ExitStack() as st:
        sb = lambda n, s, d: st.enter_context(nc.sbuf_tensor(_uniq(n), s, d))
        if 'att0_pre' in env.pre:
            wqkv, wq_b, wo, wo_b = env.pre.pop('att0_pre')
        else:
            wqkv, wq_b, wo, wo_b = att0_prefetch(p, nc, env, st)
        identb = SB(sb('identb', [128, 128], BF16))
        p.dma('pool', lambda e: e.dma_start(out=identb.t[:], in_=env.d('identf')[:, :]), writes=[identb.b])
        qkg = SB(sb('qkg', [128, 2], F32))
        p.dma('sp', lambda e: e.dma_start(out=qkg.t[:], in_=env.d('qkgain')[:, :]), writes=[qkg.b])
        G = SB(sb('G', [128, D], F32))
        sq = SB(sb('sq', [128, 512], F32))
        nm = NormMod(p, nc, st, env, ps, pb, 0, 0, [0, 1], (0, 1), 'a0', nxn=1, junk=(sq.t[:].bitcast(BF16), sq.b))
        EB = SB(sb('EB', [128, 16, 5, 128], BF16))
        EBS = SB(sb('EBS', [128, 16, 4, 128], BF16))
        xr = [SB(sb(f'xr{i}', [128, D], F32)) for i in range(2)]
        hT = [SB(sb(f'hT{i}', [128, 8, 128], BF16)) for i in range(2)]
        KT = [SB(sb(f'KT{i}', [128, 8, 128], BF16)) for i in range(NK + 2)]
        QT = [SB(sb(f'QT{i}', [128, 8, 128], BF16)) for i in range(NQ)]
        V = [SB(sb(f'V{i}', [128, 16, 65], BF16)) for i in range(NK + 2)]
        qf = SB(sb('qf', [128, 2 * D], F32))
        hs = [SB(sb(f'hs{i}', [128, 32], F32)) for i in range(2)]
        qkt = SB(sb('qkt', [128, 2 * D], BF16))
        PT = [SB(sb(f'PT{i}', [128, 896], BF16)) for i in range(3)]
        otok = SB(sb('otok', [128, D], BF16))
        oT = SB(sb('oT', [128, 8, 128], BF16))
        oraw = [SB(sb(f'oraw{i}', [128, 260], F32)) for i in range(2)]
        rc = SB(sb('rc', [128, 4], F32))
        tmp = SB(sb('tmp', [128, D], F32))
        xres = SB(sb('xres', [128, D], F32))
        stage = [SB(tmp.t[:, 0:640]), SB(xres.t[:, 0:640])]
        stage[0].b = tmp.b
        stage[1].b = xres.b
        psb2 = ps[:, 2, :].bitcast(BF16)

        for v in V:
            p.op('pool', lambda e, v=v: e.memset(v.t[:, :, 64:65], 1.0), writes=[v.b])

        def load_table(dst, src2d, ndt):
            for h in range(16):
                sg = stage[h % 2]
                w = ndt * 128
                p.dma('sp', lambda e, h=h, sg=sg, w=w: e.dma_start(out=sg.t[:, 0:w], in_=src2d[:, h * w:(h + 1) * w]),
                      writes=[sg.b])
                p.op('act', lambda e, h=h, sg=sg, w=w: e.activation(
                    out=dst.t[:, h, :, :], in_=sg.t[:, 0:w].rearrange("p (a b) -> p a b", b=128), func=AF.Exp),
                    reads=[sg.b], writes=[dst.b])

        load_table(EB, env.d('ebg'), 5)

        gbank = [0]

        def next_bank():
            gbank[0] ^= 1
            return gbank[0]

        qtasks = [('ctx', 0), ('ctx', 1)] + [('lat', i) for i in range(NTK)]

        def xsrc(task):
            kind, i = task
            return (env.d('ctxin') if kind == 'ctx' else env.d('xin'))[i * 128:(i + 1) * 128, :]

        def issue_xload(pos):
            x = xr[pos % 2]
            src = xsrc(qtasks[pos])
            p.dma('sp', lambda e: e.dma_start(out=x.t[:], in_=src), writes=[x.b])

        def kslot(kind, i):
            return NK + i if kind == 'ctx' else i % NK

        def gen_qkv(pos):
            kind, i = qtasks[pos]
            m = 1 if kind == 'ctx' else 0
            x = xr[pos % 2]
            h = hT[pos % 2]
            need_q = not (kind == 'lat' and i >= NTQ)
            ks = kslot(kind, i)
            s = hs[pos % 2]
            c0 = 0 if need_q else 2
            sched = {}

            def at(step, fn):
                sched.setdefault(step, []).append(fn)
            st8 = {}

            def f_p0():
                if pos + 2 < len(qtasks):
                    issue_xload(pos + 2)

            def f_p1next():
                if pos + 1 < len(qtasks):
                    xn_next[0] = nm.part1(xr[(pos + 1) % 2])
            at(0, f_p0)
            xn_cur = xn_next[0]
            at(1, lambda: nm.part2(xn_cur, m, h, 0))
            at(11, f_p1next)
            blocks = ([0, 1] if need_q else []) + [2, 3, 4, 5]

            def f_blk_pe(nb):
                bank = next_bank()
                st8[('bank', nb)] = bank
                for kc in range(8):
                    p.op('pe', lambda e, bank=bank, kc=kc, nb=nb: e.matmul(
                        ps[:, bank, :], lhsT=h.t[:, kc, :], rhs=wqkv[:, kc, nb * 512:(nb + 1) * 512],
                        start=(kc == 0), stop=(kc == 7)),
                        reads=[h.b, h.b2, wq_b[nb]], writes=[pb[bank]], inc=(kc == 7))

            def f_blk_evac(nb):
                bank = st8[('bank', nb)]
                if nb < 4:
                    p.op('act', lambda e: e.activation(
                        out=qf.t[:, nb * 512:(nb + 1) * 512], in_=ps[:, bank, :], func=AF.Identity),
                        reads=[pb[bank]], writes=[qf.b])
                else:
                    vv = V[ks]
                    p.op('dve', lambda e: e.tensor_copy(
                        out=vv.t[:, (nb - 4) * 8:(nb - 4) * 8 + 8, 0:64],
                        in_=ps[:, bank, :].rearrange("p (a b) -> p a b", b=64)),
                        reads=[pb[bank]], writes=[vv.b])
            for k_, nb in enumerate(blocks):
                at(2 + k_, lambda nb=nb: f_blk_pe(nb))
                at(3 + k_, lambda nb=nb: f_blk_evac(nb))

            def f_sq():
                for nb in range(c0, 4):
                    p.op('act', lambda e, nb=nb: e.activation(out=sq.t[:], in_=qf.t[:, nb * 512:(nb + 1) * 512], func=AF.Square),
                         reads=[qf.b], writes=[sq.b])
                    p.op('dve', lambda e, nb=nb: e.tensor_reduce(out=s.t[:, nb * 8:nb * 8 + 8],
                                                                 in_=sq.t[:].rearrange("p (a b) -> p a b", b=64), axis=AX.X, op=ALU.add),
                         reads=[sq.b], writes=[s.b])
            nblk = len(blocks)
            at(3 + nblk, f_sq)
            at(5 + nblk, lambda: rstd_from_ss(p, s, c0 * 8, c0 * 8, c0 * 8, 1.0 / 64, n=(4 - c0) * 8))

            def f_mul():
                for nb in range(c0, 4):
                    cc = nb * 8
                    p.op('dve', lambda e, nb=nb, cc=cc: e.tensor_tensor(
                        out=qkt.t[:, nb * 512:(nb + 1) * 512].rearrange("p (a b) -> p a b", b=64),
                        in0=qf.t[:, nb * 512:(nb + 1) * 512].rearrange("p (a b) -> p a b", b=64),
                        in1=s.t[:, cc:cc + 8].unsqueeze(2).to_broadcast([128, 8, 64]), op=ALU.mult),
                        reads=[qf.b, s.b], writes=[qkt.b])
            at(7 + nblk, f_mul)

            def f_T_pe(which):
                for j in range(8):
                    p.op('pe', lambda e, j=j: e.transpose(
                        psb2[:, j * 128:(j + 1) * 128], qkt.t[:, which * D + j * 128:which * D + (j + 1) * 128], identb.t[:]),
                        reads=[qkt.b, identb.b], writes=[pb[2]], inc=(j == 7))

            def f_T_evac(which):
                dst = (QTc[i] if kind == 'ctx' else QT[i % NQ]) if which == 0 else KT[ks]
                if which == 0:
                    p.op('dve', lambda e: e.tensor_scalar(
                        out=dst.t[:].rearrange("p a b -> p (a b)"), in0=psb2[:, 0:1024], scalar1=qkg.t[:, 0:1], scalar2=0.125,
                        op0=ALU.mult, op1=ALU.mult), reads=[pb[2], qkg.b], writes=[dst.b])
                else:
                    p.op('dve', lambda e: e.tensor_scalar(
                        out=dst.t[:].rearrange("p a b -> p (a b)"), in0=psb2[:, 0:1024], scalar1=qkg.t[:, 1:2], scalar2=None,
                        op0=ALU.mult), reads=[pb[2], qkg.b], writes=[dst.b])
            t0_ = 9 + nblk
            for which in ([0, 1] if need_q else [1]):
                at(t0_, lambda which=which: f_T_pe(which))
                at(t0_, lambda which=which: f_T_evac(which))
                t0_ += 2
            for st_ in range(max(sched) + 1):
                for fn in sched.get(st_, []):
                    fn()
                yield

        hcount = [0]
        xn_next = [None]

        def gen_att(task):
            kind, t = task
            m = 1 if kind == 'ctx' else 0
            src = xsrc(task)
            if (kind, t) in (('ctx', 0), ('lat', 0)):
                load_modB(p, nc, env, G, 2 * D, 0, m)
            if kind == 'lat' and t in (0, 1):
                load_table(EBS, env.d('ebs')[t], 4)
            if kind == 'ctx':
                chunks = [('ctx', 0), ('ctx', 1)]
                nloc = 0
                tab = None
                q = KT
                qt = None
            else:
                if t in (0, 1):
                    kts = [0, 1, 2, 3]
                    tab = EBS
                else:
                    kts = [t - 2, t - 1, t, t + 1, t + 2]
                    tab = EB
                nloc = len(kts)
                chunks = [('lat', k) for k in kts] + [('ctx', 0), ('ctx', 1)]
            if kind == 'ctx':
                qsrc = QTc[t]
            else:
                qsrc = QT[t % NQ]
            nch = len(chunks)

            def ppos(ci):
                return (0, ci * 128) if ci < 4 else (1, (ci - 4) * 128)
            hn0 = hcount[0]
            hcount[0] += 16

            def emit_qk(h):
                hp, pbs = h // 2, 64 * (h % 2)
                hn = hn0 + h
                sbank = (3 + 2 * (hn % 2), 4 + 2 * (hn % 2))
                for ci, (ck, kt) in enumerate(chunks):
                    bi, col = ppos(ci)
                    bank = sbank[bi]
                    kk = KT[kslot(ck, kt)]
                    last = (ci == nch - 1) or (ci == 3)
                    p.op('pe', lambda e, bank=bank, col=col, kk=kk, hp=hp, pbs=pbs: e.matmul(
                        ps[:, bank, col:col + 128], lhsT=kk.t[pbs:pbs + 64, hp, :], rhs=qsrc.t[pbs:pbs + 64, hp, :],
                        start=True, stop=True),
                        reads=[kk.b, qsrc.b], writes=[pb[bank]], inc=last)

            def emit_exp(h):
                hn = hn0 + h
                sb0 = 3 + 2 * (hn % 2)
                pt = PT[hn % 3]
                S2 = ps[:, sb0:sb0 + 2, :].rearrange("p b c -> p (b c)")
                rb = [pb[sb0]] + ([pb[sb0 + 1]] if nch > 4 else [])
                p.op('act', lambda e: e.activation(out=pt.t[:, 0:nch * 128], in_=S2[:, 0:nch * 128], func=AF.Exp),
                     reads=rb, writes=[pt.b])

            def emit_mult(h):
                hn = hn0 + h
                pt = PT[hn % 3]
                if nloc > 0:
                    p.op('dve', lambda e: e.tensor_tensor(
                        out=pt.t[:, 0:nloc * 128], in0=pt.t[:, 0:nloc * 128],
                        in1=tab.t[:, h, 0:nloc, :].rearrange("p a b -> p (a b)"), op=ALU.mult),
                        reads=[pt.b, tab.b], writes=[pt.b])

            def emit_pv(h):
                hn = hn0 + h
                pt = PT[hn % 3]
                g = h % 4
                for ci, (ck, kt) in enumerate(chunks):
                    off = ci * 128
                    vv = V[kslot(ck, kt)]
                    p.op('pe', lambda e, off=off, vv=vv, g=g, h=h, pt=pt, ci=ci: e.matmul(
                        ps[:, 7, g * 65:(g + 1) * 65], lhsT=pt.t[:, off:off + 128], rhs=vv.t[:, h, :],
                        start=(ci == 0), stop=(ci == nch - 1)),
                        reads=[pt.b, vv.b], writes=[pb[7]], inc=(ci == nch - 1))
                if g == 3:
                    orw = oraw[(hn // 4) % 2]
                    p.op('act', lambda e: e.activation(out=orw.t[:], in_=ps[:, 7, 0:260], func=AF.Identity),
                         reads=[pb[7]], writes=[orw.b])
                    o4 = orw.t[:].rearrange("p (g c) -> p g c", c=65)
                    p.op('dve', lambda e, o4=o4: e.reciprocal(out=rc.t[:, 0:4].unsqueeze(2), in_=o4[:, :, 64:65]),
                         reads=[orw.b], writes=[rc.b])
                    p.op('pool', lambda e, o4=o4, h=h: e.tensor_tensor(
                        out=otok.t[:, (h - 3) * 64:(h + 1) * 64].rearrange("p (a b) -> p a b", b=64),
                        in0=o4[:, :, 0:64], in1=rc.t[:, 0:4].unsqueeze(2).to_broadcast([128, 4, 64]), op=ALU.mult),
                        reads=[orw.b, rc.b], writes=[otok.b])

            emit_qk(0)
            for s_ in range(18):
                if s_ < 16:
                    emit_exp(s_)
                if 1 <= s_ <= 16:
                    emit_mult(s_ - 1)
                if s_ + 1 < 16:
                    emit_qk(s_ + 1)
                if s_ >= 2:
                    emit_pv(s_ - 2)
                yield
            for j in range(8):
                p.op('pe', lambda e, j=j: e.transpose(psb2[:, j * 128:(j + 1) * 128], otok.t[:, j * 128:(j + 1) * 128],
                                                      identb.t[:]),
                     reads=[otok.b, identb.b], writes=[pb[2]], inc=(j == 7))
            p.op('act', lambda e: e.activation(out=oT.t[:].rearrange("p a b -> p (a b)"), in_=psb2[:, 0:1024],
                                               func=AF.Identity), reads=[pb[2]], writes=[oT.b])
            yield
            p.dma('sp', lambda e: e.dma_start(out=xres.t[:], in_=src), writes=[xres.b])
            for nb in range(2):
                bank = next_bank()
                for kc in range(8):
                    p.op('pe', lambda e, bank=bank, kc=kc, nb=nb: e.matmul(
                        ps[:, bank, :], lhsT=oT.t[:, kc, :], rhs=wo[:, kc, nb * 512:(nb + 1) * 512],
                        start=(kc == 0), stop=(kc == 7)),
                        reads=[oT.b, wo_b[nb]], writes=[pb[bank]], inc=(kc == 7))
                p.op('dve', lambda e, bank=bank, nb=nb: e.tensor_tensor(
                    out=tmp.t[:, nb * 512:(nb + 1) * 512], in0=ps[:, bank, :], in1=G.t[:, nb * 512:(nb + 1) * 512], op=ALU.mult),
                    reads=[pb[bank], G.b], writes=[tmp.b])
                p.op('pool', lambda e, nb=nb: e.tensor_tensor(
                    out=tmp.t[:, nb * 512:(nb + 1) * 512], in0=tmp.t[:, nb * 512:(nb + 1) * 512],
                    in1=xres.t[:, nb * 512:(nb + 1) * 512], op=ALU.add),
                    reads=[tmp.b, xres.b], writes=[tmp.b])
                yield
            dname = 'xa_ctx' if kind == 'ctx' else 'xa'
            dst = env.d(dname)[t * 128:(t + 1) * 128, :]
            p.dma('sp', lambda e: e.dma_start(out=dst, in_=tmp.t[:]), reads=[tmp.b], writes=[env.b(dname, t)])
            yield

        QTc = [SB(sb(f'QTc{i}', [128, 8, 128], BF16)) for i in range(2)]
        atasks = [(('ctx', 0), 1), (('ctx', 1), 1)] + [(('lat', t), 2 + min(t + 3, NTK - 1)) for t in range(NTQ)]
        issue_xload(0)
        issue_xload(1)
        xn_next[0] = nm.part1(xr[0])
        ai = 0
        for pos in range(len(qtasks)):
            gens = [gen_qkv(pos)]
            if ai < len(atasks) and atasks[ai][1] <= pos - 1:
                gens.append(gen_att(atasks[ai][0]))
                ai += 1
            interleave(gens)
        while ai < len(atasks):
            interleave([gen_att(atasks[ai][0])])
            ai += 1
    p.barrier()


def ffn_prefetch(p, nc, env, st, layer):
    win = st.enter_context(nc.sbuf_tensor(_uniq('win'), [128, 8, 2 * DFF], BF16))
    win_b = [Buf() for _ in range(11)]
    wsrc = env.d('ffn_w_in')[layer]
    for s_ in [0, 5, 6, 1, 7, 2, 8, 3, 9, 4, 10]:
        srcw = wsrc[:, s_ * 512:(s_ + 1) * 512].rearrange("(c p) n -> p c n", p=128)
        p.dma('pool', lambda e, s_=s_, srcw=srcw: e.dma_start(out=win[:, :, s_ * 512:(s_ + 1) * 512], in_=srcw),
              writes=[win_b[s_]])
    return win, win_b


def make_ffn(layer, src, dst, with_ctx):
    def ph(p, nc, env, ps, pb):
        with ExitStack() as st:
            sb = lambda n, s, d: st.enter_context(nc.sbuf_tensor(_uniq(n), s, d))
            key = 'ffn%d_pre' % layer
            if key in env.pre:
                win, win_b = env.pre.pop(key)
            else:
                win, win_b = ffn_prefetch(p, nc, env, st, layer)
            wout = sb('wout', [128, 22, D], BF16)
            wout_b = [Buf() for _ in range(2)]
            cast_load_w(p, nc, wout, env.d('ffn_w_out')[layer], DFF, D, wout_b, split=2)
            sg = [SB(sb(f'sg{i}', [128, 512], F32)) for i in range(2)]
            nm = NormMod(p, nc, st, env, ps, pb, layer, 1, [0, 1] if with_ctx else [0], (0, 1), 'f%d' % layer, nxn=2,
                         junk=(sg[0].t[:].bitcast(BF16), sg[0].b))
            hid = SB(sb('hid', [128, 22, 512], BF16))
            h2T = [SB(sb(f'h2T{i}', [128, 8, 512], BF16)) for i in range(2)]
            xr = [SB(sb(f'xr{i}', [128, D], F32)) for i in range(2)]
            tmp = SB(sb('tmp', [128, D], F32))
            xe = SB(sb('xe', [128, D], F32))
            G = SB(sb('G', [128, D], F32))
            sts = []
            if with_ctx:
                sts.append([(src + '_ctx', dst + '_ctx', 0, 1), (src + '_ctx', dst + '_ctx', 1, 1)])
            for s_ in range(NTQ // 4):
                sts.append([(src, dst, s_ * 4 + j, 0) for j in range(4)])
            xcnt = [0]

            def gen_A(si):
                tl = sts[si]
                xns = {}

                def p1(j):
                    sn, dn, ti, m = tl[j]
                    x = xr[xcnt[0] % 2]
                    xcnt[0] += 1
                    srcx = env.d(sn)[ti * 128:(ti + 1) * 128, :]
                    p.dma('sp', lambda e, x=x, srcx=srcx: e.dma_start(out=x.t[:], in_=srcx), reads=[env.b(sn, ti)], writes=[x.b])
                    xns[j] = nm.part1(x)
                p1(0)
                yield
                yield
                for j in range(len(tl)):
                    if j + 1 < len(tl):
                        p1(j + 1)
                    yield
                    yield
                    yield
                    nm.part2(xns[j], tl[j][3], h2T[si % 2], j * 128)
                    yield

            pair = [0]

            def gen_BC(si):
                tiles = sts[si]
                ntok = len(tiles) * 128
                h = h2T[si % 2]
                m = tiles[0][3]
                if si == 0 or (with_ctx and si == 1):
                    load_modB(p, nc, env, G, 5 * D, layer, m)
                for oc in range(22):
                    bA = 2 + 2 * (pair[0] % 3)
                    bU = bA + 1
                    pair[0] += 1
                    for (bank, c0) in ((bA, oc * 128), (bU, DFF + oc * 128)):
                        for kc in range(8):
                            p.op('pe', lambda e, bank=bank, c0=c0, kc=kc: e.matmul(
                                ps[:, bank, 0:ntok], lhsT=win[:, kc, c0:c0 + 128], rhs=h.t[:, kc, 0:ntok],
                                start=(kc == 0), stop=(kc == 7)),
                                reads=[h.b, h.b2, win_b[c0 // 512]], writes=[pb[bank]], inc=(kc == 7))
                    s2 = sg[oc % 2]
                    p.op('act', lambda e, bA=bA, s2=s2: e.activation(out=s2.t[:, 0:ntok], in_=ps[:, bA, 0:ntok], func=AF.Silu),
                         reads=[pb[bA]], writes=[s2.b])
                    p.op('dve', lambda e, bU=bU, s2=s2, oc=oc: e.tensor_tensor(
                        out=hid.t[:, oc, 0:ntok], in0=s2.t[:, 0:ntok], in1=ps[:, bU, 0:ntok], op=ALU.mult),
                        reads=[s2.b, pb[bU]], writes=[hid.b])
                    yield
                for j, (sn, dn, ti, m) in enumerate(tiles):
                    srcx = env.d(sn)[ti * 128:(ti + 1) * 128, :]
                    p.dma('sp', lambda e, srcx=srcx: e.dma_start(out=xe.t[:], in_=srcx), reads=[env.b(sn, ti)], writes=[xe.b])
                    for nb in range(2):
                        bank = nb
                        for oc in range(22):
                            p.op('pe', lambda e, bank=bank, oc=oc, nb=nb, j=j: e.matmul(
                                ps[:, bank, :], lhsT=hid.t[:, oc, j * 128:(j + 1) * 128], rhs=wout[:, oc, nb * 512:(nb + 1) * 512],
                                start=(oc == 0), stop=(oc == 21)),
                                reads=[hid.b, wout_b[nb]], writes=[pb[bank]], inc=(oc == 21))
                        p.op('dve', lambda e, bank=bank, nb=nb: e.tensor_tensor(
                            out=tmp.t[:, nb * 512:(nb + 1) * 512], in0=ps[:, bank, :], in1=G.t[:, nb * 512:(nb + 1) * 512],
                            op=ALU.mult), reads=[pb[bank], G.b], writes=[tmp.b])
                        p.op('pool', lambda e, nb=nb: e.tensor_tensor(
                            out=tmp.t[:, nb * 512:(nb + 1) * 512], in0=tmp.t[:, nb * 512:(nb + 1) * 512],
                            in1=xe.t[:, nb * 512:(nb + 1) * 512], op=ALU.add), reads=[tmp.b, xe.b], writes=[tmp.b])
                        yield
                    dstx = env.d(dn)[ti * 128:(ti + 1) * 128, :]
                    p.dma('sp', lambda e, dstx=dstx: e.dma_start(out=dstx, in_=tmp.t[:]), reads=[tmp.b], writes=[env.b(dn, ti)])

            interleave([gen_A(0)])
            for si in range(len(sts)):
                gens = [gen_BC(si)]
                if si + 1 < len(sts):
                    gens.append(gen_A(si + 1))
                interleave(gens)
        p.barrier()
    return ph


class ScanState:
    def __init__(self, sbf, tag):
        self.S32 = sbf('S32' + tag, [128, 8, 128], F32)
        self.Sbf = [sbf(f'Sbf{i}' + tag, [128, 8, 128], BF16) for i in range(2)]
        self.b32 = [Buf() for _ in range(8)]
        self.bbf = [[Buf() for _ in range(8)] for _ in range(2)]
        self.n = 0


def scan_chunk(p, nc, ps, pb, banks, qt, qe, kn, kh, vt, ebend, ST, msk, c, fwd, o_out):
    bA, bO0, bO1, bU0, bU1 = banks
    par = c % 2
    pbs = 64 * par
    mcol = pbs if fwd else 128 + pbs
    cur = ST.n % 2
    nxt = 1 - cur
    if o_out is not None:
        aT = o_out['aT']
        for hd in range(8):
            (q_ap, q_b), (k_ap, k_b) = qt(hd), kn(hd)
            p.op('pe', lambda e, hd=hd, q_ap=q_ap, k_ap=k_ap: e.matmul(
                ps[pbs:pbs + 64, bA, hd * 64:(hd + 1) * 64], lhsT=k_ap, rhs=q_ap, start=True, stop=True),
                reads=[q_b, k_b], writes=[pb[bA]], inc=(hd == 7))
    for hd in range(8):
        bank = bU0 if hd < 4 else bU1
        kh_ap, kh_b = kh(hd)
        v_ap, v_b = vt(hd)
        p.op('pe', lambda e, hd=hd, bank=bank, kh_ap=kh_ap, v_ap=v_ap: e.matmul(
            ps[:, bank, (hd % 4) * 128:(hd % 4 + 1) * 128], lhsT=kh_ap, rhs=v_ap, start=True, stop=True),
            reads=[kh_b, v_b], writes=[pb[bank]], inc=(hd % 4 == 3))
    if o_out is not None:
        p.op('dve', lambda e: e.copy_predicated(
            out=aT.t[pbs:pbs + 64, :].rearrange("p (a b) -> p a b", b=64),
            mask=msk.t[pbs:pbs + 64, mcol:mcol + 64].bitcast(mybir.dt.uint32).unsqueeze(1).to_broadcast([64, 8, 64]),
            data=ps[pbs:pbs + 64, bA, :].rearrange("p (a b) -> p a b", b=64)),
            reads=[pb[bA], msk.b], writes=[aT.b])
        for hd in range(8):
            bank = bO0 if hd < 4 else bO1
            (q_ap, q_b) = qe(hd)
            v_ap, v_b = vt(hd)
            oap = ps[pbs:pbs + 64, bank, (hd % 4) * 128:(hd % 4 + 1) * 128]
            p.op('pe', lambda e, hd=hd, oap=oap, v_ap=v_ap: e.matmul(
                oap, lhsT=aT.t[pbs:pbs + 64, hd * 64:(hd + 1) * 64], rhs=v_ap, start=True, stop=False),
                reads=[aT.b, v_b], writes=[pb[bank]], inc=False)
            p.op('pe', lambda e, hd=hd, oap=oap, q_ap=q_ap: e.matmul(
                oap, lhsT=q_ap, rhs=ST.Sbf[cur][:, hd, :], start=False, stop=True),
                reads=[q_b, ST.bbf[cur][hd]], writes=[pb[bank]], inc=(hd % 4 == 3))
        ot = o_out['o']
        add = o_out.get('add')
        for half, bank in ((0, bO0), (1, bO1)):
            if add is None:
                p.op('act', lambda e, half=half, bank=bank: e.activation(
                    out=ot.t[pbs:pbs + 64, half * 512:(half + 1) * 512], in_=ps[pbs:pbs + 64, bank, :], func=AF.Identity),
                    reads=[pb[bank]], writes=[ot.b])
            else:
                p.op('dve', lambda e, half=half, bank=bank: e.tensor_tensor(
                    out=ot.t[pbs:pbs + 64, half * 512:(half + 1) * 512], in0=ps[pbs:pbs + 64, bank, :],
                    in1=add.t[pbs:pbs + 64, half * 512:(half + 1) * 512], op=ALU.add),
                    reads=[pb[bank], add.b], writes=[ot.b])
    for hd in range(8):
        bank = bU0 if hd < 4 else bU1
        eb_ap, eb_b = ebend(hd)
        p.op('dve', lambda e, hd=hd, bank=bank, eb_ap=eb_ap: e.scalar_tensor_tensor(
            out=ST.S32[:, hd, :], in0=ST.S32[:, hd, :], scalar=eb_ap, in1=ps[:, bank, (hd % 4) * 128:(hd % 4 + 1) * 128],
            op0=ALU.mult, op1=ALU.add), reads=[ST.b32[hd], eb_b, pb[bank]], writes=[ST.b32[hd]])
        p.op('act', lambda e, hd=hd: e.activation(out=ST.Sbf[nxt][:, hd, :], in_=ST.S32[:, hd, :], func=AF.Identity),
             reads=[ST.b32[hd]], writes=[ST.bbf[nxt][hd]])
    ST.n += 1


def ph_hg1(p, nc, env, ps, pb):
    with ExitStack() as st:
        sb = lambda n, s, d: st.enter_context(nc.sbuf_tensor(_uniq(n), s, d))
        T = 512
        w = sb('hw', [128, 8, 5 * D], BF16)
        w_b = [Buf() for _ in range(10)]
        cast_load_w(p, nc, w, env.d('hg_w_in'), D, 5 * D, w_b, split=10)
        identb = SB(sb('identb', [128, 128], BF16))
        p.dma('pool', lambda e: e.dma_start(out=identb.t[:], in_=env.d('identf')[:, :]), writes=[identb.b])
        msk = SB(sb('msk', [128, 256], F32))
        p.dma('sp', lambda e: e.dma_start(out=msk.t[:], in_=env.d('trimask')[:, :]), writes=[msk.b])
        rst = SB(sb('rst', [128, 512], BF16))
        p.dma('pool', lambda e: e.dma_start(out=rst.t[:], in_=env.d('rstmask')[:, :]), writes=[rst.b])
        low = SB(sb('low', [128, 32], F32))
        p.dma('sp', lambda e: e.dma_start(out=low.t[:], in_=env.d('hg_low')[:, :]), writes=[low.b])
        lbt = SB(sb('lbt', [128, 48], F32))
        p.op('dve', lambda e: e.tensor_tensor(out=lbt.t[:, 0:16], in0=low.t[:, 0:16], in1=low.t[:, 16:32], op=ALU.subtract),
             reads=[low.b], writes=[lbt.b])
        p.op('act', lambda e: e.activation(out=lbt.t[:, 16:32], in_=lbt.t[:, 0:16], func=AF.Sigmoid, scale=-1.0),
             reads=[lbt.b], writes=[lbt.b])
        p.op('act', lambda e: e.activation(out=lbt.t[:, 32:48], in_=lbt.t[:, 0:16], func=AF.Sigmoid),
             reads=[lbt.b], writes=[lbt.b])
        jk = SB(sb('jk', [128, 512], F32))
        nm = NormMod(p, nc, st, env, ps, pb, 1, 0, [0, 1], (0, 1), 'h1', nxn=1, junk=(jk.t[:].bitcast(BF16), jk.b))
        xr = [SB(sb(f'xr{i}', [128, D], F32)) for i in range(2)]
        hT = SB(sb('hT', [128, 8, T], BF16))
        tq = [SB(sb(f'tq{i}', [128, T], F32)) for i in range(2)]
        ta = [SB(sb(f'ta{i}', [128, T], F32)) for i in range(2)]
        tb = [SB(sb(f'tb{i}', [128, T], F32)) for i in range(2)]
        tc = [SB(sb(f'tc{i}', [128, T], F32)) for i in range(2)]
        td = [SB(sb(f'td{i}', [128, T], F32)) for i in range(2)]
        te = [SB(sb(f'te{i}', [128, T], F32)) for i in range(2)]
        tkh = [SB(sb(f'tkh{i}', [128, T], BF16)) for i in range(2)]
        QE1 = SB(sb('QE1', [128, 8, T], BF16))
        QT1 = SB(sb('QT1', [128, 8, T], BF16))
        KN1 = SB(sb('KN1', [128, 8, T], BF16))
        KH1 = SB(sb('KH1', [128, 4, D], BF16))
        EB1 = SB(sb('EB1', [128, 8, 8], F32))
        s2q = [SB(sb(f's2q{i}', [128, T], BF16)) for i in range(2)]
        s2k = [SB(sb(f's2k{i}', [128, T], BF16)) for i in range(2)]
        s2t = [SB(sb(f's2t{i}', [128, T], BF16)) for i in range(2)]
        KH2 = SB(sb('KH2', [128, 4, D], BF16))
        EB2 = SB(sb('EB2', [128, 8, 8], F32))
        Vt = SB(sb('Vt', [128, 4, D], BF16))
        SGt = SB(sb('SGt', [128, 4, D], BF16))
        ST = ScanState(sb, 'h1')
        aT = SB(sb('aT', [128, 512], BF16))
        o1t = [SB(sb(f'o1t{i}', [128, D], F32)) for i in range(1)]
        p.op('pool', lambda e: e.memset(aT.t[:], 0.0), writes=[aT.b])
        p.op('pool', lambda e: e.memset(ST.S32[:], 0.0), writes=ST.b32)
        p.op('pool', lambda e: e.memset(ST.Sbf[0][:], 0.0), writes=ST.bbf[0])
        psb = lambda bank: ps[:, bank, :].bitcast(BF16)

        sts = [[('xb_ctx', 0, 1), ('xb_ctx', 1, 1)]] + [[('xb', s_ * 4 + j, 0) for j in range(4)] for s_ in range(NTQ // 4)]
        xcnt = [0]
        rot = [0]

        def nbank():
            rot[0] = (rot[0] + 1) % 3
            return 2 + rot[0]

        b3cnt = [0]

        def gen_A(si):
            for j, (sn, ti, m) in enumerate(sts[si]):
                x = xr[xcnt[0] % 2]
                xcnt[0] += 1
                srcx = env.d(sn)[ti * 128:(ti + 1) * 128, :]
                p.dma('sp', lambda e, x=x, srcx=srcx: e.dma_start(out=x.t[:], in_=srcx), reads=[env.b(sn, ti)], writes=[x.b])
                xn_ = nm.part1(x)
                yield
                yield
                nm.part2(xn_, m, hT, j * 128)
                yield

        def do_supertile(si, tiles):
            nt = len(tiles)
            Tn = nt * 128
            nch = Tn // 64
            is_ctx = (si == 0)
            tok0 = None if is_ctx else tiles[0][1] * 128

            def emit_b3(j, nb):
                if True:
                    b3cnt[0] ^= 1
                    bank = b3cnt[0]
                    c0 = D + nb * 512
                    for kc in range(8):
                        p.op('pe', lambda e, bank=bank, kc=kc, c0=c0, j=j: e.matmul(
                            ps[:, bank, :], lhsT=hT.t[:, kc, j * 128:(j + 1) * 128], rhs=w[:, kc, c0:c0 + 512],
                            start=(kc == 0), stop=(kc == 7)), reads=[hT.b, hT.b2, w_b[c0 // 512]], writes=[pb[bank]], inc=(kc == 7))
                    if nb < 2:
                        p.op('dve', lambda e, bank=bank, j=j, nb=nb: e.tensor_copy(
                            out=Vt.t[:, j, nb * 512:(nb + 1) * 512], in_=ps[:, bank, :]), reads=[pb[bank]], writes=[Vt.b])
                    elif not is_ctx:
                        p.op('act', lambda e, bank=bank, j=j, nb=nb: e.activation(
                            out=SGt.t[:, j, (nb - 2) * 512:(nb - 1) * 512], in_=ps[:, bank, :], func=AF.Silu),
                            reads=[pb[bank]], writes=[SGt.b])
            b3_groups = [(j, nb) for j in range(nt) for nb in range(4)]
            def chain(hd, d, q_):
                a_, b_, c_, d_, e_, kh_ = ta[d], tb[d], tc[d], td[d], te[d], tkh[d]
                bank = 3 if d == 0 else 4
                lcol = d * 8 + hd
                lb_ap = lbt.t[:, 16 + lcol:17 + lcol]
                oml_ap = lbt.t[:, 32 + lcol:33 + lcol]
                p.op('act', lambda e: e.activation(out=a_.t[:, 0:Tn], in_=ps[:, bank, 0:Tn], func=AF.Sigmoid),
                     reads=[pb[bank]], writes=[a_.b])
                p.op('act', lambda e: e.activation(out=b_.t[:, 0:Tn], in_=ps[:, bank, 0:Tn], func=AF.Sigmoid, scale=-1.0),
                     reads=[pb[bank]], writes=[b_.b])
                p.op('act', lambda e: e.activation(out=a_.t[:, 0:Tn], in_=a_.t[:, 0:Tn], func=AF.Ln, scale=oml_ap, bias=lb_ap),
                     reads=[a_.b, lbt.b], writes=[a_.b])
                yield
                p.op('dve', lambda e: e.tensor_tensor_scan(
                    out=c_.t[:, 0:Tn], data0=rst.t[:, 0:Tn], data1=a_.t[:, 0:Tn], initial=0.0, op0=ALU.mult, op1=ALU.add),
                    reads=[rst.b, a_.b], writes=[c_.b])
                c3 = c_.t[:, 0:Tn].rearrange("p (c t) -> p c t", t=64)
                bend = c3[:, :, 63:64]
                p.op('dve', lambda e: e.tensor_tensor(
                    out=d_.t[:, 0:Tn].rearrange("p (c t) -> p c t", t=64), in0=bend.to_broadcast([128, nch, 64]), in1=c3,
                    op=ALU.subtract), reads=[c_.b], writes=[d_.b])
                ebt = EB1 if d == 0 else EB2
                yield
                p.op('act', lambda e: e.activation(out=ebt.t[:, hd, 0:nch].unsqueeze(2), in_=bend, func=AF.Exp),
                     reads=[c_.b], writes=[ebt.b])
                if d == 0:
                    p.op('act', lambda e: e.activation(out=e_.t[:, 0:Tn], in_=c_.t[:, 0:Tn], func=AF.Exp), reads=[c_.b], writes=[e_.b])
                    yield
                    p.op('dve', lambda e: e.tensor_tensor(out=QE1.t[:, hd, 0:Tn], in0=q_.t[:, 0:Tn], in1=e_.t[:, 0:Tn], op=ALU.mult),
                         reads=[q_.b, e_.b], writes=[QE1.b])
                    p.op('dve', lambda e: e.tensor_tensor(
                        out=e_.t[:, 0:Tn].rearrange("p (c t) -> p c t", t=64), in0=c3,
                        in1=c3[:, :, 31:32].to_broadcast([128, nch, 64]), op=ALU.subtract), reads=[c_.b], writes=[e_.b])
                    yield
                    p.op('act', lambda e: e.activation(out=a_.t[:, 0:Tn], in_=e_.t[:, 0:Tn], func=AF.Exp), reads=[e_.b], writes=[a_.b])
                    p.op('act', lambda e: e.activation(out=e_.t[:, 0:Tn], in_=e_.t[:, 0:Tn], func=AF.Exp, scale=-1.0),
                         reads=[e_.b], writes=[e_.b])
                    p.op('act', lambda e: e.activation(out=d_.t[:, 0:Tn], in_=d_.t[:, 0:Tn], func=AF.Exp), reads=[d_.b], writes=[d_.b])
                    yield
                    p.op('dve', lambda e: e.tensor_tensor(out=QT1.t[:, hd, 0:Tn], in0=q_.t[:, 0:Tn], in1=a_.t[:, 0:Tn], op=ALU.mult),
                         reads=[q_.b, a_.b], writes=[QT1.b])
                    p.op('dve', lambda e: e.scalar_tensor_tensor(
                        out=KN1.t[:, hd, 0:Tn], in0=b_.t[:, 0:Tn], scalar=oml_ap, in1=e_.t[:, 0:Tn], op0=ALU.mult, op1=ALU.mult),
                        reads=[b_.b, e_.b, lbt.b], writes=[KN1.b])
                    p.op('dve', lambda e: e.scalar_tensor_tensor(
                        out=kh_.t[:, 0:Tn], in0=b_.t[:, 0:Tn], scalar=oml_ap, in1=d_.t[:, 0:Tn], op0=ALU.mult, op1=ALU.mult),
                        reads=[b_.b, d_.b, lbt.b], writes=[kh_.b])
                    KH = KH1
                else:
                    p.op('dve', lambda e: e.tensor_tensor(out=d_.t[:, 0:Tn], in0=d_.t[:, 0:Tn], in1=a_.t[:, 0:Tn], op=ALU.add),
                         reads=[d_.b, a_.b], writes=[d_.b])
                    sq_, sk_, st_ = s2q[hd % 2], s2k[hd % 2], s2t[hd % 2]
                    d3 = d_.t[:, 0:Tn].rearrange("p (c t) -> p c t", t=64)
                    p.op('dve', lambda e: e.tensor_tensor(
                        out=e_.t[:, 0:Tn].rearrange("p (c t) -> p c t", t=64), in0=d3,
                        in1=d3[:, :, 32:33].to_broadcast([128, nch, 64]), op=ALU.subtract), reads=[d_.b], writes=[e_.b])
                    p.op('dve', lambda e: e.tensor_tensor(out=c_.t[:, 0:Tn], in0=c_.t[:, 0:Tn], in1=a_.t[:, 0:Tn], op=ALU.subtract),
                         reads=[c_.b, a_.b], writes=[c_.b])
                    yield
                    p.op('act', lambda e: e.activation(out=d_.t[:, 0:Tn], in_=d_.t[:, 0:Tn], func=AF.Exp), reads=[d_.b], writes=[d_.b])
                    p.op('act', lambda e: e.activation(out=a_.t[:, 0:Tn], in_=e_.t[:, 0:Tn], func=AF.Exp), reads=[e_.b], writes=[a_.b])
                    p.op('act', lambda e: e.activation(out=e_.t[:, 0:Tn], in_=e_.t[:, 0:Tn], func=AF.Exp, scale=-1.0),
                         reads=[e_.b], writes=[e_.b])
                    p.op('act', lambda e: e.activation(out=c_.t[:, 0:Tn], in_=c_.t[:, 0:Tn], func=AF.Exp), reads=[c_.b], writes=[c_.b])
                    yield
                    p.op('dve', lambda e: e.tensor_tensor(out=sq_.t[:, 0:Tn], in0=q_.t[:, 0:Tn], in1=d_.t[:, 0:Tn], op=ALU.mult),
                         reads=[q_.b, d_.b], writes=[sq_.b])
                    p.op('dve', lambda e: e.tensor_tensor(out=st_.t[:, 0:Tn], in0=q_.t[:, 0:Tn], in1=a_.t[:, 0:Tn], op=ALU.mult),
                         reads=[q_.b, a_.b], writes=[st_.b])
                    p.op('dve', lambda e: e.scalar_tensor_tensor(
                        out=sk_.t[:, 0:Tn], in0=b_.t[:, 0:Tn], scalar=oml_ap, in1=e_.t[:, 0:Tn], op0=ALU.mult, op1=ALU.mult),
                        reads=[b_.b, e_.b, lbt.b], writes=[sk_.b])
                    p.op('dve', lambda e: e.scalar_tensor_tensor(
                        out=kh_.t[:, 0:Tn], in0=b_.t[:, 0:Tn], scalar=oml_ap, in1=c_.t[:, 0:Tn], op0=ALU.mult, op1=ALU.mult),
                        reads=[b_.b, c_.b, lbt.b], writes=[kh_.b])
                    for (nm_, tl) in (('d2qe', sq_), ('d2kn', sk_), ('d2qt', st_)):
                        dstd = env.d(nm_)[hd, :, tok0:tok0 + Tn]
                        p.dma('sp', lambda e, dstd=dstd, tl=tl: e.dma_start(out=dstd, in_=tl.t[:, 0:Tn]), reads=[tl.b],
                              writes=[env.b(nm_, si)])
                    KH = KH2
                pendT.append(lambda: emit_T(hd, d, kh_, KH))
                yield

            def emit_T(hd, d, kh_, KH):
                tbk = 5 + d
                for j in range(nt):
                    p.op('pe', lambda e, j=j: e.transpose(
                        psb(tbk)[:, j * 128:(j + 1) * 128], kh_.t[:, j * 128:(j + 1) * 128], identb.t[:]),
                        reads=[kh_.b, identb.b], writes=[pb[tbk]], inc=(j == nt - 1))
                p.op('dve', lambda e: e.tensor_copy(
                    out=KH.t[:, 0:nt, hd * 128:(hd + 1) * 128],
                    in_=psb(tbk)[:, 0:nt * 128].rearrange("p (j k) -> p j k", k=128)),
                    reads=[pb[tbk]], writes=[KH.b])

            pendT = []
            for hd in range(8):
                bq, bf1, bf2 = 2, 3, 4
                for (bank, c0) in ((bq, hd * 128), (bf1, 3 * D + hd * 128), (bf2, 4 * D + hd * 128)):
                    if is_ctx and bank == bf2:
                        continue
                    for kc in range(8):
                        p.op('pe', lambda e, bank=bank, kc=kc, c0=c0: e.matmul(
                            ps[:, bank, 0:Tn], lhsT=w[:, kc, c0:c0 + 128], rhs=hT.t[:, kc, 0:Tn],
                            start=(kc == 0), stop=(kc == 7)), reads=[hT.b, hT.b2, w_b[c0 // 512]], writes=[pb[bank]], inc=(kc == 7))
                q_ = tq[hd % 2]
                p.op('act', lambda e, q_=q_: e.activation(out=q_.t[:, 0:Tn], in_=ps[:, bq, 0:Tn], func=AF.Silu),
                     reads=[pb[bq]], writes=[q_.b])
                ng_ = (len(b3_groups) + 7 - hd) // (8 - hd)
                for _ in range(min(ng_, len(b3_groups))):
                    emit_b3(*b3_groups.pop(0))
                for f in pendT:
                    f()
                pendT.clear()

                def delayed(g, n):
                    for _ in range(n):
                        yield
                    yield from g
                gens = [chain(hd, 0, q_)]
                if not is_ctx:
                    gens.append(delayed(chain(hd, 1, q_), 2))
                interleave(gens)
            for f in pendT:
                f()
            pendT.clear()
            while b3_groups:
                emit_b3(*b3_groups.pop(0))
            if not is_ctx:
                for (nm_, tl) in (('vtok', Vt), ('sgtok', SGt)):
                    dstd = env.d(nm_)[tok0:tok0 + Tn, :].rearrange("(j p) n -> p j n", p=128)
                    p.dma('sp', lambda e, dstd=dstd, tl=tl: e.dma_start(out=dstd, in_=tl.t[:, 0:nt, :]), reads=[tl.b],
                          writes=[env.b(nm_, si)])
            if not is_ctx:
                dstd = env.d('d2kh')[tok0:tok0 + Tn, :].rearrange("(j p) n -> p j n", p=128)
                p.dma('sp', lambda e, dstd=dstd: e.dma_start(out=dstd, in_=KH2.t[:, 0:nt, :]), reads=[KH2.b],
                      writes=[env.b('d2kh', si)])
                ch0 = tok0 // 64
                dstd = env.d('d2eb')[:, :, ch0:ch0 + nch]
                p.dma('sp', lambda e, dstd=dstd: e.dma_start(out=dstd, in_=EB2.t[:, :, 0:nch]), reads=[EB2.b],
                      writes=[env.b('d2eb', si)])
            return gen_C(si, tiles, nt, nch, is_ctx)

        def gen_C(si, tiles, nt, nch, is_ctx):
            for c in range(nch):
                j = c // 2
                pbs = 64 * (c % 2)
                qt = lambda hd, c=c: (QT1.t[:, hd, c * 64:(c + 1) * 64], QT1.b)
                qe = lambda hd, c=c: (QE1.t[:, hd, c * 64:(c + 1) * 64], QE1.b)
                kn = lambda hd, c=c: (KN1.t[:, hd, c * 64:(c + 1) * 64], KN1.b)
                kh = lambda hd, j=j, pbs=pbs: (KH1.t[pbs:pbs + 64, j, hd * 128:(hd + 1) * 128], KH1.b)
                vt = lambda hd, j=j, pbs=pbs: (Vt.t[pbs:pbs + 64, j, hd * 128:(hd + 1) * 128], Vt.b)
                eb = lambda hd, c=c: (EB1.t[:, hd, c:c + 1], EB1.b)
                oo = None
                if not is_ctx:
                    ot = o1t[0]
                    oo = {'aT': aT, 'o': ot}
                scan_chunk(p, nc, ps, pb, (7, 5, 6, 2, 3), qt, qe, kn, kh, vt, eb, ST, msk, c, True, oo)
                if oo is not None and c % 2 == 1:
                    ti = tiles[j][1]
                    dstd = env.d('o1')[ti * 128:(ti + 1) * 128, :]
                    p.dma('sp', lambda e, dstd=dstd, ot=ot: e.dma_start(out=dstd, in_=ot.t[:]), reads=[ot.b],
                          writes=[env.b('o1', ti)])
                yield

        interleave([gen_A(0)])
        for si, tiles in enumerate(sts):
            gC = do_supertile(si, tiles)
            gens = [gC]
            if si + 1 < len(sts):
                gens.append(gen_A(si + 1))
            interleave(gens)
        p.dma('sp', lambda e: e.dma_start(out=env.d('s_send')[:, :], in_=ST.S32[:].rearrange("p a b -> p (a b)")),
              reads=ST.b32, writes=[env.b('s_send')])
    p.barrier()


def ph_hg2(p, nc, env, ps, pb):
    with ExitStack() as st:
        sb = lambda n, s, d: st.enter_context(nc.sbuf_tensor(_uniq(n), s, d))
        wo = sb('hwo', [128, 8, D], BF16)
        wo_b = [Buf() for _ in range(2)]
        cast_load_w(p, nc, wo, env.d('hg_w_o'), D, D, wo_b, split=2)
        identb = SB(sb('identb', [128, 128], BF16))
        p.dma('pool', lambda e: e.dma_start(out=identb.t[:], in_=env.d('identf')[:, :]), writes=[identb.b])
        msk = SB(sb('msk', [128, 256], F32))
        p.dma('sp', lambda e: e.dma_start(out=msk.t[:], in_=env.d('trimask')[:, :]), writes=[msk.b])
        ng = SB(sb('ng', [128, D], F32))
        p.dma('sp', lambda e: e.dma_start(out=ng.t[:], in_=env.d('hg_ng')[0:1, :].partition_broadcast(128)), writes=[ng.b])
        G = SB(sb('G', [128, D], F32))
        load_modB(p, nc, env, G, 2 * D, 1, 0)
        ST = ScanState(sb, 'h2')
        p.dma('sp', lambda e: e.dma_start(out=ST.S32[:].rearrange("p a b -> p (a b)"), in_=env.d('s_recv')[:, :]),
              reads=[env.b('s_recv')], writes=ST.b32)
        p.op('pool', lambda e: e.tensor_copy(out=ST.Sbf[0][:], in_=ST.S32[:]), reads=ST.b32, writes=ST.bbf[0])
        R = 2
        QE = [SB(sb(f'QE{i}', [128, 8, 128], BF16)) for i in range(R)]
        KN = [SB(sb(f'KN{i}', [128, 8, 128], BF16)) for i in range(R)]
        QTT = [SB(sb(f'QTT{i}', [128, 8, 128], BF16)) for i in range(R)]
        KH = [SB(sb(f'KH{i}', [128, D], BF16)) for i in range(R)]
        VT = [SB(sb(f'VT{i}', [128, D], BF16)) for i in range(R)]
        SG = [SB(sb(f'SG{i}', [128, D], BF16)) for i in range(R)]
        EBt = [SB(sb(f'EBt{i}', [128, 8, 2], F32)) for i in range(R)]
        O1 = [SB(sb(f'O1{i}', [128, D], F32)) for i in range(R)]
        XB = [SB(sb(f'XB{i}', [128, D], F32)) for i in range(3)]
        o2 = SB(sb('o2', [128, D], F32))
        sq = SB(sb('sq', [128, D], F32))
        hs = SB(sb('hs', [128, 16], F32))
        yb = SB(sb('yb', [128, D], BF16))
        oT = SB(sb('oT', [128, 8, 128], BF16))
        aT = SB(sb('aT', [128, 512], BF16))
        tmp = SB(sb('tmp', [128, D], F32))
        p.op('pool', lambda e: e.memset(aT.t[:], 0.0), writes=[aT.b])
        psb2 = ps[:, 4, :].bitcast(BF16)
        order = list(range(NTQ - 1, -1, -1))

        def loads(n):
            t = order[n]
            r = n % R
            sl = slice(t * 128, (t + 1) * 128)
            si = t // 4
            p.dma('sp', lambda e: e.dma_start(out=QE[r].t[:], in_=env.d('d2qe')[:, :, sl].rearrange("h k n -> k h n")),
                  reads=[env.b('d2qe', si)], writes=[QE[r].b])
            p.dma('sp', lambda e: e.dma_start(out=KN[r].t[:], in_=env.d('d2kn')[:, :, sl].rearrange("h k n -> k h n")),
                  reads=[env.b('d2kn', si)], writes=[KN[r].b])
            p.dma('sp', lambda e: e.dma_start(out=QTT[r].t[:], in_=env.d('d2qt')[:, :, sl].rearrange("h k n -> k h n")),
                  reads=[env.b('d2qt', si)], writes=[QTT[r].b])
            p.dma('sp', lambda e: e.dma_start(out=KH[r].t[:], in_=env.d('d2kh')[sl, :]), reads=[env.b('d2kh', si)], writes=[KH[r].b])
            p.dma('sp', lambda e: e.dma_start(out=VT[r].t[:], in_=env.d('vtok')[sl, :]), reads=[env.b('vtok', si)], writes=[VT[r].b])
            p.dma('sp', lambda e: e.dma_start(out=SG[r].t[:], in_=env.d('sgtok')[sl, :]), reads=[env.b('sgtok', si)], writes=[SG[r].b])
            p.dma('sp', lambda e: e.dma_start(out=EBt[r].t[:], in_=env.d('d2eb')[:, :, 2 * t:2 * t + 2]),
                  reads=[env.b('d2eb', si)], writes=[EBt[r].b])
            p.dma('sp', lambda e: e.dma_start(out=O1[r].t[:], in_=env.d('o1')[sl, :]), reads=[env.b('o1', t)], writes=[O1[r].b])
            xb_ = XB[n % 3]
            p.dma('sp', lambda e: e.dma_start(out=xb_.t[:], in_=env.d('xb')[sl, :]), reads=[env.b('xb', t)], writes=[xb_.b])

        loads(0)

        NG = [SB(sb(f'NG{i}', [128, D], F32)) for i in range(R)]
        pend = []

        def pe_readout(t, xb_):
            for j in range(8):
                p.op('pe', lambda e, j=j: e.transpose(psb2[:, j * 128:(j + 1) * 128], yb.t[:, j * 128:(j + 1) * 128], identb.t[:]),
                     reads=[yb.b, identb.b], writes=[pb[4]], inc=(j == 7))
            p.op('act', lambda e: e.activation(out=oT.t[:].rearrange("p a b -> p (a b)"), in_=psb2[:, 0:1024], func=AF.Identity),
                 reads=[pb[4]], writes=[oT.b])
            for nb in range(2):
                bank = nb
                for kc in range(8):
                    p.op('pe', lambda e, bank=bank, kc=kc, nb=nb: e.matmul(
                        ps[:, bank, :], lhsT=oT.t[:, kc, :], rhs=wo[:, kc, nb * 512:(nb + 1) * 512], start=(kc == 0), stop=(kc == 7)),
                        reads=[oT.b, wo_b[nb]], writes=[pb[bank]], inc=(kc == 7))
                p.op('dve', lambda e, bank=bank, nb=nb: e.tensor_tensor(
                    out=tmp.t[:, nb * 512:(nb + 1) * 512], in0=ps[:, bank, :], in1=G.t[:, nb * 512:(nb + 1) * 512], op=ALU.mult),
                    reads=[pb[bank], G.b], writes=[tmp.b])
                p.op('pool', lambda e, nb=nb: e.tensor_tensor(
                    out=tmp.t[:, nb * 512:(nb + 1) * 512], in0=tmp.t[:, nb * 512:(nb + 1) * 512],
                    in1=xb_.t[:, nb * 512:(nb + 1) * 512], op=ALU.add), reads=[tmp.b, xb_.b], writes=[tmp.b])
            dst = env.d('xc')[t * 128:(t + 1) * 128, :]
            p.dma('sp', lambda e, dst=dst: e.dma_start(out=dst, in_=tmp.t[:]), reads=[tmp.b], writes=[env.b('xc', t)])

        def do_tile(n, t):
            r = n % R
            if n + 1 < len(order):
                loads(n + 1)
            p.op('pool', lambda e: e.tensor_tensor(out=NG[r].t[:], in0=SG[r].t[:], in1=ng.t[:], op=ALU.mult),
                 reads=[SG[r].b, ng.b], writes=[NG[r].b])
            for par in (1, 0):
                pbs = 64 * par
                qt = lambda hd, par=par: (QTT[r].t[:, hd, par * 64:(par + 1) * 64], QTT[r].b)
                qe = lambda hd, par=par: (QE[r].t[:, hd, par * 64:(par + 1) * 64], QE[r].b)
                kn = lambda hd, par=par: (KN[r].t[:, hd, par * 64:(par + 1) * 64], KN[r].b)
                kh = lambda hd, pbs=pbs: (KH[r].t[pbs:pbs + 64, hd * 128:(hd + 1) * 128], KH[r].b)
                vt = lambda hd, pbs=pbs: (VT[r].t[pbs:pbs + 64, hd * 128:(hd + 1) * 128], VT[r].b)
                eb = lambda hd, par=par: (EBt[r].t[:, hd, par:par + 1], EBt[r].b)
                scan_chunk(p, nc, ps, pb, (7, 5, 6, 2, 3), qt, qe, kn, kh, vt, eb, ST, msk, par, False,
                           {'aT': aT, 'o': o2, 'add': O1[r]})
                if par == 1 and pend:
                    pe_readout(*pend.pop())
            for hd in range(8):
                p.op('act', lambda e, hd=hd: e.activation(out=sq.t[:, hd * 128:(hd + 1) * 128], in_=o2.t[:, hd * 128:(hd + 1) * 128],
                                                          func=AF.Square, accum_out=hs.t[:, hd:hd + 1]),
                     reads=[o2.b], writes=[sq.b, hs.b])
            p.op('act', lambda e: e.activation(out=hs.t[:, 0:8], in_=hs.t[:, 0:8], func=AF.Ln, scale=1.0 / 128, bias=EPS),
                 reads=[hs.b], writes=[hs.b])
            p.op('act', lambda e: e.activation(out=hs.t[:, 8:16], in_=hs.t[:, 0:8], func=AF.Exp, scale=-0.5), reads=[hs.b], writes=[hs.b])
            p.op('dve', lambda e: e.tensor_tensor(
                out=sq.t[:].rearrange("p (a b) -> p a b", b=128), in0=o2.t[:].rearrange("p (a b) -> p a b", b=128),
                in1=hs.t[:, 8:16].unsqueeze(2).to_broadcast([128, 8, 128]), op=ALU.mult), reads=[o2.b, hs.b], writes=[sq.b])
            p.op('dve', lambda e: e.tensor_tensor(out=yb.t[:], in0=sq.t[:], in1=NG[r].t[:], op=ALU.mult),
                 reads=[sq.b, NG[r].b], writes=[yb.b])
            pend.append((t, XB[n % 3]))

        for n, t in enumerate(order):
            do_tile(n, t)
        while pend:
            pe_readout(*pend.pop())
    p.barrier()


def ph_xchg(p, nc, env, ps, pb):
    with ExitStack() as st:
        sb = lambda n, s, d: st.enter_context(nc.sbuf_tensor(_uniq(n), s, d))
        snd = env.d('s_send')
        gat = env.d('s_gath')
        p.collective(lambda e: e.collective_compute("AllGather", ALU.bypass, replica_groups=[[0, 1], [2, 3], [4, 5], [6, 7]],
                                                    ins=[snd.opt()], outs=[gat.opt()]),
                     reads=[env.b('s_send')], writes=[env.b('s_gath')])
        g = SB(sb('xg', [128, 2, D], F32))
        sel = SB(sb('xsel', [128, 2], F32))
        r = SB(sb('xr', [128, D], F32))
        p.dma('sp', lambda e: e.dma_start(out=g.t[:], in_=gat.rearrange("(r p) n -> p r n", p=128)), reads=[env.b('s_gath')],
              writes=[g.b])
        p.dma('sp', lambda e: e.dma_start(out=sel.t[:], in_=env.d('sel')[:, :]), writes=[sel.b])
        p.op('dve', lambda e: e.tensor_scalar(out=r.t[:], in0=g.t[:, 0, :], scalar1=sel.t[:, 0:1], scalar2=None, op0=ALU.mult),
             reads=[g.b, sel.b], writes=[r.b])
        p.op('dve', lambda e: e.scalar_tensor_tensor(out=r.t[:], in0=g.t[:, 1, :], scalar=sel.t[:, 1:2], in1=r.t[:], op0=ALU.mult,
                                                     op1=ALU.add), reads=[g.b, sel.b, r.b], writes=[r.b])
        p.dma('sp', lambda e: e.dma_start(out=env.d('s_recv')[:, :], in_=r.t[:]), reads=[r.b], writes=[env.b('s_recv')])
    p.barrier()


def _coords(tile, half):
    l = tile * 128 + np.arange(128)
    g = l if half == 0 else 8191 - l
    return g // 64, g % 64


def rpb_tables(rpb, half):
    flat = np.concatenate([rpb.reshape(16, 15 * 31), np.full((16, 1), PADV, np.float32)], axis=1)

    def table(t, kts):
        qr, qc = _coords(t, half)
        out = np.empty((128, 16, len(kts), 128), np.float32)
        rs = np.clip(qr - 4, 0, 120)
        cs = np.clip(qc - 8, 0, 48)
        for a, kt in enumerate(kts):
            kr, kc = _coords(kt, half)
            valid = ((kr[:, None] >= rs[None, :]) & (kr[:, None] <= rs[None, :] + 7) &
                     (kc[:, None] >= cs[None, :]) & (kc[:, None] <= cs[None, :] + 15))
            idx = (kr[:, None] - qr[None, :] + 7) * 31 + (kc[:, None] - qc[None, :] + 15)
            idx = np.where(valid, idx, 465)
            out[:, :, a, :] = np.transpose(flat[:, idx], (1, 0, 2))
        return out.reshape(128, -1)

    ebg = table(10, [8, 9, 10, 11, 12])
    ebs = np.stack([table(0, [0, 1, 2, 3]), table(1, [0, 1, 2, 3])], axis=0)
    return ebg, ebs


def host_consts():
    tri = np.zeros((128, 256), np.float32)
    for blk in range(2):
        s = np.arange(64)[:, None]
        t = np.arange(64)[None, :]
        tri[blk * 64:(blk + 1) * 64, blk * 64:(blk + 1) * 64] = (s <= t)
        tri[blk * 64:(blk + 1) * 64, 128 + blk * 64:128 + (blk + 1) * 64] = (s >= t)
    rst = np.ones((128, 512), np.float32)
    rst[:, 0::64] = 0.0
    return tri, rst


def prep_inputs(inp):
    f = lambda a: np.ascontiguousarray(a, dtype=np.float32)
    shared = {
        'ada_w': f(inp['ada_w']),
        'ada_b': f(inp['ada_b'].reshape(1, -1)),
        'n1g': f(inp['norm1_g'].reshape(2, 8, 128).transpose(2, 0, 1).reshape(128, 16)),
        'n2g': f(inp['norm2_g'].reshape(2, 8, 128).transpose(2, 0, 1).reshape(128, 16)),
        'identf': np.eye(128, dtype=np.float32),
        'w_qkv': f(inp['na_w_qkv'][0]),
        'w_o0': f(inp['na_w_o'][0]),
        'qkgain': f(np.stack([np.tile(inp['na_q_gain'][0], 2), np.tile(inp['na_k_gain'][0], 2)], axis=1)),
        'ffn_w_in': f(inp['ffn_w_in']),
        'ffn_w_out': f(inp['ffn_w_out']),
        'hg_w_o': f(inp['hg_w_o'][0]),
        'hg_ng': f(np.tile(inp['hg_norm_g'][0], 8).reshape(1, D)),
    }
    shared['trimask'], shared['rstmask'] = host_consts()
    hw = inp['hg_w_in'][0]
    hw_sw = f(np.concatenate([hw[:, 0:3 * D], hw[:, 4 * D:5 * D], hw[:, 3 * D:4 * D]], axis=1))
    low = inp['hg_lower'].reshape(2, 2, 8, 128)
    per_half = []
    for half in range(2):
        ebg, ebs = rpb_tables(f(inp['na_rpb'][0]), half)
        lo = low if half == 0 else low[:, ::-1]
        per_half.append({
            'ebg': ebg, 'ebs': ebs,
            'hg_w_in': f(hw) if half == 0 else hw_sw,
            'hg_low': f(lo.transpose(3, 0, 1, 2).reshape(128, 32)),
        })
    maps = []
    for b in range(4):
        for half in range(2):
            xs = inp['x'][b] if half == 0 else inp['x'][b, ::-1]
            cs = inp['ctx'][b] if half == 0 else inp['ctx'][b, ::-1]
            m = dict(shared)
            m.update(per_half[half])
            m['xin'] = f(xs[0:NTK * 128])
            m['ctxin'] = f(cs)
            m['sel'] = f(np.tile(np.array([[1.0, 0.0]] if half == 1 else [[0.0, 1.0]], np.float32), (128, 1)))
            m['csil'] = f(np.stack([inp['c'][b].reshape(8, 128).T, inp['c_ctx'].reshape(8, 128).T], axis=2).reshape(128, 16))
            maps.append(m)
    return maps


PHASES = {'ada': ph_ada, 'att0': ph_att0, 'ffn0': make_ffn(0, 'xa', 'xb', True), 'ffn1': make_ffn(1, 'xc', 'out', False), 'hg1': ph_hg1, 'hg2': ph_hg2, 'xchg': ph_xchg}


def build(phases, ext_in, ext_out):
    nc = bass.Bass("TRN2", target_bir_lowering=False)
    env = Env(nc, set(ext_in), set(ext_out), SHAPES)
    with ExitStack() as st:
        p = Prog(nc, st)
        ps = st.enter_context(nc.psum_tensor("ps", [128, 8, 512], F32))
        pb = [Buf(excl=True) for _ in range(8)]
        pre_st = None
        for i, ph in enumerate(phases):
            if ph == 'ada' and i + 1 < len(phases) and phases[i + 1] == 'att0':
                pre_st = ExitStack()
                env.pre['att0_pre'] = att0_prefetch(p, nc, env, pre_st)
            if False and ph == 'hg2' and i + 1 < len(phases) and phases[i + 1] == 'ffn1':
                pre_st = ExitStack()
                env.pre['ffn1_pre'] = ffn_prefetch(p, nc, env, pre_st, 1)
            PHASES[ph](p, nc, env, ps, pb)
            if ph in ('att0', 'ffn1') and pre_st is not None:
                pre_st.close()
                pre_st = None
        p.finish()
    return nc, env, p


def run_launch(phases, ext_in_maps, ext_out, cores):
    nc, env, p = build(phases, ext_in_maps[0].keys(), ext_out)
    used = set(env.t.keys())
    in_maps = [{k: v for k, v in m.items() if k in used} for m in ext_in_maps]
    res = run_bass_kernel_spmd(nc, in_maps, core_ids=list(range(len(cores))))
    return res


HOST_IN = ['xin', 'ctxin', 'sel', 'csil', 'ada_w', 'ada_b', 'n1g', 'n2g', 'identf', 'w_qkv', 'w_o0', 'qkgain', 'ebg', 'ebs',
           'ffn_w_in', 'ffn_w_out', 'hg_w_in', 'hg_w_o', 'hg_low', 'hg_ng', 'trimask', 'rstmask']
L1_OUT = ['modrow', 'xb', 'o1', 'vtok', 'sgtok', 'd2kh', 'd2qe', 'd2kn', 'd2qt', 'd2eb', 's_send']


def _launch(phases, maps, ext_out):
    nc, env, _ = build(phases, maps[0].keys(), ext_out)
    used = set(env.t.keys())
    in_maps = [{k: v for k, v in m.items() if k in used} for m in maps]
    res = run_bass_kernel_spmd(nc, in_maps, core_ids=list(range(len(maps))))
    return res.results


def kernel(**inputs):
    maps = prep_inputs(inputs)
    r = _launch(['ada', 'att0', 'ffn0', 'hg1', 'xchg', 'hg2', 'ffn1'], maps, ['out'])
    out = np.empty((4, 8192, D), np.float32)
    for b in range(4):
        out[b, 0:NTOK] = np.asarray(r[2 * b]['out'])
        out[b, NTOK:] = np.asarray(r[2 * b + 1]['out'])[::-1]
    return out
```
